# Optimizing a Trainium2 kernel written in Bass

```python
import jax
import jax.numpy as jnp
from jax import lax
import numpy as np

D_MODEL = 2048
BATCH = 2
SEQ = 8192
DEPTH = 2

MEM_LEN = 256
N_BRANCH = 4
MIX_WIDTH = D_MODEL // 4

NSA_HEAD_DIM = 64
NSA_HEADS = MIX_WIDTH // NSA_HEAD_DIM
NSA_KV_GROUPS = 2
NSA_GQA = NSA_HEADS // NSA_KV_GROUPS
NSA_N_BRANCH = 3
NSA_CMP_LEN = 32
NSA_CMP_STRIDE = 16
NSA_SEL_BLOCK = 64
NSA_N_SEL = 16
NSA_WINDOW = 512
NSA_Q_BLOCK = 128
NSA_FORCE_SCORE = 1e6

HGRN_HEAD_DIM = 128
HGRN_HEADS = MIX_WIDTH // HGRN_HEAD_DIM
HGRN_CHUNK = 64
HGRN_F_MIN = 1e-6

POOL_WINDOWS = (2, 4, 8, 16)
POOL_GROUPS = 4
POOL_GROUP_DIM = MIX_WIDTH // POOL_GROUPS

SG_CHUNK = 128
SG_HEADS = 4
SG_HEAD_DIM = MIX_WIDTH // SG_HEADS

X_HEADS = 4
X_HEAD_DIM = 128
X_WIDTH = X_HEADS * X_HEAD_DIM

D_FF = 5632
CONV_WIDTH = 3

LN_EPS = 1e-5
NEG_INF = -1e30
DEEPNORM_ALPHA = (2 * DEPTH) ** 0.25
DEEPNORM_BETA = (8 * DEPTH) ** -0.25

IN_SPLITS = (
    MIX_WIDTH,
    NSA_N_BRANCH * 2 * NSA_KV_GROUPS * NSA_HEAD_DIM,
    NSA_HEADS * NSA_N_BRANCH,
    4 * MIX_WIDTH,
    MIX_WIDTH,
    2 * MIX_WIDTH,
    N_BRANCH * D_MODEL,
)
W_IN_COLS = sum(IN_SPLITS)

kernel_name = 'hybrid_nsa_hgrn2_pool_sgu_deepnorm'


def layer_norm(x, g, b):
    xf = x.astype(jnp.float32)
    mu = jnp.mean(xf, -1, keepdims=True)
    var = jnp.mean(jnp.square(xf - mu), -1, keepdims=True)
    y = (xf - mu) * lax.rsqrt(var + LN_EPS) * g.astype(jnp.float32) + b.astype(jnp.float32)
    return y.astype(x.dtype)


def rms_norm(x, g):
    xf = x.astype(jnp.float32)
    y = xf * lax.rsqrt(jnp.mean(jnp.square(xf), -1, keepdims=True) + LN_EPS) * g.astype(jnp.float32)
    return y.astype(x.dtype)


def alibi_slopes(n_heads):
    return jnp.asarray(2.0 ** (-8.0 * np.arange(1, n_heads + 1) / n_heads), dtype=jnp.float32)


def masked_softmax(s, mask, axis=-1):
    p = jax.nn.softmax(jnp.where(mask, s, NEG_INF), axis=axis)
    return jnp.where(mask, p, 0.0)


def cmp_to_sel_matrix(n_cmp, n_blk):
    c0 = np.arange(n_cmp)[:, None] * NSA_CMP_STRIDE
    s0 = np.arange(n_blk)[None, :] * NSA_SEL_BLOCK
    ov = np.clip(np.minimum(c0 + NSA_CMP_LEN, s0 + NSA_SEL_BLOCK) - np.maximum(c0, s0), 0, None)
    return jnp.asarray(ov / NSA_CMP_LEN, dtype=jnp.float32)


def nsa_mixer(q, kv, gate_logits, cmp_pos, cmp_w):
    B, S, _ = q.shape
    G, J, dh, T = NSA_KV_GROUPS, NSA_GQA, NSA_HEAD_DIM, NSA_Q_BLOCK
    f32 = jnp.float32
    q = q.reshape(B, S, G, J, dh) * dh ** -0.5
    kv = kv.reshape(B, S, NSA_N_BRANCH, 2, G, dh)
    gates = jax.nn.sigmoid(gate_logits.reshape(B, S, G, J, NSA_N_BRANCH))
    slopes = alibi_slopes(NSA_HEADS).reshape(G, J)

    n_cmp = (S - NSA_CMP_LEN) // NSA_CMP_STRIDE + 1
    blk = np.arange(n_cmp)[:, None] * NSA_CMP_STRIDE + np.arange(NSA_CMP_LEN)[None, :]

    def compress(t, pos, w):
        tb = t[:, blk] + pos[None, None, :, None, :]
        return jnp.einsum('bclgd,lde->bcge', tb, w)

    k_cmp = compress(kv[:, :, 0, 0], cmp_pos[0], cmp_w[0])
    v_cmp = compress(kv[:, :, 0, 1], cmp_pos[1], cmp_w[1])
    cmp_end = jnp.arange(n_cmp) * NSA_CMP_STRIDE + (NSA_CMP_LEN - 1)

    n_blk = S // NSA_SEL_BLOCK
    n_sel = min(NSA_N_SEL, n_blk)
    sel_map = cmp_to_sel_matrix(n_cmp, n_blk)

    def to_blocks(t):
        return t.reshape(B, n_blk, NSA_SEL_BLOCK, G, dh).transpose(0, 1, 3, 2, 4)

    k_slc = to_blocks(kv[:, :, 1, 0])
    v_slc = to_blocks(kv[:, :, 1, 1])
    blk_ids = jnp.arange(n_blk)
    in_blk = jnp.arange(NSA_SEL_BLOCK)
    b_idx = jnp.arange(B)[:, None, None, None]
    g_idx = jnp.arange(G)[None, None, :, None]

    pad = jnp.zeros((B, NSA_WINDOW, G, dh), kv.dtype)
    k_win = jnp.concatenate([pad, kv[:, :, 2, 0]], axis=1)
    v_win = jnp.concatenate([pad, kv[:, :, 2, 1]], axis=1)

    def query_block(qb):
        q0 = qb * T
        t = q0 + jnp.arange(T)
        qt = lax.dynamic_slice_in_dim(q, q0, T, axis=1)
        gt = lax.dynamic_slice_in_dim(gates, q0, T, axis=1)
        sl = slopes[None, None, :, :, None]

        dist_c = (t[:, None] - cmp_end[None, :]).astype(f32)
        mask_c = (dist_c >= 0)[None, :, None, None, :]
        s_c = jnp.einsum('btgjd,bcgd->btgjc', qt, k_cmp).astype(f32) - sl * dist_c[None, :, None, None, :]
        p_c = masked_softmax(s_c, mask_c)
        o_c = jnp.einsum('btgjc,bcgd->btgjd', p_c.astype(v_cmp.dtype), v_cmp)

        imp = jnp.einsum('btgjc,cn->btgn', p_c, sel_map)
        cur = (t // NSA_SEL_BLOCK)[:, None]
        forced = (blk_ids[None] == 0) | (blk_ids[None] == cur) | (blk_ids[None] == cur - 1)
        future = blk_ids[None] * NSA_SEL_BLOCK > t[:, None]
        imp = jnp.where(forced[None, :, None], NSA_FORCE_SCORE,
                        jnp.where(future[None, :, None], NEG_INF, imp))
        _, idx = lax.top_k(imp, n_sel)
        k_s = k_slc[b_idx, idx, g_idx]
        v_s = v_slc[b_idx, idx, g_idx]
        pos_s = idx[..., None] * NSA_SEL_BLOCK + in_blk
        dist_s = (t[None, :, None, None, None] - pos_s).astype(f32)[:, :, :, None]
        mask_s = dist_s >= 0
        s_s = (jnp.einsum('btgjd,btgnkd->btgjnk', qt, k_s).astype(f32)
               - slopes[None, None, :, :, None, None] * dist_s)
        p_s = masked_softmax(s_s, mask_s, axis=(-2, -1))
        o_s = jnp.einsum('btgjnk,btgnkd->btgjd', p_s.astype(v_s.dtype), v_s)

        k_w = lax.dynamic_slice_in_dim(k_win, q0, T + NSA_WINDOW, axis=1)
        v_w = lax.dynamic_slice_in_dim(v_win, q0, T + NSA_WINDOW, axis=1)
        pos_w = q0 - NSA_WINDOW + jnp.arange(T + NSA_WINDOW)
        dist_w = t[:, None] - pos_w[None, :]
        mask_w = ((pos_w[None] >= 0) & (dist_w >= 0) & (dist_w < NSA_WINDOW))[None, :, None, None, :]
        s_w = (jnp.einsum('btgjd,bsgd->btgjs', qt, k_w).astype(f32)
               - sl * dist_w.astype(f32)[None, :, None, None, :])
        p_w = masked_softmax(s_w, mask_w)
        o_w = jnp.einsum('btgjs,bsgd->btgjd', p_w.astype(v_w.dtype), v_w)

        o = gt[..., 0:1] * o_c + gt[..., 1:2] * o_s + gt[..., 2:3] * o_w
        return o.reshape(B, T, NSA_HEADS * dh)

    out = lax.map(query_block, jnp.arange(S // T))
    return out.transpose(1, 0, 2, 3).reshape(B, S, NSA_HEADS * dh)


def hgrn2_mixer(q, f_logit, i, g, lb, norm_g):
    B, S, _ = q.shape
    H, dk, C = HGRN_HEADS, HGRN_HEAD_DIM, HGRN_CHUNK
    n = S // C
    f32 = jnp.float32
    lbh = lb.reshape(H, dk).astype(f32)
    z = f_logit.reshape(B, S, H, dk).astype(f32)
    f = jnp.maximum(lbh + (1.0 - lbh) * jax.nn.sigmoid(z), HGRN_F_MIN)
    log_f = jnp.log(f)
    k = 1.0 - f

    def to_chunks(a):
        return a.reshape(B, n, C, H, dk).transpose(1, 0, 3, 2, 4)

    xs = (to_chunks(q.astype(f32)), to_chunks(k), to_chunks(i.astype(f32)), to_chunks(log_f))
    causal = jnp.tril(jnp.ones((C, C), dtype=bool))[:, :, None]

    def step(state, inp):
        qq, kk, vv, ll = inp
        b = jnp.cumsum(ll, axis=2)
        o_inter = jnp.einsum('bhtk,bhkv->bhtv', qq * jnp.exp(b), state)
        decay = jnp.exp(jnp.where(causal, b[:, :, :, None, :] - b[:, :, None, :, :], NEG_INF))
        a = jnp.einsum('bhtk,bhtsk,bhsk->bhts', qq, decay, kk)
        o = o_inter + jnp.einsum('bhts,bhsv->bhtv', a, vv)
        b_last = b[:, :, -1:, :]
        state = (jnp.exp(b_last[:, :, 0, :, None]) * state
                 + jnp.einsum('bhsk,bhsv->bhkv', kk * jnp.exp(b_last - b), vv))
        return state, o

    s0 = jnp.zeros((B, H, dk, dk), f32)
    _, o = lax.scan(step, s0, xs)
    o = o.transpose(1, 0, 3, 2, 4).reshape(B, S, H, dk)
    o = rms_norm(o, norm_g.reshape(H, dk)).reshape(B, S, H * dk)
    return (o * jax.nn.silu(g.astype(f32))).astype(q.dtype)


def pool_mixer(c, w, scale):
    B, S, _ = c.shape
    cf = c.astype(jnp.float32).reshape(B, S, POOL_GROUPS, POOL_GROUP_DIM)
    cs = jnp.concatenate([jnp.zeros((B, 1, POOL_GROUPS, POOL_GROUP_DIM), jnp.float32),
                          jnp.cumsum(cf, axis=1)], axis=1)
    t = jnp.arange(S)
    outs = []
    for gi, win in enumerate(POOL_WINDOWS):
        lo = jnp.maximum(t + 1 - win, 0)
        cnt = jnp.minimum(t + 1, win).astype(jnp.float32)
        mean = (cs[:, 1:, gi] - cs[:, lo, gi]) / cnt[None, :, None]
        outs.append(mean - cf[:, :, gi])
    p = jnp.stack(outs, axis=2)
    y = jnp.einsum('bsgc,gcd->bsgd', p, w.astype(jnp.float32)).reshape(B, S, MIX_WIDTH)
    return (y * scale.astype(jnp.float32)).astype(c.dtype)


def sgu_mixer(u, v, ln_g, ln_b, ws, bs):
    B, S, _ = u.shape
    n = S // SG_CHUNK
    v = layer_norm(jax.nn.gelu(v), ln_g, ln_b).reshape(B, n, SG_CHUNK, SG_HEADS, SG_HEAD_DIM)
    mask = jnp.tril(jnp.ones((SG_CHUNK, SG_CHUNK), dtype=bool))
    wm = jnp.where(mask, ws, 0.0)
    vs = jnp.einsum('gqp,bnpgc->bnqgc', wm, v) + bs.T[None, None, :, :, None]
    return jax.nn.gelu(u) * vs.reshape(B, S, MIX_WIDTH)


def cross_attention(x, mem_n, wq, wk, wv, wo):
    B, S, _ = x.shape
    M = mem_n.shape[1]
    q = (x @ wq).reshape(B, S, X_HEADS, X_HEAD_DIM)
    k = (mem_n @ wk).reshape(B, M, X_HEADS, X_HEAD_DIM)
    v = (mem_n @ wv).reshape(B, M, X_HEADS, X_HEAD_DIM)
    s = jnp.einsum('bshd,bmhd->bhsm', q, k).astype(jnp.float32) * X_HEAD_DIM ** -0.5
    p = jax.nn.softmax(s, axis=-1).astype(v.dtype)
    o = jnp.einsum('bhsm,bmhd->bshd', p, v).reshape(B, S, X_WIDTH)
    return o @ wo


def conv_ffn(x, w_in, conv_w, conv_b, w_out):
    h = x @ w_in
    gate, up = jnp.split(h, 2, axis=-1)
    gate = lax.conv_general_dilated(
        gate, conv_w[:, None, :], window_strides=(1,), padding=((CONV_WIDTH - 1, 0),),
        dimension_numbers=('NWC', 'WIO', 'NWC'), feature_group_count=D_FF) + conv_b
    return (jax.nn.silu(gate) * up) @ w_out


def setup_inputs(seed: int = 0) -> dict:
    key = jax.random.key(seed)
    ks = iter(jax.random.split(key, 32))
    L, D = DEPTH, D_MODEL
    f32 = jnp.float32

    def nrm(shape, scale):
        return jax.random.normal(next(ks), shape, f32) * scale

    def gain(shape):
        return 1.0 + nrm(shape, 0.05)

    def bias(shape):
        return nrm(shape, 0.02)

    beta = DEEPNORM_BETA
    return {
        'x': nrm((BATCH, SEQ, D), 1.0),
        'mem': nrm((BATCH, MEM_LEN, D), 1.0),
        'w_in': nrm((L, D, W_IN_COLS), D ** -0.5),
        'nsa_cmp_pos': nrm((L, 2, NSA_CMP_LEN, NSA_HEAD_DIM), 0.1),
        'nsa_cmp_w': nrm((L, 2, NSA_CMP_LEN, NSA_HEAD_DIM, NSA_HEAD_DIM), (NSA_CMP_LEN * NSA_HEAD_DIM) ** -0.5),
        'hgrn_lb_logits': nrm((L, MIX_WIDTH), 1.0),
        'hgrn_norm_g': gain((L, MIX_WIDTH)),
        'pool_w': nrm((L, POOL_GROUPS, POOL_GROUP_DIM, POOL_GROUP_DIM), POOL_GROUP_DIM ** -0.5),
        'pool_scale': gain((L, MIX_WIDTH)),
        'sg_ln_g': gain((L, MIX_WIDTH)),
        'sg_ln_b': bias((L, MIX_WIDTH)),
        'sg_w': nrm((L, SG_HEADS, SG_CHUNK, SG_CHUNK), SG_CHUNK ** -0.5),
        'sg_b': gain((L, SG_HEADS, SG_CHUNK)),
        'w_branch': nrm((L, N_BRANCH, MIX_WIDTH, D), MIX_WIDTH ** -0.5),
        'w_mix_out': nrm((L, D, D), beta * D ** -0.5),
        'ln_mix_g': gain((L, D)),
        'ln_mix_b': bias((L, D)),
        'mem_ln_g': gain((D,)),
        'mem_ln_b': bias((D,)),
        'xattn_q': nrm((L, D, X_WIDTH), D ** -0.5),
        'xattn_k': nrm((L, D, X_WIDTH), D ** -0.5),
        'xattn_v': nrm((L, D, X_WIDTH), beta * D ** -0.5),
        'xattn_o': nrm((L, X_WIDTH, D), beta * X_WIDTH ** -0.5),
        'ln_x_g': gain((L, D)),
        'ln_x_b': bias((L, D)),
        'ffn_in': nrm((L, D, 2 * D_FF), D ** -0.5),
        'ffn_conv_w': nrm((L, CONV_WIDTH, D_FF), CONV_WIDTH ** -0.5),
        'ffn_conv_b': bias((L, D_FF)),
        'ffn_out': nrm((L, D_FF, D), beta * D_FF ** -0.5),
        'ln_ffn_g': gain((L, D)),
        'ln_ffn_b': bias((L, D)),
    }


def reference(x, mem, w_in, nsa_cmp_pos, nsa_cmp_w, hgrn_lb_logits, hgrn_norm_g, pool_w, pool_scale,
              sg_ln_g, sg_ln_b, sg_w, sg_b, w_branch, w_mix_out, ln_mix_g, ln_mix_b, mem_ln_g, mem_ln_b,
              xattn_q, xattn_k, xattn_v, xattn_o, ln_x_g, ln_x_b, ffn_in, ffn_conv_w, ffn_conv_b, ffn_out,
              ln_ffn_g, ln_ffn_b):
    B, S, D = x.shape
    mem_n = layer_norm(mem, mem_ln_g, mem_ln_b)
    sm = jax.nn.softmax(hgrn_lb_logits.astype(jnp.float32), axis=0)
    lower_bounds = jnp.cumsum(sm, axis=0) - sm[0]
    split_at = [int(v) for v in np.cumsum(IN_SPLITS)[:-1]]

    for l in range(DEPTH):
        h = x @ w_in[l]
        a_q, a_kv, a_g, b_in, c_in, d_in, merge = jnp.split(h, split_at, axis=-1)
        o_a = nsa_mixer(a_q, a_kv, a_g, nsa_cmp_pos[l], nsa_cmp_w[l])
        b_q, b_f, b_i, b_g = jnp.split(b_in, 4, axis=-1)
        o_b = hgrn2_mixer(b_q, b_f, b_i, b_g, lower_bounds[l], hgrn_norm_g[l])
        o_c = pool_mixer(c_in, pool_w[l], pool_scale[l])
        d_u, d_v = jnp.split(d_in, 2, axis=-1)
        o_d = sgu_mixer(d_u, d_v, sg_ln_g[l], sg_ln_b[l], sg_w[l], sg_b[l])
        branches = jnp.stack([o_a, o_b, o_c, o_d], axis=2)
        gates = jax.nn.sigmoid(merge.reshape(B, S, N_BRANCH, D))
        y = jnp.einsum('bsnw,nwd->bsnd', branches, w_branch[l])
        mixed = jnp.sum(gates * y, axis=2) @ w_mix_out[l]
        x = layer_norm(DEEPNORM_ALPHA * x + mixed, ln_mix_g[l], ln_mix_b[l])

        xa = cross_attention(x, mem_n, xattn_q[l], xattn_k[l], xattn_v[l], xattn_o[l])
        x = layer_norm(DEEPNORM_ALPHA * x + xa, ln_x_g[l], ln_x_b[l])

        ff = conv_ffn(x, ffn_in[l], ffn_conv_w[l], ffn_conv_b[l], ffn_out[l])
        x = layer_norm(DEEPNORM_ALPHA * x + ff, ln_ffn_g[l], ln_ffn_b[l])
    return x
```

```python
import contextlib
import numpy as np
import concourse.bass as bass
import concourse.mybir as mybir
from concourse.bass_utils import run_bass_kernel_spmd

F32 = mybir.dt.float32
BF16 = mybir.dt.bfloat16
AF = mybir.ActivationFunctionType
ALU = mybir.AluOpType
AX = mybir.AxisListType

D = 2048
S = 8192
NCORE = 8
ALPHA = 4 ** 0.25
EPS = 1e-5
DFF = 5632
NEG = -30000.0
SBUF_BASE = 16512
SBUF_CAP = 229344


class Buf:
    __slots__ = ("t", "w", "r")

    def __init__(self, t):
        self.t = t
        self.w = None
        self.r = []

    def __getitem__(self, k):
        return self.t[k]


class Sched:
    CE = ("pe", "act", "dve", "pool")
    DQ = ("sp", "act", "pool")

    def __init__(self, nc, es, ring=8):
        self.nc = nc
        self.es = es
        self.ops = {e: [] for e in ("pe", "act", "dve", "pool", "sp")}
        self.csem = {e: es.enter_context(nc.semaphore("c_" + e)) for e in self.CE}
        self.ccnt = {e: 0 for e in self.CE}
        self.ring = ring
        self.dsem = {q: [es.enter_context(nc.semaphore("d_%s%d" % (q, i))) for i in range(ring)] for q in self.DQ}
        self.dcnt = {q: 0 for q in self.DQ}
        self.dtok = {q: [None] * ring for q in self.DQ}
        self.seen = {e: {} for e in self.ops}
        self.n = 0

    def buf(self, name, shape, dt, psum=False):
        if psum:
            t = self.es.enter_context(self.nc.psum_tensor(name, shape, dt))
            return Buf(t)
        n = 1
        for d_ in shape[1:]:
            n *= d_
        nbytes = n * (2 if dt == BF16 else 4)
        nbytes = (nbytes + 63) // 64 * 64
        self.uid = getattr(self, "uid", 0) + 1
        off = getattr(self, "off", SBUF_BASE)
        assert off + nbytes <= SBUF_CAP, ("SBUF overflow", name, off, nbytes)
        t = self.nc.alloc_sbuf_tensor_at("%s_%d" % (name, self.uid), list(shape), dt, offset=off)
        self.off = off + nbytes
        self.peak = max(getattr(self, "peak", 0), self.off)
        return Buf(t)

    def mark(self):
        return getattr(self, "off", SBUF_BASE)

    def release(self, m):
        self.barrier()
        self.off = m

    def barrier(self):
        toks = []
        for q in self.DQ:
            for t in self.dtok[q]:
                if t is not None:
                    toks.append(t)
        for e in self.CE:
            if self.ccnt[e]:
                toks.append((self.csem[e], self.ccnt[e], "c_" + e, e))
        for eng in self.ops:
            waits = []
            seen = self.seen[eng]
            for (sem, val, key, src) in toks:
                if seen.get(key, 0) >= val:
                    continue
                seen[key] = val
                waits.append((sem, val))
            if waits:
                self.ops[eng].append((waits, None, None, 0))

    def _waits(self, eng, reads, writes, extra=()):
        deps = []
        for b in reads:
            if b.w is not None:
                deps.append(b.w)
        for b in writes:
            if b.w is not None:
                deps.append(b.w)
            deps.extend(b.r)
        deps.extend(extra)
        waits = []
        seen = self.seen[eng]
        for (sem, val, key, src) in deps:
            if src == "pe" and eng == "pe":
                continue
            if seen.get(key, 0) >= val:
                continue
            seen[key] = val
            waits.append((sem, val))
        return waits

    def op(self, eng, fn, reads=(), writes=()):
        if getattr(self, 'mute', False):
            return None
        waits = self._waits(eng, reads, writes)
        self.ccnt[eng] += 1
        tok = (self.csem[eng], self.ccnt[eng], "c_" + eng, eng)
        for b in reads:
            b.r.append(tok)
        for b in writes:
            b.w = tok
            b.r = []
        self.ops[eng].append((waits, fn, self.csem[eng], 1))
        self.n += 1
        return tok

    def dma(self, q, out, in_, reads=(), writes=()):
        if getattr(self, 'mute', False):
            return None
        i = self.dcnt[q]
        slot = i % self.ring
        extra = []
        if self.dtok[q][slot] is not None:
            extra.append(self.dtok[q][slot])
        waits = self._waits(q, reads, writes, extra)
        self.dcnt[q] += 1
        sem = self.dsem[q][slot]
        tok = (sem, 16 * (i // self.ring + 1), "d_%s%d" % (q, slot), "dma_" + q)
        self.dtok[q][slot] = tok
        for b in reads:
            b.r.append(tok)
        for b in writes:
            b.w = tok
            b.r = []
        self.ops[q].append((waits, lambda e, o=out, i_=in_: e.dma_start(out=o, in_=i_), sem, 16))
        self.n += 1
        return tok

    def finish(self):
        extra = []
        for q in self.DQ:
            for t in self.dtok[q]:
                if t is not None:
                    extra.append(t)
        for e in self.CE:
            if self.ccnt[e]:
                extra.append((self.csem[e], self.ccnt[e], "c_" + e, e))
        waits = self._waits("sp", (), (), extra)
        self.ops["sp"].append((waits, None, None, 0))

    def emit(self):
        self.finish()
        nc = self.nc
        ops = self.ops

        def replay(name, e):
            for waits, fn, sem, inc in ops[name]:
                for (s_, v_) in waits:
                    e.wait_ge(s_, v_)
                if fn is not None:
                    fn(e).then_inc(sem, inc)

        with nc.Block() as block:
            @block.tensor
            def _(e):
                replay("pe", e)

            @block.scalar
            def _(e):
                replay("act", e)

            @block.vector
            def _(e):
                replay("dve", e)

            @block.gpsimd
            def _(e):
                replay("pool", e)

            @block.sync
            def _(e):
                replay("sp", e)

    def mm(self, out, lhsT, rhs, start, stop, reads, writes):
        return self.op("pe", lambda e, a=(out, lhsT, rhs, start, stop): e.matmul(a[0], a[1], a[2], start=a[3], stop=a[4]), reads, writes)

    def tr(self, out, in_, ident, reads, writes):
        return self.op("pe", lambda e, a=(out, in_, ident): e.transpose(a[0], a[1], a[2]), reads, writes)

    def act(self, out, in_, func, reads, writes, bias=None, scale=None, eng="act"):
        kw = {}
        if bias is not None:
            kw["bias"] = bias
        if scale is not None:
            kw["scale"] = scale
        return self.op("act", lambda e, a=(out, in_, func, kw): e.activation(out=a[0], in_=a[1], func=a[2], **a[3]), reads, writes)

    def tt(self, eng, out, in0, in1, op, reads, writes):
        return self.op(eng, lambda e, a=(out, in0, in1, op): e.tensor_tensor(a[0], a[1], a[2], a[3]), reads, writes)

    def ts(self, eng, out, in0, s1, s2, op0, op1, reads, writes):
        if s2 is None:
            return self.op(eng, lambda e, a=(out, in0, s1, op0): e.tensor_scalar(a[0], a[1], a[2], None, a[3]), reads, writes)
        return self.op(eng, lambda e, a=(out, in0, s1, s2, op0, op1): e.tensor_scalar(a[0], a[1], a[2], a[3], a[4], a[5]), reads, writes)

    def stt(self, eng, out, in0, scalar, in1, op0, op1, reads, writes):
        return self.op(eng, lambda e, a=(out, in0, scalar, in1, op0, op1): e.scalar_tensor_tensor(a[0], a[1], a[2], a[3], a[4], a[5]), reads, writes)

    def copy(self, eng, out, in_, reads, writes):
        if eng == "act":
            return self.op("act", lambda e, a=(out, in_): e.copy(a[0], a[1]), reads, writes)
        return self.op(eng, lambda e, a=(out, in_): e.tensor_copy(a[0], a[1]), reads, writes)

    def memset(self, eng, ap, val, writes):
        return self.op(eng, lambda e, a=(ap, val): e.memset(a[0], a[1]), (), writes)

    def rsum(self, eng, out, in_, reads, writes):
        return self.op(eng, lambda e, a=(out, in_): e.reduce_sum(a[0], a[1], AX.X), reads, writes)

    def recip(self, out, in_, reads, writes):
        return self.op("dve", lambda e, a=(out, in_): e.reciprocal(a[0], a[1]), reads, writes)


class Ring:
    def __init__(self, bufs):
        self.bufs = bufs
        self.i = 0

    def next(self):
        b = self.bufs[self.i % len(self.bufs)]
        self.i += 1
        return b


def _dram(nc, name, shape, dt=F32, kind="ExternalInput"):
    return nc.dram_tensor(name, list(shape), dt, kind=kind).ap()


TT = 2176
NB = TT // 128
TTILES = [(0, 512), (512, 512), (1024, 512), (1536, 512), (2048, 128)]
NFC = DFF // 128

R_INPUTS = {
    "xT": (D, TT), "xtok": (TT, D), "oT": (1536, TT),
    "w_uv": (128, 16, 1024), "w_mg": (16, 128, 4, 16, 128), "w_br": (16, 128, 4, 4, 128), "w_mix": (128, 16, D),
    "sgw": (128, 4, 128), "sgb": (128, 4), "sg_g": (128, 512), "sg_b": (128, 512), "trimask": (128, 128),
    "ln1_g": (128, D), "ln1_b": (128, D), "ln2_g": (128, D), "ln2_b": (128, D), "ln3_g": (128, D), "ln3_b": (128, D),
    "mem": (256, D), "memln_g": (128, D), "memln_b": (128, D),
    "wq": (128, 16, 512), "wk": (128, 16, 512), "wv": (128, 16, 512), "wo": (128, 4, D),
    "ffn_in": (NFC, 128, 2, 16, 128), "ffn_out": (128, NFC, D), "convw": (128, NFC, 4),
    "ident": (128, 128), "flag": (128, 1), "ones": (128, 128),
}


def build_R():
    nc = bass.Bass("TRN2", target_bir_lowering=False)
    I = {k: _dram(nc, k, v) for k, v in R_INPUTS.items()}
    xout = _dram(nc, "xout", (2048, D), kind="ExternalOutput")
    odT_d = _dram(nc, "odT_d", (4, 128, TT), BF16, kind="Internal")
    preT_d = _dram(nc, "preT_d", (16, 128, TT), BF16, kind="Internal")
    x1_d = _dram(nc, "x1_d", (TT, D), F32, kind="Internal")
    x1T_d = _dram(nc, "x1T_d", (16, 128, TT), BF16, kind="Internal")
    aT_d = _dram(nc, "aT_d", (4, 128, TT), BF16, kind="Internal")
    x2_d = _dram(nc, "x2_d", (TT, D), F32, kind="Internal")
    x2T_d = _dram(nc, "x2T_d", (16, 128, TT), BF16, kind="Internal")
    act_d = _dram(nc, "act_d", (NFC, 128, TT), BF16, kind="Internal")
    ffo_d = _dram(nc, "ffo_d", (TT, D), F32, kind="Internal")
    with contextlib.ExitStack() as es:
        s = Sched(nc, es)
        B = s.buf
        ident = B("ident", [128, 128], BF16)
        ones = B("ones", [128, 128], BF16)
        flag = B("flag", [128, 1], F32)
        kT = B("kT", [128, 4, 256], BF16)
        vtok = B("vtok", [128, 2, 512], BF16)
        ps = Ring([B("ps%d" % i, [128, 512], F32, psum=True) for i in range(6)])
        pst = Ring([B("pst%d" % i, [128, 8, 128], BF16, psum=True) for i in range(2)])
        s.dma("pool", ident[:], I["ident"][:, :], writes=[ident])
        s.dma("pool", ones[:], I["ones"][:, :], writes=[ones])
        s.dma("sp", flag[:], I["flag"][:, :], writes=[flag])

        class LN:
            def __init__(self, gname, bname):
                self.g = B("lng", [128, D], F32)
                self.b = B("lnb", [128, D], F32)
                self.sq = B("lnsq", [128, D], F32)
                self.zt = Ring([B("zt%d" % i, [128, D], F32) for i in range(2)])
                self.zb = Ring([B("zb%d" % i, [128, D], BF16) for i in range(2)])
                self.xr = Ring([B("xr%d" % i, [128, D], F32) for i in range(3)])
                self.st = Ring([B("st%d" % i, [128, 8], F32) for i in range(4)])
                self.ob = Ring([B("lnob%d" % i, [128, 16, 128], BF16) for i in range(2)])
                s.dma("sp", self.g[:], I[gname][:, :], writes=[self.g])
                s.dma("sp", self.b[:], I[bname][:, :], writes=[self.b])

            def norm(self, z, out):
                t = self.st.next()
                sq = self.sq
                s.op("act", lambda e, a=(sq, z, t): e.activation(out=a[0][:], in_=a[1][:], func=AF.Identity, accum_out=a[2][:, 0:1]), [z], [sq, t])
                s.op("act", lambda e, a=(sq, z, t): e.activation(out=a[0][:], in_=a[1][:], func=AF.Square, accum_out=a[2][:, 1:2]), [z], [sq, t])
                s.ts("dve", t[:, 2:3], t[:, 0:1], 1.0 / D, None, ALU.mult, None, [t], [t])
                s.ts("dve", t[:, 3:4], t[:, 1:2], 1.0 / D, None, ALU.mult, None, [t], [t])
                s.stt("dve", t[:, 4:5], t[:, 2:3], -1.0, t[:, 2:3], ALU.mult, ALU.mult, [t], [t])
                s.tt("dve", t[:, 5:6], t[:, 3:4], t[:, 4:5], ALU.add, [t], [t])
                s.act(t[:, 6:7], t[:, 5:6], AF.Sqrt, [t], [t], bias=EPS)
                s.recip(t[:, 7:8], t[:, 6:7], [t], [t])
                s.ts("dve", out[:], z[:], t[:, 2:3], t[:, 7:8], ALU.subtract, ALU.mult, [z, t], [out])
                s.tt("pool", out[:], out[:], self.g[:], ALU.mult, [out, self.g], [out])
                s.tt("pool", out[:], out[:], self.b[:], ALU.add, [out, self.b], [out])

            def to_featmajor(self, xf, dst_d, dst_db, tb, width=16):
                xb = self.zb.next()
                s.copy("act", xb[:, 0:width * 128], xf[:, 0:width * 128], [xf], [xb])
                ob = self.ob.next()
                for h in range((width + 7) // 8):
                    p = pst.next()
                    nj = min(8, width - h * 8)
                    for j in range(nj):
                        kc = h * 8 + j
                        s.tr(p[:, j, :], xb[:, kc * 128:(kc + 1) * 128], ident[:], [xb, ident], [p])
                    s.copy("dve", ob[:, h * 8:h * 8 + nj, :], p[:, 0:nj, :], [p], [ob])
                s.dma("sp", dst_d[0:width, :, tb * 128:(tb + 1) * 128].rearrange("c p t -> p c t"), ob[:, 0:width, :], reads=[ob], writes=[dst_db[tb]])

            def proj_ln(self, tb, lhs_fn, nk, w, xres_ap, xres_db, out_d, out_db, outT_d, outT_db, final_out=None):
                z = self.zt.next()
                xres = self.xr.next()
                s.dma("sp", xres[:], xres_ap, reads=list(xres_db), writes=[xres])
                for dt_ in range(4):
                    p = ps.next()
                    for k in range(nk):
                        lt, lbuf = lhs_fn(k)
                        s.mm(p[:], lt, w[:, k, dt_ * 512:(dt_ + 1) * 512], k == 0, k == nk - 1, [lbuf, w], [p])
                    s.stt("dve", z[:, dt_ * 512:(dt_ + 1) * 512], xres[:, dt_ * 512:(dt_ + 1) * 512], ALPHA, p[:], ALU.mult, ALU.add, [xres, p], [z])
                o = self.xr.next()
                self.norm(z, o)
                if final_out is not None:
                    if tb >= 1:
                        s.dma("sp", final_out[(tb - 1) * 128:tb * 128, :], o[:], reads=[o])
                else:
                    s.dma("sp", out_d[tb * 128:(tb + 1) * 128, :], o[:], reads=[o], writes=[out_db[tb]])
                    self.to_featmajor(o, outT_d, outT_db, tb)

        def DB(n):
            return [Buf(None) for _ in range(n)]

        odT_db, preT_db, x1_db, x1T_db, aT_db, x2_db, x2T_db, act_db, ffo_db = DB(NB), DB(16), DB(NB), DB(NB), DB(4), DB(NB), DB(NB), DB(NFC), DB(NB * 4)
        base = s.mark()

        ln = LN("memln_g", "memln_b")
        memT_d = _dram(nc, "memT_d", (16, 128, 256), BF16, kind="Internal")
        memT_db = DB(2)
        for mb in range(2):
            z = ln.zt.next()
            o = ln.xr.next()
            s.dma("sp", z[:], I["mem"][mb * 128:(mb + 1) * 128, :], writes=[z])
            ln.norm(z, o)
            ln.to_featmajor(o, memT_d, memT_db, mb)
        memT = B("memT", [128, 16, 256], BF16)
        wk = B("wk", [128, 16, 512], BF16)
        wv = B("wv", [128, 16, 512], BF16)
        s.dma("sp", memT[:], memT_d[:, :, :].rearrange("c p t -> p c t"), reads=memT_db, writes=[memT])
        s.dma("pool", wk[:], I["wk"][:, :, :], writes=[wk])
        s.dma("pool", wv[:], I["wv"][:, :, :], writes=[wv])
        for h in range(4):
            p = ps.next()
            for kc in range(16):
                s.mm(p[:, 0:256], wk[:, kc, h * 128:(h + 1) * 128], memT[:, kc, :], kc == 0, kc == 15, [wk, memT], [p])
            s.copy("dve", kT[:, h, :], p[:, 0:256], [p], [kT])
        for mb in range(2):
            p = ps.next()
            for kc in range(16):
                s.mm(p[:], memT[:, kc, mb * 128:(mb + 1) * 128], wv[:, kc, :], kc == 0, kc == 15, [wv, memT], [p])
            s.copy("act", vtok[:, mb, :], p[:], [p], [vtok])
        s.release(base)

        xT = B("xT", [128, 16, TT], BF16)
        for kc in range(16):
            s.dma("pool", xT[:, kc, :], I["xT"][kc * 128:(kc + 1) * 128, :], writes=[xT])
        w_uv = B("w_uv", [128, 16, 1024], BF16)
        sgw = B("sgw", [128, 4, 128], F32)
        sgwb = B("sgwb", [128, 4, 128], BF16)
        tri = B("tri", [128, 128], F32)
        sgb = B("sgb", [128, 4], F32)
        sg_g = B("sg_g", [128, 512], F32)
        sg_b = B("sg_b", [128, 512], F32)
        for kc in range(16):
            s.dma("pool", w_uv[:, kc, :], I["w_uv"][:, kc, :], writes=[w_uv])
        s.dma("sp", sgw[:], I["sgw"][:, :, :], writes=[sgw])
        s.dma("sp", tri[:], I["trimask"][:, :], writes=[tri])
        s.dma("sp", sgb[:], I["sgb"][:, :], writes=[sgb])
        s.dma("sp", sg_g[:], I["sg_g"][:, :], writes=[sg_g])
        s.dma("sp", sg_b[:], I["sg_b"][:, :], writes=[sg_b])
        for g in range(4):
            s.tt("dve", sgwb[:, g, :], sgw[:, g, :], tri[:], ALU.mult, [sgw, tri], [sgwb])
        uvt = Ring([B("uvt%d" % i, [128, 512], F32) for i in range(5)])
        gt = Ring([B("gt%d" % i, [128, 512], F32) for i in range(4)])
        vlb = Ring([B("vlb%d" % i, [128, 512], BF16) for i in range(2)])
        odt = Ring([B("odt%d" % i, [128, 512], BF16) for i in range(2)])
        odo = Ring([B("odo%d" % i, [128, 4, 128], BF16) for i in range(2)])
        stA = Ring([B("stA%d" % i, [128, 8], F32) for i in range(4)])

        def gelu(src_ps, dst):
            x = uvt.next()
            t1 = uvt.next()
            s.copy("act", x[:], src_ps[:], [src_ps], [x])
            s.tt("dve", t1[:], x[:], x[:], ALU.mult, [x], [t1])
            s.ts("dve", t1[:], t1[:], 0.044715, 1.0, ALU.mult, ALU.add, [t1], [t1])
            s.tt("dve", t1[:], t1[:], x[:], ALU.mult, [t1, x], [t1])
            s.act(t1[:], t1[:], AF.Sigmoid, [t1], [t1], scale=1.5957691216057308)
            s.tt("pool", dst[:], t1[:], x[:], ALU.mult, [t1, x], [dst])

        for tb in range(NB):
            pu = ps.next()
            pv = ps.next()
            for (p, c0) in ((pu, 0), (pv, 512)):
                for kc in range(16):
                    s.mm(p[:], xT[:, kc, tb * 128:(tb + 1) * 128], w_uv[:, kc, c0:c0 + 512], kc == 0, kc == 15, [xT, w_uv], [p])
            gu = gt.next()
            gv = gt.next()
            gelu(pu, gu)
            gelu(pv, gv)
            t = stA.next()
            scr = uvt.next()
            s.op("act", lambda e, a=(scr, gv, t): e.activation(out=a[0][:], in_=a[1][:], func=AF.Identity, accum_out=a[2][:, 0:1]), [gv], [scr, t])
            s.op("act", lambda e, a=(scr, gv, t): e.activation(out=a[0][:], in_=a[1][:], func=AF.Square, accum_out=a[2][:, 1:2]), [gv], [scr, t])
            s.ts("dve", t[:, 2:3], t[:, 0:1], 1.0 / 512, None, ALU.mult, None, [t], [t])
            s.ts("dve", t[:, 3:4], t[:, 1:2], 1.0 / 512, None, ALU.mult, None, [t], [t])
            s.stt("dve", t[:, 4:5], t[:, 2:3], -1.0, t[:, 2:3], ALU.mult, ALU.mult, [t], [t])
            s.tt("dve", t[:, 5:6], t[:, 3:4], t[:, 4:5], ALU.add, [t], [t])
            s.act(t[:, 6:7], t[:, 5:6], AF.Sqrt, [t], [t], bias=EPS)
            s.recip(t[:, 7:8], t[:, 6:7], [t], [t])
            s.ts("dve", gv[:], gv[:], t[:, 2:3], t[:, 7:8], ALU.subtract, ALU.mult, [gv, t], [gv])
            s.tt("pool", gv[:], gv[:], sg_g[:], ALU.mult, [gv, sg_g], [gv])
            vb = vlb.next()
            s.tt("pool", vb[:], gv[:], sg_b[:], ALU.add, [gv, sg_b], [vb])
            pg = ps.next()
            for g in range(4):
                s.mm(pg[:, g * 128:(g + 1) * 128], sgwb[:, g, :], vb[:, g * 128:(g + 1) * 128], True, True, [sgwb, vb], [pg])
            od = odt.next()
            for g in range(4):
                s.stt("dve", od[:, g * 128:(g + 1) * 128], pg[:, g * 128:(g + 1) * 128], sgb[:, g:g + 1], gu[:, g * 128:(g + 1) * 128],
                      ALU.add, ALU.mult, [pg, sgb, gu], [od])
            p = pst.next()
            for j in range(4):
                s.tr(p[:, j, :], od[:, j * 128:(j + 1) * 128], ident[:], [od, ident], [p])
            oo = odo.next()
            s.copy("act", oo[:], p[:, 0:4, :], [p], [oo])
            s.dma("sp", odT_d[:, :, tb * 128:(tb + 1) * 128].rearrange("c p t -> p c t"), oo[:], reads=[oo], writes=[odT_db[tb]])
        mB = s.mark()
        s.barrier()
        s.off = xT_end = base + 16 * TT * 2
        assert xT_end % 64 == 0

        oT = B("oT", [128, 16, TT], BF16)
        for kc in range(12):
            s.dma("pool", oT[:, kc, :], I["oT"][kc * 128:(kc + 1) * 128, :], writes=[oT])
        s.dma("sp", oT[:, 12:16, :], odT_d[:, :, :].rearrange("c p t -> p c t"), reads=odT_db, writes=[oT])
        wmg = Ring([B("wmg%d" % i, [128, 4, 16, 128], BF16) for i in range(2)])
        wbr = Ring([B("wbr%d" % i, [128, 4, 4, 128], BF16) for i in range(2)])
        gsb = Ring([B("gsb%d" % i, [128, 512], F32) for i in range(3)])
        acc = Ring([B("acc%d" % i, [128, 512], F32) for i in range(2)])
        preo = Ring([B("preo%d" % i, [128, TT], BF16) for i in range(2)])
        for dc in range(16):
            wm = wmg.next()
            wb = wbr.next()
            s.dma("pool", wm[:], I["w_mg"][dc, :, :, :, :], writes=[wm])
            s.dma("pool", wb[:], I["w_br"][dc, :, :, :, :], writes=[wb])
            po = preo.next()
            for (t0, tw) in TTILES:
                a = acc.next()
                for n in range(4):
                    pm = ps.next()
                    py = ps.next()
                    for kc in range(16):
                        s.mm(pm[:, 0:tw], wm[:, n, kc, :], xT[:, kc, t0:t0 + tw], kc == 0, kc == 15, [wm, xT], [pm])
                    for k4 in range(4):
                        s.mm(py[:, 0:tw], wb[:, n, k4, :], oT[:, n * 4 + k4, t0:t0 + tw], k4 == 0, k4 == 3, [wb, oT], [py])
                    gs = gsb.next()
                    s.act(gs[:, 0:tw], pm[:, 0:tw], AF.Sigmoid, [pm], [gs])
                    if n == 0:
                        s.tt("dve", a[:, 0:tw], gs[:, 0:tw], py[:, 0:tw], ALU.mult, [gs, py], [a])
                    else:
                        s.tt("dve", gs[:, 0:tw], gs[:, 0:tw], py[:, 0:tw], ALU.mult, [gs, py], [gs])
                        if n < 3:
                            s.tt("pool", a[:, 0:tw], a[:, 0:tw], gs[:, 0:tw], ALU.add, [a, gs], [a])
                        else:
                            s.tt("pool", po[:, t0:t0 + tw], a[:, 0:tw], gs[:, 0:tw], ALU.add, [a, gs], [po])
            s.dma("sp", preT_d[dc, :, :], po[:], reads=[po], writes=[preT_db[dc]])
        s.release(base)

        ln = LN("ln1_g", "ln1_b")
        wmix = B("wmix", [128, 16, D], BF16)
        for kc in range(16):
            s.dma("pool", wmix[:, kc, :], I["w_mix"][:, kc, :], writes=[wmix])
        prb = Ring([B("prb%d" % i, [128, 16, 128], BF16) for i in range(2)])
        for tb in range(NB):
            pb = prb.next()
            s.dma("sp", pb[:], preT_d[:, :, tb * 128:(tb + 1) * 128].rearrange("c p t -> p c t"), reads=preT_db, writes=[pb])
            ln.proj_ln(tb, lambda k, pb=pb: (pb[:, k, :], pb), 16, wmix, I["xtok"][tb * 128:(tb + 1) * 128, :], (), x1_d, x1_db, x1T_d, x1T_db)
        s.release(base)

        x1T = B("x1T", [128, 16, TT], BF16)
        s.dma("sp", x1T[:], x1T_d[:, :, :].rearrange("c p t -> p c t"), reads=x1T_db, writes=[x1T])
        wq = B("wq", [128, 16, 512], BF16)
        s.dma("pool", wq[:], I["wq"][:, :, :], writes=[wq])
        qTb = Ring([B("qTb%d" % i, [128, 512], BF16) for i in range(2)])
        pTb = Ring([B("pTb%d" % i, [128, 512], BF16) for i in range(4)])
        rdb = Ring([B("rdb%d" % i, [128, 512], F32) for i in range(2)])
        aTo = Ring([B("aTo%d" % i, [128, TT], BF16) for i in range(2)])
        for h in range(4):
            ao = aTo.next()
            for (t0, tw) in TTILES:
                p = ps.next()
                for kc in range(16):
                    s.mm(p[:, 0:tw], wq[:, kc, h * 128:(h + 1) * 128], x1T[:, kc, t0:t0 + tw], kc == 0, kc == 15, [wq, x1T], [p])
                q = qTb.next()
                s.act(q[:, 0:tw], p[:, 0:tw], AF.Identity, [p], [q], scale=128 ** -0.5)
                pts = []
                for mb in range(2):
                    p2 = ps.next()
                    s.mm(p2[:, 0:tw], kT[:, h, mb * 128:(mb + 1) * 128], q[:, 0:tw], True, True, [kT, q], [p2])
                    pt = pTb.next()
                    s.act(pt[:, 0:tw], p2[:, 0:tw], AF.Exp, [p2], [pt])
                    pts.append(pt)
                po_ = ps.next()
                pd = ps.next()
                for mb in range(2):
                    s.mm(po_[:, 0:tw], vtok[:, mb, h * 128:(h + 1) * 128], pts[mb][:, 0:tw], mb == 0, mb == 1, [vtok, pts[mb]], [po_])
                for mb in range(2):
                    s.mm(pd[:, 0:tw], ones[:], pts[mb][:, 0:tw], mb == 0, mb == 1, [ones, pts[mb]], [pd])
                rd = rdb.next()
                s.recip(rd[:, 0:tw], pd[:, 0:tw], [pd], [rd])
                s.tt("dve", ao[:, t0:t0 + tw], po_[:, 0:tw], rd[:, 0:tw], ALU.mult, [po_, rd], [ao])
            s.dma("sp", aT_d[h, :, :], ao[:], reads=[ao], writes=[aT_db[h]])
        s.release(base)

        ln = LN("ln2_g", "ln2_b")
        wo = B("wo", [128, 4, D], BF16)
        s.dma("pool", wo[:], I["wo"][:, :, :], writes=[wo])
        aTr = Ring([B("aTr%d" % i, [128, 4, 128], BF16) for i in range(2)])
        for tb in range(NB):
            ab = aTr.next()
            s.dma("sp", ab[:], aT_d[:, :, tb * 128:(tb + 1) * 128].rearrange("c p t -> p c t"), reads=aT_db, writes=[ab])
            ln.proj_ln(tb, lambda k, ab=ab: (ab[:, k, :], ab), 4, wo, x1_d[tb * 128:(tb + 1) * 128, :], [x1_db[tb]], x2_d, x2_db, x2T_d, x2T_db)
        s.release(base)

        x2T = B("x2T", [128, 16, TT], BF16)
        s.dma("sp", x2T[:], x2T_d[:, :, :].rearrange("c p t -> p c t"), reads=x2T_db, writes=[x2T])
        cw = B("cw", [128, NFC, 4], F32)
        s.dma("sp", cw[:], I["convw"][:, :, :], writes=[cw])
        wf = Ring([B("wf%d" % i, [128, 2, 16, 128], BF16) for i in range(3)])
        gbuf = Ring([B("gbuf%d" % i, [128, TT + 2], F32) for i in range(2)])
        ubuf = Ring([B("ubuf%d" % i, [128, TT], F32) for i in range(2)])
        cbuf = Ring([B("cbuf%d" % i, [128, TT], F32) for i in range(2)])
        sbuf_ = Ring([B("sbuf%d" % i, [128, TT], F32) for i in range(2)])
        abuf = Ring([B("abuf%d" % i, [128, TT], BF16) for i in range(2)])
        for g_ in gbuf.bufs:
            s.memset("dve", g_[:, 0:2], 0.0, [g_])
        for fc in range(NFC):
            w = wf.next()
            s.dma("pool", w[:], I["ffn_in"][fc, :, :, :, :], writes=[w])
            gb = gbuf.next()
            ub = ubuf.next()
            for (t0, tw) in TTILES:
                pg = ps.next()
                pu = ps.next()
                for kc in range(16):
                    s.mm(pg[:, 0:tw], w[:, 0, kc, :], x2T[:, kc, t0:t0 + tw], kc == 0, kc == 15, [w, x2T], [pg])
                for kc in range(16):
                    s.mm(pu[:, 0:tw], w[:, 1, kc, :], x2T[:, kc, t0:t0 + tw], kc == 0, kc == 15, [w, x2T], [pu])
                if t0 == 0:
                    s.op("act", lambda e, a=(gb, pg, flag): e.activation(out=a[0][:, 2:130], in_=a[1][:, 0:128], func=AF.Copy, scale=a[2][:, 0:1]), [pg, flag], [gb])
                    s.copy("act", gb[:, 130:2 + tw], pg[:, 128:tw], [pg], [gb])
                else:
                    s.copy("act", gb[:, 2 + t0:2 + t0 + tw], pg[:, 0:tw], [pg], [gb])
                s.copy("dve", ub[:, t0:t0 + tw], pu[:, 0:tw], [pu], [ub])
            cb = cbuf.next()
            s.ts("dve", cb[:], gb[:, 2:TT + 2], cw[:, fc, 2:3], cw[:, fc, 3:4], ALU.mult, ALU.add, [gb, cw], [cb])
            s.stt("dve", cb[:], gb[:, 1:TT + 1], cw[:, fc, 1:2], cb[:], ALU.mult, ALU.add, [gb, cw, cb], [cb])
            s.stt("dve", cb[:], gb[:, 0:TT], cw[:, fc, 0:1], cb[:], ALU.mult, ALU.add, [gb, cw, cb], [cb])
            sb = sbuf_.next()
            s.act(sb[:], cb[:], AF.Silu, [cb], [sb])
            ab = abuf.next()
            s.tt("pool", ab[:], sb[:], ub[:], ALU.mult, [sb, ub], [ab])
            s.dma("sp", act_d[fc, :, :], ab[:], reads=[ab], writes=[act_db[fc]])
        s.release(base)

        wfo = Ring([B("wfo%d" % i, [128, NFC, 512], BF16) for i in range(2)])
        acb = Ring([B("acb%d" % i, [128, NFC, 128], BF16) for i in range(2)])
        fob = Ring([B("fob%d" % i, [128, 512], F32) for i in range(3)])
        for dt_ in range(4):
            w = wfo.next()
            for f0 in range(0, NFC, 11):
                s.dma("pool", w[:, f0:f0 + 11, :], I["ffn_out"][:, f0:f0 + 11, dt_ * 512:(dt_ + 1) * 512], writes=[w])
            for tb in range(NB):
                ab = acb.next()
                s.dma("sp", ab[:], act_d[:, :, tb * 128:(tb + 1) * 128].rearrange("c p t -> p c t"), reads=act_db, writes=[ab])
                p = ps.next()
                for fc in range(NFC):
                    s.mm(p[:], ab[:, fc, :], w[:, fc, :], fc == 0, fc == NFC - 1, [ab, w], [p])
                fo = fob.next()
                if tb % 2 == 0:
                    s.copy("act", fo[:], p[:], [p], [fo])
                else:
                    s.copy("dve", fo[:], p[:], [p], [fo])
                s.dma("sp", ffo_d[tb * 128:(tb + 1) * 128, dt_ * 512:(dt_ + 1) * 512], fo[:], reads=[fo], writes=[ffo_db[tb * 4 + dt_]])
        s.release(base)

        ln = LN("ln3_g", "ln3_b")
        for tb in range(1, NB):
            z = ln.zt.next()
            xres = ln.xr.next()
            ff = ln.xr.next()
            s.dma("sp", xres[:], x2_d[tb * 128:(tb + 1) * 128, :], reads=[x2_db[tb]], writes=[xres])
            s.dma("sp", ff[:], ffo_d[tb * 128:(tb + 1) * 128, :], reads=ffo_db[tb * 4:tb * 4 + 4], writes=[ff])
            s.stt("dve", z[:], xres[:], ALPHA, ff[:], ALU.mult, ALU.add, [xres, ff], [z])
            o = ln.xr.next()
            ln.norm(z, o)
            s.dma("sp", xout[(tb - 1) * 128:tb * 128, :], o[:], reads=[o])
        s.emit()
        print("R program: ops", s.n, "sbuf peak", s.peak, flush=True)
    return nc


OFF_Q, OFF_KV, OFF_G, OFF_H, OFF_P, OFF_SG, OFF_MG = 0, 512, 1280, 1304, 3352, 3864, 4888


def _bc(v, n=128):
    return np.ascontiguousarray(np.broadcast_to(np.asarray(v, np.float32)[None, :], (n, v.shape[0])))


def _c(a):
    return np.ascontiguousarray(a, dtype=np.float32)


def prep_R_shared(inp, l):
    w_in = inp["w_in"][l]
    sh = {}
    sh["w_uv"] = _c(w_in[:, OFF_SG:OFF_SG + 1024].reshape(16, 128, 1024).transpose(1, 0, 2))
    mg = w_in[:, OFF_MG:OFF_MG + 8192].reshape(16, 128, 4, 16, 128)
    sh["w_mg"] = _c(mg.transpose(3, 1, 2, 0, 4))
    br = inp["w_branch"][l].reshape(4, 4, 128, 16, 128)
    sh["w_br"] = _c(br.transpose(3, 2, 0, 1, 4))
    sh["w_mix"] = _c(inp["w_mix_out"][l].reshape(16, 128, D).transpose(1, 0, 2))
    sh["sgw"] = _c(inp["sg_w"][l].transpose(2, 0, 1))
    sh["sgb"] = _c(inp["sg_b"][l].T)
    sh["sg_g"] = _bc(inp["sg_ln_g"][l])
    sh["sg_b"] = _bc(inp["sg_ln_b"][l])
    sh["trimask"] = _c(np.triu(np.ones((128, 128), np.float32)))
    for i, nm in ((1, "ln_mix"), (2, "ln_x"), (3, "ln_ffn")):
        sh["ln%d_g" % i] = _bc(inp[nm + "_g"][l])
        sh["ln%d_b" % i] = _bc(inp[nm + "_b"][l])
    sh["memln_g"] = _bc(inp["mem_ln_g"])
    sh["memln_b"] = _bc(inp["mem_ln_b"])
    for nm, k in (("wq", "xattn_q"), ("wk", "xattn_k"), ("wv", "xattn_v")):
        sh[nm] = _c(inp[k][l].reshape(16, 128, 512).transpose(1, 0, 2))
    sh["wo"] = _c(inp["xattn_o"][l].reshape(4, 128, D).transpose(1, 0, 2))
    fi = inp["ffn_in"][l].reshape(16, 128, 2, NFC, 128)
    sh["ffn_in"] = _c(fi.transpose(3, 1, 2, 0, 4))
    sh["ffn_out"] = _c(inp["ffn_out"][l].reshape(NFC, 128, D).transpose(1, 0, 2))
    cw = np.concatenate([inp["ffn_conv_w"][l], inp["ffn_conv_b"][l][None, :]], axis=0)
    sh["convw"] = _c(cw.reshape(4, NFC, 128).transpose(2, 1, 0))
    sh["ident"] = np.eye(128, dtype=np.float32)
    sh["ones"] = np.ones((128, 128), np.float32)
    return sh


def prep_R(inp, l, x, oabc):
    sh = prep_R_shared(inp, l)
    maps = []
    for c in range(NCORE):
        b, u = divmod(c, 4)
        t0 = 2048 * u
        m = dict(sh)
        xs = np.zeros((TT, D), np.float32)
        os_ = np.zeros((TT, 1536), np.float32)
        lo = t0 - 128
        if u == 0:
            xs[128:] = x[b, 0:2048]
            os_[128:] = oabc[b, 0:2048]
        else:
            xs[:] = x[b, lo:lo + TT]
            os_[:] = oabc[b, lo:lo + TT]
        m["xtok"] = xs
        m["xT"] = _c(xs.T)
        m["oT"] = _c(os_.T)
        m["mem"] = _c(inp["mem"][b])
        m["flag"] = np.full((128, 1), 0.0 if u == 0 else 1.0, np.float32)
        maps.append(m)
    return maps


_NC_CACHE = {}


def run_R(inp, l, x, oabc):
    if "R" not in _NC_CACHE:
        _NC_CACHE["R"] = build_R()
    maps = prep_R(inp, l, x, oabc)
    res = run_bass_kernel_spmd(_NC_CACHE["R"], maps, core_ids=list(range(NCORE)))
    out = np.zeros((2, S, D), np.float32)
    for c in range(NCORE):
        b, u = divmod(c, 4)
        out[b, 2048 * u:2048 * (u + 1)] = res.results[c]["xout"]
    return out


NQT = S // 512
M_INPUTS = {
    "xT": (D, S), "wfm": (128, 16, 768), "wtm": (128, 16, 520),
    "qpos": (4, 4, S), "kpos": (4, S), "cpos": (4, 512),
    "cmask": (5, 128, 512), "dmask": (8, 128, 512), "E_all": (128, 32, 128), "selmap": (128, 4, 129),
    "keep": (128, 256), "addm": (128, 256), "ident": (128, 128), "tri2": (128, 64),
    "cw_k": (64, 32, 64), "cw_v": (64, 32, 64), "cp_k": (64, 32), "cp_v": (64, 32),
    "lblog": (128, 2), "lbsel": (128, 1), "normg": (128, 128),
    "poolP": (128, 3, 128), "poolw": (128, 128), "poolscale": (128, 128),
}


def build_M(nqt=NQT, ST=(1, 2, 3, 4, 5, 6, 7, 8, 9, 10, 11)):
    nc = bass.Bass("TRN2", target_bir_lowering=False)
    I = {k: _dram(nc, k, v) for k, v in M_INPUTS.items()}
    om = _dram(nc, "om", (S, 384), kind="ExternalOutput")
    with contextlib.ExitStack() as es:
        s = Sched(nc, es)
        B = s.buf

        def PB(name, shape, dt):
            return B(name, shape, dt, psum=True)

        psr = Ring([PB("psr%d" % i, [128, 512], F32) for i in range(2)])
        pst = PB("pst", [128, 8, 128], BF16)
        oacc_ps = Ring([PB("oacc%d" % i, [128, 512], F32) for i in range(2)])
        impb = PB("impb", [128, 4, 128], F32)
        misc = PB("misc", [128, 512], F32)
        denb = misc
        hg = PB("hg", [128, 4, 128], F32)

        wfm = B("wfm", [128, 16, 768], BF16)
        wtm = B("wtm", [128, 16, 520], BF16)
        for kc in range(16):
            s.dma("pool", wfm[:, kc, :], I["wfm"][:, kc, :], writes=[wfm])
            s.dma("pool", wtm[:, kc, :], I["wtm"][:, kc, :], writes=[wtm])
        identb = B("identb", [128, 128], BF16)
        identf = B("identf", [128, 128], F32)
        s.dma("pool", identb[:], I["ident"][:, :], writes=[identb])
        s.dma("sp", identf[:], I["ident"][:, :], writes=[identf])
        cmask = B("cmask", [128, 5, 512], BF16)
        dmask = B("dmask", [128, 8, 512], BF16)
        for i in range(5):
            s.dma("pool", cmask[:, i, :], I["cmask"][i, :, :], writes=[cmask])
        for i in range(8):
            s.dma("pool", dmask[:, i, :], I["dmask"][i, :, :], writes=[dmask])
        E_all = B("E_all", [128, 32, 128], BF16)
        s.dma("pool", E_all[:], I["E_all"][:, :, :], writes=[E_all])
        selmap = B("selmap", [128, 4, 144], BF16)
        s.dma("pool", selmap[:, :, 0:129], I["selmap"][:, :, :], writes=[selmap])
        keep = B("keep", [128, 256], F32)
        addm = B("addm", [128, 256], F32)
        tri2 = B("tri2", [128, 64], F32)
        s.dma("sp", keep[:], I["keep"][:, :], writes=[keep])
        s.dma("sp", addm[:], I["addm"][:, :], writes=[addm])
        s.dma("sp", tri2[:], I["tri2"][:, :], writes=[tri2])
        cw_k = B("cw_k", [64, 32, 64], BF16)
        cw_v = B("cw_v", [64, 32, 64], BF16)
        cp_k = B("cp_k", [64, 32], BF16)
        cp_v = B("cp_v", [64, 32], BF16)
        s.dma("pool", cw_k[:], I["cw_k"][:, :, :], writes=[cw_k])
        s.dma("pool", cw_v[:], I["cw_v"][:, :, :], writes=[cw_v])
        s.dma("pool", cp_k[:], I["cp_k"][:, :], writes=[cp_k])
        s.dma("pool", cp_v[:], I["cp_v"][:, :], writes=[cp_v])
        normg = B("normg", [128, 128], F32)
        poolP = B("poolP", [128, 3, 128], F32)
        poolw = B("poolw", [128, 128], BF16)
        poolscale = B("poolscale", [128, 128], F32)
        lbl = B("lbl", [128, 8], F32)
        s.dma("sp", normg[:], I["normg"][:, :], writes=[normg])
        s.dma("sp", poolP[:], I["poolP"][:, :, :], writes=[poolP])
        s.dma("pool", poolw[:], I["poolw"][:, :], writes=[poolw])
        s.dma("sp", poolscale[:], I["poolscale"][:, :], writes=[poolscale])
        s.dma("sp", lbl[:, 0:2], I["lblog"][:, :], writes=[lbl])
        s.dma("sp", lbl[:, 2:3], I["lbsel"][:, :], writes=[lbl])
        s.tt("dve", lbl[:, 3:4], lbl[:, 1:2], lbl[:, 0:1], ALU.subtract, [lbl], [lbl])
        s.act(lbl[:, 4:5], lbl[:, 3:4], AF.Sigmoid, [lbl], [lbl])
        s.tt("dve", lbl[:, 5:6], lbl[:, 4:5], lbl[:, 2:3], ALU.mult, [lbl], [lbl])
        s.ts("dve", lbl[:, 6:7], lbl[:, 5:6], -1.0, 1.0, ALU.mult, ALU.add, [lbl], [lbl])

        kslc = B("kslc", [68, S], BF16)
        s.dma("pool", kslc[64:68, :], I["kpos"][:, :], writes=[kslc])
        kwin = B("kwin", [68, 8, 128], BF16)
        vslc = B("vslc", [128, 64, 72], BF16)
        vwin = B("vwin", [128, 8, 72], BF16)
        s.memset("dve", vslc[:, :, 64:65], 1.0, [vslc])
        s.memset("pool", vwin[:, :, 64:65], 1.0, [vwin])
        kcaug = B("kcaug", [68, 512], BF16)
        s.memset("dve", kcaug[0:64, :], 0.0, [kcaug])
        s.dma("pool", kcaug[64:68, :], I["cpos"][:, :], writes=[kcaug])
        vcaug = B("vcaug", [128, 4, 72], BF16)
        s.memset("dve", vcaug[:], 0.0, [vcaug])
        s.memset("dve", vcaug[:, :, 64:65], 1.0, [vcaug])
        kcraw = B("kcraw", [64, 528], BF16)
        vcraw = B("vcraw", [64, 528], BF16)
        s.memset("dve", kcraw[:], 0.0, [kcraw])
        s.memset("dve", vcraw[:], 0.0, [vcraw])
        cbias = B("cbias", [64, 2], F32)
        for (cw_, cp_, col) in ((cw_k, cp_k, 0), (cw_v, cp_v, 1)):
            p = psr.next()
            for l_ in range(32):
                s.mm(p[0:64, 0:1], cw_[:, l_, :], cp_[:, l_:l_ + 1], l_ == 0, l_ == 31, [cw_, cp_], [p])
            s.copy("dve", cbias[:, col:col + 1], p[0:64, 0:1], [p], [cbias])

        state = B("state", [128, 128], F32)
        stbf = Ring([B("stbf%d" % i, [128, 128], BF16) for i in range(2)])
        s.memset("dve", state[:], 0.0, [state])
        st_cur = stbf.next()
        s.memset("dve", st_cur[:], 0.0, [st_cur])
        qbpad = B("qbpad", [128, 4, 2, 128], BF16)
        s.memset("pool", qbpad[:], 0.0, [qbpad])
        atpad = Ring([B("atpad%d" % i, [128, 128], BF16) for i in range(2)])
        for a_ in atpad.bufs:
            s.memset("pool", a_[:], 0.0, [a_])
        pczero = B("pczero", [128, 128], F32)
        s.memset("pool", pczero[:], 0.0, [pczero])

        xtile = Ring([B("xtile%d" % i, [128, 16, 512], BF16) for i in range(1)])
        qaug = [Ring([B("qaug%d_%d" % (j, i), [68, 512], BF16) for i in range(2)]) for j in range(4)]
        hqT = B("hqT", [128, 512], F32)
        hzT = B("hzT", [128, 512], F32)
        gate = Ring([B("gate%d" % i, [128, 8], F32) for i in range(8)])
        pcb = Ring([B("pcb%d" % i, [128, 128], F32) for i in range(6)])
        vvb = Ring([B("vvb%d" % i, [128, 128], BF16) for i in range(4)])
        sgb = Ring([B("sgb%d" % i, [128, 128], F32) for i in range(4)])
        outb = Ring([B("outb%d" % i, [128, 384], F32) for i in range(4)])
        ET = [[B("ET%d_%d" % (j, c), [128, 512], BF16) for c in range(4)] for j in range(2)]
        PT = Ring([B("PT%d" % i, [128, 512], BF16) for i in range(3)])
        impacc = B("impacc", [128, 4, 128], F32)
        rden = Ring([B("rden%d" % i, [128, 4], F32) for i in range(4)])
        selw = Ring([B("selw%d" % i, [128, 128], F32) for i in range(3)])
        selb = Ring([B("selb%d" % i, [128, 128], BF16) for i in range(2)])
        m8 = Ring([B("m8_%d" % i, [128, 16], F32) for i in range(2)])
        mbT = Ring([B("mbT%d" % i, [128, 512], BF16) for i in range(2)])
        oTs = Ring([B("oTs%d" % i, [65, 512], F32) for i in range(2)])
        coef = Ring([B("coef%d" % i, [128, 4], F32) for i in range(4)])
        vcT = Ring([B("vcT%d" % i, [64, 32], BF16) for i in range(2)])
        vct = Ring([B("vct%d" % i, [32, 64], BF16) for i in range(2)])
        hw_ = [B("hw%d" % i, [128, 512], F32) for i in range(7)]
        hb_ = [B("hb%d" % i, [128, 512], BF16) for i in range(4)]
        hs_ = Ring([B("hs%d" % i, [128, 16], F32) for i in range(2)])
        kdb = Ring([B("kdb%d" % i, [128, 128], BF16) for i in range(2)])
        hst = Ring([B("hst%d" % i, [128, 4], F32) for i in range(4)])
        hsq = B("hsq", [128, 128], F32)
        hsq2 = B("hsq2", [128, 128], F32)
        ptb = Ring([B("ptb%d" % i, [128, 128], BF16) for i in range(2)])
        pc_prev = pczero
        trs = Ring([B("trs%d" % i, [128, 4, 65], F32) for i in range(2)])
        imps = Ring([B("imps%d" % i, [128, 4, 128], F32) for i in range(2)])
        asb = Ring([B("asb%d" % i, [128, 128], F32) for i in range(2)])
        stgA = Ring([B("stgA%d" % i, [128, 136], F32) for i in range(2)])
        stgB = Ring([B("stgB%d" % i, [128, 384], F32) for i in range(2)])

        JA = 0
        trf = Buf(misc.t)
        trv = misc.t[:, 128:388].rearrange("p (a b) -> p a b", a=4)
        hstate = {"cur": st_cur}

        def epilogue(oa, hl, br, gts, obs):
            o_sb = oTs.next()
            s.copy("act", o_sb[:], oa[0:65, :], [oa], [o_sb])
            for tb in range(4):
                s.tr(trv[:, tb, :], o_sb[:, tb * 128:(tb + 1) * 128], identf[0:65, 0:65], [o_sb, identf], [trf])
            tv = trs.next()
            s.copy("dve", tv[:], trv[:, :, :], [trf], [tv])
            cf = coef.next()
            s.ts("dve", cf[:], tv[:, :, 64], 1e-30, None, ALU.max, None, [tv], [cf])
            s.recip(cf[:], cf[:], [cf], [cf])
            for tb in range(4):
                s.tt("dve", cf[:, tb:tb + 1], cf[:, tb:tb + 1], gts[tb][:, hl * 3 + br:hl * 3 + br + 1], ALU.mult, [cf, gts[tb]], [cf])
            for tb in range(4):
                dst = obs[tb][:, hl * 64:(hl + 1) * 64]
                if br == 0:
                    s.ts("dve", dst, tv[:, tb, 0:64], cf[:, tb:tb + 1], None, ALU.mult, None, [tv, cf], [obs[tb]])
                else:
                    s.stt("dve", dst, tv[:, tb, 0:64], cf[:, tb:tb + 1], dst, ALU.mult, ALU.add, [tv, cf, obs[tb]], [obs[tb]])

        def hgrn_tile(qt, vvs, sgs, obs):
            W = hw_
            s.act(W[0][:], hzT[:], AF.Sigmoid, [hzT], [W[0]])
            s.ts("dve", W[0][:], W[0][:], lbl[:, 6:7], lbl[:, 5:6], ALU.mult, ALU.add, [W[0], lbl], [W[0]])
            s.ts("dve", W[0][:], W[0][:], 1e-6, None, ALU.max, None, [W[0]], [W[0]])
            s.ts("dve", W[1][:], W[0][:], -1.0, 1.0, ALU.mult, ALU.add, [W[0]], [W[1]])
            s.act(W[2][:], W[0][:], AF.Ln, [W[0]], [W[2]])
            src, dst = W[2], W[3]
            for sh in (1, 2, 4, 8, 16, 32):
                sv = src[:].rearrange("p (c t) -> p c t", t=64)
                dv = dst[:].rearrange("p (c t) -> p c t", t=64)
                s.copy("pool", dv[:, :, 0:sh], sv[:, :, 0:sh], [src], [dst])
                s.tt("dve", dv[:, :, sh:64], sv[:, :, sh:64], sv[:, :, 0:64 - sh], ALU.add, [src], [dst])
                src, dst = dst, src
            b = src
            bv = b[:].rearrange("p (c t) -> p c t", t=64)
            hs = hs_.next()
            nh = hs_.next()
            s.copy("dve", hs[:, 0:8], bv[:, :, 31], [b], [hs])
            s.copy("dve", hs[:, 8:16], bv[:, :, 63], [b], [hs])
            s.ts("dve", nh[:, 0:8], hs[:, 0:8], -1.0, None, ALU.mult, None, [hs], [nh])
            s.act(nh[:, 8:16], hs[:, 8:16], AF.Exp, [hs], [nh])
            for c in range(8):
                cs = slice(c * 64, (c + 1) * 64)
                s.act(W[3][:, cs], b[:, cs], AF.Exp, [b, nh], [W[3]], bias=nh[:, c:c + 1])
                s.act(W[4][:, cs], b[:, cs], AF.Exp, [b, hs], [W[4]], bias=hs[:, c:c + 1], scale=-1.0)
                s.act(W[5][:, cs], b[:, cs], AF.Exp, [b, hs], [W[5]], bias=hs[:, 8 + c:9 + c], scale=-1.0)
            s.act(W[6][:], b[:], AF.Exp, [b], [W[6]])
            s.tt("pool", hb_[0][:], hqT[:], W[3][:], ALU.mult, [hqT, W[3]], [hb_[0]])
            s.tt("dve", hb_[1][:], W[1][:], W[4][:], ALU.mult, [W[1], W[4]], [hb_[1]])
            s.tt("pool", hb_[2][:], W[1][:], W[5][:], ALU.mult, [W[1], W[5]], [hb_[2]])
            hq4 = hqT[:].rearrange("p (a c t) -> p a c t", a=4, c=2)
            eb4 = W[6][:].rearrange("p (a c t) -> p a c t", a=4, c=2)
            s.tt("dve", qbpad[:, :, 0, 0:64], hq4[:, :, 0, :], eb4[:, :, 0, :], ALU.mult, [hqT, W[6]], [qbpad])
            s.tt("pool", qbpad[:, :, 1, 64:128], hq4[:, :, 1, :], eb4[:, :, 1, :], ALU.mult, [hqT, W[6]], [qbpad])
            for tb in range(4):
                blk = slice(tb * 128, (tb + 1) * 128)
                s.mm(hg[:, 0, :], hb_[1][:, blk], hb_[0][:, blk], True, True, [hb_[1], hb_[0]], [hg])
                at = atpad.next()
                a_sb = asb.next()
                s.copy("dve", a_sb[:], hg[:, 0, :], [hg], [a_sb])
                s.tt("pool", at[0:64, 0:64], a_sb[0:64, 0:64], tri2[0:64, :], ALU.mult, [a_sb, tri2], [at])
                s.tt("pool", at[64:128, 64:128], a_sb[64:128, 64:128], tri2[64:128, :], ALU.mult, [a_sb, tri2], [at])
                s.tr(pst[:, 5, :], hb_[2][:, blk], identb[:], [hb_[2], identb], [pst])
                kb = kdb.next()
                s.copy("act", kb[:], pst[:, 5, :], [pst], [kb])
                st0 = hstate["cur"]
                s.mm(hg[:, 1, :], qbpad[:, tb, 0, :], st0[:], True, False, [qbpad, st0], [hg])
                s.mm(hg[:, 1, :], at[:], vvs[tb][:], False, True, [at, vvs[tb]], [hg])
                s.mm(hg[:, 2, :], kb[0:64, :], vvs[tb][0:64, :], True, True, [kb, vvs[tb]], [hg])
                s.stt("dve", state[:], state[:], nh[:, 8 + 2 * tb:9 + 2 * tb], hg[:, 2, :], ALU.mult, ALU.add, [state, nh, hg], [state])
                st1 = stbf.next()
                s.copy("act", st1[:], state[:], [state], [st1])
                oa_sb = hsq
                s.copy("dve", oa_sb[:], hg[:, 1, :], [hg], [oa_sb])
                s.mm(hg[:, 3, :], qbpad[:, tb, 1, :], st1[:], True, True, [qbpad, st1], [hg])
                s.tt("dve", oa_sb[:], oa_sb[:], hg[:, 3, :], ALU.add, [oa_sb, hg], [oa_sb])
                s.mm(hg[:, 2, :], kb[64:128, :], vvs[tb][64:128, :], True, True, [kb, vvs[tb]], [hg])
                s.stt("dve", state[:], state[:], nh[:, 9 + 2 * tb:10 + 2 * tb], hg[:, 2, :], ALU.mult, ALU.add, [state, nh, hg], [state])
                st2 = stbf.next()
                s.copy("act", st2[:], state[:], [state], [st2])
                hstate["cur"] = st2
                t_ = hst.next()
                s.op("act", lambda e, a=(hsq2, oa_sb, t_): e.activation(out=a[0][:], in_=a[1][:], func=AF.Square, accum_out=a[2][:, 0:1]), [oa_sb], [hsq2, t_])
                s.ts("dve", t_[:, 1:2], t_[:, 0:1], 1.0 / 128, EPS, ALU.mult, ALU.add, [t_], [t_])
                s.act(t_[:, 2:3], t_[:, 1:2], AF.Sqrt, [t_], [t_])
                s.recip(t_[:, 3:4], t_[:, 2:3], [t_], [t_])
                s.stt("dve", obs[tb][:, 128:256], oa_sb[:], t_[:, 3:4], normg[:], ALU.mult, ALU.mult, [oa_sb, t_, normg], [obs[tb]])
                s.tt("pool", obs[tb][:, 128:256], obs[tb][:, 128:256], sgs[tb][:], ALU.mult, [obs[tb], sgs[tb]], [obs[tb]])

        g1w, g2w = 136, 384

        for qt in range(nqt):
            q0 = qt * 512
            s.mute = 1 not in ST
            xt = xtile.next()
            for kc in range(16):
                s.dma("pool", xt[:, kc, :], I["xT"][kc * 128:(kc + 1) * 128, q0:q0 + 512], writes=[xt])
            qa = [qaug[j].next() for j in range(4)]
            for j in range(4):
                s.dma("pool", qa[j][64:68, :], I["qpos"][j, :, q0:q0 + 512], writes=[qa[j]])
            slot0 = (4 * qt) % 8
            s.dma("pool", kwin[64:68, slot0:slot0 + 4, :], I["kpos"][:, q0:q0 + 512].rearrange("r (a b) -> r a b", a=4), writes=[kwin])
            if qt > 0:
                s.copy("dve", kcraw[:, 0:16], kcraw[:, 512:528], [kcraw], [kcraw])
                s.copy("dve", vcraw[:, 0:16], vcraw[:, 512:528], [vcraw], [vcraw])

            s.mute = 2 not in ST
            def fm(col0, M):
                p = psr.next()
                for kc in range(16):
                    s.mm(p[0:M, :], wfm[:, kc, col0:col0 + M], xt[:, kc, :], kc == 0, kc == 15, [wfm, xt], [p])
                return p
            for j in range(4):
                p = fm(j * 64, 64)
                s.act(qa[j][0:64, :], p[0:64, :], AF.Identity, [p], [qa[j]], scale=0.125)
            p = fm(256, 64)
            s.copy("act", kcraw[:, 16:528], p[0:64, :], [p], [kcraw])
            p = fm(320, 64)
            s.copy("dve", kslc[0:64, q0:q0 + 512], p[0:64, :], [p], [kslc])
            p = fm(384, 64)
            s.copy("act", kwin[0:64, slot0:slot0 + 4, :], p[0:64, :].rearrange("p (a b) -> p a b", a=4), [p], [kwin])
            p = fm(448, 64)
            s.copy("dve", vcraw[:, 16:528], p[0:64, :], [p], [vcraw])
            p = fm(512, 128)
            s.copy("act", hqT[:], p[:], [p], [hqT])
            p = fm(640, 128)
            s.copy("dve", hzT[:], p[:], [p], [hzT])

            s.mute = 3 not in ST
            gts, pcs, vvs, sgs, obs = [], [], [], [], []
            for tb in range(4):
                kt = 4 * qt + tb
                p = psr.next()
                for kc in range(16):
                    s.mm(p[:, 0:g1w], xt[:, kc, tb * 128:(tb + 1) * 128], wtm[:, kc, 0:g1w], kc == 0, kc == 15, [wtm, xt], [p])
                sA = stgA.next()
                s.copy("dve", sA[:, 0:g1w], p[:, 0:g1w], [p], [sA])
                s.copy("pool", vslc[:, kt, 0:64], sA[:, 0:64], [sA], [vslc])
                s.copy("pool", vwin[:, kt % 8, 0:64], sA[:, 64:128], [sA], [vwin])
                g_ = gate.next()
                s.act(g_[:], sA[:, 128:136], AF.Sigmoid, [sA], [g_])
                gts.append(g_)
                p = psr.next()
                for kc in range(16):
                    s.mm(p[:, 0:g2w], xt[:, kc, tb * 128:(tb + 1) * 128], wtm[:, kc, g1w:g1w + g2w], kc == 0, kc == 15, [wtm, xt], [p])
                pc = pcb.next()
                vv = vvb.next()
                sg = sgb.next()
                sB = stgB.next()
                s.copy("act", sB[:, 0:g2w], p[:, 0:g2w], [p], [sB])
                s.copy("pool", pc[:], sB[:, 0:128], [sB], [pc])
                s.copy("dve", vv[:], sB[:, 128:256], [sB], [vv])
                s.act(sg[:], sB[:, 256:384], AF.Silu, [sB], [sg])
                pcs.append(pc)
                vvs.append(vv)
                sgs.append(sg)
                obs.append(outb.next())

            s.mute = 4 not in ST
            c_lo = 32 * qt - 1 if qt > 0 else 0
            c_hi = 32 * qt + 30
            ncb = c_hi - c_lo + 1
            i0 = 0 if qt > 0 else 1
            kview = kcraw[:].rearrange("p (c s) -> p c s", s=16)
            vview = vcraw[:].rearrange("p (c s) -> p c s", s=16)
            p = psr.next()
            for l_ in range(32):
                s.mm(p[0:64, 0:ncb], cw_k[:, l_, :], kview[:, i0 + l_ // 16:i0 + l_ // 16 + ncb, l_ % 16], l_ == 0, l_ == 31, [cw_k, kcraw], [p])
            s.act(kcaug[0:64, c_lo:c_hi + 1], p[0:64, 0:ncb], AF.Identity, [p, cbias], [kcaug], bias=cbias[:, 0:1])
            p = psr.next()
            for l_ in range(32):
                s.mm(p[0:64, 0:ncb], cw_v[:, l_, :], vview[:, i0 + l_ // 16:i0 + l_ // 16 + ncb, l_ % 16], l_ == 0, l_ == 31, [cw_v, vcraw], [p])
            vT_ = vcT.next()
            s.act(vT_[:, 0:ncb], p[0:64, 0:ncb], AF.Identity, [p, cbias], [vT_], bias=cbias[:, 1:2])
            s.tr(pst[0:ncb, 0, 0:64], vT_[:, 0:ncb], identb[0:64, 0:64], [vT_, identb], [pst])
            vt_ = vct.next()
            s.copy("dve", vt_[0:ncb, :], pst[0:ncb, 0, 0:64], [pst], [vt_])
            c = c_lo
            while c <= c_hi:
                ct_ = c // 128
                n_ = min(c_hi + 1, (ct_ + 1) * 128) - c
                s.dma("sp", vcaug[c % 128:c % 128 + n_, ct_, 0:64], vt_[c - c_lo:c - c_lo + n_, :], reads=[vt_], writes=[vcaug])
                c += n_

            s.mute = 5 not in ST
            nct = (32 * qt + 30) // 128 + 1
            oc_ps = {}
            for j in range(4):
                mine = j in (JA, JA + 1)
                ets = []
                for ct_ in range(nct):
                    dl = qt - 4 * ct_
                    p = psr.next()
                    last = dl > 4
                    s.mm(p[:], kcaug[0:68, ct_ * 128:(ct_ + 1) * 128], qa[j][0:68, :], True, last, [kcaug, qa[j]], [p])
                    if not last:
                        s.mm(p[:], identb[:], cmask[:, dl, :], False, True, [identb, cmask], [p])
                    e_ = ET[j % 2][ct_]
                    s.act(e_[:], p[:], AF.Exp, [p], [e_])
                    ets.append(e_)
                if mine:
                    oa = oacc_ps.next()
                    for ct_ in range(nct):
                        s.mm(oa[0:65, :], vcaug[:, ct_, 0:65], ets[ct_][:], ct_ == 0, ct_ == nct - 1, [vcaug, ets[ct_]], [oa])
                    oc_ps[j] = oa
                for tb in range(4):
                    for ct_ in range(nct):
                        s.mm(impb[:, tb, :], ets[ct_][:, tb * 128:(tb + 1) * 128], selmap[:, ct_, 0:128], ct_ == 0, ct_ == nct - 1, [ets[ct_], selmap], [impb])
                for tb in range(4):
                    for ct_ in range(nct):
                        s.mm(denb[:, tb:tb + 1], ets[ct_][:, tb * 128:(tb + 1) * 128], selmap[:, ct_, 128:129], ct_ == 0, ct_ == nct - 1, [ets[ct_], selmap], [denb])
                rd = rden.next()
                s.ts("dve", rd[:], denb[:, 0:4], 1e-30, None, ALU.max, None, [denb], [rd])
                s.recip(rd[:], rd[:], [rd], [rd])
                im = imps.next()
                s.copy("act", im[:], impb[:], [impb], [im])
                for tb in range(4):
                    if j == 0:
                        s.ts("dve", impacc[:, tb, :], im[:, tb, :], rd[:, tb:tb + 1], None, ALU.mult, None, [im, rd], [impacc])
                    else:
                        s.stt("dve", impacc[:, tb, :], im[:, tb, :], rd[:, tb:tb + 1], impacc[:, tb, :], ALU.mult, ALU.add, [im, rd, impacc], [impacc])
                if mine:
                    epilogue(oc_ps[j], j - JA, 0, gts, obs)

            s.mute = 6 not in ST
            mb = mbT.next()
            for tb in range(4):
                tbg = 4 * qt + tb
                c0 = 126 - 2 * tbg
                w = selw.next()
                s.tt("dve", w[:], impacc[:, tb, :], keep[:, c0:c0 + 128], ALU.mult, [impacc, keep], [w])
                s.tt("dve", w[:], w[:], addm[:, c0:c0 + 128], ALU.add, [w, addm], [w])
                s.memset("dve", w[:, 0:1], 1e6, [w])
                m = m8.next()
                w2 = selw.next()
                s.op("dve", lambda e, a=(m, w): e.max(out=a[0][:, 0:8], in_=a[1][:]), [w], [m])
                s.op("dve", lambda e, a=(w2, m, w): e.match_replace(out=a[0][:], in_to_replace=a[1][:, 0:8], in_values=a[2][:], imm_value=-3e38), [w, m], [w2])
                s.op("dve", lambda e, a=(m, w2): e.max(out=a[0][:, 8:16], in_=a[1][:]), [w2], [m])
                s.ts("dve", w2[:], w[:], m[:, 15:16], None, ALU.subtract, None, [w, m], [w2])
                s.ts("dve", w2[:], w2[:], 0.0, None, ALU.is_ge, None, [w2], [w2])
                sb_ = selb.next()
                s.ts("dve", sb_[:], w2[:], -NEG, NEG, ALU.mult, ALU.add, [w2], [sb_])
                s.tr(pst[:, 1 + tb, :], sb_[:], identb[:], [sb_, identb], [pst])
            s.copy("act", mb[:], pst[:, 1:5, :].rearrange("p a b -> p (a b)"), [pst], [mb])

            s.mute = 7 not in ST
            for hl in range(2):
                j = JA + hl
                oa = oacc_ps.next()
                nk = 4 * qt + 4
                for kt in range(nk):
                    p = psr.next()
                    s.mm(p[:], kslc[0:68, kt * 128:(kt + 1) * 128], qa[j][0:68, :], True, False, [kslc, qa[j]], [p])
                    diag = kt >= 4 * qt
                    hi_ = 64 * (kt // 32)
                    s.mm(p[:], E_all[hi_:hi_ + 64, kt % 32, :], mb[hi_:hi_ + 64, :], False, not diag, [E_all, mb], [p])
                    if diag:
                        s.mm(p[:], identb[:], dmask[:, kt - 4 * qt + 4, :], False, True, [identb, dmask], [p])
                    pt = PT.next()
                    s.act(pt[:], p[:], AF.Exp, [p], [pt])
                    s.mm(oa[0:65, :], vslc[:, kt, 0:65], pt[:], kt == 0, kt == nk - 1, [vslc, pt], [oa])
                epilogue(oa, hl, 1, gts, obs)

            s.mute = 8 not in ST
            for hl in range(2):
                j = JA + hl
                oa = oacc_ps.next()
                k_lo = max(0, 4 * qt - 4)
                nk = 4 * qt + 4
                for kt in range(k_lo, nk):
                    p = psr.next()
                    s.mm(p[:], kwin[0:68, kt % 8, :], qa[j][0:68, :], True, False, [kwin, qa[j]], [p])
                    s.mm(p[:], identb[:], dmask[:, kt - 4 * qt + 4, :], False, True, [identb, dmask], [p])
                    pt = PT.next()
                    s.act(pt[:], p[:], AF.Exp, [p], [pt])
                    s.mm(oa[0:65, :], vwin[:, kt % 8, 0:65], pt[:], kt == k_lo, kt == nk - 1, [vwin, pt], [oa])
                epilogue(oa, hl, 2, gts, obs)

            s.mute = 9 not in ST
            hgrn_tile(qt, vvs, sgs, obs)

            s.mute = 10 not in ST
            for tb in range(4):
                first = (qt == 0 and tb == 0)
                p = psr.next()
                s.mm(p[:, 0:128], pcs[tb][:], poolP[:, 2 if first else 0, :], True, False, [pcs[tb], poolP], [p])
                s.mm(p[:, 0:128], pc_prev[:], poolP[:, 1, :], False, True, [pc_prev, poolP], [p])
                pt_ = ptb.next()
                s.copy("act", pt_[:], p[:, 0:128], [p], [pt_])
                p2 = psr.next()
                s.mm(p2[:, 0:128], pt_[:], poolw[:], True, True, [pt_, poolw], [p2])
                s.tt("dve", obs[tb][:, 256:384], p2[:, 0:128], poolscale[:], ALU.mult, [p2, poolscale], [obs[tb]])
                pc_prev = pcs[tb]

            s.mute = 11 not in ST
            for tb in range(4):
                s.dma("sp", om[q0 + tb * 128:q0 + (tb + 1) * 128, :], obs[tb][:], reads=[obs[tb]])
        s.emit()
        print("M program: ops", s.n, "sbuf peak", s.peak, "gate/pcb/vvb/sgb/outb offs", [x.bufs[0].t.manual_sbuf_range for x in (gate, pcb, vvb, sgb, outb)], flush=True)
    return nc


def _m_consts():
    cst = {}
    t = np.arange(S)
    cst["kpos"] = np.stack([np.ones(S), np.ones(S), t // 64, t % 64]).astype(np.float32)
    cp = 16 * np.arange(512) + 31
    cst["cpos"] = np.stack([np.ones(512), np.ones(512), cp // 64, cp % 64]).astype(np.float32)
    cl = np.arange(128)[:, None]
    tl = np.arange(512)[None, :]
    cst["cmask"] = np.stack([np.where(16 * cl + 31 - tl <= 512 * dl, 0.0, NEG) for dl in range(5)]).astype(np.float32)
    dm = []
    for rel in range(-4, 4):
        dist = tl - cl - 128 * rel
        dm.append(np.where((dist >= 0) & (dist < 512), 0.0, NEG))
    cst["dmask"] = np.stack(dm).astype(np.float32)
    rr = (np.arange(128) % 64)[:, None, None]
    cst["E_all"] = (rr == 2 * np.arange(32)[None, :, None] + (np.arange(128)[None, None, :] // 64)).astype(np.float32)
    c0 = np.arange(511)[:, None] * 16
    s0 = np.arange(128)[None, :] * 64
    ov = np.clip(np.minimum(c0 + 32, s0 + 64) - np.maximum(c0, s0), 0, None) / 32.0
    sm = np.zeros((512, 129), np.float32)
    sm[:511, :128] = ov
    sm[:, 128] = 1.0
    cst["selmap"] = _c(sm.reshape(4, 128, 129).transpose(1, 0, 2))
    r = np.arange(256)[None, :] - 126
    hi = (np.arange(128)[:, None] >= 64).astype(np.int64)
    forced = (r == hi) | (r == hi - 1)
    future = r >= hi + 1
    cst["keep"] = np.where(forced | future, 0.0, 1.0).astype(np.float32)
    cst["addm"] = np.where(forced, 1e6, np.where(future, -1e30, 0.0)).astype(np.float32)
    cst["ident"] = np.eye(128, dtype=np.float32)
    cst["tri2"] = ((np.arange(128)[:, None] % 64) <= np.arange(64)[None, :]).astype(np.float32)
    return cst


def prep_M(inp, l, x):
    cst = _m_consts()
    w_in = inp["w_in"][l]
    maps = []
    xTs = [_c(x[b].T) for b in range(2)]
    tt_ = np.arange(S)
    for c in range(NCORE):
        b, u = divmod(c, 4)
        g = u // 2
        ja = 2 * (u % 2)
        heads = [ja, ja + 1] + [j for j in range(4) if j not in (ja, ja + 1)]
        m = dict(cst)
        m["xT"] = xTs[b]

        def kvcol(br, kv):
            return OFF_KV + ((br * 2 + kv) * 2 + g) * 64
        cols = []
        for j in heads:
            cols += list(range(OFF_Q + (g * 4 + j) * 64, OFF_Q + (g * 4 + j + 1) * 64))
        for (br, kv) in ((0, 0), (1, 0), (2, 0), (0, 1)):
            cols += list(range(kvcol(br, kv), kvcol(br, kv) + 64))
        cols += list(range(OFF_H + u * 128, OFF_H + (u + 1) * 128))
        cols += list(range(OFF_H + 512 + u * 128, OFF_H + 512 + (u + 1) * 128))
        m["wfm"] = _c(w_in[:, cols].reshape(16, 128, 768).transpose(1, 0, 2))
        cols = list(range(kvcol(1, 1), kvcol(1, 1) + 64)) + list(range(kvcol(2, 1), kvcol(2, 1) + 64))
        gcols = [OFF_G + (g * 4 + j) * 3 + br for j in (ja, ja + 1) for br in range(3)]
        cols += gcols + [gcols[0], gcols[0]]
        cols += list(range(OFF_P + u * 128, OFF_P + (u + 1) * 128))
        cols += list(range(OFF_H + 1024 + u * 128, OFF_H + 1024 + (u + 1) * 128))
        cols += list(range(OFF_H + 1536 + u * 128, OFF_H + 1536 + (u + 1) * 128))
        m["wtm"] = _c(w_in[:, cols].reshape(16, 128, 520).transpose(1, 0, 2))
        qp = np.zeros((4, 4, S), np.float32)
        for i, j in enumerate(heads):
            sl = 2.0 ** (-(g * 4 + j + 1))
            qp[i, 0] = -64.0 * sl * (tt_ // 64)
            qp[i, 1] = -sl * (tt_ % 64)
            qp[i, 2] = 64.0 * sl
            qp[i, 3] = sl
        m["qpos"] = qp
        m["cw_k"] = _c(inp["nsa_cmp_w"][l][0].transpose(1, 0, 2))
        m["cw_v"] = _c(inp["nsa_cmp_w"][l][1].transpose(1, 0, 2))
        m["cp_k"] = _c(inp["nsa_cmp_pos"][l][0].T)
        m["cp_v"] = _c(inp["nsa_cmp_pos"][l][1].T)
        m["lblog"] = _c(inp["hgrn_lb_logits"][:, u * 128:(u + 1) * 128].T)
        m["lbsel"] = np.full((128, 1), float(l), np.float32)
        m["normg"] = _bc(inp["hgrn_norm_g"][l][u * 128:(u + 1) * 128])
        win = (2, 4, 8, 16)[u]
        sI = np.arange(128)[:, None]
        tI = np.arange(128)[None, :]
        P = np.zeros((128, 3, 128), np.float32)
        P[:, 0, :] = np.where((sI > tI - win) & (sI <= tI), 1.0 / win, 0.0) - (sI == tI)
        P[:, 1, :] = np.where(sI >= 128 + tI - win + 1, 1.0 / win, 0.0)
        cnt = np.minimum(tI + 1, win).astype(np.float32)
        P[:, 2, :] = np.where((sI > tI - win) & (sI <= tI), 1.0 / cnt, 0.0) - (sI == tI)
        m["poolP"] = P
        m["poolw"] = _c(inp["pool_w"][l][u])
        m["poolscale"] = _bc(inp["pool_scale"][l][u * 128:(u + 1) * 128])
        maps.append(m)
    return maps


def run_M(inp, l, x):
    if "M" not in _NC_CACHE:
        _NC_CACHE["M"] = build_M()
    maps = prep_M(inp, l, x)
    res = run_bass_kernel_spmd(_NC_CACHE["M"], maps, core_ids=list(range(NCORE)))
    oabc = np.zeros((2, S, 1536), np.float32)
    for c in range(NCORE):
        b, u = divmod(c, 4)
        o = res.results[c]["om"]
        for k in range(3):
            oabc[b, :, 512 * k + 128 * u:512 * k + 128 * (u + 1)] = o[:, 128 * k:128 * (k + 1)]
    return oabc


def kernel(**inputs):
    inp = {k: np.asarray(v) for k, v in inputs.items()}
    x = np.ascontiguousarray(inp["x"], dtype=np.float32)
    for l in range(2):
        oabc = run_M(inp, l, x)
        x = run_R(inp, l, x, oabc)
    return x
```

```python
import contextlib
import numpy as np
import concourse.bass as bass
import concourse.mybir as mybir
from concourse.bass_utils import run_bass_kernel_spmd

F32 = mybir.dt.float32
BF16 = mybir.dt.bfloat16
AF = mybir.ActivationFunctionType
ALU = mybir.AluOpType
AX = mybir.AxisListType

D = 2048
S = 8192
NCORE = 8
ALPHA = 4 ** 0.25
EPS = 1e-5
DFF = 5632
NEG = -30000.0
SBUF_BASE = 16512
SBUF_CAP = 229344


class Buf:
    __slots__ = ("t", "w", "r")

    def __init__(self, t):
        self.t = t
        self.w = None
        self.r = []

    def __getitem__(self, k):
        return self.t[k]


class Sched:
    CE = ("pe", "act", "dve", "pool")
    DQ = ("sp", "act", "pool")

    def __init__(self, nc, es, ring=8):
        self.nc = nc
        self.es = es
        self.ops = {e: [] for e in ("pe", "act", "dve", "pool", "sp")}
        self.csem = {e: es.enter_context(nc.semaphore("c_" + e)) for e in self.CE}
        self.ccnt = {e: 0 for e in self.CE}
        self.ring = ring
        self.dsem = {q: [es.enter_context(nc.semaphore("d_%s%d" % (q, i))) for i in range(ring)] for q in self.DQ}
        self.dcnt = {q: 0 for q in self.DQ}
        self.dtok = {q: [None] * ring for q in self.DQ}
        self.seen = {e: {} for e in self.ops}
        self.n = 0

    def buf(self, name, shape, dt, psum=False):
        if psum:
            t = self.es.enter_context(self.nc.psum_tensor(name, shape, dt))
            return Buf(t)
        n = 1
        for d_ in shape[1:]:
            n *= d_
        nbytes = n * (2 if dt == BF16 else 4)
        nbytes = (nbytes + 63) // 64 * 64
        self.uid = getattr(self, "uid", 0) + 1
        off = getattr(self, "off", SBUF_BASE)
        assert off + nbytes <= SBUF_CAP, ("SBUF overflow", name, off, nbytes)
        t = self.nc.alloc_sbuf_tensor_at("%s_%d" % (name, self.uid), list(shape), dt, offset=off)
        self.off = off + nbytes
        self.peak = max(getattr(self, "peak", 0), self.off)
        return Buf(t)

    def mark(self):
        return getattr(self, "off", SBUF_BASE)

    def release(self, m):
        self.barrier()
        self.off = m

    def barrier(self):
        toks = []
        for q in self.DQ:
            for t in self.dtok[q]:
                if t is not None:
                    toks.append(t)
        for e in self.CE:
            if self.ccnt[e]:
                toks.append((self.csem[e], self.ccnt[e], "c_" + e, e))
        for eng in self.ops:
            waits = []
            seen = self.seen[eng]
            for (sem, val, key, src) in toks:
                if seen.get(key, 0) >= val:
                    continue
                seen[key] = val
                waits.append((sem, val))
            if waits:
                self.ops[eng].append((waits, None, None, 0))

    def _waits(self, eng, reads, writes, extra=()):
        deps = []
        for b in reads:
            if b.w is not None:
                deps.append(b.w)
        for b in writes:
            if b.w is not None:
                deps.append(b.w)
            deps.extend(b.r)
        deps.extend(extra)
        waits = []
        seen = self.seen[eng]
        for (sem, val, key, src) in deps:
            if src == "pe" and eng == "pe":
                continue
            if seen.get(key, 0) >= val:
                continue
            seen[key] = val
            waits.append((sem, val))
        return waits

    def op(self, eng, fn, reads=(), writes=()):
        if getattr(self, 'mute', False):
            return None
        waits = self._waits(eng, reads, writes)
        self.ccnt[eng] += 1
        tok = (self.csem[eng], self.ccnt[eng], "c_" + eng, eng)
        for b in reads:
            b.r.append(tok)
        for b in writes:
            b.w = tok
            b.r = []
        self.ops[eng].append((waits, fn, self.csem[eng], 1))
        self.n += 1
        return tok

    def dma(self, q, out, in_, reads=(), writes=()):
        if getattr(self, 'mute', False):
            return None
        i = self.dcnt[q]
        slot = i % self.ring
        extra = []
        if self.dtok[q][slot] is not None:
            extra.append(self.dtok[q][slot])
        waits = self._waits(q, reads, writes, extra)
        self.dcnt[q] += 1
        sem = self.dsem[q][slot]
        tok = (sem, 16 * (i // self.ring + 1), "d_%s%d" % (q, slot), "dma_" + q)
        self.dtok[q][slot] = tok
        for b in reads:
            b.r.append(tok)
        for b in writes:
            b.w = tok
            b.r = []
        self.ops[q].append((waits, lambda e, o=out, i_=in_: e.dma_start(out=o, in_=i_), sem, 16))
        self.n += 1
        return tok

    def coll(self, kind, ins, outs, groups, reads=(), writes=()):
        q = "pool"
        i = self.dcnt[q]
        slot = i % self.ring
        extra = []
        if self.dtok[q][slot] is not None:
            extra.append(self.dtok[q][slot])
        waits = self._waits(q, reads, writes, extra)
        self.dcnt[q] += 1
        sem = self.dsem[q][slot]
        tok = (sem, 16 * (i // self.ring + 1), "d_%s%d" % (q, slot), "dma_" + q)
        self.dtok[q][slot] = tok
        for b in reads:
            b.r.append(tok)
        for b in writes:
            b.w = tok
            b.r = []
        self.ops[q].append((waits, lambda e, a=(kind, ins, outs, groups): e.collective_compute(a[0], ALU.bypass, replica_groups=a[3], ins=a[1], outs=a[2]), sem, 16))
        self.n += 1
        return tok

    def finish(self):
        extra = []
        for q in self.DQ:
            for t in self.dtok[q]:
                if t is not None:
                    extra.append(t)
        for e in self.CE:
            if self.ccnt[e]:
                extra.append((self.csem[e], self.ccnt[e], "c_" + e, e))
        waits = self._waits("sp", (), (), extra)
        self.ops["sp"].append((waits, None, None, 0))

    def emit(self):
        self.finish()
        nc = self.nc
        ops = self.ops

        def replay(name, e):
            for waits, fn, sem, inc in ops[name]:
                for (s_, v_) in waits:
                    e.wait_ge(s_, v_)
                if fn is not None:
                    fn(e).then_inc(sem, inc)

        with nc.Block() as block:
            @block.tensor
            def _(e):
                replay("pe", e)

            @block.scalar
            def _(e):
                replay("act", e)

            @block.vector
            def _(e):
                replay("dve", e)

            @block.gpsimd
            def _(e):
                replay("pool", e)

            @block.sync
            def _(e):
                replay("sp", e)

    def mm(self, out, lhsT, rhs, start, stop, reads, writes):
        return self.op("pe", lambda e, a=(out, lhsT, rhs, start, stop): e.matmul(a[0], a[1], a[2], start=a[3], stop=a[4]), reads, writes)

    def tr(self, out, in_, ident, reads, writes):
        return self.op("pe", lambda e, a=(out, in_, ident): e.transpose(a[0], a[1], a[2]), reads, writes)

    def act(self, out, in_, func, reads, writes, bias=None, scale=None, eng="act"):
        kw = {}
        if bias is not None:
            kw["bias"] = bias
        if scale is not None:
            kw["scale"] = scale
        return self.op("act", lambda e, a=(out, in_, func, kw): e.activation(out=a[0], in_=a[1], func=a[2], **a[3]), reads, writes)

    def tt(self, eng, out, in0, in1, op, reads, writes):
        return self.op(eng, lambda e, a=(out, in0, in1, op): e.tensor_tensor(a[0], a[1], a[2], a[3]), reads, writes)

    def ts(self, eng, out, in0, s1, s2, op0, op1, reads, writes):
        if s2 is None:
            return self.op(eng, lambda e, a=(out, in0, s1, op0): e.tensor_scalar(a[0], a[1], a[2], None, a[3]), reads, writes)
        return self.op(eng, lambda e, a=(out, in0, s1, s2, op0, op1): e.tensor_scalar(a[0], a[1], a[2], a[3], a[4], a[5]), reads, writes)

    def stt(self, eng, out, in0, scalar, in1, op0, op1, reads, writes):
        return self.op(eng, lambda e, a=(out, in0, scalar, in1, op0, op1): e.scalar_tensor_tensor(a[0], a[1], a[2], a[3], a[4], a[5]), reads, writes)

    def copy(self, eng, out, in_, reads, writes):
        if eng == "act":
            return self.op("act", lambda e, a=(out, in_): e.copy(a[0], a[1]), reads, writes)
        return self.op(eng, lambda e, a=(out, in_): e.tensor_copy(a[0], a[1]), reads, writes)

    def memset(self, eng, ap, val, writes):
        return self.op(eng, lambda e, a=(ap, val): e.memset(a[0], a[1]), (), writes)

    def rsum(self, eng, out, in_, reads, writes):
        return self.op(eng, lambda e, a=(out, in_): e.reduce_sum(a[0], a[1], AX.X), reads, writes)

    def recip(self, out, in_, reads, writes):
        return self.op("dve", lambda e, a=(out, in_): e.reciprocal(a[0], a[1]), reads, writes)


class Ring:
    def __init__(self, bufs):
        self.bufs = bufs
        self.i = 0

    def next(self):
        b = self.bufs[self.i % len(self.bufs)]
        self.i += 1
        return b


def _dram(nc, name, shape, dt=F32, kind="ExternalInput"):
    return nc.dram_tensor(name, list(shape), dt, kind=kind).ap()


TT = 2176
NB = TT // 128
TTILES = [(0, 512), (512, 512), (1024, 512), (1536, 512), (2048, 128)]
NFC = DFF // 128

R_INPUTS = {
    "xT": (D, TT), "xtok": (TT, D), "oT": (1536, TT),
    "w_uv": (128, 16, 1024), "w_mg": (16, 128, 4, 16, 128), "w_br": (16, 128, 4, 4, 128), "w_mix": (128, 16, D),
    "sgw": (128, 4, 128), "sgb": (128, 4), "sg_g": (128, 512), "sg_b": (128, 512), "trimask": (128, 128),
    "ln1_g": (128, D), "ln1_b": (128, D), "ln2_g": (128, D), "ln2_b": (128, D), "ln3_g": (128, D), "ln3_b": (128, D),
    "mem": (256, D), "memln_g": (128, D), "memln_b": (128, D),
    "wq": (128, 16, 512), "wk": (128, 16, 512), "wv": (128, 16, 512), "wo": (128, 4, D),
    "ffn_in": (NFC, 128, 2, 16, 128), "ffn_out": (128, NFC, D), "convw": (128, NFC, 4),
    "ident": (128, 128), "flag": (128, 1), "ones": (128, 128),
}


def build_R():
    nc = bass.Bass("TRN2", target_bir_lowering=False)
    I = {k: _dram(nc, k, v) for k, v in R_INPUTS.items()}
    xout = _dram(nc, "xout", (2048, D), kind="ExternalOutput")
    odT_d = _dram(nc, "odT_d", (4, 128, TT), BF16, kind="Internal")
    preT_d = _dram(nc, "preT_d", (16, 128, TT), BF16, kind="Internal")
    x1_d = _dram(nc, "x1_d", (TT, D), F32, kind="Internal")
    x1T_d = _dram(nc, "x1T_d", (16, 128, TT), BF16, kind="Internal")
    aT_d = _dram(nc, "aT_d", (4, 128, TT), BF16, kind="Internal")
    x2_d = _dram(nc, "x2_d", (TT, D), F32, kind="Internal")
    x2T_d = _dram(nc, "x2T_d", (16, 128, TT), BF16, kind="Internal")
    act_d = _dram(nc, "act_d", (NFC, 128, TT), BF16, kind="Internal")
    ffo_d = _dram(nc, "ffo_d", (TT, D), F32, kind="Internal")
    with contextlib.ExitStack() as es:
        s = Sched(nc, es)
        B = s.buf
        ident = B("ident", [128, 128], BF16)
        ones = B("ones", [128, 128], BF16)
        flag = B("flag", [128, 1], F32)
        kT = B("kT", [128, 4, 256], BF16)
        vtok = B("vtok", [128, 2, 512], BF16)
        ps = Ring([B("ps%d" % i, [128, 512], F32, psum=True) for i in range(6)])
        pst = Ring([B("pst%d" % i, [128, 8, 128], BF16, psum=True) for i in range(2)])
        s.dma("pool", ident[:], I["ident"][:, :], writes=[ident])
        s.dma("pool", ones[:], I["ones"][:, :], writes=[ones])
        s.dma("sp", flag[:], I["flag"][:, :], writes=[flag])

        class LN:
            def __init__(self, gname, bname):
                self.g = B("lng", [128, D], F32)
                self.b = B("lnb", [128, D], F32)
                self.sq = B("lnsq", [128, D], F32)
                self.zt = Ring([B("zt%d" % i, [128, D], F32) for i in range(3)])
                self.zb = Ring([B("zb%d" % i, [128, D], BF16) for i in range(3)])
                self.xr = Ring([B("xr%d" % i, [128, D], F32) for i in range(5)])
                self.st = Ring([B("st%d" % i, [128, 8], F32) for i in range(6)])
                self.ob = Ring([B("lnob%d" % i, [128, 16, 128], BF16) for i in range(3)])
                s.dma("sp", self.g[:], I[gname][:, :], writes=[self.g])
                s.dma("sp", self.b[:], I[bname][:, :], writes=[self.b])

            def norm_gen(self, z, out):
                t = self.st.next()
                sq = self.sq
                s.op("act", lambda e, a=(sq, z, t): e.activation(out=a[0][:], in_=a[1][:], func=AF.Identity, accum_out=a[2][:, 0:1]), [z], [sq, t])
                s.op("act", lambda e, a=(sq, z, t): e.activation(out=a[0][:], in_=a[1][:], func=AF.Square, accum_out=a[2][:, 1:2]), [z], [sq, t])
                s.ts("dve", t[:, 2:3], t[:, 0:1], 1.0 / D, None, ALU.mult, None, [t], [t])
                s.ts("dve", t[:, 3:4], t[:, 1:2], 1.0 / D, None, ALU.mult, None, [t], [t])
                s.stt("dve", t[:, 4:5], t[:, 2:3], -1.0, t[:, 2:3], ALU.mult, ALU.mult, [t], [t])
                s.tt("dve", t[:, 5:6], t[:, 3:4], t[:, 4:5], ALU.add, [t], [t])
                s.act(t[:, 6:7], t[:, 5:6], AF.Sqrt, [t], [t], bias=EPS)
                s.recip(t[:, 7:8], t[:, 6:7], [t], [t])
                yield
                s.stt("dve", out[:], z[:], t[:, 2:3], self.g[:], ALU.subtract, ALU.mult, [z, t, self.g], [out])
                s.stt("dve", out[:], out[:], t[:, 7:8], self.b[:], ALU.mult, ALU.add, [out, t, self.b], [out])

            def norm(self, z, out):
                for _ in self.norm_gen(z, out):
                    pass

            def to_featmajor(self, xf, dst_d, dst_db, tb, width=16):
                xb = self.zb.next()
                s.copy("act", xb[:, 0:width * 128], xf[:, 0:width * 128], [xf], [xb])
                ob = self.ob.next()
                for h in range((width + 7) // 8):
                    p = pst.next()
                    nj = min(8, width - h * 8)
                    for j in range(nj):
                        kc = h * 8 + j
                        s.tr(p[:, j, :], xb[:, kc * 128:(kc + 1) * 128], ident[:], [xb, ident], [p])
                    s.copy("dve", ob[:, h * 8:h * 8 + nj, :], p[:, 0:nj, :], [p], [ob])
                s.dma("sp", dst_d[0:width, :, tb * 128:(tb + 1) * 128].rearrange("c p t -> p c t"), ob[:, 0:width, :], reads=[ob], writes=[dst_db[tb]])

            def proj_ln_gen(self, tb, lhs_fn, nk, w, xres_ap, xres_db, out_d, out_db, outT_d, outT_db, final_out=None):
                z = self.zt.next()
                xres = self.xr.next()
                s.dma("sp", xres[:], xres_ap, reads=list(xres_db), writes=[xres])
                for dt_ in range(4):
                    p = ps.next()
                    for k in range(nk):
                        lt, lbuf = lhs_fn(k)
                        s.mm(p[:], lt, w[:, k, dt_ * 512:(dt_ + 1) * 512], k == 0, k == nk - 1, [lbuf, w], [p])
                    s.stt("dve", z[:, dt_ * 512:(dt_ + 1) * 512], xres[:, dt_ * 512:(dt_ + 1) * 512], ALPHA, p[:], ALU.mult, ALU.add, [xres, p], [z])
                yield
                o = self.xr.next()
                for _ in self.norm_gen(z, o):
                    yield
                yield
                if final_out is not None:
                    if tb >= 1:
                        s.dma("sp", final_out[(tb - 1) * 128:tb * 128, :], o[:], reads=[o])
                else:
                    s.dma("sp", out_d[tb * 128:(tb + 1) * 128, :], o[:], reads=[o], writes=[out_db[tb]])
                    self.to_featmajor(o, outT_d, outT_db, tb)

        def run_pipelined(make_gen, n, depth=2):
            active = []
            nxt = 0
            while nxt < n or active:
                if nxt < n and len(active) < depth:
                    active.append(make_gen(nxt))
                    nxt += 1
                for g_ in list(active):
                    try:
                        next(g_)
                    except StopIteration:
                        active.remove(g_)

        def DB(n):
            return [Buf(None) for _ in range(n)]

        odT_db, preT_db, x1_db, x1T_db, aT_db, x2_db, x2T_db, act_db, ffo_db = DB(NB), DB(16), DB(NB), DB(NB), DB(4), DB(NB), DB(NB), DB(NFC), DB(NB * 4)
        base = s.mark()

        ln = LN("memln_g", "memln_b")
        memT_d = _dram(nc, "memT_d", (16, 128, 256), BF16, kind="Internal")
        memT_db = DB(2)
        for mb in range(2):
            z = ln.zt.next()
            o = ln.xr.next()
            s.dma("sp", z[:], I["mem"][mb * 128:(mb + 1) * 128, :], writes=[z])
            ln.norm(z, o)
            ln.to_featmajor(o, memT_d, memT_db, mb)
        memT = B("memT", [128, 16, 256], BF16)
        wk = B("wk", [128, 16, 512], BF16)
        wv = B("wv", [128, 16, 512], BF16)
        s.dma("sp", memT[:], memT_d[:, :, :].rearrange("c p t -> p c t"), reads=memT_db, writes=[memT])
        s.dma("pool", wk[:], I["wk"][:, :, :], writes=[wk])
        s.dma("pool", wv[:], I["wv"][:, :, :], writes=[wv])
        for h in range(4):
            p = ps.next()
            for kc in range(16):
                s.mm(p[:, 0:256], wk[:, kc, h * 128:(h + 1) * 128], memT[:, kc, :], kc == 0, kc == 15, [wk, memT], [p])
            s.copy("dve", kT[:, h, :], p[:, 0:256], [p], [kT])
        for mb in range(2):
            p = ps.next()
            for kc in range(16):
                s.mm(p[:], memT[:, kc, mb * 128:(mb + 1) * 128], wv[:, kc, :], kc == 0, kc == 15, [wv, memT], [p])
            s.copy("act", vtok[:, mb, :], p[:], [p], [vtok])
        s.release(base)

        xT = B("xT", [128, 16, TT], BF16)
        for kc in range(16):
            s.dma("pool", xT[:, kc, :], I["xT"][kc * 128:(kc + 1) * 128, :], writes=[xT])
        w_uv = B("w_uv", [128, 16, 1024], BF16)
        sgw = B("sgw", [128, 4, 128], F32)
        sgwb = B("sgwb", [128, 4, 128], BF16)
        tri = B("tri", [128, 128], F32)
        sgb = B("sgb", [128, 4], F32)
        sg_g = B("sg_g", [128, 512], F32)
        sg_b = B("sg_b", [128, 512], F32)
        for kc in range(16):
            s.dma("pool", w_uv[:, kc, :], I["w_uv"][:, kc, :], writes=[w_uv])
        s.dma("sp", sgw[:], I["sgw"][:, :, :], writes=[sgw])
        s.dma("sp", tri[:], I["trimask"][:, :], writes=[tri])
        s.dma("sp", sgb[:], I["sgb"][:, :], writes=[sgb])
        s.dma("sp", sg_g[:], I["sg_g"][:, :], writes=[sg_g])
        s.dma("sp", sg_b[:], I["sg_b"][:, :], writes=[sg_b])
        for g in range(4):
            s.tt("dve", sgwb[:, g, :], sgw[:, g, :], tri[:], ALU.mult, [sgw, tri], [sgwb])
        uvt = Ring([B("uvt%d" % i, [128, 512], F32) for i in range(12)])
        gt = Ring([B("gt%d" % i, [128, 512], F32) for i in range(6)])
        vlb = Ring([B("vlb%d" % i, [128, 512], BF16) for i in range(3)])
        odt = Ring([B("odt%d" % i, [128, 512], BF16) for i in range(3)])
        odo = Ring([B("odo%d" % i, [128, 4, 128], BF16) for i in range(3)])
        stA = Ring([B("stA%d" % i, [128, 8], F32) for i in range(6)])

        def gelu(src_ps, dst):
            x = uvt.next()
            t1 = uvt.next()
            s.copy("act", x[:], src_ps[:], [src_ps], [x])
            s.tt("dve", t1[:], x[:], x[:], ALU.mult, [x], [t1])
            s.ts("dve", t1[:], t1[:], 0.044715, 1.0, ALU.mult, ALU.add, [t1], [t1])
            s.tt("dve", t1[:], t1[:], x[:], ALU.mult, [t1, x], [t1])
            s.act(t1[:], t1[:], AF.Sigmoid, [t1], [t1], scale=1.5957691216057308)
            s.tt("pool", dst[:], t1[:], x[:], ALU.mult, [t1, x], [dst])

        def genA(tb):
            pu = ps.next()
            pv = ps.next()
            for (p, c0) in ((pu, 0), (pv, 512)):
                for kc in range(16):
                    s.mm(p[:], xT[:, kc, tb * 128:(tb + 1) * 128], w_uv[:, kc, c0:c0 + 512], kc == 0, kc == 15, [xT, w_uv], [p])
            gu = gt.next()
            gv = gt.next()
            gelu(pu, gu)
            gelu(pv, gv)
            yield
            t = stA.next()
            scr = uvt.next()
            s.op("act", lambda e, a=(scr, gv, t): e.activation(out=a[0][:], in_=a[1][:], func=AF.Identity, accum_out=a[2][:, 0:1]), [gv], [scr, t])
            s.op("act", lambda e, a=(scr, gv, t): e.activation(out=a[0][:], in_=a[1][:], func=AF.Square, accum_out=a[2][:, 1:2]), [gv], [scr, t])
            s.ts("dve", t[:, 2:3], t[:, 0:1], 1.0 / 512, None, ALU.mult, None, [t], [t])
            s.ts("dve", t[:, 3:4], t[:, 1:2], 1.0 / 512, None, ALU.mult, None, [t], [t])
            s.stt("dve", t[:, 4:5], t[:, 2:3], -1.0, t[:, 2:3], ALU.mult, ALU.mult, [t], [t])
            s.tt("dve", t[:, 5:6], t[:, 3:4], t[:, 4:5], ALU.add, [t], [t])
            s.act(t[:, 6:7], t[:, 5:6], AF.Sqrt, [t], [t], bias=EPS)
            s.recip(t[:, 7:8], t[:, 6:7], [t], [t])
            yield
            s.stt("dve", gv[:], gv[:], t[:, 2:3], sg_g[:], ALU.subtract, ALU.mult, [gv, t, sg_g], [gv])
            vb = vlb.next()
            s.stt("dve", vb[:], gv[:], t[:, 7:8], sg_b[:], ALU.mult, ALU.add, [gv, t, sg_b], [vb])
            yield
            pg = ps.next()
            for g in range(4):
                s.mm(pg[:, g * 128:(g + 1) * 128], sgwb[:, g, :], vb[:, g * 128:(g + 1) * 128], True, True, [sgwb, vb], [pg])
            od = odt.next()
            for g in range(4):
                s.stt("dve", od[:, g * 128:(g + 1) * 128], pg[:, g * 128:(g + 1) * 128], sgb[:, g:g + 1], gu[:, g * 128:(g + 1) * 128],
                      ALU.add, ALU.mult, [pg, sgb, gu], [od])
            yield
            p = pst.next()
            for j_ in range(4):
                s.tr(p[:, j_, :], od[:, j_ * 128:(j_ + 1) * 128], ident[:], [od, ident], [p])
            oo = odo.next()
            s.copy("act", oo[:], p[:, 0:4, :], [p], [oo])
            s.dma("sp", odT_d[:, :, tb * 128:(tb + 1) * 128].rearrange("c p t -> p c t"), oo[:], reads=[oo], writes=[odT_db[tb]])
        run_pipelined(genA, NB)
        mB = s.mark()
        s.barrier()
        s.off = xT_end = base + 16 * TT * 2
        assert xT_end % 64 == 0

        oT = B("oT", [128, 16, TT], BF16)
        for kc in range(12):
            s.dma("pool", oT[:, kc, :], I["oT"][kc * 128:(kc + 1) * 128, :], writes=[oT])
        s.dma("sp", oT[:, 12:16, :], odT_d[:, :, :].rearrange("c p t -> p c t"), reads=odT_db, writes=[oT])
        wmg = Ring([B("wmg%d" % i, [128, 4, 16, 128], BF16) for i in range(2)])
        wbr = Ring([B("wbr%d" % i, [128, 4, 4, 128], BF16) for i in range(2)])
        gsb = Ring([B("gsb%d" % i, [128, 512], F32) for i in range(3)])
        acc = Ring([B("acc%d" % i, [128, 512], F32) for i in range(2)])
        preo = Ring([B("preo%d" % i, [128, TT], BF16) for i in range(2)])
        wq_ = []
        for dc in range(min(1, 16)):
            wm = wmg.next()
            wb = wbr.next()
            s.dma("pool", wm[:], I["w_mg"][dc, :, :, :, :], writes=[wm])
            s.dma("pool", wb[:], I["w_br"][dc, :, :, :, :], writes=[wb])
            wq_.append((wm, wb))
        for dc in range(16):
            if dc + 1 < 16:
                wm = wmg.next()
                wb = wbr.next()
                s.dma("pool", wm[:], I["w_mg"][dc + 1, :, :, :, :], writes=[wm])
                s.dma("pool", wb[:], I["w_br"][dc + 1, :, :, :, :], writes=[wb])
                wq_.append((wm, wb))
            wm, wb = wq_[dc]
            po = preo.next()
            for (t0, tw) in TTILES:
                a = acc.next()
                for n in range(4):
                    pm = ps.next()
                    py = ps.next()
                    for kc in range(16):
                        s.mm(pm[:, 0:tw], wm[:, n, kc, :], xT[:, kc, t0:t0 + tw], kc == 0, kc == 15, [wm, xT], [pm])
                    for k4 in range(4):
                        s.mm(py[:, 0:tw], wb[:, n, k4, :], oT[:, n * 4 + k4, t0:t0 + tw], k4 == 0, k4 == 3, [wb, oT], [py])
                    gs = gsb.next()
                    s.act(gs[:, 0:tw], pm[:, 0:tw], AF.Sigmoid, [pm], [gs])
                    if n == 0:
                        s.tt("dve", a[:, 0:tw], gs[:, 0:tw], py[:, 0:tw], ALU.mult, [gs, py], [a])
                    else:
                        s.tt("dve", gs[:, 0:tw], gs[:, 0:tw], py[:, 0:tw], ALU.mult, [gs, py], [gs])
                        if n < 3:
                            s.tt("pool", a[:, 0:tw], a[:, 0:tw], gs[:, 0:tw], ALU.add, [a, gs], [a])
                        else:
                            s.tt("pool", po[:, t0:t0 + tw], a[:, 0:tw], gs[:, 0:tw], ALU.add, [a, gs], [po])
            s.dma("sp", preT_d[dc, :, :], po[:], reads=[po], writes=[preT_db[dc]])
        s.release(base)

        ln = LN("ln1_g", "ln1_b")
        wmix = B("wmix", [128, 16, D], BF16)
        for kc in range(16):
            s.dma("pool", wmix[:, kc, :], I["w_mix"][:, kc, :], writes=[wmix])
        prb = Ring([B("prb%d" % i, [128, 16, 128], BF16) for i in range(3)])

        def genC(tb):
            pb = prb.next()
            s.dma("sp", pb[:], preT_d[:, :, tb * 128:(tb + 1) * 128].rearrange("c p t -> p c t"), reads=preT_db, writes=[pb])
            yield from ln.proj_ln_gen(tb, lambda k, pb=pb: (pb[:, k, :], pb), 16, wmix, I["xtok"][tb * 128:(tb + 1) * 128, :], (), x1_d, x1_db, x1T_d, x1T_db)
        run_pipelined(genC, NB)
        s.release(base)

        x1T = B("x1T", [128, 16, TT], BF16)
        s.dma("sp", x1T[:], x1T_d[:, :, :].rearrange("c p t -> p c t"), reads=x1T_db, writes=[x1T])
        wq = B("wq", [128, 16, 512], BF16)
        s.dma("pool", wq[:], I["wq"][:, :, :], writes=[wq])
        qTb = Ring([B("qTb%d" % i, [128, 512], BF16) for i in range(2)])
        pTb = Ring([B("pTb%d" % i, [128, 512], BF16) for i in range(4)])
        rdb = Ring([B("rdb%d" % i, [128, 512], F32) for i in range(2)])
        aTo = Ring([B("aTo%d" % i, [128, TT], BF16) for i in range(2)])
        for h in range(4):
            ao = aTo.next()
            for (t0, tw) in TTILES:
                p = ps.next()
                for kc in range(16):
                    s.mm(p[:, 0:tw], wq[:, kc, h * 128:(h + 1) * 128], x1T[:, kc, t0:t0 + tw], kc == 0, kc == 15, [wq, x1T], [p])
                q = qTb.next()
                s.act(q[:, 0:tw], p[:, 0:tw], AF.Identity, [p], [q], scale=128 ** -0.5)
                pts = []
                for mb in range(2):
                    p2 = ps.next()
                    s.mm(p2[:, 0:tw], kT[:, h, mb * 128:(mb + 1) * 128], q[:, 0:tw], True, True, [kT, q], [p2])
                    pt = pTb.next()
                    s.act(pt[:, 0:tw], p2[:, 0:tw], AF.Exp, [p2], [pt])
                    pts.append(pt)
                po_ = ps.next()
                pd = ps.next()
                for mb in range(2):
                    s.mm(po_[:, 0:tw], vtok[:, mb, h * 128:(h + 1) * 128], pts[mb][:, 0:tw], mb == 0, mb == 1, [vtok, pts[mb]], [po_])
                for mb in range(2):
                    s.mm(pd[:, 0:tw], ones[:], pts[mb][:, 0:tw], mb == 0, mb == 1, [ones, pts[mb]], [pd])
                rd = rdb.next()
                s.recip(rd[:, 0:tw], pd[:, 0:tw], [pd], [rd])
                s.tt("dve", ao[:, t0:t0 + tw], po_[:, 0:tw], rd[:, 0:tw], ALU.mult, [po_, rd], [ao])
            s.dma("sp", aT_d[h, :, :], ao[:], reads=[ao], writes=[aT_db[h]])
        s.release(base)

        ln = LN("ln2_g", "ln2_b")
        wo = B("wo", [128, 4, D], BF16)
        s.dma("pool", wo[:], I["wo"][:, :, :], writes=[wo])
        aTr = Ring([B("aTr%d" % i, [128, 4, 128], BF16) for i in range(3)])

        def genD2(tb):
            ab = aTr.next()
            s.dma("sp", ab[:], aT_d[:, :, tb * 128:(tb + 1) * 128].rearrange("c p t -> p c t"), reads=aT_db, writes=[ab])
            yield from ln.proj_ln_gen(tb, lambda k, ab=ab: (ab[:, k, :], ab), 4, wo, x1_d[tb * 128:(tb + 1) * 128, :], [x1_db[tb]], x2_d, x2_db, x2T_d, x2T_db)
        run_pipelined(genD2, NB)
        s.release(base)

        x2T = B("x2T", [128, 16, TT], BF16)
        s.dma("sp", x2T[:], x2T_d[:, :, :].rearrange("c p t -> p c t"), reads=x2T_db, writes=[x2T])
        cw = B("cw", [128, NFC, 4], F32)
        s.dma("sp", cw[:], I["convw"][:, :, :], writes=[cw])
        wf = Ring([B("wf%d" % i, [128, 2, 16, 128], BF16) for i in range(3)])
        gbuf = Ring([B("gbuf%d" % i, [128, TT + 2], F32) for i in range(2)])
        ubuf = Ring([B("ubuf%d" % i, [128, TT], F32) for i in range(2)])
        cbuf = Ring([B("cbuf%d" % i, [128, TT], F32) for i in range(2)])
        sbuf_ = Ring([B("sbuf%d" % i, [128, TT], F32) for i in range(2)])
        abuf = Ring([B("abuf%d" % i, [128, TT], BF16) for i in range(2)])
        for g_ in gbuf.bufs:
            s.memset("dve", g_[:, 0:2], 0.0, [g_])
        wfq = []
        for fc in range(2):
            w = wf.next()
            s.dma("pool", w[:], I["ffn_in"][fc, :, :, :, :], writes=[w])
            wfq.append(w)
        for fc in range(NFC):
            if fc + 2 < NFC:
                w = wf.next()
                s.dma("pool", w[:], I["ffn_in"][fc + 2, :, :, :, :], writes=[w])
                wfq.append(w)
            w = wfq[fc]
            gb = gbuf.next()
            ub = ubuf.next()
            for (t0, tw) in TTILES:
                pg = ps.next()
                pu = ps.next()
                for kc in range(16):
                    s.mm(pg[:, 0:tw], w[:, 0, kc, :], x2T[:, kc, t0:t0 + tw], kc == 0, kc == 15, [w, x2T], [pg])
                for kc in range(16):
                    s.mm(pu[:, 0:tw], w[:, 1, kc, :], x2T[:, kc, t0:t0 + tw], kc == 0, kc == 15, [w, x2T], [pu])
                if t0 == 0:
                    s.op("act", lambda e, a=(gb, pg, flag): e.activation(out=a[0][:, 2:130], in_=a[1][:, 0:128], func=AF.Copy, scale=a[2][:, 0:1]), [pg, flag], [gb])
                    s.copy("act", gb[:, 130:2 + tw], pg[:, 128:tw], [pg], [gb])
                else:
                    s.copy("act", gb[:, 2 + t0:2 + t0 + tw], pg[:, 0:tw], [pg], [gb])
                s.copy("dve", ub[:, t0:t0 + tw], pu[:, 0:tw], [pu], [ub])
            cb = cbuf.next()
            s.ts("dve", cb[:], gb[:, 2:TT + 2], cw[:, fc, 2:3], cw[:, fc, 3:4], ALU.mult, ALU.add, [gb, cw], [cb])
            s.stt("dve", cb[:], gb[:, 1:TT + 1], cw[:, fc, 1:2], cb[:], ALU.mult, ALU.add, [gb, cw, cb], [cb])
            s.stt("dve", cb[:], gb[:, 0:TT], cw[:, fc, 0:1], cb[:], ALU.mult, ALU.add, [gb, cw, cb], [cb])
            sb = sbuf_.next()
            s.act(sb[:], cb[:], AF.Silu, [cb], [sb])
            ab = abuf.next()
            s.tt("pool", ab[:], sb[:], ub[:], ALU.mult, [sb, ub], [ab])
            s.dma("sp", act_d[fc, :, :], ab[:], reads=[ab], writes=[act_db[fc]])
        s.release(base)

        wfo = Ring([B("wfo%d" % i, [128, NFC, 512], BF16) for i in range(2)])
        acb = Ring([B("acb%d" % i, [128, NFC, 128], BF16) for i in range(4)])
        fob = Ring([B("fob%d" % i, [128, 512], F32) for i in range(3)])
        for dt_ in range(4):
            w = wfo.next()
            for f0 in range(0, NFC, 11):
                s.dma("pool", w[:, f0:f0 + 11, :], I["ffn_out"][:, f0:f0 + 11, dt_ * 512:(dt_ + 1) * 512], writes=[w])
            for tb in range(NB):
                step = dt_ * NB + tb
                if step == 0:
                    abq = []
                    for st_ in range(3):
                        ab = acb.next()
                        tb_ = st_ % NB
                        s.dma("sp", ab[:], act_d[:, :, tb_ * 128:(tb_ + 1) * 128].rearrange("c p t -> p c t"), reads=act_db, writes=[ab])
                        abq.append(ab)
                if step + 3 < 4 * NB:
                    ab = acb.next()
                    tb_ = (step + 3) % NB
                    s.dma("sp", ab[:], act_d[:, :, tb_ * 128:(tb_ + 1) * 128].rearrange("c p t -> p c t"), reads=act_db, writes=[ab])
                    abq.append(ab)
                ab = abq[step]
                p = ps.next()
                for fc in range(NFC):
                    s.mm(p[:], ab[:, fc, :], w[:, fc, :], fc == 0, fc == NFC - 1, [ab, w], [p])
                fo = fob.next()
                if tb % 2 == 0:
                    s.copy("act", fo[:], p[:], [p], [fo])
                else:
                    s.copy("dve", fo[:], p[:], [p], [fo])
                s.dma("sp", ffo_d[tb * 128:(tb + 1) * 128, dt_ * 512:(dt_ + 1) * 512], fo[:], reads=[fo], writes=[ffo_db[tb * 4 + dt_]])
        s.release(base)

        ln = LN("ln3_g", "ln3_b")
        def genF2(i):
            tb = i + 1
            z = ln.zt.next()
            xres = ln.xr.next()
            ff = ln.xr.next()
            s.dma("sp", xres[:], x2_d[tb * 128:(tb + 1) * 128, :], reads=[x2_db[tb]], writes=[xres])
            s.dma("sp", ff[:], ffo_d[tb * 128:(tb + 1) * 128, :], reads=ffo_db[tb * 4:tb * 4 + 4], writes=[ff])
            s.stt("dve", z[:], xres[:], ALPHA, ff[:], ALU.mult, ALU.add, [xres, ff], [z])
            yield
            for _ in ln.norm_gen(z, z):
                yield
            yield
            s.dma("sp", xout[(tb - 1) * 128:tb * 128, :], z[:], reads=[z])
        run_pipelined(genF2, NB - 1)
        s.emit()
        print("R program: ops", s.n, "sbuf peak", s.peak, flush=True)
    return nc


OFF_Q, OFF_KV, OFF_G, OFF_H, OFF_P, OFF_SG, OFF_MG = 0, 512, 1280, 1304, 3352, 3864, 4888


def _bc(v, n=128):
    return np.ascontiguousarray(np.broadcast_to(np.asarray(v, np.float32)[None, :], (n, v.shape[0])))


def _c(a):
    return np.ascontiguousarray(a, dtype=np.float32)


def prep_R_shared(inp, l):
    w_in = inp["w_in"][l]
    sh = {}
    sh["w_uv"] = _c(w_in[:, OFF_SG:OFF_SG + 1024].reshape(16, 128, 1024).transpose(1, 0, 2))
    mg = w_in[:, OFF_MG:OFF_MG + 8192].reshape(16, 128, 4, 16, 128)
    sh["w_mg"] = _c(mg.transpose(3, 1, 2, 0, 4))
    br = inp["w_branch"][l].reshape(4, 4, 128, 16, 128)
    sh["w_br"] = _c(br.transpose(3, 2, 0, 1, 4))
    sh["w_mix"] = _c(inp["w_mix_out"][l].reshape(16, 128, D).transpose(1, 0, 2))
    sh["sgw"] = _c(inp["sg_w"][l].transpose(2, 0, 1))
    sh["sgb"] = _c(inp["sg_b"][l].T)
    sh["sg_g"] = _bc(inp["sg_ln_g"][l])
    sh["sg_b"] = _bc(inp["sg_ln_b"][l])
    sh["trimask"] = _c(np.triu(np.ones((128, 128), np.float32)))
    for i, nm in ((1, "ln_mix"), (2, "ln_x"), (3, "ln_ffn")):
        sh["ln%d_g" % i] = _bc(inp[nm + "_g"][l])
        sh["ln%d_b" % i] = _bc(inp[nm + "_b"][l])
    sh["memln_g"] = _bc(inp["mem_ln_g"])
    sh["memln_b"] = _bc(inp["mem_ln_b"])
    for nm, k in (("wq", "xattn_q"), ("wk", "xattn_k"), ("wv", "xattn_v")):
        sh[nm] = _c(inp[k][l].reshape(16, 128, 512).transpose(1, 0, 2))
    sh["wo"] = _c(inp["xattn_o"][l].reshape(4, 128, D).transpose(1, 0, 2))
    fi = inp["ffn_in"][l].reshape(16, 128, 2, NFC, 128)
    sh["ffn_in"] = _c(fi.transpose(3, 1, 2, 0, 4))
    sh["ffn_out"] = _c(inp["ffn_out"][l].reshape(NFC, 128, D).transpose(1, 0, 2))
    cw = np.concatenate([inp["ffn_conv_w"][l], inp["ffn_conv_b"][l][None, :]], axis=0)
    sh["convw"] = _c(cw.reshape(4, NFC, 128).transpose(2, 1, 0))
    sh["ident"] = np.eye(128, dtype=np.float32)
    sh["ones"] = np.ones((128, 128), np.float32)
    return sh


def prep_R(inp, l, x, oabc):
    sh = prep_R_shared(inp, l)
    maps = []
    for c in range(NCORE):
        b, u = divmod(c, 4)
        t0 = 2048 * u
        m = dict(sh)
        xs = np.zeros((TT, D), np.float32)
        os_ = np.zeros((TT, 1536), np.float32)
        lo = t0 - 128
        if u == 0:
            xs[128:] = x[b, 0:2048]
            os_[128:] = oabc[b, 0:2048]
        else:
            xs[:] = x[b, lo:lo + TT]
            os_[:] = oabc[b, lo:lo + TT]
        m["xtok"] = xs
        m["xT"] = _c(xs.T)
        m["oT"] = _c(os_.T)
        m["mem"] = _c(inp["mem"][b])
        m["flag"] = np.full((128, 1), 0.0 if u == 0 else 1.0, np.float32)
        maps.append(m)
    return maps


_NC_CACHE = {}


def run_R(inp, l, x, oabc):
    if "R" not in _NC_CACHE:
        _NC_CACHE["R"] = build_R()
    maps = prep_R(inp, l, x, oabc)
    res = run_bass_kernel_spmd(_NC_CACHE["R"], maps, core_ids=list(range(NCORE)))
    out = np.zeros((2, S, D), np.float32)
    for c in range(NCORE):
        b, u = divmod(c, 4)
        out[b, 2048 * u:2048 * (u + 1)] = res.results[c]["xout"]
    return out


NQT = S // 512
M_INPUTS = {
    "xT": (D, S), "wfm": (128, 16, 768), "wtm": (128, 16, 520),
    "qpos": (4, 4, S), "kpos": (4, S), "cpos": (4, 512),
    "cmask": (5, 128, 512), "dmask": (8, 128, 512), "E_all": (128, 32, 128), "selmap": (128, 4, 129),
    "keep": (128, 256), "addm": (128, 256), "ident": (128, 128), "tri2": (128, 64),
    "cw_k": (64, 32, 64), "cw_v": (64, 32, 64), "cp_k": (64, 32), "cp_v": (64, 32),
    "lblog": (128, 2), "lbsel": (128, 1), "normg": (128, 128),
    "poolP": (128, 3, 128), "poolw": (128, 128), "poolscale": (128, 128),
}


def build_M(nqt=NQT, ST=(1, 2, 3, 4, 5, 6, 7, 8, 9, 10, 11)):
    nc = bass.Bass("TRN2", target_bir_lowering=False)
    I = {k: _dram(nc, k, v) for k, v in M_INPUTS.items()}
    om = _dram(nc, "om", (S, 384), kind="ExternalOutput")
    with contextlib.ExitStack() as es:
        s = Sched(nc, es)
        B = s.buf

        def PB(name, shape, dt):
            return B(name, shape, dt, psum=True)

        psr = Ring([PB("psr%d" % i, [128, 512], F32) for i in range(2)])
        pst = PB("pst", [128, 8, 128], BF16)
        oacc_ps = Ring([PB("oacc%d" % i, [128, 512], F32) for i in range(2)])
        impb = PB("impb", [128, 4, 128], F32)
        misc = PB("misc", [128, 512], F32)
        denb = misc
        hg = PB("hg", [128, 4, 128], F32)

        wfm = B("wfm", [128, 16, 768], BF16)
        wtm = B("wtm", [128, 16, 520], BF16)
        for kc in range(16):
            s.dma("pool", wfm[:, kc, :], I["wfm"][:, kc, :], writes=[wfm])
            s.dma("pool", wtm[:, kc, :], I["wtm"][:, kc, :], writes=[wtm])
        identb = B("identb", [128, 128], BF16)
        identf = B("identf", [128, 128], F32)
        s.dma("pool", identb[:], I["ident"][:, :], writes=[identb])
        s.dma("sp", identf[:], I["ident"][:, :], writes=[identf])
        cmask = B("cmask", [128, 5, 512], BF16)
        dmask = B("dmask", [128, 8, 512], BF16)
        for i in range(5):
            s.dma("pool", cmask[:, i, :], I["cmask"][i, :, :], writes=[cmask])
        for i in range(8):
            s.dma("pool", dmask[:, i, :], I["dmask"][i, :, :], writes=[dmask])
        E_all = B("E_all", [128, 32, 128], BF16)
        s.dma("pool", E_all[:], I["E_all"][:, :, :], writes=[E_all])
        selmap = B("selmap", [128, 4, 144], BF16)
        s.dma("pool", selmap[:, :, 0:129], I["selmap"][:, :, :], writes=[selmap])
        keep = B("keep", [128, 256], F32)
        addm = B("addm", [128, 256], F32)
        tri2 = B("tri2", [128, 64], F32)
        s.dma("sp", keep[:], I["keep"][:, :], writes=[keep])
        s.dma("sp", addm[:], I["addm"][:, :], writes=[addm])
        s.dma("sp", tri2[:], I["tri2"][:, :], writes=[tri2])
        cw_k = B("cw_k", [64, 32, 64], BF16)
        cw_v = B("cw_v", [64, 32, 64], BF16)
        cp_k = B("cp_k", [64, 32], BF16)
        cp_v = B("cp_v", [64, 32], BF16)
        s.dma("pool", cw_k[:], I["cw_k"][:, :, :], writes=[cw_k])
        s.dma("pool", cw_v[:], I["cw_v"][:, :, :], writes=[cw_v])
        s.dma("pool", cp_k[:], I["cp_k"][:, :], writes=[cp_k])
        s.dma("pool", cp_v[:], I["cp_v"][:, :], writes=[cp_v])
        normg = B("normg", [128, 128], F32)
        poolP = B("poolP", [128, 3, 128], F32)
        poolw = B("poolw", [128, 128], BF16)
        poolscale = B("poolscale", [128, 128], F32)
        lbl = B("lbl", [128, 8], F32)
        s.dma("sp", normg[:], I["normg"][:, :], writes=[normg])
        s.dma("sp", poolP[:], I["poolP"][:, :, :], writes=[poolP])
        s.dma("pool", poolw[:], I["poolw"][:, :], writes=[poolw])
        s.dma("sp", poolscale[:], I["poolscale"][:, :], writes=[poolscale])
        s.dma("sp", lbl[:, 0:2], I["lblog"][:, :], writes=[lbl])
        s.dma("sp", lbl[:, 2:3], I["lbsel"][:, :], writes=[lbl])
        s.tt("dve", lbl[:, 3:4], lbl[:, 1:2], lbl[:, 0:1], ALU.subtract, [lbl], [lbl])
        s.act(lbl[:, 4:5], lbl[:, 3:4], AF.Sigmoid, [lbl], [lbl])
        s.tt("dve", lbl[:, 5:6], lbl[:, 4:5], lbl[:, 2:3], ALU.mult, [lbl], [lbl])
        s.ts("dve", lbl[:, 6:7], lbl[:, 5:6], -1.0, 1.0, ALU.mult, ALU.add, [lbl], [lbl])

        kslc = B("kslc", [128, S], BF16)
        s.memset("dve", kslc[64:128, :], 0.0, [kslc])
        s.dma("pool", kslc[64:68, :], I["kpos"][:, :], writes=[kslc])
        kwin = B("kwin", [128, 8, 128], BF16)
        s.memset("dve", kwin[64:128, :, :], 0.0, [kwin])
        vslc = B("vslc", [128, 65, 72], BF16)
        vwin = B("vwin", [128, 10, 72], BF16)
        s.memset("dve", vslc[:], 0.0, [vslc])
        s.memset("dve", vwin[:], 0.0, [vwin])
        s.memset("dve", vslc[:, 0:64, 64:65], 1.0, [vslc])
        s.memset("dve", vwin[:, 0:8, 64:65], 1.0, [vwin])
        kcaug = B("kcaug", [128, 512], BF16)
        s.memset("dve", kcaug[:], 0.0, [kcaug])
        s.dma("pool", kcaug[64:68, :], I["cpos"][:, :], writes=[kcaug])
        vcaug = B("vcaug", [128, 6, 72], BF16)
        s.memset("dve", vcaug[:], 0.0, [vcaug])
        s.memset("dve", vcaug[:, 0:4, 64:65], 1.0, [vcaug])
        kcraw = B("kcraw", [64, 528], BF16)
        vcraw = B("vcraw", [64, 528], BF16)
        s.memset("dve", kcraw[:], 0.0, [kcraw])
        s.memset("dve", vcraw[:], 0.0, [vcraw])
        cbias = B("cbias", [64, 2], F32)
        for (cw_, cp_, col) in ((cw_k, cp_k, 0), (cw_v, cp_v, 1)):
            p = psr.next()
            for l_ in range(32):
                s.mm(p[0:64, 0:1], cw_[:, l_, :], cp_[:, l_:l_ + 1], l_ == 0, l_ == 31, [cw_, cp_], [p])
            s.copy("dve", cbias[:, col:col + 1], p[0:64, 0:1], [p], [cbias])

        state = B("state", [128, 128], F32)
        stbf = Ring([B("stbf%d" % i, [128, 128], BF16) for i in range(2)])
        s.memset("dve", state[:], 0.0, [state])
        st_cur = stbf.next()
        s.memset("dve", st_cur[:], 0.0, [st_cur])
        qbpad = B("qbpad", [128, 4, 2, 128], BF16)
        s.memset("pool", qbpad[:], 0.0, [qbpad])
        atpad = Ring([B("atpad%d" % i, [128, 128], BF16) for i in range(2)])
        for a_ in atpad.bufs:
            s.memset("pool", a_[:], 0.0, [a_])
        pczero = B("pczero", [128, 128], F32)
        s.memset("pool", pczero[:], 0.0, [pczero])

        xtile = Ring([B("xtile%d" % i, [128, 16, 512], BF16) for i in range(1)])
        qaug = [Ring([B("qaug%d_%d" % (j, i), [128, 512], BF16) for i in range(2)]) for j in range(4)]
        for r_ in qaug:
            for b_ in r_.bufs:
                s.memset("dve", b_[64:128, :], 0.0, [b_])
        hqT = B("hqT", [128, 512], F32)
        hzT = B("hzT", [128, 512], F32)
        gate = Ring([B("gate%d" % i, [128, 8], F32) for i in range(8)])
        pcb = Ring([B("pcb%d" % i, [128, 128], F32) for i in range(6)])
        vvb = Ring([B("vvb%d" % i, [128, 128], BF16) for i in range(4)])
        sgb = Ring([B("sgb%d" % i, [128, 128], F32) for i in range(4)])
        outb = Ring([B("outb%d" % i, [128, 384], F32) for i in range(4)])
        ET = [[B("ET%d_%d" % (j, c), [128, 512], BF16) for c in range(4)] for j in range(2)]
        PT = Ring([B("PT%d" % i, [128, 512], BF16) for i in range(3)])
        impacc = B("impacc", [128, 4, 128], F32)
        rden = Ring([B("rden%d" % i, [128, 4], F32) for i in range(4)])
        selw = Ring([B("selw%d" % i, [128, 128], F32) for i in range(3)])
        selb = Ring([B("selb%d" % i, [128, 128], BF16) for i in range(2)])
        m8 = Ring([B("m8_%d" % i, [128, 16], F32) for i in range(2)])
        mbT = Ring([B("mbT%d" % i, [128, 512], BF16) for i in range(2)])
        oTs = Ring([B("oTs%d" % i, [65, 512], F32) for i in range(2)])
        coef = Ring([B("coef%d" % i, [128, 4], F32) for i in range(4)])
        vcT = Ring([B("vcT%d" % i, [64, 32], BF16) for i in range(2)])
        vct = Ring([B("vct%d" % i, [32, 64], BF16) for i in range(2)])
        hw_ = [B("hw%d" % i, [128, 512], F32) for i in range(7)]
        hb_ = [B("hb%d" % i, [128, 512], BF16) for i in range(4)]
        hs_ = Ring([B("hs%d" % i, [128, 16], F32) for i in range(2)])
        kdb = Ring([B("kdb%d" % i, [128, 128], BF16) for i in range(2)])
        hst = Ring([B("hst%d" % i, [128, 4], F32) for i in range(4)])
        hsq = B("hsq", [128, 128], F32)
        hsq2 = B("hsq2", [128, 128], F32)
        ptb = Ring([B("ptb%d" % i, [128, 128], BF16) for i in range(2)])
        pc_prev = pczero
        trs = Ring([B("trs%d" % i, [128, 4, 65], F32) for i in range(1)])
        mbh = [Ring([B("mbh%d_%d" % (h, i), [128, 512], BF16) for i in range(2)]) for h in range(2)]
        for h_ in range(2):
            for b_ in mbh[h_].bufs:
                s.memset("dve", b_[:], 0.0, [b_])
        imps = Ring([B("imps%d" % i, [128, 4, 128], F32) for i in range(1)])
        asb = Ring([B("asb%d" % i, [128, 128], F32) for i in range(2)])
        stgA = Ring([B("stgA%d" % i, [128, 136], F32) for i in range(1)])
        stgB = Ring([B("stgB%d" % i, [128, 384], F32) for i in range(1)])

        vslc_f = vslc[:].rearrange("p a b -> p (a b)")
        vwin_f = vwin[:].rearrange("p a b -> p (a b)")
        vcaug_f = vcaug[:].rearrange("p a b -> p (a b)")
        JA = 0
        trf = Buf(misc.t)
        trv = misc.t[:, 128:388].rearrange("p (a b) -> p a b", a=4)
        hstate = {"cur": st_cur}

        def epilogue(oa, hl, br, gts, obs):
            o_sb = oTs.next()
            s.copy("act", o_sb[:], oa[0:65, :], [oa], [o_sb])
            for tb in range(4):
                s.tr(trv[:, tb, :], o_sb[:, tb * 128:(tb + 1) * 128], identf[0:65, 0:65], [o_sb, identf], [trf])
            tv = trs.next()
            s.copy("dve", tv[:], trv[:, :, :], [trf], [tv])
            cf = coef.next()
            s.ts("dve", cf[:], tv[:, :, 64], 1e-30, None, ALU.max, None, [tv], [cf])
            s.recip(cf[:], cf[:], [cf], [cf])
            for tb in range(4):
                s.tt("dve", cf[:, tb:tb + 1], cf[:, tb:tb + 1], gts[tb][:, hl * 3 + br:hl * 3 + br + 1], ALU.mult, [cf, gts[tb]], [cf])
            for tb in range(4):
                dst = obs[tb][:, hl * 64:(hl + 1) * 64]
                if br == 0:
                    s.ts("dve", dst, tv[:, tb, 0:64], cf[:, tb:tb + 1], None, ALU.mult, None, [tv, cf], [obs[tb]])
                else:
                    s.stt("dve", dst, tv[:, tb, 0:64], cf[:, tb:tb + 1], dst, ALU.mult, ALU.add, [tv, cf, obs[tb]], [obs[tb]])

        def hgrn_tile(qt, vvs, sgs, obs):
            W = hw_
            s.act(W[0][:], hzT[:], AF.Sigmoid, [hzT], [W[0]])
            s.ts("dve", W[0][:], W[0][:], lbl[:, 6:7], lbl[:, 5:6], ALU.mult, ALU.add, [W[0], lbl], [W[0]])
            s.ts("dve", W[0][:], W[0][:], 1e-6, None, ALU.max, None, [W[0]], [W[0]])
            s.ts("dve", W[1][:], W[0][:], -1.0, 1.0, ALU.mult, ALU.add, [W[0]], [W[1]])
            s.act(W[2][:], W[0][:], AF.Ln, [W[0]], [W[2]])
            src, dst = W[2], W[3]
            for sh in (1, 2, 4, 8, 16, 32):
                sv = src[:].rearrange("p (c t) -> p c t", t=64)
                dv = dst[:].rearrange("p (c t) -> p c t", t=64)
                s.copy("pool", dv[:, :, 0:sh], sv[:, :, 0:sh], [src], [dst])
                s.tt("dve", dv[:, :, sh:64], sv[:, :, sh:64], sv[:, :, 0:64 - sh], ALU.add, [src], [dst])
                src, dst = dst, src
            b = src
            bv = b[:].rearrange("p (c t) -> p c t", t=64)
            hs = hs_.next()
            nh = hs_.next()
            s.copy("dve", hs[:, 0:8], bv[:, :, 31], [b], [hs])
            s.copy("dve", hs[:, 8:16], bv[:, :, 63], [b], [hs])
            s.ts("dve", nh[:, 0:8], hs[:, 0:8], -1.0, None, ALU.mult, None, [hs], [nh])
            s.act(nh[:, 8:16], hs[:, 8:16], AF.Exp, [hs], [nh])
            for c in range(8):
                cs = slice(c * 64, (c + 1) * 64)
                s.act(W[3][:, cs], b[:, cs], AF.Exp, [b, nh], [W[3]], bias=nh[:, c:c + 1])
                s.act(W[4][:, cs], b[:, cs], AF.Exp, [b, hs], [W[4]], bias=hs[:, c:c + 1], scale=-1.0)
                s.act(W[5][:, cs], b[:, cs], AF.Exp, [b, hs], [W[5]], bias=hs[:, 8 + c:9 + c], scale=-1.0)
            s.act(W[6][:], b[:], AF.Exp, [b], [W[6]])
            s.tt("pool", hb_[0][:], hqT[:], W[3][:], ALU.mult, [hqT, W[3]], [hb_[0]])
            s.tt("dve", hb_[1][:], W[1][:], W[4][:], ALU.mult, [W[1], W[4]], [hb_[1]])
            s.tt("pool", hb_[2][:], W[1][:], W[5][:], ALU.mult, [W[1], W[5]], [hb_[2]])
            hq4 = hqT[:].rearrange("p (a c t) -> p a c t", a=4, c=2)
            eb4 = W[6][:].rearrange("p (a c t) -> p a c t", a=4, c=2)
            s.tt("dve", qbpad[:, :, 0, 0:64], hq4[:, :, 0, :], eb4[:, :, 0, :], ALU.mult, [hqT, W[6]], [qbpad])
            s.tt("pool", qbpad[:, :, 1, 64:128], hq4[:, :, 1, :], eb4[:, :, 1, :], ALU.mult, [hqT, W[6]], [qbpad])
            for tb in range(4):
                blk = slice(tb * 128, (tb + 1) * 128)
                s.mm(hg[:, 0, :], hb_[1][:, blk], hb_[0][:, blk], True, True, [hb_[1], hb_[0]], [hg])
                at = atpad.next()
                a_sb = asb.next()
                s.copy("dve", a_sb[:], hg[:, 0, :], [hg], [a_sb])
                s.tt("pool", at[0:64, 0:64], a_sb[0:64, 0:64], tri2[0:64, :], ALU.mult, [a_sb, tri2], [at])
                s.tt("pool", at[64:128, 64:128], a_sb[64:128, 64:128], tri2[64:128, :], ALU.mult, [a_sb, tri2], [at])
                s.tr(pst[:, 5, :], hb_[2][:, blk], identb[:], [hb_[2], identb], [pst])
                kb = kdb.next()
                s.copy("act", kb[:], pst[:, 5, :], [pst], [kb])
                st0 = hstate["cur"]
                s.mm(hg[:, 1, :], qbpad[:, tb, 0, :], st0[:], True, False, [qbpad, st0], [hg])
                s.mm(hg[:, 1, :], at[:], vvs[tb][:], False, True, [at, vvs[tb]], [hg])
                s.mm(hg[:, 2, :], kb[0:64, :], vvs[tb][0:64, :], True, True, [kb, vvs[tb]], [hg])
                s.stt("dve", state[:], state[:], nh[:, 8 + 2 * tb:9 + 2 * tb], hg[:, 2, :], ALU.mult, ALU.add, [state, nh, hg], [state])
                st1 = stbf.next()
                s.copy("act", st1[:], state[:], [state], [st1])
                oa_sb = hsq
                s.copy("dve", oa_sb[:], hg[:, 1, :], [hg], [oa_sb])
                s.mm(hg[:, 3, :], qbpad[:, tb, 1, :], st1[:], True, True, [qbpad, st1], [hg])
                s.tt("dve", oa_sb[:], oa_sb[:], hg[:, 3, :], ALU.add, [oa_sb, hg], [oa_sb])
                s.mm(hg[:, 2, :], kb[64:128, :], vvs[tb][64:128, :], True, True, [kb, vvs[tb]], [hg])
                s.stt("dve", state[:], state[:], nh[:, 9 + 2 * tb:10 + 2 * tb], hg[:, 2, :], ALU.mult, ALU.add, [state, nh, hg], [state])
                st2 = stbf.next()
                s.copy("act", st2[:], state[:], [state], [st2])
                hstate["cur"] = st2
                t_ = hst.next()
                s.op("act", lambda e, a=(hsq2, oa_sb, t_): e.activation(out=a[0][:], in_=a[1][:], func=AF.Square, accum_out=a[2][:, 0:1]), [oa_sb], [hsq2, t_])
                s.ts("dve", t_[:, 1:2], t_[:, 0:1], 1.0 / 128, EPS, ALU.mult, ALU.add, [t_], [t_])
                s.act(t_[:, 2:3], t_[:, 1:2], AF.Sqrt, [t_], [t_])
                s.recip(t_[:, 3:4], t_[:, 2:3], [t_], [t_])
                s.stt("dve", obs[tb][:, 128:256], oa_sb[:], t_[:, 3:4], normg[:], ALU.mult, ALU.mult, [oa_sb, t_, normg], [obs[tb]])
                s.tt("pool", obs[tb][:, 128:256], obs[tb][:, 128:256], sgs[tb][:], ALU.mult, [obs[tb], sgs[tb]], [obs[tb]])

        g1w, g2w = 136, 384

        xtk = [Buf(xtile.bufs[0].t) for _ in range(16)]

        def load_x(qt_):
            for kc in range(16):
                s.dma("pool", xtile.bufs[0][:, kc, :], I["xT"][kc * 128:(kc + 1) * 128, qt_ * 512:(qt_ + 1) * 512], writes=[xtk[kc]])

        for qt in range(nqt):
            q0 = qt * 512
            s.mute = 1 not in ST
            xt = xtile.next()
            if qt == 0:
                load_x(0)
            qa = [qaug[j].next() for j in range(4)]
            for j in range(4):
                s.dma("pool", qa[j][64:68, :], I["qpos"][j, :, q0:q0 + 512], writes=[qa[j]])
            slot0 = (4 * qt) % 8
            s.dma("pool", kwin[64:68, slot0:slot0 + 4, :], I["kpos"][:, q0:q0 + 512].rearrange("r (a b) -> r a b", a=4), writes=[kwin])
            if qt > 0:
                s.copy("dve", kcraw[:, 0:16], kcraw[:, 512:528], [kcraw], [kcraw])
                s.copy("dve", vcraw[:, 0:16], vcraw[:, 512:528], [vcraw], [vcraw])

            s.mute = 2 not in ST
            def fm(col0, M):
                p = psr.next()
                for kc in range(16):
                    s.mm(p[0:M, :], wfm[:, kc, col0:col0 + M], xt[:, kc, :], kc == 0, kc == 15, [wfm, xtk[kc]], [p])
                return p
            for j in range(4):
                p = fm(j * 64, 64)
                s.act(qa[j][0:64, :], p[0:64, :], AF.Identity, [p], [qa[j]], scale=0.125)
            p = fm(256, 64)
            s.copy("act", kcraw[:, 16:528], p[0:64, :], [p], [kcraw])
            p = fm(320, 64)
            s.copy("dve", kslc[0:64, q0:q0 + 512], p[0:64, :], [p], [kslc])
            p = fm(384, 64)
            s.copy("act", kwin[0:64, slot0:slot0 + 4, :], p[0:64, :].rearrange("p (a b) -> p a b", a=4), [p], [kwin])
            p = fm(448, 64)
            s.copy("dve", vcraw[:, 16:528], p[0:64, :], [p], [vcraw])
            p = fm(512, 128)
            s.copy("act", hqT[:], p[:], [p], [hqT])
            p = fm(640, 128)
            s.copy("dve", hzT[:], p[:], [p], [hzT])

            s.mute = 3 not in ST
            gts, pcs, vvs, sgs, obs = [], [], [], [], []
            for tb in range(4):
                kt = 4 * qt + tb
                p = psr.next()
                for kc in range(16):
                    s.mm(p[:, 0:g1w], xt[:, kc, tb * 128:(tb + 1) * 128], wtm[:, kc, 0:g1w], kc == 0, kc == 15, [wtm, xtk[kc]], [p])
                sA = stgA.next()
                s.copy("dve", sA[:, 0:g1w], p[:, 0:g1w], [p], [sA])
                s.copy("pool", vslc[:, kt, 0:64], sA[:, 0:64], [sA], [vslc])
                s.copy("pool", vwin[:, kt % 8, 0:64], sA[:, 64:128], [sA], [vwin])
                g_ = gate.next()
                s.act(g_[:], sA[:, 128:136], AF.Sigmoid, [sA], [g_])
                gts.append(g_)
                p = psr.next()
                for kc in range(16):
                    s.mm(p[:, 0:g2w], xt[:, kc, tb * 128:(tb + 1) * 128], wtm[:, kc, g1w:g1w + g2w], kc == 0, kc == 15, [wtm, xtk[kc]], [p])
                pc = pcb.next()
                vv = vvb.next()
                sg = sgb.next()
                sB = stgB.next()
                s.copy("act", sB[:, 0:g2w], p[:, 0:g2w], [p], [sB])
                s.copy("pool", pc[:], sB[:, 0:128], [sB], [pc])
                s.copy("dve", vv[:], sB[:, 128:256], [sB], [vv])
                s.act(sg[:], sB[:, 256:384], AF.Silu, [sB], [sg])
                pcs.append(pc)
                vvs.append(vv)
                sgs.append(sg)
                obs.append(outb.next())

            if qt + 1 < nqt and 1 in ST:
                s.mute = False
                load_x(qt + 1)
            s.mute = 4 not in ST
            c_lo = 32 * qt - 1 if qt > 0 else 0
            c_hi = 32 * qt + 30
            ncb = c_hi - c_lo + 1
            i0 = 0 if qt > 0 else 1
            kview = kcraw[:].rearrange("p (c s) -> p c s", s=16)
            vview = vcraw[:].rearrange("p (c s) -> p c s", s=16)
            p = psr.next()
            for l_ in range(32):
                s.mm(p[0:64, 0:ncb], cw_k[:, l_, :], kview[:, i0 + l_ // 16:i0 + l_ // 16 + ncb, l_ % 16], l_ == 0, l_ == 31, [cw_k, kcraw], [p])
            s.act(kcaug[0:64, c_lo:c_hi + 1], p[0:64, 0:ncb], AF.Identity, [p, cbias], [kcaug], bias=cbias[:, 0:1])
            p = psr.next()
            for l_ in range(32):
                s.mm(p[0:64, 0:ncb], cw_v[:, l_, :], vview[:, i0 + l_ // 16:i0 + l_ // 16 + ncb, l_ % 16], l_ == 0, l_ == 31, [cw_v, vcraw], [p])
            vT_ = vcT.next()
            s.act(vT_[:, 0:ncb], p[0:64, 0:ncb], AF.Identity, [p, cbias], [vT_], bias=cbias[:, 1:2])
            s.tr(pst[0:ncb, 0, 0:64], vT_[:, 0:ncb], identb[0:64, 0:64], [vT_, identb], [pst])
            vt_ = vct.next()
            s.copy("dve", vt_[0:ncb, :], pst[0:ncb, 0, 0:64], [pst], [vt_])
            c = c_lo
            while c <= c_hi:
                ct_ = c // 128
                n_ = min(c_hi + 1, (ct_ + 1) * 128) - c
                s.dma("sp", vcaug[c % 128:c % 128 + n_, ct_, 0:64], vt_[c - c_lo:c - c_lo + n_, :], reads=[vt_], writes=[vcaug])
                c += n_

            s.mute = 5 not in ST
            nct = (32 * qt + 30) // 128 + 1
            oc_ps = {}
            for j in range(4):
                mine = j in (JA, JA + 1)
                ets = []
                for ct_ in range(nct):
                    dl = qt - 4 * ct_
                    p = psr.next()
                    last = dl > 4
                    s.mm(p[:], kcaug[:, ct_ * 128:(ct_ + 1) * 128], qa[j][:, :], True, last, [kcaug, qa[j]], [p])
                    if not last:
                        s.mm(p[:], identb[:], cmask[:, dl, :], False, True, [identb, cmask], [p])
                    e_ = ET[j % 2][ct_]
                    s.act(e_[:], p[:], AF.Exp, [p], [e_])
                    ets.append(e_)
                if mine:
                    oa = oacc_ps.next()
                    for ct_ in range(nct):
                        s.mm(oa[:, :], vcaug_f[:, ct_ * 72:ct_ * 72 + 128], ets[ct_][:], ct_ == 0, ct_ == nct - 1, [vcaug, ets[ct_]], [oa])
                    oc_ps[j] = oa
                for tb in range(4):
                    for ct_ in range(nct):
                        s.mm(impb[:, tb, :], ets[ct_][:, tb * 128:(tb + 1) * 128], selmap[:, ct_, 0:128], ct_ == 0, ct_ == nct - 1, [ets[ct_], selmap], [impb])
                for tb in range(4):
                    for ct_ in range(nct):
                        s.mm(denb[:, tb:tb + 1], ets[ct_][:, tb * 128:(tb + 1) * 128], selmap[:, ct_, 128:129], ct_ == 0, ct_ == nct - 1, [ets[ct_], selmap], [denb])
                rd = rden.next()
                s.ts("dve", rd[:], denb[:, 0:4], 1e-30, None, ALU.max, None, [denb], [rd])
                s.recip(rd[:], rd[:], [rd], [rd])
                im = imps.next()
                s.copy("act", im[:], impb[:], [impb], [im])
                for tb in range(4):
                    if j == 0:
                        s.ts("dve", impacc[:, tb, :], im[:, tb, :], rd[:, tb:tb + 1], None, ALU.mult, None, [im, rd], [impacc])
                    else:
                        s.stt("dve", impacc[:, tb, :], im[:, tb, :], rd[:, tb:tb + 1], impacc[:, tb, :], ALU.mult, ALU.add, [im, rd, impacc], [impacc])
                if mine:
                    epilogue(oc_ps[j], j - JA, 0, gts, obs)

            s.mute = 6 not in ST
            mb = mbT.next()
            for tb in range(4):
                tbg = 4 * qt + tb
                c0 = 126 - 2 * tbg
                w = selw.next()
                s.tt("dve", w[:], impacc[:, tb, :], keep[:, c0:c0 + 128], ALU.mult, [impacc, keep], [w])
                s.tt("dve", w[:], w[:], addm[:, c0:c0 + 128], ALU.add, [w, addm], [w])
                s.memset("dve", w[:, 0:1], 1e6, [w])
                m = m8.next()
                w2 = selw.next()
                s.op("dve", lambda e, a=(m, w): e.max(out=a[0][:, 0:8], in_=a[1][:]), [w], [m])
                s.op("dve", lambda e, a=(w2, m, w): e.match_replace(out=a[0][:], in_to_replace=a[1][:, 0:8], in_values=a[2][:], imm_value=-3e38), [w, m], [w2])
                s.op("dve", lambda e, a=(m, w2): e.max(out=a[0][:, 8:16], in_=a[1][:]), [w2], [m])
                s.ts("dve", w2[:], w[:], m[:, 15:16], None, ALU.subtract, None, [w, m], [w2])
                s.ts("dve", w2[:], w2[:], 0.0, None, ALU.is_ge, None, [w2], [w2])
                sb_ = selb.next()
                s.ts("dve", sb_[:], w2[:], -NEG, NEG, ALU.mult, ALU.add, [w2], [sb_])
                s.tr(pst[:, 1 + tb, :], sb_[:], identb[:], [sb_, identb], [pst])
            s.copy("act", mb[:], pst[:, 1:5, :].rearrange("p a b -> p (a b)"), [pst], [mb])
            mh = [mbh[0].next(), mbh[1].next()]
            s.copy("pool", mh[0][0:64, :], mb[0:64, :], [mb], [mh[0]])
            s.copy("pool", mh[1][64:128, :], mb[64:128, :], [mb], [mh[1]])

            s.mute = 7 not in ST
            for hl in range(2):
                j = JA + hl
                oa = oacc_ps.next()
                nk = 4 * qt + 4
                pend = None
                for kt in range(nk):
                    p = psr.next()
                    s.mm(p[:], kslc[:, kt * 128:(kt + 1) * 128], qa[j][:, :], True, False, [kslc, qa[j]], [p])
                    diag = kt >= 4 * qt
                    s.mm(p[:], E_all[:, kt % 32, :], mh[kt // 32][:], False, not diag, [E_all, mh[kt // 32]], [p])
                    if diag:
                        s.mm(p[:], identb[:], dmask[:, kt - 4 * qt + 4, :], False, True, [identb, dmask], [p])
                    pt = PT.next()
                    s.act(pt[:], p[:], AF.Exp, [p], [pt])
                    if pend is not None:
                        s.mm(oa[:, :], vslc_f[:, pend[0] * 72:pend[0] * 72 + 128], pend[1][:], pend[0] == 0, False, [vslc, pend[1]], [oa])
                    pend = (kt, pt)
                s.mm(oa[:, :], vslc_f[:, pend[0] * 72:pend[0] * 72 + 128], pend[1][:], pend[0] == 0, True, [vslc, pend[1]], [oa])
                epilogue(oa, hl, 1, gts, obs)

            s.mute = 8 not in ST
            for hl in range(2):
                j = JA + hl
                oa = oacc_ps.next()
                k_lo = max(0, 4 * qt - 4)
                nk = 4 * qt + 4
                pend = None
                for kt in range(k_lo, nk):
                    p = psr.next()
                    s.mm(p[:], kwin[:, kt % 8, :], qa[j][:, :], True, False, [kwin, qa[j]], [p])
                    s.mm(p[:], identb[:], dmask[:, kt - 4 * qt + 4, :], False, True, [identb, dmask], [p])
                    pt = PT.next()
                    s.act(pt[:], p[:], AF.Exp, [p], [pt])
                    if pend is not None:
                        s.mm(oa[:, :], vwin_f[:, (pend[0] % 8) * 72:(pend[0] % 8) * 72 + 128], pend[1][:], pend[0] == k_lo, False, [vwin, pend[1]], [oa])
                    pend = (kt, pt)
                s.mm(oa[:, :], vwin_f[:, (pend[0] % 8) * 72:(pend[0] % 8) * 72 + 128], pend[1][:], pend[0] == k_lo, True, [vwin, pend[1]], [oa])
                epilogue(oa, hl, 2, gts, obs)

            s.mute = 9 not in ST
            hgrn_tile(qt, vvs, sgs, obs)

            s.mute = 10 not in ST
            for tb in range(4):
                first = (qt == 0 and tb == 0)
                p = psr.next()
                s.mm(p[:, 0:128], pcs[tb][:], poolP[:, 2 if first else 0, :], True, False, [pcs[tb], poolP], [p])
                s.mm(p[:, 0:128], pc_prev[:], poolP[:, 1, :], False, True, [pc_prev, poolP], [p])
                pt_ = ptb.next()
                s.copy("act", pt_[:], p[:, 0:128], [p], [pt_])
                p2 = psr.next()
                s.mm(p2[:, 0:128], pt_[:], poolw[:], True, True, [pt_, poolw], [p2])
                s.tt("dve", obs[tb][:, 256:384], p2[:, 0:128], poolscale[:], ALU.mult, [p2, poolscale], [obs[tb]])
                pc_prev = pcs[tb]

            s.mute = 11 not in ST
            for tb in range(4):
                s.dma("sp", om[q0 + tb * 128:q0 + (tb + 1) * 128, :], obs[tb][:], reads=[obs[tb]])
        s.emit()
        print("M program: ops", s.n, "sbuf peak", s.peak, "gate/pcb/vvb/sgb/outb offs", [x.bufs[0].t.manual_sbuf_range for x in (gate, pcb, vvb, sgb, outb)], flush=True)
    return nc


def _m_consts():
    cst = {}
    t = np.arange(S)
    cst["kpos"] = np.stack([np.ones(S), np.ones(S), t // 64, t % 64]).astype(np.float32)
    cp = 16 * np.arange(512) + 31
    cst["cpos"] = np.stack([np.ones(512), np.ones(512), cp // 64, cp % 64]).astype(np.float32)
    cl = np.arange(128)[:, None]
    tl = np.arange(512)[None, :]
    cst["cmask"] = np.stack([np.where(16 * cl + 31 - tl <= 512 * dl, 0.0, NEG) for dl in range(5)]).astype(np.float32)
    dm = []
    for rel in range(-4, 4):
        dist = tl - cl - 128 * rel
        dm.append(np.where((dist >= 0) & (dist < 512), 0.0, NEG))
    cst["dmask"] = np.stack(dm).astype(np.float32)
    rr = (np.arange(128) % 64)[:, None, None]
    cst["E_all"] = (rr == 2 * np.arange(32)[None, :, None] + (np.arange(128)[None, None, :] // 64)).astype(np.float32)
    c0 = np.arange(511)[:, None] * 16
    s0 = np.arange(128)[None, :] * 64
    ov = np.clip(np.minimum(c0 + 32, s0 + 64) - np.maximum(c0, s0), 0, None) / 32.0
    sm = np.zeros((512, 129), np.float32)
    sm[:511, :128] = ov
    sm[:, 128] = 1.0
    cst["selmap"] = _c(sm.reshape(4, 128, 129).transpose(1, 0, 2))
    r = np.arange(256)[None, :] - 126
    hi = (np.arange(128)[:, None] >= 64).astype(np.int64)
    forced = (r == hi) | (r == hi - 1)
    future = r >= hi + 1
    cst["keep"] = np.where(forced | future, 0.0, 1.0).astype(np.float32)
    cst["addm"] = np.where(forced, 1e6, np.where(future, -1e30, 0.0)).astype(np.float32)
    cst["ident"] = np.eye(128, dtype=np.float32)
    cst["tri2"] = ((np.arange(128)[:, None] % 64) <= np.arange(64)[None, :]).astype(np.float32)
    return cst


def prep_M(inp, l, x):
    cst = _m_consts()
    w_in = inp["w_in"][l]
    maps = []
    xTs = [_c(x[b].T) for b in range(2)]
    tt_ = np.arange(S)
    for c in range(NCORE):
        b, u = divmod(c, 4)
        g = u // 2
        ja = 2 * (u % 2)
        heads = [ja, ja + 1] + [j for j in range(4) if j not in (ja, ja + 1)]
        m = dict(cst)
        m["xT"] = xTs[b]

        def kvcol(br, kv):
            return OFF_KV + ((br * 2 + kv) * 2 + g) * 64
        cols = []
        for j in heads:
            cols += list(range(OFF_Q + (g * 4 + j) * 64, OFF_Q + (g * 4 + j + 1) * 64))
        for (br, kv) in ((0, 0), (1, 0), (2, 0), (0, 1)):
            cols += list(range(kvcol(br, kv), kvcol(br, kv) + 64))
        cols += list(range(OFF_H + u * 128, OFF_H + (u + 1) * 128))
        cols += list(range(OFF_H + 512 + u * 128, OFF_H + 512 + (u + 1) * 128))
        m["wfm"] = _c(w_in[:, cols].reshape(16, 128, 768).transpose(1, 0, 2))
        cols = list(range(kvcol(1, 1), kvcol(1, 1) + 64)) + list(range(kvcol(2, 1), kvcol(2, 1) + 64))
        gcols = [OFF_G + (g * 4 + j) * 3 + br for j in (ja, ja + 1) for br in range(3)]
        cols += gcols + [gcols[0], gcols[0]]
        cols += list(range(OFF_P + u * 128, OFF_P + (u + 1) * 128))
        cols += list(range(OFF_H + 1024 + u * 128, OFF_H + 1024 + (u + 1) * 128))
        cols += list(range(OFF_H + 1536 + u * 128, OFF_H + 1536 + (u + 1) * 128))
        m["wtm"] = _c(w_in[:, cols].reshape(16, 128, 520).transpose(1, 0, 2))
        qp = np.zeros((4, 4, S), np.float32)
        for i, j in enumerate(heads):
            sl = 2.0 ** (-(g * 4 + j + 1))
            qp[i, 0] = -64.0 * sl * (tt_ // 64)
            qp[i, 1] = -sl * (tt_ % 64)
            qp[i, 2] = 64.0 * sl
            qp[i, 3] = sl
        m["qpos"] = qp
        m["cw_k"] = _c(inp["nsa_cmp_w"][l][0].transpose(1, 0, 2))
        m["cw_v"] = _c(inp["nsa_cmp_w"][l][1].transpose(1, 0, 2))
        m["cp_k"] = _c(inp["nsa_cmp_pos"][l][0].T)
        m["cp_v"] = _c(inp["nsa_cmp_pos"][l][1].T)
        m["lblog"] = _c(inp["hgrn_lb_logits"][:, u * 128:(u + 1) * 128].T)
        m["lbsel"] = np.full((128, 1), float(l), np.float32)
        m["normg"] = _bc(inp["hgrn_norm_g"][l][u * 128:(u + 1) * 128])
        win = (2, 4, 8, 16)[u]
        sI = np.arange(128)[:, None]
        tI = np.arange(128)[None, :]
        P = np.zeros((128, 3, 128), np.float32)
        P[:, 0, :] = np.where((sI > tI - win) & (sI <= tI), 1.0 / win, 0.0) - (sI == tI)
        P[:, 1, :] = np.where(sI >= 128 + tI - win + 1, 1.0 / win, 0.0)
        cnt = np.minimum(tI + 1, win).astype(np.float32)
        P[:, 2, :] = np.where((sI > tI - win) & (sI <= tI), 1.0 / cnt, 0.0) - (sI == tI)
        m["poolP"] = P
        m["poolw"] = _c(inp["pool_w"][l][u])
        m["poolscale"] = _bc(inp["pool_scale"][l][u * 128:(u + 1) * 128])
        maps.append(m)
    return maps


def run_M(inp, l, x):
    if "M" not in _NC_CACHE:
        _NC_CACHE["M"] = build_M()
    maps = prep_M(inp, l, x)
    res = run_bass_kernel_spmd(_NC_CACHE["M"], maps, core_ids=list(range(NCORE)))
    oabc = np.zeros((2, S, 1536), np.float32)
    for c in range(NCORE):
        b, u = divmod(c, 4)
        o = res.results[c]["om"]
        for k in range(3):
            oabc[b, :, 512 * k + 128 * u:512 * k + 128 * (u + 1)] = o[:, 128 * k:128 * (k + 1)]
    return oabc


def kernel(**inputs):
    inp = {k: np.asarray(v) for k, v in inputs.items()}
    x = np.ascontiguousarray(inp["x"], dtype=np.float32)
    for l in range(2):
        oabc = run_M(inp, l, x)
        x = run_R(inp, l, x, oabc)
    return x
```

```python
import contextlib
import numpy as np
import concourse.bass as bass
import concourse.mybir as mybir
from concourse.bass_utils import run_bass_kernel_spmd

F32 = mybir.dt.float32
BF16 = mybir.dt.bfloat16
AF = mybir.ActivationFunctionType
ALU = mybir.AluOpType
AX = mybir.AxisListType

D = 2048
S = 8192
NCORE = 8
ALPHA = 4 ** 0.25
EPS = 1e-5
DFF = 5632
NEG = -30000.0
SBUF_BASE = 16512
SBUF_CAP = 229344


class Buf:
    __slots__ = ("t", "w", "r")

    def __init__(self, t):
        self.t = t
        self.w = None
        self.r = []

    def __getitem__(self, k):
        return self.t[k]


class Sched:
    CE = ("pe", "act", "dve", "pool")
    DQ = ("sp", "act", "pool")

    def __init__(self, nc, es, ring=8):
        self.nc = nc
        self.es = es
        self.ops = {e: [] for e in ("pe", "act", "dve", "pool", "sp")}
        self.csem = {e: es.enter_context(nc.semaphore("c_" + e)) for e in self.CE}
        self.ccnt = {e: 0 for e in self.CE}
        self.ring = ring
        self.dsem = {q: [es.enter_context(nc.semaphore("d_%s%d" % (q, i))) for i in range(ring)] for q in self.DQ}
        self.dcnt = {q: 0 for q in self.DQ}
        self.dtok = {q: [None] * ring for q in self.DQ}
        self.seen = {e: {} for e in self.ops}
        self.n = 0

    def buf(self, name, shape, dt, psum=False):
        if psum:
            t = self.es.enter_context(self.nc.psum_tensor(name, shape, dt))
            return Buf(t)
        n = 1
        for d_ in shape[1:]:
            n *= d_
        nbytes = n * (2 if dt == BF16 else 4)
        nbytes = (nbytes + 63) // 64 * 64
        self.uid = getattr(self, "uid", 0) + 1
        off = getattr(self, "off", SBUF_BASE)
        assert off + nbytes <= SBUF_CAP, ("SBUF overflow", name, off, nbytes)
        t = self.nc.alloc_sbuf_tensor_at("%s_%d" % (name, self.uid), list(shape), dt, offset=off)
        self.off = off + nbytes
        self.peak = max(getattr(self, "peak", 0), self.off)
        return Buf(t)

    def mark(self):
        return getattr(self, "off", SBUF_BASE)

    def release(self, m):
        self.barrier()
        self.off = m

    def barrier(self):
        toks = []
        for q in self.DQ:
            for t in self.dtok[q]:
                if t is not None:
                    toks.append(t)
        for e in self.CE:
            if self.ccnt[e]:
                toks.append((self.csem[e], self.ccnt[e], "c_" + e, e))
        for eng in self.ops:
            waits = []
            seen = self.seen[eng]
            for (sem, val, key, src) in toks:
                if seen.get(key, 0) >= val:
                    continue
                seen[key] = val
                waits.append((sem, val))
            if waits:
                self.ops[eng].append((waits, None, None, 0))

    def _waits(self, eng, reads, writes, extra=()):
        deps = []
        for b in reads:
            if b.w is not None:
                deps.append(b.w)
        for b in writes:
            if b.w is not None:
                deps.append(b.w)
            deps.extend(b.r)
        deps.extend(extra)
        waits = []
        seen = self.seen[eng]
        for (sem, val, key, src) in deps:
            if src == "pe" and eng == "pe":
                continue
            if seen.get(key, 0) >= val:
                continue
            seen[key] = val
            waits.append((sem, val))
        return waits

    def op(self, eng, fn, reads=(), writes=()):
        if getattr(self, 'mute', False):
            return None
        waits = self._waits(eng, reads, writes)
        self.ccnt[eng] += 1
        tok = (self.csem[eng], self.ccnt[eng], "c_" + eng, eng)
        for b in reads:
            b.r.append(tok)
        for b in writes:
            b.w = tok
            b.r = []
        self.ops[eng].append((waits, fn, self.csem[eng], 1))
        self.n += 1
        return tok

    def dma(self, q, out, in_, reads=(), writes=()):
        if getattr(self, 'mute', False):
            return None
        i = self.dcnt[q]
        slot = i % self.ring
        extra = []
        if self.dtok[q][slot] is not None:
            extra.append(self.dtok[q][slot])
        waits = self._waits(q, reads, writes, extra)
        self.dcnt[q] += 1
        sem = self.dsem[q][slot]
        tok = (sem, 16 * (i // self.ring + 1), "d_%s%d" % (q, slot), "dma_" + q)
        self.dtok[q][slot] = tok
        for b in reads:
            b.r.append(tok)
        for b in writes:
            b.w = tok
            b.r = []
        self.ops[q].append((waits, lambda e, o=out, i_=in_: e.dma_start(out=o, in_=i_), sem, 16))
        self.n += 1
        return tok

    def coll(self, kind, ins, outs, groups, reads=(), writes=()):
        q = "pool"
        i = self.dcnt[q]
        slot = i % self.ring
        extra = []
        if self.dtok[q][slot] is not None:
            extra.append(self.dtok[q][slot])
        waits = self._waits(q, reads, writes, extra)
        self.dcnt[q] += 1
        sem = self.dsem[q][slot]
        tok = (sem, 16 * (i // self.ring + 1), "d_%s%d" % (q, slot), "dma_" + q)
        self.dtok[q][slot] = tok
        for b in reads:
            b.r.append(tok)
        for b in writes:
            b.w = tok
            b.r = []
        self.ops[q].append((waits, lambda e, a=(kind, ins, outs, groups): e.collective_compute(a[0], ALU.bypass, replica_groups=a[3], ins=a[1], outs=a[2]), sem, 16))
        self.n += 1
        return tok

    def finish(self):
        extra = []
        for q in self.DQ:
            for t in self.dtok[q]:
                if t is not None:
                    extra.append(t)
        for e in self.CE:
            if self.ccnt[e]:
                extra.append((self.csem[e], self.ccnt[e], "c_" + e, e))
        waits = self._waits("sp", (), (), extra)
        self.ops["sp"].append((waits, None, None, 0))

    def emit(self):
        self.finish()
        nc = self.nc
        ops = self.ops

        def replay(name, e):
            for waits, fn, sem, inc in ops[name]:
                for (s_, v_) in waits:
                    e.wait_ge(s_, v_)
                if fn is not None:
                    fn(e).then_inc(sem, inc)

        with nc.Block() as block:
            @block.tensor
            def _(e):
                replay("pe", e)

            @block.scalar
            def _(e):
                replay("act", e)

            @block.vector
            def _(e):
                replay("dve", e)

            @block.gpsimd
            def _(e):
                replay("pool", e)

            @block.sync
            def _(e):
                replay("sp", e)

    def mm(self, out, lhsT, rhs, start, stop, reads, writes):
        return self.op("pe", lambda e, a=(out, lhsT, rhs, start, stop): e.matmul(a[0], a[1], a[2], start=a[3], stop=a[4]), reads, writes)

    def tr(self, out, in_, ident, reads, writes):
        return self.op("pe", lambda e, a=(out, in_, ident): e.transpose(a[0], a[1], a[2]), reads, writes)

    def act(self, out, in_, func, reads, writes, bias=None, scale=None, eng="act"):
        kw = {}
        if bias is not None:
            kw["bias"] = bias
        if scale is not None:
            kw["scale"] = scale
        return self.op("act", lambda e, a=(out, in_, func, kw): e.activation(out=a[0], in_=a[1], func=a[2], **a[3]), reads, writes)

    def tt(self, eng, out, in0, in1, op, reads, writes):
        return self.op(eng, lambda e, a=(out, in0, in1, op): e.tensor_tensor(a[0], a[1], a[2], a[3]), reads, writes)

    def ts(self, eng, out, in0, s1, s2, op0, op1, reads, writes):
        if s2 is None:
            return self.op(eng, lambda e, a=(out, in0, s1, op0): e.tensor_scalar(a[0], a[1], a[2], None, a[3]), reads, writes)
        return self.op(eng, lambda e, a=(out, in0, s1, s2, op0, op1): e.tensor_scalar(a[0], a[1], a[2], a[3], a[4], a[5]), reads, writes)

    def stt(self, eng, out, in0, scalar, in1, op0, op1, reads, writes):
        return self.op(eng, lambda e, a=(out, in0, scalar, in1, op0, op1): e.scalar_tensor_tensor(a[0], a[1], a[2], a[3], a[4], a[5]), reads, writes)

    def copy(self, eng, out, in_, reads, writes):
        if eng == "act":
            return self.op("act", lambda e, a=(out, in_): e.copy(a[0], a[1]), reads, writes)
        return self.op(eng, lambda e, a=(out, in_): e.tensor_copy(a[0], a[1]), reads, writes)

    def memset(self, eng, ap, val, writes):
        return self.op(eng, lambda e, a=(ap, val): e.memset(a[0], a[1]), (), writes)

    def rsum(self, eng, out, in_, reads, writes):
        return self.op(eng, lambda e, a=(out, in_): e.reduce_sum(a[0], a[1], AX.X), reads, writes)

    def recip(self, out, in_, reads, writes):
        return self.op("dve", lambda e, a=(out, in_): e.reciprocal(a[0], a[1]), reads, writes)


class Ring:
    def __init__(self, bufs):
        self.bufs = bufs
        self.i = 0

    def next(self):
        b = self.bufs[self.i % len(self.bufs)]
        self.i += 1
        return b


def _dram(nc, name, shape, dt=F32, kind="ExternalInput"):
    return nc.dram_tensor(name, list(shape), dt, kind=kind).ap()


TT = 2176
NB = TT // 128
TTILES = [(0, 512), (512, 512), (1024, 512), (1536, 512), (2048, 128)]
NFC = DFF // 128

R_INPUTS = {
    "xT": (D, TT), "xtok": (TT, D), "oT": (1536, TT),
    "w_uv": (128, 16, 1024), "w_mg": (16, 128, 4, 16, 128), "w_br": (16, 128, 4, 4, 128), "w_mix": (128, 16, D),
    "sgw": (128, 4, 128), "sgb": (128, 4), "sg_g": (128, 512), "sg_b": (128, 512), "trimask": (128, 128),
    "ln1_g": (128, D), "ln1_b": (128, D), "ln2_g": (128, D), "ln2_b": (128, D), "ln3_g": (128, D), "ln3_b": (128, D),
    "mem": (256, D), "memln_g": (128, D), "memln_b": (128, D),
    "wq": (128, 16, 512), "wk": (128, 16, 512), "wv": (128, 16, 512), "wo": (128, 4, D),
    "ffn_in": (NFC, 128, 2, 16, 128), "ffn_out": (128, NFC, D), "convw": (128, NFC, 4),
    "ident": (128, 128), "flag": (128, 1), "ones": (128, 128),
}


def build_R():
    nc = bass.Bass("TRN2", target_bir_lowering=False)
    I = {k: _dram(nc, k, v) for k, v in R_INPUTS.items()}
    xout = _dram(nc, "xout", (2048, D), kind="ExternalOutput")
    odT_d = _dram(nc, "odT_d", (4, 128, TT), BF16, kind="Internal")
    preT_d = _dram(nc, "preT_d", (16, 128, TT), BF16, kind="Internal")
    x1_d = _dram(nc, "x1_d", (TT, D), F32, kind="Internal")
    x1T_d = _dram(nc, "x1T_d", (16, 128, TT), BF16, kind="Internal")
    aT_d = _dram(nc, "aT_d", (4, 128, TT), BF16, kind="Internal")
    x2_d = _dram(nc, "x2_d", (TT, D), F32, kind="Internal")
    x2T_d = _dram(nc, "x2T_d", (16, 128, TT), BF16, kind="Internal")
    act_d = _dram(nc, "act_d", (NFC, 128, TT), BF16, kind="Internal")
    ffo_d = _dram(nc, "ffo_d", (TT, D), F32, kind="Internal")
    with contextlib.ExitStack() as es:
        s = Sched(nc, es)
        B = s.buf
        ident = B("ident", [128, 128], BF16)
        ones = B("ones", [128, 128], BF16)
        flag = B("flag", [128, 1], F32)
        kT = B("kT", [128, 4, 256], BF16)
        vtok = B("vtok", [128, 2, 512], BF16)
        ps = Ring([B("ps%d" % i, [128, 512], F32, psum=True) for i in range(6)])
        pst = Ring([B("pst%d" % i, [128, 8, 128], BF16, psum=True) for i in range(2)])
        s.dma("pool", ident[:], I["ident"][:, :], writes=[ident])
        s.dma("pool", ones[:], I["ones"][:, :], writes=[ones])
        s.dma("sp", flag[:], I["flag"][:, :], writes=[flag])

        class LN:
            def __init__(self, gname, bname):
                self.g = B("lng", [128, D], F32)
                self.b = B("lnb", [128, D], F32)
                self.sq = B("lnsq", [128, D], F32)
                self.zt = Ring([B("zt%d" % i, [128, D], F32) for i in range(3)])
                self.zb = Ring([B("zb%d" % i, [128, D], BF16) for i in range(3)])
                self.xr = Ring([B("xr%d" % i, [128, D], F32) for i in range(5)])
                self.st = Ring([B("st%d" % i, [128, 8], F32) for i in range(6)])
                self.ob = Ring([B("lnob%d" % i, [128, 16, 128], BF16) for i in range(3)])
                s.dma("sp", self.g[:], I[gname][:, :], writes=[self.g])
                s.dma("sp", self.b[:], I[bname][:, :], writes=[self.b])

            def norm_gen(self, z, out):
                t = self.st.next()
                sq = self.sq
                s.op("act", lambda e, a=(sq, z, t): e.activation(out=a[0][:], in_=a[1][:], func=AF.Identity, accum_out=a[2][:, 0:1]), [z], [sq, t])
                s.op("act", lambda e, a=(sq, z, t): e.activation(out=a[0][:], in_=a[1][:], func=AF.Square, accum_out=a[2][:, 1:2]), [z], [sq, t])
                s.ts("dve", t[:, 2:3], t[:, 0:1], 1.0 / D, None, ALU.mult, None, [t], [t])
                s.ts("dve", t[:, 3:4], t[:, 1:2], 1.0 / D, None, ALU.mult, None, [t], [t])
                s.stt("dve", t[:, 4:5], t[:, 2:3], -1.0, t[:, 2:3], ALU.mult, ALU.mult, [t], [t])
                s.tt("dve", t[:, 5:6], t[:, 3:4], t[:, 4:5], ALU.add, [t], [t])
                s.act(t[:, 6:7], t[:, 5:6], AF.Sqrt, [t], [t], bias=EPS)
                s.recip(t[:, 7:8], t[:, 6:7], [t], [t])
                yield
                s.stt("dve", out[:], z[:], t[:, 2:3], self.g[:], ALU.subtract, ALU.mult, [z, t, self.g], [out])
                s.stt("dve", out[:], out[:], t[:, 7:8], self.b[:], ALU.mult, ALU.add, [out, t, self.b], [out])

            def norm(self, z, out):
                for _ in self.norm_gen(z, out):
                    pass

            def to_featmajor(self, xf, dst_d, dst_db, tb, width=16):
                xb = self.zb.next()
                s.copy("act", xb[:, 0:width * 128], xf[:, 0:width * 128], [xf], [xb])
                ob = self.ob.next()
                for h in range((width + 7) // 8):
                    p = pst.next()
                    nj = min(8, width - h * 8)
                    for j in range(nj):
                        kc = h * 8 + j
                        s.tr(p[:, j, :], xb[:, kc * 128:(kc + 1) * 128], ident[:], [xb, ident], [p])
                    s.copy("dve", ob[:, h * 8:h * 8 + nj, :], p[:, 0:nj, :], [p], [ob])
                s.dma("sp", dst_d[0:width, :, tb * 128:(tb + 1) * 128].rearrange("c p t -> p c t"), ob[:, 0:width, :], reads=[ob], writes=[dst_db[tb]])

            def proj_ln_gen(self, tb, lhs_fn, nk, w, xres_ap, xres_db, out_d, out_db, outT_d, outT_db, final_out=None):
                z = self.zt.next()
                xres = self.xr.next()
                s.dma("sp", xres[:], xres_ap, reads=list(xres_db), writes=[xres])
                for dt_ in range(4):
                    p = ps.next()
                    for k in range(nk):
                        lt, lbuf = lhs_fn(k)
                        s.mm(p[:], lt, w[:, k, dt_ * 512:(dt_ + 1) * 512], k == 0, k == nk - 1, [lbuf, w], [p])
                    s.stt("dve", z[:, dt_ * 512:(dt_ + 1) * 512], xres[:, dt_ * 512:(dt_ + 1) * 512], ALPHA, p[:], ALU.mult, ALU.add, [xres, p], [z])
                yield
                o = self.xr.next()
                for _ in self.norm_gen(z, o):
                    yield
                yield
                if final_out is not None:
                    if tb >= 1:
                        s.dma("sp", final_out[(tb - 1) * 128:tb * 128, :], o[:], reads=[o])
                else:
                    s.dma("sp", out_d[tb * 128:(tb + 1) * 128, :], o[:], reads=[o], writes=[out_db[tb]])
                    self.to_featmajor(o, outT_d, outT_db, tb)

        def run_pipelined(make_gen, n, depth=2):
            active = []
            nxt = 0
            while nxt < n or active:
                if nxt < n and len(active) < depth:
                    active.append(make_gen(nxt))
                    nxt += 1
                for g_ in list(active):
                    try:
                        next(g_)
                    except StopIteration:
                        active.remove(g_)

        def DB(n):
            return [Buf(None) for _ in range(n)]

        odT_db, preT_db, x1_db, x1T_db, aT_db, x2_db, x2T_db, act_db, ffo_db = DB(NB), DB(16), DB(NB), DB(NB), DB(4), DB(NB), DB(NB), DB(NFC), DB(NB * 4)
        base = s.mark()

        ln = LN("memln_g", "memln_b")
        memT_d = _dram(nc, "memT_d", (16, 128, 256), BF16, kind="Internal")
        memT_db = DB(2)
        for mb in range(2):
            z = ln.zt.next()
            o = ln.xr.next()
            s.dma("sp", z[:], I["mem"][mb * 128:(mb + 1) * 128, :], writes=[z])
            ln.norm(z, o)
            ln.to_featmajor(o, memT_d, memT_db, mb)
        memT = B("memT", [128, 16, 256], BF16)
        wk = B("wk", [128, 16, 512], BF16)
        wv = B("wv", [128, 16, 512], BF16)
        s.dma("sp", memT[:], memT_d[:, :, :].rearrange("c p t -> p c t"), reads=memT_db, writes=[memT])
        s.dma("pool", wk[:], I["wk"][:, :, :], writes=[wk])
        s.dma("pool", wv[:], I["wv"][:, :, :], writes=[wv])
        for h in range(4):
            p = ps.next()
            for kc in range(16):
                s.mm(p[:, 0:256], wk[:, kc, h * 128:(h + 1) * 128], memT[:, kc, :], kc == 0, kc == 15, [wk, memT], [p])
            s.copy("dve", kT[:, h, :], p[:, 0:256], [p], [kT])
        for mb in range(2):
            p = ps.next()
            for kc in range(16):
                s.mm(p[:], memT[:, kc, mb * 128:(mb + 1) * 128], wv[:, kc, :], kc == 0, kc == 15, [wv, memT], [p])
            s.copy("act", vtok[:, mb, :], p[:], [p], [vtok])
        s.release(base)

        xT = B("xT", [128, 16, TT], BF16)
        for kc in range(16):
            s.dma("pool", xT[:, kc, :], I["xT"][kc * 128:(kc + 1) * 128, :], writes=[xT])
        w_uv = B("w_uv", [128, 16, 1024], BF16)
        sgw = B("sgw", [128, 4, 128], F32)
        sgwb = B("sgwb", [128, 4, 128], BF16)
        tri = B("tri", [128, 128], F32)
        sgb = B("sgb", [128, 4], F32)
        sg_g = B("sg_g", [128, 512], F32)
        sg_b = B("sg_b", [128, 512], F32)
        for kc in range(16):
            s.dma("pool", w_uv[:, kc, :], I["w_uv"][:, kc, :], writes=[w_uv])
        s.dma("sp", sgw[:], I["sgw"][:, :, :], writes=[sgw])
        s.dma("sp", tri[:], I["trimask"][:, :], writes=[tri])
        s.dma("sp", sgb[:], I["sgb"][:, :], writes=[sgb])
        s.dma("sp", sg_g[:], I["sg_g"][:, :], writes=[sg_g])
        s.dma("sp", sg_b[:], I["sg_b"][:, :], writes=[sg_b])
        for g in range(4):
            s.tt("dve", sgwb[:, g, :], sgw[:, g, :], tri[:], ALU.mult, [sgw, tri], [sgwb])
        uvt = Ring([B("uvt%d" % i, [128, 512], F32) for i in range(12)])
        gt = Ring([B("gt%d" % i, [128, 512], F32) for i in range(6)])
        vlb = Ring([B("vlb%d" % i, [128, 512], BF16) for i in range(3)])
        odt = Ring([B("odt%d" % i, [128, 512], BF16) for i in range(3)])
        odo = Ring([B("odo%d" % i, [128, 4, 128], BF16) for i in range(3)])
        stA = Ring([B("stA%d" % i, [128, 8], F32) for i in range(6)])

        def gelu(src_ps, dst):
            x = uvt.next()
            t1 = uvt.next()
            s.copy("act", x[:], src_ps[:], [src_ps], [x])
            s.tt("dve", t1[:], x[:], x[:], ALU.mult, [x], [t1])
            s.ts("dve", t1[:], t1[:], 0.044715, 1.0, ALU.mult, ALU.add, [t1], [t1])
            s.tt("dve", t1[:], t1[:], x[:], ALU.mult, [t1, x], [t1])
            s.act(t1[:], t1[:], AF.Sigmoid, [t1], [t1], scale=1.5957691216057308)
            s.tt("pool", dst[:], t1[:], x[:], ALU.mult, [t1, x], [dst])

        def genA(tb):
            pu = ps.next()
            pv = ps.next()
            for (p, c0) in ((pu, 0), (pv, 512)):
                for kc in range(16):
                    s.mm(p[:], xT[:, kc, tb * 128:(tb + 1) * 128], w_uv[:, kc, c0:c0 + 512], kc == 0, kc == 15, [xT, w_uv], [p])
            gu = gt.next()
            gv = gt.next()
            gelu(pu, gu)
            gelu(pv, gv)
            yield
            t = stA.next()
            scr = uvt.next()
            s.op("act", lambda e, a=(scr, gv, t): e.activation(out=a[0][:], in_=a[1][:], func=AF.Identity, accum_out=a[2][:, 0:1]), [gv], [scr, t])
            s.op("act", lambda e, a=(scr, gv, t): e.activation(out=a[0][:], in_=a[1][:], func=AF.Square, accum_out=a[2][:, 1:2]), [gv], [scr, t])
            s.ts("dve", t[:, 2:3], t[:, 0:1], 1.0 / 512, None, ALU.mult, None, [t], [t])
            s.ts("dve", t[:, 3:4], t[:, 1:2], 1.0 / 512, None, ALU.mult, None, [t], [t])
            s.stt("dve", t[:, 4:5], t[:, 2:3], -1.0, t[:, 2:3], ALU.mult, ALU.mult, [t], [t])
            s.tt("dve", t[:, 5:6], t[:, 3:4], t[:, 4:5], ALU.add, [t], [t])
            s.act(t[:, 6:7], t[:, 5:6], AF.Sqrt, [t], [t], bias=EPS)
            s.recip(t[:, 7:8], t[:, 6:7], [t], [t])
            yield
            s.stt("dve", gv[:], gv[:], t[:, 2:3], sg_g[:], ALU.subtract, ALU.mult, [gv, t, sg_g], [gv])
            vb = vlb.next()
            s.stt("dve", vb[:], gv[:], t[:, 7:8], sg_b[:], ALU.mult, ALU.add, [gv, t, sg_b], [vb])
            yield
            pg = ps.next()
            for g in range(4):
                s.mm(pg[:, g * 128:(g + 1) * 128], sgwb[:, g, :], vb[:, g * 128:(g + 1) * 128], True, True, [sgwb, vb], [pg])
            od = odt.next()
            for g in range(4):
                s.stt("dve", od[:, g * 128:(g + 1) * 128], pg[:, g * 128:(g + 1) * 128], sgb[:, g:g + 1], gu[:, g * 128:(g + 1) * 128],
                      ALU.add, ALU.mult, [pg, sgb, gu], [od])
            yield
            p = pst.next()
            for j_ in range(4):
                s.tr(p[:, j_, :], od[:, j_ * 128:(j_ + 1) * 128], ident[:], [od, ident], [p])
            oo = odo.next()
            s.copy("act", oo[:], p[:, 0:4, :], [p], [oo])
            s.dma("sp", odT_d[:, :, tb * 128:(tb + 1) * 128].rearrange("c p t -> p c t"), oo[:], reads=[oo], writes=[odT_db[tb]])
        run_pipelined(genA, NB)
        mB = s.mark()
        s.barrier()
        s.off = xT_end = base + 16 * TT * 2
        assert xT_end % 64 == 0

        oT = B("oT", [128, 16, TT], BF16)
        for kc in range(12):
            s.dma("pool", oT[:, kc, :], I["oT"][kc * 128:(kc + 1) * 128, :], writes=[oT])
        s.dma("sp", oT[:, 12:16, :], odT_d[:, :, :].rearrange("c p t -> p c t"), reads=odT_db, writes=[oT])
        wmg = Ring([B("wmg%d" % i, [128, 4, 16, 128], BF16) for i in range(2)])
        wbr = Ring([B("wbr%d" % i, [128, 4, 4, 128], BF16) for i in range(2)])
        gsb = Ring([B("gsb%d" % i, [128, 512], F32) for i in range(3)])
        acc = Ring([B("acc%d" % i, [128, 512], F32) for i in range(2)])
        preo = Ring([B("preo%d" % i, [128, TT], BF16) for i in range(2)])
        wq_ = []
        for dc in range(min(1, 16)):
            wm = wmg.next()
            wb = wbr.next()
            s.dma("pool", wm[:], I["w_mg"][dc, :, :, :, :], writes=[wm])
            s.dma("pool", wb[:], I["w_br"][dc, :, :, :, :], writes=[wb])
            wq_.append((wm, wb))
        for dc in range(16):
            if dc + 1 < 16:
                wm = wmg.next()
                wb = wbr.next()
                s.dma("pool", wm[:], I["w_mg"][dc + 1, :, :, :, :], writes=[wm])
                s.dma("pool", wb[:], I["w_br"][dc + 1, :, :, :, :], writes=[wb])
                wq_.append((wm, wb))
            wm, wb = wq_[dc]
            po = preo.next()
            for (t0, tw) in TTILES:
                a = acc.next()
                for n in range(4):
                    pm = ps.next()
                    py = ps.next()
                    for kc in range(16):
                        s.mm(pm[:, 0:tw], wm[:, n, kc, :], xT[:, kc, t0:t0 + tw], kc == 0, kc == 15, [wm, xT], [pm])
                    for k4 in range(4):
                        s.mm(py[:, 0:tw], wb[:, n, k4, :], oT[:, n * 4 + k4, t0:t0 + tw], k4 == 0, k4 == 3, [wb, oT], [py])
                    gs = gsb.next()
                    s.act(gs[:, 0:tw], pm[:, 0:tw], AF.Sigmoid, [pm], [gs])
                    if n == 0:
                        s.tt("dve", a[:, 0:tw], gs[:, 0:tw], py[:, 0:tw], ALU.mult, [gs, py], [a])
                    else:
                        s.tt("dve", gs[:, 0:tw], gs[:, 0:tw], py[:, 0:tw], ALU.mult, [gs, py], [gs])
                        if n < 3:
                            s.tt("pool", a[:, 0:tw], a[:, 0:tw], gs[:, 0:tw], ALU.add, [a, gs], [a])
                        else:
                            s.tt("pool", po[:, t0:t0 + tw], a[:, 0:tw], gs[:, 0:tw], ALU.add, [a, gs], [po])
            s.dma("sp", preT_d[dc, :, :], po[:], reads=[po], writes=[preT_db[dc]])
        s.release(base)

        ln = LN("ln1_g", "ln1_b")
        wmix = B("wmix", [128, 16, D], BF16)
        for kc in range(16):
            s.dma("pool", wmix[:, kc, :], I["w_mix"][:, kc, :], writes=[wmix])
        prb = Ring([B("prb%d" % i, [128, 16, 128], BF16) for i in range(3)])

        def genC(tb):
            pb = prb.next()
            s.dma("sp", pb[:], preT_d[:, :, tb * 128:(tb + 1) * 128].rearrange("c p t -> p c t"), reads=preT_db, writes=[pb])
            yield from ln.proj_ln_gen(tb, lambda k, pb=pb: (pb[:, k, :], pb), 16, wmix, I["xtok"][tb * 128:(tb + 1) * 128, :], (), x1_d, x1_db, x1T_d, x1T_db)
        run_pipelined(genC, NB)
        s.release(base)

        x1T = B("x1T", [128, 16, TT], BF16)
        s.dma("sp", x1T[:], x1T_d[:, :, :].rearrange("c p t -> p c t"), reads=x1T_db, writes=[x1T])
        wq = B("wq", [128, 16, 512], BF16)
        s.dma("pool", wq[:], I["wq"][:, :, :], writes=[wq])
        qTb = Ring([B("qTb%d" % i, [128, 512], BF16) for i in range(2)])
        pTb = Ring([B("pTb%d" % i, [128, 512], BF16) for i in range(4)])
        rdb = Ring([B("rdb%d" % i, [128, 512], F32) for i in range(2)])
        aTo = Ring([B("aTo%d" % i, [128, TT], BF16) for i in range(2)])
        for h in range(4):
            ao = aTo.next()
            for (t0, tw) in TTILES:
                p = ps.next()
                for kc in range(16):
                    s.mm(p[:, 0:tw], wq[:, kc, h * 128:(h + 1) * 128], x1T[:, kc, t0:t0 + tw], kc == 0, kc == 15, [wq, x1T], [p])
                q = qTb.next()
                s.act(q[:, 0:tw], p[:, 0:tw], AF.Identity, [p], [q], scale=128 ** -0.5)
                pts = []
                for mb in range(2):
                    p2 = ps.next()
                    s.mm(p2[:, 0:tw], kT[:, h, mb * 128:(mb + 1) * 128], q[:, 0:tw], True, True, [kT, q], [p2])
                    pt = pTb.next()
                    s.act(pt[:, 0:tw], p2[:, 0:tw], AF.Exp, [p2], [pt])
                    pts.append(pt)
                po_ = ps.next()
                pd = ps.next()
                for mb in range(2):
                    s.mm(po_[:, 0:tw], vtok[:, mb, h * 128:(h + 1) * 128], pts[mb][:, 0:tw], mb == 0, mb == 1, [vtok, pts[mb]], [po_])
                for mb in range(2):
                    s.mm(pd[:, 0:tw], ones[:], pts[mb][:, 0:tw], mb == 0, mb == 1, [ones, pts[mb]], [pd])
                rd = rdb.next()
                s.recip(rd[:, 0:tw], pd[:, 0:tw], [pd], [rd])
                s.tt("dve", ao[:, t0:t0 + tw], po_[:, 0:tw], rd[:, 0:tw], ALU.mult, [po_, rd], [ao])
            s.dma("sp", aT_d[h, :, :], ao[:], reads=[ao], writes=[aT_db[h]])
        s.release(base)

        ln = LN("ln2_g", "ln2_b")
        wo = B("wo", [128, 4, D], BF16)
        s.dma("pool", wo[:], I["wo"][:, :, :], writes=[wo])
        aTr = Ring([B("aTr%d" % i, [128, 4, 128], BF16) for i in range(3)])

        def genD2(tb):
            ab = aTr.next()
            s.dma("sp", ab[:], aT_d[:, :, tb * 128:(tb + 1) * 128].rearrange("c p t -> p c t"), reads=aT_db, writes=[ab])
            yield from ln.proj_ln_gen(tb, lambda k, ab=ab: (ab[:, k, :], ab), 4, wo, x1_d[tb * 128:(tb + 1) * 128, :], [x1_db[tb]], x2_d, x2_db, x2T_d, x2T_db)
        run_pipelined(genD2, NB)
        s.release(base)

        x2T = B("x2T", [128, 16, TT], BF16)
        s.dma("sp", x2T[:], x2T_d[:, :, :].rearrange("c p t -> p c t"), reads=x2T_db, writes=[x2T])
        cw = B("cw", [128, NFC, 4], F32)
        s.dma("sp", cw[:], I["convw"][:, :, :], writes=[cw])
        wf = Ring([B("wf%d" % i, [128, 2, 16, 128], BF16) for i in range(3)])
        gbuf = Ring([B("gbuf%d" % i, [128, TT + 2], F32) for i in range(2)])
        ubuf = Ring([B("ubuf%d" % i, [128, TT], F32) for i in range(2)])
        cbuf = Ring([B("cbuf%d" % i, [128, TT], F32) for i in range(2)])
        sbuf_ = Ring([B("sbuf%d" % i, [128, TT], F32) for i in range(2)])
        abuf = Ring([B("abuf%d" % i, [128, TT], BF16) for i in range(2)])
        for g_ in gbuf.bufs:
            s.memset("dve", g_[:, 0:2], 0.0, [g_])
        wfq = []
        for fc in range(2):
            w = wf.next()
            s.dma("pool", w[:], I["ffn_in"][fc, :, :, :, :], writes=[w])
            wfq.append(w)
        for fc in range(NFC):
            if fc + 2 < NFC:
                w = wf.next()
                s.dma("pool", w[:], I["ffn_in"][fc + 2, :, :, :, :], writes=[w])
                wfq.append(w)
            w = wfq[fc]
            gb = gbuf.next()
            ub = ubuf.next()
            for (t0, tw) in TTILES:
                pg = ps.next()
                pu = ps.next()
                for kc in range(16):
                    s.mm(pg[:, 0:tw], w[:, 0, kc, :], x2T[:, kc, t0:t0 + tw], kc == 0, kc == 15, [w, x2T], [pg])
                for kc in range(16):
                    s.mm(pu[:, 0:tw], w[:, 1, kc, :], x2T[:, kc, t0:t0 + tw], kc == 0, kc == 15, [w, x2T], [pu])
                if t0 == 0:
                    s.op("act", lambda e, a=(gb, pg, flag): e.activation(out=a[0][:, 2:130], in_=a[1][:, 0:128], func=AF.Copy, scale=a[2][:, 0:1]), [pg, flag], [gb])
                    s.copy("act", gb[:, 130:2 + tw], pg[:, 128:tw], [pg], [gb])
                else:
                    s.copy("act", gb[:, 2 + t0:2 + t0 + tw], pg[:, 0:tw], [pg], [gb])
                s.copy("dve", ub[:, t0:t0 + tw], pu[:, 0:tw], [pu], [ub])
            cb = cbuf.next()
            s.ts("dve", cb[:], gb[:, 2:TT + 2], cw[:, fc, 2:3], cw[:, fc, 3:4], ALU.mult, ALU.add, [gb, cw], [cb])
            s.stt("dve", cb[:], gb[:, 1:TT + 1], cw[:, fc, 1:2], cb[:], ALU.mult, ALU.add, [gb, cw, cb], [cb])
            s.stt("dve", cb[:], gb[:, 0:TT], cw[:, fc, 0:1], cb[:], ALU.mult, ALU.add, [gb, cw, cb], [cb])
            sb = sbuf_.next()
            s.act(sb[:], cb[:], AF.Silu, [cb], [sb])
            ab = abuf.next()
            s.tt("pool", ab[:], sb[:], ub[:], ALU.mult, [sb, ub], [ab])
            s.dma("sp", act_d[fc, :, :], ab[:], reads=[ab], writes=[act_db[fc]])
        s.release(base)

        wfo = Ring([B("wfo%d" % i, [128, NFC, 512], BF16) for i in range(2)])
        acb = Ring([B("acb%d" % i, [128, NFC, 128], BF16) for i in range(4)])
        fob = Ring([B("fob%d" % i, [128, 512], F32) for i in range(3)])
        for dt_ in range(4):
            w = wfo.next()
            for f0 in range(0, NFC, 11):
                s.dma("pool", w[:, f0:f0 + 11, :], I["ffn_out"][:, f0:f0 + 11, dt_ * 512:(dt_ + 1) * 512], writes=[w])
            for tb in range(NB):
                step = dt_ * NB + tb
                if step == 0:
                    abq = []
                    for st_ in range(3):
                        ab = acb.next()
                        tb_ = st_ % NB
                        s.dma("sp", ab[:], act_d[:, :, tb_ * 128:(tb_ + 1) * 128].rearrange("c p t -> p c t"), reads=act_db, writes=[ab])
                        abq.append(ab)
                if step + 3 < 4 * NB:
                    ab = acb.next()
                    tb_ = (step + 3) % NB
                    s.dma("sp", ab[:], act_d[:, :, tb_ * 128:(tb_ + 1) * 128].rearrange("c p t -> p c t"), reads=act_db, writes=[ab])
                    abq.append(ab)
                ab = abq[step]
                p = ps.next()
                for fc in range(NFC):
                    s.mm(p[:], ab[:, fc, :], w[:, fc, :], fc == 0, fc == NFC - 1, [ab, w], [p])
                fo = fob.next()
                if tb % 2 == 0:
                    s.copy("act", fo[:], p[:], [p], [fo])
                else:
                    s.copy("dve", fo[:], p[:], [p], [fo])
                s.dma("sp", ffo_d[tb * 128:(tb + 1) * 128, dt_ * 512:(dt_ + 1) * 512], fo[:], reads=[fo], writes=[ffo_db[tb * 4 + dt_]])
        s.release(base)

        ln = LN("ln3_g", "ln3_b")
        def genF2(i):
            tb = i + 1
            z = ln.zt.next()
            xres = ln.xr.next()
            ff = ln.xr.next()
            s.dma("sp", xres[:], x2_d[tb * 128:(tb + 1) * 128, :], reads=[x2_db[tb]], writes=[xres])
            s.dma("sp", ff[:], ffo_d[tb * 128:(tb + 1) * 128, :], reads=ffo_db[tb * 4:tb * 4 + 4], writes=[ff])
            s.stt("dve", z[:], xres[:], ALPHA, ff[:], ALU.mult, ALU.add, [xres, ff], [z])
            yield
            for _ in ln.norm_gen(z, z):
                yield
            yield
            s.dma("sp", xout[(tb - 1) * 128:tb * 128, :], z[:], reads=[z])
        run_pipelined(genF2, NB - 1)
        s.emit()
        print("R program: ops", s.n, "sbuf peak", s.peak, flush=True)
    return nc


OFF_Q, OFF_KV, OFF_G, OFF_H, OFF_P, OFF_SG, OFF_MG = 0, 512, 1280, 1304, 3352, 3864, 4888


def _bc(v, n=128):
    return np.ascontiguousarray(np.broadcast_to(np.asarray(v, np.float32)[None, :], (n, v.shape[0])))


def _c(a):
    return np.ascontiguousarray(a, dtype=np.float32)


def prep_R_shared(inp, l):
    w_in = inp["w_in"][l]
    sh = {}
    sh["w_uv"] = _c(w_in[:, OFF_SG:OFF_SG + 1024].reshape(16, 128, 1024).transpose(1, 0, 2))
    mg = w_in[:, OFF_MG:OFF_MG + 8192].reshape(16, 128, 4, 16, 128)
    sh["w_mg"] = _c(mg.transpose(3, 1, 2, 0, 4))
    br = inp["w_branch"][l].reshape(4, 4, 128, 16, 128)
    sh["w_br"] = _c(br.transpose(3, 2, 0, 1, 4))
    sh["w_mix"] = _c(inp["w_mix_out"][l].reshape(16, 128, D).transpose(1, 0, 2))
    sh["sgw"] = _c(inp["sg_w"][l].transpose(2, 0, 1))
    sh["sgb"] = _c(inp["sg_b"][l].T)
    sh["sg_g"] = _bc(inp["sg_ln_g"][l])
    sh["sg_b"] = _bc(inp["sg_ln_b"][l])
    sh["trimask"] = _c(np.triu(np.ones((128, 128), np.float32)))
    for i, nm in ((1, "ln_mix"), (2, "ln_x"), (3, "ln_ffn")):
        sh["ln%d_g" % i] = _bc(inp[nm + "_g"][l])
        sh["ln%d_b" % i] = _bc(inp[nm + "_b"][l])
    sh["memln_g"] = _bc(inp["mem_ln_g"])
    sh["memln_b"] = _bc(inp["mem_ln_b"])
    for nm, k in (("wq", "xattn_q"), ("wk", "xattn_k"), ("wv", "xattn_v")):
        sh[nm] = _c(inp[k][l].reshape(16, 128, 512).transpose(1, 0, 2))
    sh["wo"] = _c(inp["xattn_o"][l].reshape(4, 128, D).transpose(1, 0, 2))
    fi = inp["ffn_in"][l].reshape(16, 128, 2, NFC, 128)
    sh["ffn_in"] = _c(fi.transpose(3, 1, 2, 0, 4))
    sh["ffn_out"] = _c(inp["ffn_out"][l].reshape(NFC, 128, D).transpose(1, 0, 2))
    cw = np.concatenate([inp["ffn_conv_w"][l], inp["ffn_conv_b"][l][None, :]], axis=0)
    sh["convw"] = _c(cw.reshape(4, NFC, 128).transpose(2, 1, 0))
    sh["ident"] = np.eye(128, dtype=np.float32)
    sh["ones"] = np.ones((128, 128), np.float32)
    return sh


def prep_R(inp, l, x, oabc):
    sh = prep_R_shared(inp, l)
    maps = []
    for c in range(NCORE):
        b, u = divmod(c, 4)
        t0 = 2048 * u
        m = dict(sh)
        xs = np.zeros((TT, D), np.float32)
        os_ = np.zeros((TT, 1536), np.float32)
        lo = t0 - 128
        if u == 0:
            xs[128:] = x[b, 0:2048]
            os_[128:] = oabc[b, 0:2048]
        else:
            xs[:] = x[b, lo:lo + TT]
            os_[:] = oabc[b, lo:lo + TT]
        m["xtok"] = xs
        m["xT"] = _c(xs.T)
        m["oT"] = _c(os_.T)
        m["mem"] = _c(inp["mem"][b])
        m["flag"] = np.full((128, 1), 0.0 if u == 0 else 1.0, np.float32)
        maps.append(m)
    return maps


_NC_CACHE = {}


def run_R(inp, l, x, oabc):
    if "R" not in _NC_CACHE:
        _NC_CACHE["R"] = build_R()
    maps = prep_R(inp, l, x, oabc)
    res = run_bass_kernel_spmd(_NC_CACHE["R"], maps, core_ids=list(range(NCORE)))
    out = np.zeros((2, S, D), np.float32)
    for c in range(NCORE):
        b, u = divmod(c, 4)
        out[b, 2048 * u:2048 * (u + 1)] = res.results[c]["xout"]
    return out


NQT = S // 512
M_INPUTS = {
    "xT": (D, S), "wfm": (128, 16, 768), "wtm": (128, 16, 520),
    "qpos": (4, 4, S), "kpos": (4, S), "cpos": (4, 512),
    "cmask": (5, 128, 512), "dmask": (8, 128, 512), "E_all": (128, 32, 128), "selmap": (128, 4, 129),
    "keep": (128, 256), "addm": (128, 256), "ident": (128, 128), "tri2": (128, 64),
    "cw_k": (64, 32, 64), "cw_v": (64, 32, 64), "cp_k": (64, 32), "cp_v": (64, 32),
    "lblog": (128, 2), "lbsel": (128, 1), "normg": (128, 128),
    "poolP": (128, 3, 128), "poolw": (128, 128), "poolscale": (128, 128),
}


def build_M(nqt=NQT, ST=(1, 2, 3, 4, 5, 6, 7, 8, 9, 10, 11)):
    nc = bass.Bass("TRN2", target_bir_lowering=False)
    I = {k: _dram(nc, k, v) for k, v in M_INPUTS.items()}
    om = _dram(nc, "om", (S, 384), kind="ExternalOutput")
    with contextlib.ExitStack() as es:
        s = Sched(nc, es)
        B = s.buf

        def PB(name, shape, dt):
            return B(name, shape, dt, psum=True)

        psr = Ring([PB("psr%d" % i, [128, 512], F32) for i in range(2)])
        pst = PB("pst", [128, 8, 128], BF16)
        oacc_ps = Ring([PB("oacc%d" % i, [128, 512], F32) for i in range(2)])
        impb = PB("impb", [128, 4, 128], F32)
        misc = PB("misc", [128, 512], F32)
        denb = misc
        hg = PB("hg", [128, 4, 128], F32)

        wfm = B("wfm", [128, 16, 768], BF16)
        wtm = B("wtm", [128, 16, 520], BF16)
        for kc in range(16):
            s.dma("pool", wfm[:, kc, :], I["wfm"][:, kc, :], writes=[wfm])
            s.dma("pool", wtm[:, kc, :], I["wtm"][:, kc, :], writes=[wtm])
        identb = B("identb", [128, 128], BF16)
        identf = B("identf", [128, 128], F32)
        s.dma("pool", identb[:], I["ident"][:, :], writes=[identb])
        s.dma("sp", identf[:], I["ident"][:, :], writes=[identf])
        cmask = B("cmask", [128, 5, 512], BF16)
        dmask = B("dmask", [128, 8, 512], BF16)
        for i in range(5):
            s.dma("pool", cmask[:, i, :], I["cmask"][i, :, :], writes=[cmask])
        for i in range(8):
            s.dma("pool", dmask[:, i, :], I["dmask"][i, :, :], writes=[dmask])
        E_all = B("E_all", [128, 32, 128], BF16)
        s.dma("pool", E_all[:], I["E_all"][:, :, :], writes=[E_all])
        selmap = B("selmap", [128, 4, 144], BF16)
        s.dma("pool", selmap[:, :, 0:129], I["selmap"][:, :, :], writes=[selmap])
        keep = B("keep", [128, 256], F32)
        addm = B("addm", [128, 256], F32)
        tri2 = B("tri2", [128, 64], F32)
        s.dma("sp", keep[:], I["keep"][:, :], writes=[keep])
        s.dma("sp", addm[:], I["addm"][:, :], writes=[addm])
        s.dma("sp", tri2[:], I["tri2"][:, :], writes=[tri2])
        cw_k = B("cw_k", [64, 32, 64], BF16)
        cw_v = B("cw_v", [64, 32, 64], BF16)
        cp_k = B("cp_k", [64, 32], BF16)
        cp_v = B("cp_v", [64, 32], BF16)
        s.dma("pool", cw_k[:], I["cw_k"][:, :, :], writes=[cw_k])
        s.dma("pool", cw_v[:], I["cw_v"][:, :, :], writes=[cw_v])
        s.dma("pool", cp_k[:], I["cp_k"][:, :], writes=[cp_k])
        s.dma("pool", cp_v[:], I["cp_v"][:, :], writes=[cp_v])
        normg = B("normg", [128, 128], F32)
        poolP = B("poolP", [128, 3, 128], F32)
        poolw = B("poolw", [128, 128], BF16)
        poolscale = B("poolscale", [128, 128], F32)
        lbl = B("lbl", [128, 8], F32)
        s.dma("sp", normg[:], I["normg"][:, :], writes=[normg])
        s.dma("sp", poolP[:], I["poolP"][:, :, :], writes=[poolP])
        s.dma("pool", poolw[:], I["poolw"][:, :], writes=[poolw])
        s.dma("sp", poolscale[:], I["poolscale"][:, :], writes=[poolscale])
        s.dma("sp", lbl[:, 0:2], I["lblog"][:, :], writes=[lbl])
        s.dma("sp", lbl[:, 2:3], I["lbsel"][:, :], writes=[lbl])
        s.tt("dve", lbl[:, 3:4], lbl[:, 1:2], lbl[:, 0:1], ALU.subtract, [lbl], [lbl])
        s.act(lbl[:, 4:5], lbl[:, 3:4], AF.Sigmoid, [lbl], [lbl])
        s.tt("dve", lbl[:, 5:6], lbl[:, 4:5], lbl[:, 2:3], ALU.mult, [lbl], [lbl])
        s.ts("dve", lbl[:, 6:7], lbl[:, 5:6], -1.0, 1.0, ALU.mult, ALU.add, [lbl], [lbl])

        kslc = B("kslc", [128, S], BF16)
        s.memset("dve", kslc[64:128, :], 0.0, [kslc])
        s.dma("pool", kslc[64:68, :], I["kpos"][:, :], writes=[kslc])
        kwin = B("kwin", [128, 8, 128], BF16)
        s.memset("dve", kwin[64:128, :, :], 0.0, [kwin])
        vslc = B("vslc", [128, 65, 72], BF16)
        vwin = B("vwin", [128, 10, 72], BF16)
        s.memset("dve", vslc[:], 0.0, [vslc])
        s.memset("dve", vwin[:], 0.0, [vwin])
        s.memset("dve", vslc[:, 0:64, 64:65], 1.0, [vslc])
        s.memset("dve", vwin[:, 0:8, 64:65], 1.0, [vwin])
        kcaug = B("kcaug", [128, 512], BF16)
        s.memset("dve", kcaug[:], 0.0, [kcaug])
        s.dma("pool", kcaug[64:68, :], I["cpos"][:, :], writes=[kcaug])
        vcaug = B("vcaug", [128, 6, 72], BF16)
        s.memset("dve", vcaug[:], 0.0, [vcaug])
        s.memset("dve", vcaug[:, 0:4, 64:65], 1.0, [vcaug])
        kcraw = B("kcraw", [64, 528], BF16)
        vcraw = B("vcraw", [64, 528], BF16)
        s.memset("dve", kcraw[:], 0.0, [kcraw])
        s.memset("dve", vcraw[:], 0.0, [vcraw])
        cbias = B("cbias", [64, 2], F32)
        for (cw_, cp_, col) in ((cw_k, cp_k, 0), (cw_v, cp_v, 1)):
            p = psr.next()
            for l_ in range(32):
                s.mm(p[0:64, 0:1], cw_[:, l_, :], cp_[:, l_:l_ + 1], l_ == 0, l_ == 31, [cw_, cp_], [p])
            s.copy("dve", cbias[:, col:col + 1], p[0:64, 0:1], [p], [cbias])

        state = B("state", [128, 128], F32)
        stbf = Ring([B("stbf%d" % i, [128, 128], BF16) for i in range(2)])
        s.memset("dve", state[:], 0.0, [state])
        st_cur = stbf.next()
        s.memset("dve", st_cur[:], 0.0, [st_cur])
        qbpad = B("qbpad", [128, 4, 2, 128], BF16)
        s.memset("pool", qbpad[:], 0.0, [qbpad])
        atpad = Ring([B("atpad%d" % i, [128, 128], BF16) for i in range(2)])
        for a_ in atpad.bufs:
            s.memset("pool", a_[:], 0.0, [a_])
        pczero = B("pczero", [128, 128], F32)
        s.memset("pool", pczero[:], 0.0, [pczero])

        xtile = Ring([B("xtile%d" % i, [128, 16, 512], BF16) for i in range(1)])
        qaug = [Ring([B("qaug%d_%d" % (j, i), [128, 512], BF16) for i in range(2)]) for j in range(4)]
        for r_ in qaug:
            for b_ in r_.bufs:
                s.memset("dve", b_[64:128, :], 0.0, [b_])
        hqT = B("hqT", [128, 512], F32)
        hzT = B("hzT", [128, 512], F32)
        gate = Ring([B("gate%d" % i, [128, 8], F32) for i in range(8)])
        pcb = Ring([B("pcb%d" % i, [128, 128], F32) for i in range(6)])
        vvb = Ring([B("vvb%d" % i, [128, 128], BF16) for i in range(4)])
        sgb = Ring([B("sgb%d" % i, [128, 128], F32) for i in range(4)])
        outb = Ring([B("outb%d" % i, [128, 384], F32) for i in range(4)])
        ET = [[B("ET%d_%d" % (j, c), [128, 512], BF16) for c in range(4)] for j in range(2)]
        PT = Ring([B("PT%d" % i, [128, 512], BF16) for i in range(3)])
        impacc = B("impacc", [128, 4, 128], F32)
        rden = Ring([B("rden%d" % i, [128, 4], F32) for i in range(4)])
        selw = Ring([B("selw%d" % i, [128, 128], F32) for i in range(3)])
        selb = Ring([B("selb%d" % i, [128, 128], BF16) for i in range(2)])
        m8 = Ring([B("m8_%d" % i, [128, 16], F32) for i in range(2)])
        mbT = Ring([B("mbT%d" % i, [128, 512], BF16) for i in range(2)])
        oTs = Ring([B("oTs%d" % i, [65, 512], F32) for i in range(2)])
        coef = Ring([B("coef%d" % i, [128, 4], F32) for i in range(4)])
        vcT = Ring([B("vcT%d" % i, [64, 32], BF16) for i in range(2)])
        vct = Ring([B("vct%d" % i, [32, 64], BF16) for i in range(2)])
        hw_ = [B("hw%d" % i, [128, 512], F32) for i in range(7)]
        hb_ = [B("hb%d" % i, [128, 512], BF16) for i in range(4)]
        hs_ = Ring([B("hs%d" % i, [128, 16], F32) for i in range(2)])
        kdb = Ring([B("kdb%d" % i, [128, 128], BF16) for i in range(2)])
        hst = Ring([B("hst%d" % i, [128, 4], F32) for i in range(4)])
        hsq = B("hsq", [128, 128], F32)
        hsq2 = B("hsq2", [128, 128], F32)
        ptb = Ring([B("ptb%d" % i, [128, 128], BF16) for i in range(2)])
        pc_prev = pczero
        trs = Ring([B("trs%d" % i, [128, 4, 65], F32) for i in range(1)])
        mbh = [Ring([B("mbh%d_%d" % (h, i), [128, 512], BF16) for i in range(2)]) for h in range(2)]
        for h_ in range(2):
            for b_ in mbh[h_].bufs:
                s.memset("dve", b_[:], 0.0, [b_])
        imps = Ring([B("imps%d" % i, [128, 4, 128], F32) for i in range(1)])
        asb = Ring([B("asb%d" % i, [128, 128], F32) for i in range(2)])
        stgA = Ring([B("stgA%d" % i, [128, 136], F32) for i in range(1)])
        stgB = Ring([B("stgB%d" % i, [128, 384], F32) for i in range(1)])

        vslc_f = vslc[:].rearrange("p a b -> p (a b)")
        vwin_f = vwin[:].rearrange("p a b -> p (a b)")
        vcaug_f = vcaug[:].rearrange("p a b -> p (a b)")
        JA = 0
        trf = Buf(misc.t)
        trv = misc.t[:, 128:388].rearrange("p (a b) -> p a b", a=4)
        hstate = {"cur": st_cur}

        def epilogue(oa, hl, br, gts, obs):
            o_sb = oTs.next()
            s.copy("act", o_sb[:], oa[0:65, :], [oa], [o_sb])
            for tb in range(4):
                s.tr(trv[:, tb, :], o_sb[:, tb * 128:(tb + 1) * 128], identf[0:65, 0:65], [o_sb, identf], [trf])
            tv = trs.next()
            s.copy("dve", tv[:], trv[:, :, :], [trf], [tv])
            cf = coef.next()
            s.ts("dve", cf[:], tv[:, :, 64], 1e-30, None, ALU.max, None, [tv], [cf])
            s.recip(cf[:], cf[:], [cf], [cf])
            for tb in range(4):
                s.tt("dve", cf[:, tb:tb + 1], cf[:, tb:tb + 1], gts[tb][:, hl * 3 + br:hl * 3 + br + 1], ALU.mult, [cf, gts[tb]], [cf])
            for tb in range(4):
                dst = obs[tb][:, hl * 64:(hl + 1) * 64]
                if br == 0:
                    s.ts("dve", dst, tv[:, tb, 0:64], cf[:, tb:tb + 1], None, ALU.mult, None, [tv, cf], [obs[tb]])
                else:
                    s.stt("dve", dst, tv[:, tb, 0:64], cf[:, tb:tb + 1], dst, ALU.mult, ALU.add, [tv, cf, obs[tb]], [obs[tb]])

        def hgrn_tile(qt, vvs, sgs, obs):
            W = hw_
            s.act(W[0][:], hzT[:], AF.Sigmoid, [hzT], [W[0]])
            s.ts("dve", W[0][:], W[0][:], lbl[:, 6:7], lbl[:, 5:6], ALU.mult, ALU.add, [W[0], lbl], [W[0]])
            s.ts("dve", W[0][:], W[0][:], 1e-6, None, ALU.max, None, [W[0]], [W[0]])
            s.ts("dve", W[1][:], W[0][:], -1.0, 1.0, ALU.mult, ALU.add, [W[0]], [W[1]])
            s.act(W[2][:], W[0][:], AF.Ln, [W[0]], [W[2]])
            yield
            src, dst = W[2], W[3]
            for sh in (1, 2, 4, 8, 16, 32):
                sv = src[:].rearrange("p (c t) -> p c t", t=64)
                dv = dst[:].rearrange("p (c t) -> p c t", t=64)
                s.copy("pool", dv[:, :, 0:sh], sv[:, :, 0:sh], [src], [dst])
                s.tt("dve", dv[:, :, sh:64], sv[:, :, sh:64], sv[:, :, 0:64 - sh], ALU.add, [src], [dst])
                src, dst = dst, src
                yield
            b = src
            bv = b[:].rearrange("p (c t) -> p c t", t=64)
            hs = hs_.next()
            nh = hs_.next()
            s.copy("dve", hs[:, 0:8], bv[:, :, 31], [b], [hs])
            s.copy("dve", hs[:, 8:16], bv[:, :, 63], [b], [hs])
            s.ts("dve", nh[:, 0:8], hs[:, 0:8], -1.0, None, ALU.mult, None, [hs], [nh])
            s.act(nh[:, 8:16], hs[:, 8:16], AF.Exp, [hs], [nh])
            yield
            for c in range(8):
                cs = slice(c * 64, (c + 1) * 64)
                s.act(W[3][:, cs], b[:, cs], AF.Exp, [b, nh], [W[3]], bias=nh[:, c:c + 1])
                s.act(W[4][:, cs], b[:, cs], AF.Exp, [b, hs], [W[4]], bias=hs[:, c:c + 1], scale=-1.0)
                s.act(W[5][:, cs], b[:, cs], AF.Exp, [b, hs], [W[5]], bias=hs[:, 8 + c:9 + c], scale=-1.0)
                yield
            s.act(W[6][:], b[:], AF.Exp, [b], [W[6]])
            s.tt("pool", hb_[0][:], hqT[:], W[3][:], ALU.mult, [hqT, W[3]], [hb_[0]])
            s.tt("dve", hb_[1][:], W[1][:], W[4][:], ALU.mult, [W[1], W[4]], [hb_[1]])
            s.tt("pool", hb_[2][:], W[1][:], W[5][:], ALU.mult, [W[1], W[5]], [hb_[2]])
            yield
            hq4 = hqT[:].rearrange("p (a c t) -> p a c t", a=4, c=2)
            eb4 = W[6][:].rearrange("p (a c t) -> p a c t", a=4, c=2)
            s.tt("dve", qbpad[:, :, 0, 0:64], hq4[:, :, 0, :], eb4[:, :, 0, :], ALU.mult, [hqT, W[6]], [qbpad])
            s.tt("pool", qbpad[:, :, 1, 64:128], hq4[:, :, 1, :], eb4[:, :, 1, :], ALU.mult, [hqT, W[6]], [qbpad])
            yield
            for tb in range(4):
                blk = slice(tb * 128, (tb + 1) * 128)
                s.mm(hg[:, 0, :], hb_[1][:, blk], hb_[0][:, blk], True, True, [hb_[1], hb_[0]], [hg])
                at = atpad.next()
                a_sb = asb.next()
                s.copy("dve", a_sb[:], hg[:, 0, :], [hg], [a_sb])
                s.tt("pool", at[0:64, 0:64], a_sb[0:64, 0:64], tri2[0:64, :], ALU.mult, [a_sb, tri2], [at])
                s.tt("pool", at[64:128, 64:128], a_sb[64:128, 64:128], tri2[64:128, :], ALU.mult, [a_sb, tri2], [at])
                yield
                s.tr(pst[:, 5, :], hb_[2][:, blk], identb[:], [hb_[2], identb], [pst])
                kb = kdb.next()
                s.copy("act", kb[:], pst[:, 5, :], [pst], [kb])
                yield
                st0 = hstate["cur"]
                s.mm(hg[:, 1, :], qbpad[:, tb, 0, :], st0[:], True, False, [qbpad, st0], [hg])
                s.mm(hg[:, 1, :], at[:], vvs[tb][:], False, True, [at, vvs[tb]], [hg])
                s.mm(hg[:, 2, :], kb[0:64, :], vvs[tb][0:64, :], True, True, [kb, vvs[tb]], [hg])
                yield
                s.stt("dve", state[:], state[:], nh[:, 8 + 2 * tb:9 + 2 * tb], hg[:, 2, :], ALU.mult, ALU.add, [state, nh, hg], [state])
                st1 = stbf.next()
                s.copy("act", st1[:], state[:], [state], [st1])
                yield
                oa_sb = hsq
                s.copy("dve", oa_sb[:], hg[:, 1, :], [hg], [oa_sb])
                s.mm(hg[:, 3, :], qbpad[:, tb, 1, :], st1[:], True, True, [qbpad, st1], [hg])
                s.tt("dve", oa_sb[:], oa_sb[:], hg[:, 3, :], ALU.add, [oa_sb, hg], [oa_sb])
                yield
                s.mm(hg[:, 2, :], kb[64:128, :], vvs[tb][64:128, :], True, True, [kb, vvs[tb]], [hg])
                s.stt("dve", state[:], state[:], nh[:, 9 + 2 * tb:10 + 2 * tb], hg[:, 2, :], ALU.mult, ALU.add, [state, nh, hg], [state])
                st2 = stbf.next()
                s.copy("act", st2[:], state[:], [state], [st2])
                yield
                hstate["cur"] = st2
                t_ = hst.next()
                s.op("act", lambda e, a=(hsq2, oa_sb, t_): e.activation(out=a[0][:], in_=a[1][:], func=AF.Square, accum_out=a[2][:, 0:1]), [oa_sb], [hsq2, t_])
                s.ts("dve", t_[:, 1:2], t_[:, 0:1], 1.0 / 128, EPS, ALU.mult, ALU.add, [t_], [t_])
                s.act(t_[:, 2:3], t_[:, 1:2], AF.Sqrt, [t_], [t_])
                s.recip(t_[:, 3:4], t_[:, 2:3], [t_], [t_])
                yield
                s.stt("dve", obs[tb][:, 128:256], oa_sb[:], t_[:, 3:4], normg[:], ALU.mult, ALU.mult, [oa_sb, t_, normg], [obs[tb]])
                s.tt("pool", obs[tb][:, 128:256], obs[tb][:, 128:256], sgs[tb][:], ALU.mult, [obs[tb], sgs[tb]], [obs[tb]])

        g1w, g2w = 136, 384

        xtk = [Buf(xtile.bufs[0].t) for _ in range(16)]

        def load_x(qt_):
            for kc in range(16):
                s.dma("pool", xtile.bufs[0][:, kc, :], I["xT"][kc * 128:(kc + 1) * 128, qt_ * 512:(qt_ + 1) * 512], writes=[xtk[kc]])

        for qt in range(nqt):
            q0 = qt * 512
            s.mute = 1 not in ST
            xt = xtile.next()
            if qt == 0:
                load_x(0)
            qa = [qaug[j].next() for j in range(4)]
            for j in range(4):
                s.dma("pool", qa[j][64:68, :], I["qpos"][j, :, q0:q0 + 512], writes=[qa[j]])
            slot0 = (4 * qt) % 8
            s.dma("pool", kwin[64:68, slot0:slot0 + 4, :], I["kpos"][:, q0:q0 + 512].rearrange("r (a b) -> r a b", a=4), writes=[kwin])
            if qt > 0:
                s.copy("dve", kcraw[:, 0:16], kcraw[:, 512:528], [kcraw], [kcraw])
                s.copy("dve", vcraw[:, 0:16], vcraw[:, 512:528], [vcraw], [vcraw])

            s.mute = 2 not in ST
            def fm(col0, M):
                p = psr.next()
                for kc in range(16):
                    s.mm(p[0:M, :], wfm[:, kc, col0:col0 + M], xt[:, kc, :], kc == 0, kc == 15, [wfm, xtk[kc]], [p])
                return p
            for j in range(4):
                p = fm(j * 64, 64)
                s.act(qa[j][0:64, :], p[0:64, :], AF.Identity, [p], [qa[j]], scale=0.125)
            p = fm(256, 64)
            s.copy("act", kcraw[:, 16:528], p[0:64, :], [p], [kcraw])
            p = fm(320, 64)
            s.copy("dve", kslc[0:64, q0:q0 + 512], p[0:64, :], [p], [kslc])
            p = fm(384, 64)
            s.copy("act", kwin[0:64, slot0:slot0 + 4, :], p[0:64, :].rearrange("p (a b) -> p a b", a=4), [p], [kwin])
            p = fm(448, 64)
            s.copy("dve", vcraw[:, 16:528], p[0:64, :], [p], [vcraw])
            p = fm(512, 128)
            s.copy("act", hqT[:], p[:], [p], [hqT])
            p = fm(640, 128)
            s.copy("dve", hzT[:], p[:], [p], [hzT])

            s.mute = 3 not in ST
            gts, pcs, vvs, sgs, obs = [], [], [], [], []
            for tb in range(4):
                kt = 4 * qt + tb
                p = psr.next()
                for kc in range(16):
                    s.mm(p[:, 0:g1w], xt[:, kc, tb * 128:(tb + 1) * 128], wtm[:, kc, 0:g1w], kc == 0, kc == 15, [wtm, xtk[kc]], [p])
                sA = stgA.next()
                s.copy("dve", sA[:, 0:g1w], p[:, 0:g1w], [p], [sA])
                s.copy("pool", vslc[:, kt, 0:64], sA[:, 0:64], [sA], [vslc])
                s.copy("pool", vwin[:, kt % 8, 0:64], sA[:, 64:128], [sA], [vwin])
                g_ = gate.next()
                s.act(g_[:], sA[:, 128:136], AF.Sigmoid, [sA], [g_])
                gts.append(g_)
                p = psr.next()
                for kc in range(16):
                    s.mm(p[:, 0:g2w], xt[:, kc, tb * 128:(tb + 1) * 128], wtm[:, kc, g1w:g1w + g2w], kc == 0, kc == 15, [wtm, xtk[kc]], [p])
                pc = pcb.next()
                vv = vvb.next()
                sg = sgb.next()
                sB = stgB.next()
                s.copy("act", sB[:, 0:g2w], p[:, 0:g2w], [p], [sB])
                s.copy("pool", pc[:], sB[:, 0:128], [sB], [pc])
                s.copy("dve", vv[:], sB[:, 128:256], [sB], [vv])
                s.act(sg[:], sB[:, 256:384], AF.Silu, [sB], [sg])
                pcs.append(pc)
                vvs.append(vv)
                sgs.append(sg)
                obs.append(outb.next())

            if qt + 1 < nqt and 1 in ST:
                s.mute = False
                load_x(qt + 1)
            s.mute = 4 not in ST
            c_lo = 32 * qt - 1 if qt > 0 else 0
            c_hi = 32 * qt + 30
            ncb = c_hi - c_lo + 1
            i0 = 0 if qt > 0 else 1
            kview = kcraw[:].rearrange("p (c s) -> p c s", s=16)
            vview = vcraw[:].rearrange("p (c s) -> p c s", s=16)
            p = psr.next()
            for l_ in range(32):
                s.mm(p[0:64, 0:ncb], cw_k[:, l_, :], kview[:, i0 + l_ // 16:i0 + l_ // 16 + ncb, l_ % 16], l_ == 0, l_ == 31, [cw_k, kcraw], [p])
            s.act(kcaug[0:64, c_lo:c_hi + 1], p[0:64, 0:ncb], AF.Identity, [p, cbias], [kcaug], bias=cbias[:, 0:1])
            p = psr.next()
            for l_ in range(32):
                s.mm(p[0:64, 0:ncb], cw_v[:, l_, :], vview[:, i0 + l_ // 16:i0 + l_ // 16 + ncb, l_ % 16], l_ == 0, l_ == 31, [cw_v, vcraw], [p])
            vT_ = vcT.next()
            s.act(vT_[:, 0:ncb], p[0:64, 0:ncb], AF.Identity, [p, cbias], [vT_], bias=cbias[:, 1:2])
            s.tr(pst[0:ncb, 0, 0:64], vT_[:, 0:ncb], identb[0:64, 0:64], [vT_, identb], [pst])
            vt_ = vct.next()
            s.copy("dve", vt_[0:ncb, :], pst[0:ncb, 0, 0:64], [pst], [vt_])
            c = c_lo
            while c <= c_hi:
                ct_ = c // 128
                n_ = min(c_hi + 1, (ct_ + 1) * 128) - c
                s.dma("sp", vcaug[c % 128:c % 128 + n_, ct_, 0:64], vt_[c - c_lo:c - c_lo + n_, :], reads=[vt_], writes=[vcaug])
                c += n_

            hgen = hgrn_tile(qt, vvs, sgs, obs) if 9 in ST else iter(())
            s.mute = 5 not in ST
            nct = (32 * qt + 30) // 128 + 1
            oc_ps = {}
            for j in range(4):
                mine = j in (JA, JA + 1)
                ets = []
                for ct_ in range(nct):
                    dl = qt - 4 * ct_
                    p = psr.next()
                    last = dl > 4
                    s.mm(p[:], kcaug[:, ct_ * 128:(ct_ + 1) * 128], qa[j][:, :], True, last, [kcaug, qa[j]], [p])
                    if not last:
                        s.mm(p[:], identb[:], cmask[:, dl, :], False, True, [identb, cmask], [p])
                    e_ = ET[j % 2][ct_]
                    s.act(e_[:], p[:], AF.Exp, [p], [e_])
                    ets.append(e_)
                if mine:
                    oa = oacc_ps.next()
                    for ct_ in range(nct):
                        s.mm(oa[:, :], vcaug_f[:, ct_ * 72:ct_ * 72 + 128], ets[ct_][:], ct_ == 0, ct_ == nct - 1, [vcaug, ets[ct_]], [oa])
                    oc_ps[j] = oa
                for tb in range(4):
                    for ct_ in range(nct):
                        s.mm(impb[:, tb, :], ets[ct_][:, tb * 128:(tb + 1) * 128], selmap[:, ct_, 0:128], ct_ == 0, ct_ == nct - 1, [ets[ct_], selmap], [impb])
                for tb in range(4):
                    for ct_ in range(nct):
                        s.mm(denb[:, tb:tb + 1], ets[ct_][:, tb * 128:(tb + 1) * 128], selmap[:, ct_, 128:129], ct_ == 0, ct_ == nct - 1, [ets[ct_], selmap], [denb])
                rd = rden.next()
                s.ts("dve", rd[:], denb[:, 0:4], 1e-30, None, ALU.max, None, [denb], [rd])
                s.recip(rd[:], rd[:], [rd], [rd])
                im = imps.next()
                s.copy("act", im[:], impb[:], [impb], [im])
                for tb in range(4):
                    if j == 0:
                        s.ts("dve", impacc[:, tb, :], im[:, tb, :], rd[:, tb:tb + 1], None, ALU.mult, None, [im, rd], [impacc])
                    else:
                        s.stt("dve", impacc[:, tb, :], im[:, tb, :], rd[:, tb:tb + 1], impacc[:, tb, :], ALU.mult, ALU.add, [im, rd, impacc], [impacc])
                if mine:
                    epilogue(oc_ps[j], j - JA, 0, gts, obs)

            s.mute = 6 not in ST
            mb = mbT.next()
            for tb in range(4):
                tbg = 4 * qt + tb
                c0 = 126 - 2 * tbg
                w = selw.next()
                s.tt("dve", w[:], impacc[:, tb, :], keep[:, c0:c0 + 128], ALU.mult, [impacc, keep], [w])
                s.tt("dve", w[:], w[:], addm[:, c0:c0 + 128], ALU.add, [w, addm], [w])
                s.memset("dve", w[:, 0:1], 1e6, [w])
                m = m8.next()
                w2 = selw.next()
                s.op("dve", lambda e, a=(m, w): e.max(out=a[0][:, 0:8], in_=a[1][:]), [w], [m])
                s.op("dve", lambda e, a=(w2, m, w): e.match_replace(out=a[0][:], in_to_replace=a[1][:, 0:8], in_values=a[2][:], imm_value=-3e38), [w, m], [w2])
                s.op("dve", lambda e, a=(m, w2): e.max(out=a[0][:, 8:16], in_=a[1][:]), [w2], [m])
                s.ts("dve", w2[:], w[:], m[:, 15:16], None, ALU.subtract, None, [w, m], [w2])
                s.ts("dve", w2[:], w2[:], 0.0, None, ALU.is_ge, None, [w2], [w2])
                sb_ = selb.next()
                s.ts("dve", sb_[:], w2[:], -NEG, NEG, ALU.mult, ALU.add, [w2], [sb_])
                s.tr(pst[:, 1 + tb, :], sb_[:], identb[:], [sb_, identb], [pst])
            s.copy("act", mb[:], pst[:, 1:5, :].rearrange("p a b -> p (a b)"), [pst], [mb])
            mh = [mbh[0].next(), mbh[1].next()]
            s.copy("pool", mh[0][0:64, :], mb[0:64, :], [mb], [mh[0]])
            s.copy("pool", mh[1][64:128, :], mb[64:128, :], [mb], [mh[1]])

            s.mute = 7 not in ST
            for hl in range(2):
                j = JA + hl
                oa = oacc_ps.next()
                nk = 4 * qt + 4
                pend = None
                for kt in range(nk):
                    p = psr.next()
                    s.mm(p[:], kslc[:, kt * 128:(kt + 1) * 128], qa[j][:, :], True, False, [kslc, qa[j]], [p])
                    diag = kt >= 4 * qt
                    s.mm(p[:], E_all[:, kt % 32, :], mh[kt // 32][:], False, not diag, [E_all, mh[kt // 32]], [p])
                    if diag:
                        s.mm(p[:], identb[:], dmask[:, kt - 4 * qt + 4, :], False, True, [identb, dmask], [p])
                    pt = PT.next()
                    s.act(pt[:], p[:], AF.Exp, [p], [pt])
                    if pend is not None:
                        s.mm(oa[:, :], vslc_f[:, pend[0] * 72:pend[0] * 72 + 128], pend[1][:], pend[0] == 0, False, [vslc, pend[1]], [oa])
                    pend = (kt, pt)
                    if kt % 2 == 1:
                        _m = s.mute
                        s.mute = 9 not in ST
                        next(hgen, None)
                        s.mute = _m
                s.mm(oa[:, :], vslc_f[:, pend[0] * 72:pend[0] * 72 + 128], pend[1][:], pend[0] == 0, True, [vslc, pend[1]], [oa])
                epilogue(oa, hl, 1, gts, obs)

            s.mute = 8 not in ST
            for hl in range(2):
                j = JA + hl
                oa = oacc_ps.next()
                k_lo = max(0, 4 * qt - 4)
                nk = 4 * qt + 4
                pend = None
                for kt in range(k_lo, nk):
                    p = psr.next()
                    s.mm(p[:], kwin[:, kt % 8, :], qa[j][:, :], True, False, [kwin, qa[j]], [p])
                    s.mm(p[:], identb[:], dmask[:, kt - 4 * qt + 4, :], False, True, [identb, dmask], [p])
                    pt = PT.next()
                    s.act(pt[:], p[:], AF.Exp, [p], [pt])
                    if pend is not None:
                        s.mm(oa[:, :], vwin_f[:, (pend[0] % 8) * 72:(pend[0] % 8) * 72 + 128], pend[1][:], pend[0] == k_lo, False, [vwin, pend[1]], [oa])
                    pend = (kt, pt)
                s.mm(oa[:, :], vwin_f[:, (pend[0] % 8) * 72:(pend[0] % 8) * 72 + 128], pend[1][:], pend[0] == k_lo, True, [vwin, pend[1]], [oa])
                epilogue(oa, hl, 2, gts, obs)

            s.mute = 9 not in ST
            for _ in hgen:
                pass

            s.mute = 10 not in ST
            for tb in range(4):
                first = (qt == 0 and tb == 0)
                p = psr.next()
                s.mm(p[:, 0:128], pcs[tb][:], poolP[:, 2 if first else 0, :], True, False, [pcs[tb], poolP], [p])
                s.mm(p[:, 0:128], pc_prev[:], poolP[:, 1, :], False, True, [pc_prev, poolP], [p])
                pt_ = ptb.next()
                s.copy("act", pt_[:], p[:, 0:128], [p], [pt_])
                p2 = psr.next()
                s.mm(p2[:, 0:128], pt_[:], poolw[:], True, True, [pt_, poolw], [p2])
                s.tt("dve", obs[tb][:, 256:384], p2[:, 0:128], poolscale[:], ALU.mult, [p2, poolscale], [obs[tb]])
                pc_prev = pcs[tb]

            s.mute = 11 not in ST
            for tb in range(4):
                s.dma("sp", om[q0 + tb * 128:q0 + (tb + 1) * 128, :], obs[tb][:], reads=[obs[tb]])
        s.emit()
        print("M program: ops", s.n, "sbuf peak", s.peak, "gate/pcb/vvb/sgb/outb offs", [x.bufs[0].t.manual_sbuf_range for x in (gate, pcb, vvb, sgb, outb)], flush=True)
    return nc


def _m_consts():
    cst = {}
    t = np.arange(S)
    cst["kpos"] = np.stack([np.ones(S), np.ones(S), t // 64, t % 64]).astype(np.float32)
    cp = 16 * np.arange(512) + 31
    cst["cpos"] = np.stack([np.ones(512), np.ones(512), cp // 64, cp % 64]).astype(np.float32)
    cl = np.arange(128)[:, None]
    tl = np.arange(512)[None, :]
    cst["cmask"] = np.stack([np.where(16 * cl + 31 - tl <= 512 * dl, 0.0, NEG) for dl in range(5)]).astype(np.float32)
    dm = []
    for rel in range(-4, 4):
        dist = tl - cl - 128 * rel
        dm.append(np.where((dist >= 0) & (dist < 512), 0.0, NEG))
    cst["dmask"] = np.stack(dm).astype(np.float32)
    rr = (np.arange(128) % 64)[:, None, None]
    cst["E_all"] = (rr == 2 * np.arange(32)[None, :, None] + (np.arange(128)[None, None, :] // 64)).astype(np.float32)
    c0 = np.arange(511)[:, None] * 16
    s0 = np.arange(128)[None, :] * 64
    ov = np.clip(np.minimum(c0 + 32, s0 + 64) - np.maximum(c0, s0), 0, None) / 32.0
    sm = np.zeros((512, 129), np.float32)
    sm[:511, :128] = ov
    sm[:, 128] = 1.0
    cst["selmap"] = _c(sm.reshape(4, 128, 129).transpose(1, 0, 2))
    r = np.arange(256)[None, :] - 126
    hi = (np.arange(128)[:, None] >= 64).astype(np.int64)
    forced = (r == hi) | (r == hi - 1)
    future = r >= hi + 1
    cst["keep"] = np.where(forced | future, 0.0, 1.0).astype(np.float32)
    cst["addm"] = np.where(forced, 1e6, np.where(future, -1e30, 0.0)).astype(np.float32)
    cst["ident"] = np.eye(128, dtype=np.float32)
    cst["tri2"] = ((np.arange(128)[:, None] % 64) <= np.arange(64)[None, :]).astype(np.float32)
    return cst


def prep_M(inp, l, x):
    cst = _m_consts()
    w_in = inp["w_in"][l]
    maps = []
    xTs = [_c(x[b].T) for b in range(2)]
    tt_ = np.arange(S)
    for c in range(NCORE):
        b, u = divmod(c, 4)
        g = u // 2
        ja = 2 * (u % 2)
        heads = [ja, ja + 1] + [j for j in range(4) if j not in (ja, ja + 1)]
        m = dict(cst)
        m["xT"] = xTs[b]

        def kvcol(br, kv):
            return OFF_KV + ((br * 2 + kv) * 2 + g) * 64
        cols = []
        for j in heads:
            cols += list(range(OFF_Q + (g * 4 + j) * 64, OFF_Q + (g * 4 + j + 1) * 64))
        for (br, kv) in ((0, 0), (1, 0), (2, 0), (0, 1)):
            cols += list(range(kvcol(br, kv), kvcol(br, kv) + 64))
        cols += list(range(OFF_H + u * 128, OFF_H + (u + 1) * 128))
        cols += list(range(OFF_H + 512 + u * 128, OFF_H + 512 + (u + 1) * 128))
        m["wfm"] = _c(w_in[:, cols].reshape(16, 128, 768).transpose(1, 0, 2))
        cols = list(range(kvcol(1, 1), kvcol(1, 1) + 64)) + list(range(kvcol(2, 1), kvcol(2, 1) + 64))
        gcols = [OFF_G + (g * 4 + j) * 3 + br for j in (ja, ja + 1) for br in range(3)]
        cols += gcols + [gcols[0], gcols[0]]
        cols += list(range(OFF_P + u * 128, OFF_P + (u + 1) * 128))
        cols += list(range(OFF_H + 1024 + u * 128, OFF_H + 1024 + (u + 1) * 128))
        cols += list(range(OFF_H + 1536 + u * 128, OFF_H + 1536 + (u + 1) * 128))
        m["wtm"] = _c(w_in[:, cols].reshape(16, 128, 520).transpose(1, 0, 2))
        qp = np.zeros((4, 4, S), np.float32)
        for i, j in enumerate(heads):
            sl = 2.0 ** (-(g * 4 + j + 1))
            qp[i, 0] = -64.0 * sl * (tt_ // 64)
            qp[i, 1] = -sl * (tt_ % 64)
            qp[i, 2] = 64.0 * sl
            qp[i, 3] = sl
        m["qpos"] = qp
        m["cw_k"] = _c(inp["nsa_cmp_w"][l][0].transpose(1, 0, 2))
        m["cw_v"] = _c(inp["nsa_cmp_w"][l][1].transpose(1, 0, 2))
        m["cp_k"] = _c(inp["nsa_cmp_pos"][l][0].T)
        m["cp_v"] = _c(inp["nsa_cmp_pos"][l][1].T)
        m["lblog"] = _c(inp["hgrn_lb_logits"][:, u * 128:(u + 1) * 128].T)
        m["lbsel"] = np.full((128, 1), float(l), np.float32)
        m["normg"] = _bc(inp["hgrn_norm_g"][l][u * 128:(u + 1) * 128])
        win = (2, 4, 8, 16)[u]
        sI = np.arange(128)[:, None]
        tI = np.arange(128)[None, :]
        P = np.zeros((128, 3, 128), np.float32)
        P[:, 0, :] = np.where((sI > tI - win) & (sI <= tI), 1.0 / win, 0.0) - (sI == tI)
        P[:, 1, :] = np.where(sI >= 128 + tI - win + 1, 1.0 / win, 0.0)
        cnt = np.minimum(tI + 1, win).astype(np.float32)
        P[:, 2, :] = np.where((sI > tI - win) & (sI <= tI), 1.0 / cnt, 0.0) - (sI == tI)
        m["poolP"] = P
        m["poolw"] = _c(inp["pool_w"][l][u])
        m["poolscale"] = _bc(inp["pool_scale"][l][u * 128:(u + 1) * 128])
        maps.append(m)
    return maps


def run_M(inp, l, x):
    if "M" not in _NC_CACHE:
        _NC_CACHE["M"] = build_M()
    maps = prep_M(inp, l, x)
    res = run_bass_kernel_spmd(_NC_CACHE["M"], maps, core_ids=list(range(NCORE)))
    oabc = np.zeros((2, S, 1536), np.float32)
    for c in range(NCORE):
        b, u = divmod(c, 4)
        o = res.results[c]["om"]
        for k in range(3):
            oabc[b, :, 512 * k + 128 * u:512 * k + 128 * (u + 1)] = o[:, 128 * k:128 * (k + 1)]
    return oabc


def kernel(**inputs):
    inp = {k: np.asarray(v) for k, v in inputs.items()}
    x = np.ascontiguousarray(inp["x"], dtype=np.float32)
    for l in range(2):
        oabc = run_M(inp, l, x)
        x = run_R(inp, l, x, oabc)
    return x
```

```python
import contextlib
import numpy as np
import concourse.bass as bass
import concourse.mybir as mybir
from concourse.bass_utils import run_bass_kernel_spmd

F32 = mybir.dt.float32
BF16 = mybir.dt.bfloat16
AF = mybir.ActivationFunctionType
ALU = mybir.AluOpType
AX = mybir.AxisListType

D = 2048
S = 8192
NCORE = 8
ALPHA = 4 ** 0.25
EPS = 1e-5
DFF = 5632
NEG = -30000.0
SBUF_BASE = 16512
SBUF_CAP = 229344


class Buf:
    __slots__ = ("t", "w", "r")

    def __init__(self, t):
        self.t = t
        self.w = None
        self.r = []

    def __getitem__(self, k):
        return self.t[k]


class Sched:
    CE = ("pe", "act", "dve", "pool")
    DQ = ("sp", "act", "pool")

    def __init__(self, nc, es, ring=8):
        self.nc = nc
        self.es = es
        self.ops = {e: [] for e in ("pe", "act", "dve", "pool", "sp")}
        self.csem = {e: es.enter_context(nc.semaphore("c_" + e)) for e in self.CE}
        self.ccnt = {e: 0 for e in self.CE}
        self.ring = ring
        self.dsem = {q: [es.enter_context(nc.semaphore("d_%s%d" % (q, i))) for i in range(ring)] for q in self.DQ}
        self.dcnt = {q: 0 for q in self.DQ}
        self.dtok = {q: [None] * ring for q in self.DQ}
        self.seen = {e: {} for e in self.ops}
        self.n = 0

    def buf(self, name, shape, dt, psum=False):
        if psum:
            t = self.es.enter_context(self.nc.psum_tensor(name, shape, dt))
            return Buf(t)
        n = 1
        for d_ in shape[1:]:
            n *= d_
        nbytes = n * (2 if dt == BF16 else 4)
        nbytes = (nbytes + 63) // 64 * 64
        self.uid = getattr(self, "uid", 0) + 1
        off = getattr(self, "off", SBUF_BASE)
        assert off + nbytes <= SBUF_CAP, ("SBUF overflow", name, off, nbytes)
        t = self.nc.alloc_sbuf_tensor_at("%s_%d" % (name, self.uid), list(shape), dt, offset=off)
        self.off = off + nbytes
        self.peak = max(getattr(self, "peak", 0), self.off)
        return Buf(t)

    def mark(self):
        return getattr(self, "off", SBUF_BASE)

    def release(self, m):
        self.barrier()
        self.off = m

    def barrier(self):
        toks = []
        for q in self.DQ:
            for t in self.dtok[q]:
                if t is not None:
                    toks.append(t)
        for e in self.CE:
            if self.ccnt[e]:
                toks.append((self.csem[e], self.ccnt[e], "c_" + e, e))
        for eng in self.ops:
            waits = []
            seen = self.seen[eng]
            for (sem, val, key, src) in toks:
                if seen.get(key, 0) >= val:
                    continue
                seen[key] = val
                waits.append((sem, val))
            if waits:
                self.ops[eng].append((waits, None, None, 0))

    def _waits(self, eng, reads, writes, extra=()):
        deps = []
        for b in reads:
            if b.w is not None:
                deps.append(b.w)
        for b in writes:
            if b.w is not None:
                deps.append(b.w)
            deps.extend(b.r)
        deps.extend(extra)
        waits = []
        seen = self.seen[eng]
        for (sem, val, key, src) in deps:
            if src == "pe" and eng == "pe":
                continue
            if seen.get(key, 0) >= val:
                continue
            seen[key] = val
            waits.append((sem, val))
        return waits

    def op(self, eng, fn, reads=(), writes=()):
        if getattr(self, 'mute', False):
            return None
        waits = self._waits(eng, reads, writes)
        self.ccnt[eng] += 1
        tok = (self.csem[eng], self.ccnt[eng], "c_" + eng, eng)
        for b in reads:
            b.r.append(tok)
        for b in writes:
            b.w = tok
            b.r = []
        self.ops[eng].append((waits, fn, self.csem[eng], 1))
        self.n += 1
        return tok

    def dma(self, q, out, in_, reads=(), writes=()):
        if getattr(self, 'mute', False):
            return None
        i = self.dcnt[q]
        slot = i % self.ring
        extra = []
        if self.dtok[q][slot] is not None:
            extra.append(self.dtok[q][slot])
        waits = self._waits(q, reads, writes, extra)
        self.dcnt[q] += 1
        sem = self.dsem[q][slot]
        tok = (sem, 16 * (i // self.ring + 1), "d_%s%d" % (q, slot), "dma_" + q)
        self.dtok[q][slot] = tok
        for b in reads:
            b.r.append(tok)
        for b in writes:
            b.w = tok
            b.r = []
        self.ops[q].append((waits, lambda e, o=out, i_=in_: e.dma_start(out=o, in_=i_), sem, 16))
        self.n += 1
        return tok

    def coll(self, kind, ins, outs, groups, reads=(), writes=()):
        q = "pool"
        i = self.dcnt[q]
        slot = i % self.ring
        extra = []
        if self.dtok[q][slot] is not None:
            extra.append(self.dtok[q][slot])
        waits = self._waits(q, reads, writes, extra)
        self.dcnt[q] += 1
        sem = self.dsem[q][slot]
        tok = (sem, 16 * (i // self.ring + 1), "d_%s%d" % (q, slot), "dma_" + q)
        self.dtok[q][slot] = tok
        for b in reads:
            b.r.append(tok)
        for b in writes:
            b.w = tok
            b.r = []
        self.ops[q].append((waits, lambda e, a=(kind, ins, outs, groups): e.collective_compute(a[0], ALU.bypass, replica_groups=a[3], ins=a[1], outs=a[2]), sem, 16))
        self.n += 1
        return tok

    def finish(self):
        extra = []
        for q in self.DQ:
            for t in self.dtok[q]:
                if t is not None:
                    extra.append(t)
        for e in self.CE:
            if self.ccnt[e]:
                extra.append((self.csem[e], self.ccnt[e], "c_" + e, e))
        waits = self._waits("sp", (), (), extra)
        self.ops["sp"].append((waits, None, None, 0))

    def emit(self):
        self.finish()
        nc = self.nc
        ops = self.ops

        def replay(name, e):
            for waits, fn, sem, inc in ops[name]:
                for (s_, v_) in waits:
                    e.wait_ge(s_, v_)
                if fn is not None:
                    fn(e).then_inc(sem, inc)

        with nc.Block() as block:
            @block.tensor
            def _(e):
                replay("pe", e)

            @block.scalar
            def _(e):
                replay("act", e)

            @block.vector
            def _(e):
                replay("dve", e)

            @block.gpsimd
            def _(e):
                replay("pool", e)

            @block.sync
            def _(e):
                replay("sp", e)

    def mm(self, out, lhsT, rhs, start, stop, reads, writes):
        return self.op("pe", lambda e, a=(out, lhsT, rhs, start, stop): e.matmul(a[0], a[1], a[2], start=a[3], stop=a[4]), reads, writes)

    def tr(self, out, in_, ident, reads, writes):
        return self.op("pe", lambda e, a=(out, in_, ident): e.transpose(a[0], a[1], a[2]), reads, writes)

    def act(self, out, in_, func, reads, writes, bias=None, scale=None, eng="act"):
        kw = {}
        if bias is not None:
            kw["bias"] = bias
        if scale is not None:
            kw["scale"] = scale
        return self.op("act", lambda e, a=(out, in_, func, kw): e.activation(out=a[0], in_=a[1], func=a[2], **a[3]), reads, writes)

    def tt(self, eng, out, in0, in1, op, reads, writes):
        return self.op(eng, lambda e, a=(out, in0, in1, op): e.tensor_tensor(a[0], a[1], a[2], a[3]), reads, writes)

    def ts(self, eng, out, in0, s1, s2, op0, op1, reads, writes):
        if s2 is None:
            return self.op(eng, lambda e, a=(out, in0, s1, op0): e.tensor_scalar(a[0], a[1], a[2], None, a[3]), reads, writes)
        return self.op(eng, lambda e, a=(out, in0, s1, s2, op0, op1): e.tensor_scalar(a[0], a[1], a[2], a[3], a[4], a[5]), reads, writes)

    def stt(self, eng, out, in0, scalar, in1, op0, op1, reads, writes):
        return self.op(eng, lambda e, a=(out, in0, scalar, in1, op0, op1): e.scalar_tensor_tensor(a[0], a[1], a[2], a[3], a[4], a[5]), reads, writes)

    def copy(self, eng, out, in_, reads, writes):
        if eng == "act":
            return self.op("act", lambda e, a=(out, in_): e.copy(a[0], a[1]), reads, writes)
        return self.op(eng, lambda e, a=(out, in_): e.tensor_copy(a[0], a[1]), reads, writes)

    def memset(self, eng, ap, val, writes):
        return self.op(eng, lambda e, a=(ap, val): e.memset(a[0], a[1]), (), writes)

    def rsum(self, eng, out, in_, reads, writes):
        return self.op(eng, lambda e, a=(out, in_): e.reduce_sum(a[0], a[1], AX.X), reads, writes)

    def recip(self, out, in_, reads, writes):
        return self.op("dve", lambda e, a=(out, in_): e.reciprocal(a[0], a[1]), reads, writes)


class Ring:
    def __init__(self, bufs):
        self.bufs = bufs
        self.i = 0

    def next(self):
        b = self.bufs[self.i % len(self.bufs)]
        self.i += 1
        return b


def _dram(nc, name, shape, dt=F32, kind="ExternalInput"):
    return nc.dram_tensor(name, list(shape), dt, kind=kind).ap()


TT = 2176
NB = TT // 128
TTILES = [(0, 512), (512, 512), (1024, 512), (1536, 512), (2048, 128)]
NFC = DFF // 128

R_INPUTS = {
    "xT": (D, TT), "xtok": (TT, D), "oT": (1536, TT),
    "w_uv": (128, 16, 1024), "w_mg": (16, 128, 4, 16, 128), "w_br": (16, 128, 4, 4, 128), "w_mix": (128, 16, D),
    "sgw": (128, 4, 128), "sgb": (128, 4), "sg_g": (128, 512), "sg_b": (128, 512), "trimask": (128, 128),
    "ln1_g": (128, D), "ln1_b": (128, D), "ln2_g": (128, D), "ln2_b": (128, D), "ln3_g": (128, D), "ln3_b": (128, D),
    "mem": (256, D), "memln_g": (128, D), "memln_b": (128, D),
    "wq": (128, 16, 512), "wk": (128, 16, 512), "wv": (128, 16, 512), "wo": (128, 4, D),
    "ffn_in": (NFC, 128, 2, 16, 128), "ffn_out": (128, NFC, D), "convw": (128, NFC, 4),
    "ident": (128, 128), "flag": (128, 1), "ones": (128, 128),
}


def build_R():
    nc = bass.Bass("TRN2", target_bir_lowering=False)
    I = {k: _dram(nc, k, v) for k, v in R_INPUTS.items()}
    xout = _dram(nc, "xout", (2048, D), kind="ExternalOutput")
    odT_d = _dram(nc, "odT_d", (4, 128, TT), BF16, kind="Internal")
    preT_d = _dram(nc, "preT_d", (16, 128, TT), BF16, kind="Internal")
    x1_d = _dram(nc, "x1_d", (TT, D), F32, kind="Internal")
    x1T_d = _dram(nc, "x1T_d", (16, 128, TT), BF16, kind="Internal")
    aT_d = _dram(nc, "aT_d", (4, 128, TT), BF16, kind="Internal")
    x2_d = _dram(nc, "x2_d", (TT, D), F32, kind="Internal")
    x2T_d = _dram(nc, "x2T_d", (16, 128, TT), BF16, kind="Internal")
    act_d = _dram(nc, "act_d", (NFC, 128, TT), BF16, kind="Internal")
    ffo_d = _dram(nc, "ffo_d", (TT, D), F32, kind="Internal")
    with contextlib.ExitStack() as es:
        s = Sched(nc, es)
        B = s.buf
        ident = B("ident", [128, 128], BF16)
        ones = B("ones", [128, 128], BF16)
        flag = B("flag", [128, 1], F32)
        kT = B("kT", [128, 4, 256], BF16)
        vtok = B("vtok", [128, 2, 512], BF16)
        ps = Ring([B("ps%d" % i, [128, 512], F32, psum=True) for i in range(6)])
        pst = Ring([B("pst%d" % i, [128, 8, 128], BF16, psum=True) for i in range(2)])
        s.dma("pool", ident[:], I["ident"][:, :], writes=[ident])
        s.dma("pool", ones[:], I["ones"][:, :], writes=[ones])
        s.dma("sp", flag[:], I["flag"][:, :], writes=[flag])

        class LN:
            def __init__(self, gname, bname):
                self.g = B("lng", [128, D], F32)
                self.b = B("lnb", [128, D], F32)
                self.sq = B("lnsq", [128, D], F32)
                self.zt = Ring([B("zt%d" % i, [128, D], F32) for i in range(3)])
                self.zb = Ring([B("zb%d" % i, [128, D], BF16) for i in range(3)])
                self.xr = Ring([B("xr%d" % i, [128, D], F32) for i in range(5)])
                self.st = Ring([B("st%d" % i, [128, 8], F32) for i in range(6)])
                self.ob = Ring([B("lnob%d" % i, [128, 16, 128], BF16) for i in range(3)])
                s.dma("sp", self.g[:], I[gname][:, :], writes=[self.g])
                s.dma("sp", self.b[:], I[bname][:, :], writes=[self.b])

            def norm_gen(self, z, out):
                t = self.st.next()
                sq = self.sq
                s.op("act", lambda e, a=(sq, z, t): e.activation(out=a[0][:], in_=a[1][:], func=AF.Identity, accum_out=a[2][:, 0:1]), [z], [sq, t])
                s.op("act", lambda e, a=(sq, z, t): e.activation(out=a[0][:], in_=a[1][:], func=AF.Square, accum_out=a[2][:, 1:2]), [z], [sq, t])
                s.ts("dve", t[:, 2:3], t[:, 0:1], 1.0 / D, None, ALU.mult, None, [t], [t])
                s.ts("dve", t[:, 3:4], t[:, 1:2], 1.0 / D, None, ALU.mult, None, [t], [t])
                s.stt("dve", t[:, 4:5], t[:, 2:3], -1.0, t[:, 2:3], ALU.mult, ALU.mult, [t], [t])
                s.tt("dve", t[:, 5:6], t[:, 3:4], t[:, 4:5], ALU.add, [t], [t])
                s.act(t[:, 6:7], t[:, 5:6], AF.Sqrt, [t], [t], bias=EPS)
                s.recip(t[:, 7:8], t[:, 6:7], [t], [t])
                yield
                s.stt("dve", out[:], z[:], t[:, 2:3], self.g[:], ALU.subtract, ALU.mult, [z, t, self.g], [out])
                s.stt("dve", out[:], out[:], t[:, 7:8], self.b[:], ALU.mult, ALU.add, [out, t, self.b], [out])

            def norm(self, z, out):
                for _ in self.norm_gen(z, out):
                    pass

            def to_featmajor(self, xf, dst_d, dst_db, tb, width=16):
                xb = self.zb.next()
                s.copy("act", xb[:, 0:width * 128], xf[:, 0:width * 128], [xf], [xb])
                ob = self.ob.next()
                for h in range((width + 7) // 8):
                    p = pst.next()
                    nj = min(8, width - h * 8)
                    for j in range(nj):
                        kc = h * 8 + j
                        s.tr(p[:, j, :], xb[:, kc * 128:(kc + 1) * 128], ident[:], [xb, ident], [p])
                    s.copy("dve", ob[:, h * 8:h * 8 + nj, :], p[:, 0:nj, :], [p], [ob])
                s.dma("sp", dst_d[0:width, :, tb * 128:(tb + 1) * 128].rearrange("c p t -> p c t"), ob[:, 0:width, :], reads=[ob], writes=[dst_db[tb]])

            def proj_ln_gen(self, tb, lhs_fn, nk, w, xres_ap, xres_db, out_d, out_db, outT_d, outT_db, final_out=None):
                z = self.zt.next()
                xres = self.xr.next()
                s.dma("sp", xres[:], xres_ap, reads=list(xres_db), writes=[xres])
                for dt_ in range(4):
                    p = ps.next()
                    for k in range(nk):
                        lt, lbuf = lhs_fn(k)
                        s.mm(p[:], lt, w[:, k, dt_ * 512:(dt_ + 1) * 512], k == 0, k == nk - 1, [lbuf, w], [p])
                    s.stt("dve", z[:, dt_ * 512:(dt_ + 1) * 512], xres[:, dt_ * 512:(dt_ + 1) * 512], ALPHA, p[:], ALU.mult, ALU.add, [xres, p], [z])
                yield
                o = self.xr.next()
                for _ in self.norm_gen(z, o):
                    yield
                yield
                if final_out is not None:
                    if tb >= 1:
                        s.dma("sp", final_out[(tb - 1) * 128:tb * 128, :], o[:], reads=[o])
                else:
                    s.dma("sp", out_d[tb * 128:(tb + 1) * 128, :], o[:], reads=[o], writes=[out_db[tb]])
                    self.to_featmajor(o, outT_d, outT_db, tb)

        def run_pipelined(make_gen, n, depth=2):
            active = []
            nxt = 0
            while nxt < n or active:
                if nxt < n and len(active) < depth:
                    active.append(make_gen(nxt))
                    nxt += 1
                for g_ in list(active):
                    try:
                        next(g_)
                    except StopIteration:
                        active.remove(g_)

        def DB(n):
            return [Buf(None) for _ in range(n)]

        odT_db, preT_db, x1_db, x1T_db, aT_db, x2_db, x2T_db, act_db, ffo_db = DB(NB), DB(16), DB(NB), DB(NB), DB(4), DB(NB), DB(NB), DB(NFC), DB(NB * 4)
        base = s.mark()

        ln = LN("memln_g", "memln_b")
        memT_d = _dram(nc, "memT_d", (16, 128, 256), BF16, kind="Internal")
        memT_db = DB(2)
        for mb in range(2):
            z = ln.zt.next()
            o = ln.xr.next()
            s.dma("sp", z[:], I["mem"][mb * 128:(mb + 1) * 128, :], writes=[z])
            ln.norm(z, o)
            ln.to_featmajor(o, memT_d, memT_db, mb)
        memT = B("memT", [128, 16, 256], BF16)
        wk = B("wk", [128, 16, 512], BF16)
        wv = B("wv", [128, 16, 512], BF16)
        s.dma("sp", memT[:], memT_d[:, :, :].rearrange("c p t -> p c t"), reads=memT_db, writes=[memT])
        s.dma("pool", wk[:], I["wk"][:, :, :], writes=[wk])
        s.dma("pool", wv[:], I["wv"][:, :, :], writes=[wv])
        for h in range(4):
            p = ps.next()
            for kc in range(16):
                s.mm(p[:, 0:256], wk[:, kc, h * 128:(h + 1) * 128], memT[:, kc, :], kc == 0, kc == 15, [wk, memT], [p])
            s.copy("dve", kT[:, h, :], p[:, 0:256], [p], [kT])
        for mb in range(2):
            p = ps.next()
            for kc in range(16):
                s.mm(p[:], memT[:, kc, mb * 128:(mb + 1) * 128], wv[:, kc, :], kc == 0, kc == 15, [wv, memT], [p])
            s.copy("act", vtok[:, mb, :], p[:], [p], [vtok])
        s.release(base)

        xT = B("xT", [128, 16, TT], BF16)
        s.dma("pool", xT[:, 0:8, :], I["xT"][0:1024, :].rearrange("(kc p) t -> p kc t", p=128), writes=[xT])
        s.dma("pool", xT[:, 8:16, :], I["xT"][1024:2048, :].rearrange("(kc p) t -> p kc t", p=128), writes=[xT])
        w_uv = B("w_uv", [128, 16, 1024], BF16)
        sgw = B("sgw", [128, 4, 128], F32)
        sgwb = B("sgwb", [128, 4, 128], BF16)
        tri = B("tri", [128, 128], F32)
        sgb = B("sgb", [128, 4], F32)
        sg_g = B("sg_g", [128, 512], F32)
        sg_b = B("sg_b", [128, 512], F32)
        s.dma("pool", w_uv[:, 0:8, :], I["w_uv"][:, 0:8, :], writes=[w_uv])
        s.dma("pool", w_uv[:, 8:16, :], I["w_uv"][:, 8:16, :], writes=[w_uv])
        s.dma("sp", sgw[:], I["sgw"][:, :, :], writes=[sgw])
        s.dma("sp", tri[:], I["trimask"][:, :], writes=[tri])
        s.dma("sp", sgb[:], I["sgb"][:, :], writes=[sgb])
        s.dma("sp", sg_g[:], I["sg_g"][:, :], writes=[sg_g])
        s.dma("sp", sg_b[:], I["sg_b"][:, :], writes=[sg_b])
        for g in range(4):
            s.tt("dve", sgwb[:, g, :], sgw[:, g, :], tri[:], ALU.mult, [sgw, tri], [sgwb])
        uvt = Ring([B("uvt%d" % i, [128, 512], F32) for i in range(12)])
        gt = Ring([B("gt%d" % i, [128, 512], F32) for i in range(6)])
        vlb = Ring([B("vlb%d" % i, [128, 512], BF16) for i in range(3)])
        odt = Ring([B("odt%d" % i, [128, 512], BF16) for i in range(3)])
        odo = Ring([B("odo%d" % i, [128, 4, 128], BF16) for i in range(3)])
        stA = Ring([B("stA%d" % i, [128, 8], F32) for i in range(6)])

        def gelu(src_ps, dst):
            x = uvt.next()
            t1 = uvt.next()
            s.copy("act", x[:], src_ps[:], [src_ps], [x])
            s.tt("dve", t1[:], x[:], x[:], ALU.mult, [x], [t1])
            s.ts("dve", t1[:], t1[:], 0.044715, 1.0, ALU.mult, ALU.add, [t1], [t1])
            s.tt("dve", t1[:], t1[:], x[:], ALU.mult, [t1, x], [t1])
            s.act(t1[:], t1[:], AF.Sigmoid, [t1], [t1], scale=1.5957691216057308)
            s.tt("pool", dst[:], t1[:], x[:], ALU.mult, [t1, x], [dst])

        def genA(tb):
            pu = ps.next()
            pv = ps.next()
            for (p, c0) in ((pu, 0), (pv, 512)):
                for kc in range(16):
                    s.mm(p[:], xT[:, kc, tb * 128:(tb + 1) * 128], w_uv[:, kc, c0:c0 + 512], kc == 0, kc == 15, [xT, w_uv], [p])
            gu = gt.next()
            gv = gt.next()
            gelu(pu, gu)
            gelu(pv, gv)
            yield
            t = stA.next()
            scr = uvt.next()
            s.op("act", lambda e, a=(scr, gv, t): e.activation(out=a[0][:], in_=a[1][:], func=AF.Identity, accum_out=a[2][:, 0:1]), [gv], [scr, t])
            s.op("act", lambda e, a=(scr, gv, t): e.activation(out=a[0][:], in_=a[1][:], func=AF.Square, accum_out=a[2][:, 1:2]), [gv], [scr, t])
            s.ts("dve", t[:, 2:3], t[:, 0:1], 1.0 / 512, None, ALU.mult, None, [t], [t])
            s.ts("dve", t[:, 3:4], t[:, 1:2], 1.0 / 512, None, ALU.mult, None, [t], [t])
            s.stt("dve", t[:, 4:5], t[:, 2:3], -1.0, t[:, 2:3], ALU.mult, ALU.mult, [t], [t])
            s.tt("dve", t[:, 5:6], t[:, 3:4], t[:, 4:5], ALU.add, [t], [t])
            s.act(t[:, 6:7], t[:, 5:6], AF.Sqrt, [t], [t], bias=EPS)
            s.recip(t[:, 7:8], t[:, 6:7], [t], [t])
            yield
            s.stt("dve", gv[:], gv[:], t[:, 2:3], sg_g[:], ALU.subtract, ALU.mult, [gv, t, sg_g], [gv])
            vb = vlb.next()
            s.stt("dve", vb[:], gv[:], t[:, 7:8], sg_b[:], ALU.mult, ALU.add, [gv, t, sg_b], [vb])
            yield
            pg = ps.next()
            for g in range(4):
                s.mm(pg[:, g * 128:(g + 1) * 128], sgwb[:, g, :], vb[:, g * 128:(g + 1) * 128], True, True, [sgwb, vb], [pg])
            od = odt.next()
            for g in range(4):
                s.stt("dve", od[:, g * 128:(g + 1) * 128], pg[:, g * 128:(g + 1) * 128], sgb[:, g:g + 1], gu[:, g * 128:(g + 1) * 128],
                      ALU.add, ALU.mult, [pg, sgb, gu], [od])
            yield
            p = pst.next()
            for j_ in range(4):
                s.tr(p[:, j_, :], od[:, j_ * 128:(j_ + 1) * 128], ident[:], [od, ident], [p])
            oo = odo.next()
            s.copy("act", oo[:], p[:, 0:4, :], [p], [oo])
            s.dma("sp", odT_d[:, :, tb * 128:(tb + 1) * 128].rearrange("c p t -> p c t"), oo[:], reads=[oo], writes=[odT_db[tb]])
        run_pipelined(genA, NB)
        mB = s.mark()
        s.barrier()
        s.off = xT_end = base + 16 * TT * 2
        assert xT_end % 64 == 0

        oT = B("oT", [128, 16, TT], BF16)
        s.dma("pool", oT[:, 0:6, :], I["oT"][0:768, :].rearrange("(kc p) t -> p kc t", p=128), writes=[oT])
        s.dma("pool", oT[:, 6:12, :], I["oT"][768:1536, :].rearrange("(kc p) t -> p kc t", p=128), writes=[oT])
        s.dma("sp", oT[:, 12:16, :], odT_d[:, :, :].rearrange("c p t -> p c t"), reads=odT_db, writes=[oT])
        wmg = Ring([B("wmg%d" % i, [128, 4, 16, 128], BF16) for i in range(2)])
        wbr = Ring([B("wbr%d" % i, [128, 4, 4, 128], BF16) for i in range(2)])
        gsb = Ring([B("gsb%d" % i, [128, 512], F32) for i in range(3)])
        acc = Ring([B("acc%d" % i, [128, 512], F32) for i in range(2)])
        preo = Ring([B("preo%d" % i, [128, TT], BF16) for i in range(2)])
        wq_ = []
        for dc in range(min(1, 16)):
            wm = wmg.next()
            wb = wbr.next()
            s.dma("pool", wm[:], I["w_mg"][dc, :, :, :, :], writes=[wm])
            s.dma("pool", wb[:], I["w_br"][dc, :, :, :, :], writes=[wb])
            wq_.append((wm, wb))
        for dc in range(16):
            if dc + 1 < 16:
                wm = wmg.next()
                wb = wbr.next()
                s.dma("pool", wm[:], I["w_mg"][dc + 1, :, :, :, :], writes=[wm])
                s.dma("pool", wb[:], I["w_br"][dc + 1, :, :, :, :], writes=[wb])
                wq_.append((wm, wb))
            wm, wb = wq_[dc]
            po = preo.next()
            for (t0, tw) in TTILES:
                a = acc.next()
                for n in range(4):
                    pm = ps.next()
                    py = ps.next()
                    for kc in range(16):
                        s.mm(pm[:, 0:tw], wm[:, n, kc, :], xT[:, kc, t0:t0 + tw], kc == 0, kc == 15, [wm, xT], [pm])
                    for k4 in range(4):
                        s.mm(py[:, 0:tw], wb[:, n, k4, :], oT[:, n * 4 + k4, t0:t0 + tw], k4 == 0, k4 == 3, [wb, oT], [py])
                    gs = gsb.next()
                    s.act(gs[:, 0:tw], pm[:, 0:tw], AF.Sigmoid, [pm], [gs])
                    if n == 0:
                        s.tt("dve", a[:, 0:tw], gs[:, 0:tw], py[:, 0:tw], ALU.mult, [gs, py], [a])
                    else:
                        s.tt("dve", gs[:, 0:tw], gs[:, 0:tw], py[:, 0:tw], ALU.mult, [gs, py], [gs])
                        if n < 3:
                            s.tt("pool", a[:, 0:tw], a[:, 0:tw], gs[:, 0:tw], ALU.add, [a, gs], [a])
                        else:
                            s.tt("pool", po[:, t0:t0 + tw], a[:, 0:tw], gs[:, 0:tw], ALU.add, [a, gs], [po])
            s.dma("sp", preT_d[dc, :, :], po[:], reads=[po], writes=[preT_db[dc]])
        s.release(base)

        ln = LN("ln1_g", "ln1_b")
        wmix = B("wmix", [128, 16, D], BF16)
        s.dma("pool", wmix[:, 0:8, :], I["w_mix"][:, 0:8, :], writes=[wmix])
        s.dma("pool", wmix[:, 8:16, :], I["w_mix"][:, 8:16, :], writes=[wmix])
        prb = Ring([B("prb%d" % i, [128, 16, 128], BF16) for i in range(3)])

        def genC(tb):
            pb = prb.next()
            s.dma("sp", pb[:], preT_d[:, :, tb * 128:(tb + 1) * 128].rearrange("c p t -> p c t"), reads=preT_db, writes=[pb])
            yield from ln.proj_ln_gen(tb, lambda k, pb=pb: (pb[:, k, :], pb), 16, wmix, I["xtok"][tb * 128:(tb + 1) * 128, :], (), x1_d, x1_db, x1T_d, x1T_db)
        run_pipelined(genC, NB)
        s.release(base)

        x1T = B("x1T", [128, 16, TT], BF16)
        s.dma("sp", x1T[:], x1T_d[:, :, :].rearrange("c p t -> p c t"), reads=x1T_db, writes=[x1T])
        wq = B("wq", [128, 16, 512], BF16)
        s.dma("pool", wq[:], I["wq"][:, :, :], writes=[wq])
        qTb = Ring([B("qTb%d" % i, [128, 512], BF16) for i in range(2)])
        pTb = Ring([B("pTb%d" % i, [128, 512], BF16) for i in range(4)])
        rdb = Ring([B("rdb%d" % i, [128, 512], F32) for i in range(2)])
        aTo = Ring([B("aTo%d" % i, [128, TT], BF16) for i in range(2)])
        for h in range(4):
            ao = aTo.next()
            for (t0, tw) in TTILES:
                p = ps.next()
                for kc in range(16):
                    s.mm(p[:, 0:tw], wq[:, kc, h * 128:(h + 1) * 128], x1T[:, kc, t0:t0 + tw], kc == 0, kc == 15, [wq, x1T], [p])
                q = qTb.next()
                s.act(q[:, 0:tw], p[:, 0:tw], AF.Identity, [p], [q], scale=128 ** -0.5)
                pts = []
                for mb in range(2):
                    p2 = ps.next()
                    s.mm(p2[:, 0:tw], kT[:, h, mb * 128:(mb + 1) * 128], q[:, 0:tw], True, True, [kT, q], [p2])
                    pt = pTb.next()
                    s.act(pt[:, 0:tw], p2[:, 0:tw], AF.Exp, [p2], [pt])
                    pts.append(pt)
                po_ = ps.next()
                pd = ps.next()
                for mb in range(2):
                    s.mm(po_[:, 0:tw], vtok[:, mb, h * 128:(h + 1) * 128], pts[mb][:, 0:tw], mb == 0, mb == 1, [vtok, pts[mb]], [po_])
                for mb in range(2):
                    s.mm(pd[:, 0:tw], ones[:], pts[mb][:, 0:tw], mb == 0, mb == 1, [ones, pts[mb]], [pd])
                rd = rdb.next()
                s.recip(rd[:, 0:tw], pd[:, 0:tw], [pd], [rd])
                s.tt("dve", ao[:, t0:t0 + tw], po_[:, 0:tw], rd[:, 0:tw], ALU.mult, [po_, rd], [ao])
            s.dma("sp", aT_d[h, :, :], ao[:], reads=[ao], writes=[aT_db[h]])
        s.release(base)

        ln = LN("ln2_g", "ln2_b")
        wo = B("wo", [128, 4, D], BF16)
        s.dma("pool", wo[:], I["wo"][:, :, :], writes=[wo])
        aTr = Ring([B("aTr%d" % i, [128, 4, 128], BF16) for i in range(3)])

        def genD2(tb):
            ab = aTr.next()
            s.dma("sp", ab[:], aT_d[:, :, tb * 128:(tb + 1) * 128].rearrange("c p t -> p c t"), reads=aT_db, writes=[ab])
            yield from ln.proj_ln_gen(tb, lambda k, ab=ab: (ab[:, k, :], ab), 4, wo, x1_d[tb * 128:(tb + 1) * 128, :], [x1_db[tb]], x2_d, x2_db, x2T_d, x2T_db)
        run_pipelined(genD2, NB)
        s.release(base)

        x2T = B("x2T", [128, 16, TT], BF16)
        s.dma("sp", x2T[:], x2T_d[:, :, :].rearrange("c p t -> p c t"), reads=x2T_db, writes=[x2T])
        cw = B("cw", [128, NFC, 4], F32)
        s.dma("sp", cw[:], I["convw"][:, :, :], writes=[cw])
        wf = Ring([B("wf%d" % i, [128, 2, 16, 128], BF16) for i in range(3)])
        gbuf = Ring([B("gbuf%d" % i, [128, TT + 2], F32) for i in range(2)])
        ubuf = Ring([B("ubuf%d" % i, [128, TT], F32) for i in range(2)])
        cbuf = Ring([B("cbuf%d" % i, [128, TT], F32) for i in range(2)])
        sbuf_ = Ring([B("sbuf%d" % i, [128, TT], F32) for i in range(2)])
        abuf = Ring([B("abuf%d" % i, [128, TT], BF16) for i in range(2)])
        for g_ in gbuf.bufs:
            s.memset("dve", g_[:, 0:2], 0.0, [g_])
        wfq = []
        for fc in range(2):
            w = wf.next()
            s.dma("pool", w[:], I["ffn_in"][fc, :, :, :, :], writes=[w])
            wfq.append(w)
        for fc in range(NFC):
            if fc + 2 < NFC:
                w = wf.next()
                s.dma("pool", w[:], I["ffn_in"][fc + 2, :, :, :, :], writes=[w])
                wfq.append(w)
            w = wfq[fc]
            gb = gbuf.next()
            ub = ubuf.next()
            for (t0, tw) in TTILES:
                pg = ps.next()
                pu = ps.next()
                for kc in range(16):
                    s.mm(pg[:, 0:tw], w[:, 0, kc, :], x2T[:, kc, t0:t0 + tw], kc == 0, kc == 15, [w, x2T], [pg])
                for kc in range(16):
                    s.mm(pu[:, 0:tw], w[:, 1, kc, :], x2T[:, kc, t0:t0 + tw], kc == 0, kc == 15, [w, x2T], [pu])
                if t0 == 0:
                    s.op("act", lambda e, a=(gb, pg, flag): e.activation(out=a[0][:, 2:130], in_=a[1][:, 0:128], func=AF.Copy, scale=a[2][:, 0:1]), [pg, flag], [gb])
                    s.copy("act", gb[:, 130:2 + tw], pg[:, 128:tw], [pg], [gb])
                else:
                    s.copy("act", gb[:, 2 + t0:2 + t0 + tw], pg[:, 0:tw], [pg], [gb])
                s.copy("dve", ub[:, t0:t0 + tw], pu[:, 0:tw], [pu], [ub])
            cb = cbuf.next()
            s.ts("dve", cb[:], gb[:, 2:TT + 2], cw[:, fc, 2:3], cw[:, fc, 3:4], ALU.mult, ALU.add, [gb, cw], [cb])
            s.stt("dve", cb[:], gb[:, 1:TT + 1], cw[:, fc, 1:2], cb[:], ALU.mult, ALU.add, [gb, cw, cb], [cb])
            s.stt("dve", cb[:], gb[:, 0:TT], cw[:, fc, 0:1], cb[:], ALU.mult, ALU.add, [gb, cw, cb], [cb])
            sb = sbuf_.next()
            s.act(sb[:], cb[:], AF.Silu, [cb], [sb])
            ab = abuf.next()
            s.tt("pool", ab[:], sb[:], ub[:], ALU.mult, [sb, ub], [ab])
            s.dma("sp", act_d[fc, :, :], ab[:], reads=[ab], writes=[act_db[fc]])
        s.release(base)

        wfo = Ring([B("wfo%d" % i, [128, NFC, 512], BF16) for i in range(2)])
        acb = Ring([B("acb%d" % i, [128, NFC, 128], BF16) for i in range(4)])
        fob = Ring([B("fob%d" % i, [128, 512], F32) for i in range(3)])
        for dt_ in range(4):
            w = wfo.next()
            for f0 in range(0, NFC, 11):
                s.dma("pool", w[:, f0:f0 + 11, :], I["ffn_out"][:, f0:f0 + 11, dt_ * 512:(dt_ + 1) * 512], writes=[w])
            for tb in range(NB):
                step = dt_ * NB + tb
                if step == 0:
                    abq = []
                    for st_ in range(3):
                        ab = acb.next()
                        tb_ = st_ % NB
                        s.dma("sp", ab[:], act_d[:, :, tb_ * 128:(tb_ + 1) * 128].rearrange("c p t -> p c t"), reads=act_db, writes=[ab])
                        abq.append(ab)
                if step + 3 < 4 * NB:
                    ab = acb.next()
                    tb_ = (step + 3) % NB
                    s.dma("sp", ab[:], act_d[:, :, tb_ * 128:(tb_ + 1) * 128].rearrange("c p t -> p c t"), reads=act_db, writes=[ab])
                    abq.append(ab)
                ab = abq[step]
                p = ps.next()
                for fc in range(NFC):
                    s.mm(p[:], ab[:, fc, :], w[:, fc, :], fc == 0, fc == NFC - 1, [ab, w], [p])
                fo = fob.next()
                if tb % 2 == 0:
                    s.copy("act", fo[:], p[:], [p], [fo])
                else:
                    s.copy("dve", fo[:], p[:], [p], [fo])
                s.dma("sp", ffo_d[tb * 128:(tb + 1) * 128, dt_ * 512:(dt_ + 1) * 512], fo[:], reads=[fo], writes=[ffo_db[tb * 4 + dt_]])
        s.release(base)

        ln = LN("ln3_g", "ln3_b")
        def genF2(i):
            tb = i + 1
            z = ln.zt.next()
            xres = ln.xr.next()
            ff = ln.xr.next()
            s.dma("sp", xres[:], x2_d[tb * 128:(tb + 1) * 128, :], reads=[x2_db[tb]], writes=[xres])
            s.dma("sp", ff[:], ffo_d[tb * 128:(tb + 1) * 128, :], reads=ffo_db[tb * 4:tb * 4 + 4], writes=[ff])
            s.stt("dve", z[:], xres[:], ALPHA, ff[:], ALU.mult, ALU.add, [xres, ff], [z])
            yield
            for _ in ln.norm_gen(z, z):
                yield
            yield
            s.dma("sp", xout[(tb - 1) * 128:tb * 128, :], z[:], reads=[z])
        run_pipelined(genF2, NB - 1)
        s.emit()
        print("R program: ops", s.n, "sbuf peak", s.peak, flush=True)
    return nc


OFF_Q, OFF_KV, OFF_G, OFF_H, OFF_P, OFF_SG, OFF_MG = 0, 512, 1280, 1304, 3352, 3864, 4888


def _bc(v, n=128):
    return np.ascontiguousarray(np.broadcast_to(np.asarray(v, np.float32)[None, :], (n, v.shape[0])))


def _c(a):
    return np.ascontiguousarray(a, dtype=np.float32)


def prep_R_shared(inp, l):
    w_in = inp["w_in"][l]
    sh = {}
    sh["w_uv"] = _c(w_in[:, OFF_SG:OFF_SG + 1024].reshape(16, 128, 1024).transpose(1, 0, 2))
    mg = w_in[:, OFF_MG:OFF_MG + 8192].reshape(16, 128, 4, 16, 128)
    sh["w_mg"] = _c(mg.transpose(3, 1, 2, 0, 4))
    br = inp["w_branch"][l].reshape(4, 4, 128, 16, 128)
    sh["w_br"] = _c(br.transpose(3, 2, 0, 1, 4))
    sh["w_mix"] = _c(inp["w_mix_out"][l].reshape(16, 128, D).transpose(1, 0, 2))
    sh["sgw"] = _c(inp["sg_w"][l].transpose(2, 0, 1))
    sh["sgb"] = _c(inp["sg_b"][l].T)
    sh["sg_g"] = _bc(inp["sg_ln_g"][l])
    sh["sg_b"] = _bc(inp["sg_ln_b"][l])
    sh["trimask"] = _c(np.triu(np.ones((128, 128), np.float32)))
    for i, nm in ((1, "ln_mix"), (2, "ln_x"), (3, "ln_ffn")):
        sh["ln%d_g" % i] = _bc(inp[nm + "_g"][l])
        sh["ln%d_b" % i] = _bc(inp[nm + "_b"][l])
    sh["memln_g"] = _bc(inp["mem_ln_g"])
    sh["memln_b"] = _bc(inp["mem_ln_b"])
    for nm, k in (("wq", "xattn_q"), ("wk", "xattn_k"), ("wv", "xattn_v")):
        sh[nm] = _c(inp[k][l].reshape(16, 128, 512).transpose(1, 0, 2))
    sh["wo"] = _c(inp["xattn_o"][l].reshape(4, 128, D).transpose(1, 0, 2))
    fi = inp["ffn_in"][l].reshape(16, 128, 2, NFC, 128)
    sh["ffn_in"] = _c(fi.transpose(3, 1, 2, 0, 4))
    sh["ffn_out"] = _c(inp["ffn_out"][l].reshape(NFC, 128, D).transpose(1, 0, 2))
    cw = np.concatenate([inp["ffn_conv_w"][l], inp["ffn_conv_b"][l][None, :]], axis=0)
    sh["convw"] = _c(cw.reshape(4, NFC, 128).transpose(2, 1, 0))
    sh["ident"] = np.eye(128, dtype=np.float32)
    sh["ones"] = np.ones((128, 128), np.float32)
    return sh


def prep_R(inp, l, x, oabc):
    sh = prep_R_shared(inp, l)
    maps = []
    for c in range(NCORE):
        b, u = divmod(c, 4)
        t0 = 2048 * u
        m = dict(sh)
        xs = np.zeros((TT, D), np.float32)
        os_ = np.zeros((TT, 1536), np.float32)
        lo = t0 - 128
        if u == 0:
            xs[128:] = x[b, 0:2048]
            os_[128:] = oabc[b, 0:2048]
        else:
            xs[:] = x[b, lo:lo + TT]
            os_[:] = oabc[b, lo:lo + TT]
        m["xtok"] = xs
        m["xT"] = _c(xs.T)
        m["oT"] = _c(os_.T)
        m["mem"] = _c(inp["mem"][b])
        m["flag"] = np.full((128, 1), 0.0 if u == 0 else 1.0, np.float32)
        maps.append(m)
    return maps


_NC_CACHE = {}


def run_R(inp, l, x, oabc):
    if "R" not in _NC_CACHE:
        _NC_CACHE["R"] = build_R()
    maps = prep_R(inp, l, x, oabc)
    res = run_bass_kernel_spmd(_NC_CACHE["R"], maps, core_ids=list(range(NCORE)))
    out = np.zeros((2, S, D), np.float32)
    for c in range(NCORE):
        b, u = divmod(c, 4)
        out[b, 2048 * u:2048 * (u + 1)] = res.results[c]["xout"]
    return out


NQT = S // 512
M_INPUTS = {
    "xT": (D, S), "wfm": (128, 16, 768), "wtm": (128, 16, 520),
    "qpos": (4, 4, S), "kpos": (4, S), "cpos": (4, 512),
    "cmask": (5, 128, 512), "dmask": (8, 128, 512), "E_all": (128, 32, 128), "selmap": (128, 4, 129),
    "keep": (128, 256), "addm": (128, 256), "ident": (128, 128), "tri2": (128, 64),
    "cw_k": (64, 32, 64), "cw_v": (64, 32, 64), "cp_k": (64, 32), "cp_v": (64, 32),
    "lblog": (128, 2), "lbsel": (128, 1), "normg": (128, 128),
    "poolP": (128, 3, 128), "poolw": (128, 128), "poolscale": (128, 128),
}


def build_M(nqt=NQT, ST=(1, 2, 3, 4, 5, 6, 7, 8, 9, 10, 11)):
    nc = bass.Bass("TRN2", target_bir_lowering=False)
    I = {k: _dram(nc, k, v) for k, v in M_INPUTS.items()}
    om = _dram(nc, "om", (S, 384), kind="ExternalOutput")
    with contextlib.ExitStack() as es:
        s = Sched(nc, es)
        B = s.buf

        def PB(name, shape, dt):
            return B(name, shape, dt, psum=True)

        psr = Ring([PB("psr%d" % i, [128, 512], F32) for i in range(2)])
        pst = PB("pst", [128, 8, 128], BF16)
        oacc_ps = Ring([PB("oacc%d" % i, [128, 512], F32) for i in range(2)])
        impb = PB("impb", [128, 4, 128], F32)
        misc = PB("misc", [128, 512], F32)
        denb = misc
        hg = PB("hg", [128, 4, 128], F32)

        wfm = B("wfm", [128, 16, 768], BF16)
        wtm = B("wtm", [128, 16, 520], BF16)
        for kc in range(16):
            s.dma("pool", wfm[:, kc, :], I["wfm"][:, kc, :], writes=[wfm])
            s.dma("pool", wtm[:, kc, :], I["wtm"][:, kc, :], writes=[wtm])
        identb = B("identb", [128, 128], BF16)
        identf = B("identf", [128, 128], F32)
        s.dma("pool", identb[:], I["ident"][:, :], writes=[identb])
        s.dma("sp", identf[:], I["ident"][:, :], writes=[identf])
        cmask = B("cmask", [128, 5, 512], BF16)
        dmask = B("dmask", [128, 8, 512], BF16)
        for i in range(5):
            s.dma("pool", cmask[:, i, :], I["cmask"][i, :, :], writes=[cmask])
        for i in range(8):
            s.dma("pool", dmask[:, i, :], I["dmask"][i, :, :], writes=[dmask])
        E_all = B("E_all", [128, 32, 128], BF16)
        s.dma("pool", E_all[:], I["E_all"][:, :, :], writes=[E_all])
        selmap = B("selmap", [128, 4, 144], BF16)
        s.dma("pool", selmap[:, :, 0:129], I["selmap"][:, :, :], writes=[selmap])
        keep = B("keep", [128, 256], F32)
        addm = B("addm", [128, 256], F32)
        tri2 = B("tri2", [128, 64], F32)
        s.dma("sp", keep[:], I["keep"][:, :], writes=[keep])
        s.dma("sp", addm[:], I["addm"][:, :], writes=[addm])
        s.dma("sp", tri2[:], I["tri2"][:, :], writes=[tri2])
        cw_k = B("cw_k", [64, 32, 64], BF16)
        cw_v = B("cw_v", [64, 32, 64], BF16)
        cp_k = B("cp_k", [64, 32], BF16)
        cp_v = B("cp_v", [64, 32], BF16)
        s.dma("pool", cw_k[:], I["cw_k"][:, :, :], writes=[cw_k])
        s.dma("pool", cw_v[:], I["cw_v"][:, :, :], writes=[cw_v])
        s.dma("pool", cp_k[:], I["cp_k"][:, :], writes=[cp_k])
        s.dma("pool", cp_v[:], I["cp_v"][:, :], writes=[cp_v])
        normg = B("normg", [128, 128], F32)
        poolP = B("poolP", [128, 3, 128], F32)
        poolw = B("poolw", [128, 128], BF16)
        poolscale = B("poolscale", [128, 128], F32)
        lbl = B("lbl", [128, 8], F32)
        s.dma("sp", normg[:], I["normg"][:, :], writes=[normg])
        s.dma("sp", poolP[:], I["poolP"][:, :, :], writes=[poolP])
        s.dma("pool", poolw[:], I["poolw"][:, :], writes=[poolw])
        s.dma("sp", poolscale[:], I["poolscale"][:, :], writes=[poolscale])
        s.dma("sp", lbl[:, 0:2], I["lblog"][:, :], writes=[lbl])
        s.dma("sp", lbl[:, 2:3], I["lbsel"][:, :], writes=[lbl])
        s.tt("dve", lbl[:, 3:4], lbl[:, 1:2], lbl[:, 0:1], ALU.subtract, [lbl], [lbl])
        s.act(lbl[:, 4:5], lbl[:, 3:4], AF.Sigmoid, [lbl], [lbl])
        s.tt("dve", lbl[:, 5:6], lbl[:, 4:5], lbl[:, 2:3], ALU.mult, [lbl], [lbl])
        s.ts("dve", lbl[:, 6:7], lbl[:, 5:6], -1.0, 1.0, ALU.mult, ALU.add, [lbl], [lbl])

        kslc = B("kslc", [128, S], BF16)
        s.memset("dve", kslc[64:128, :], 0.0, [kslc])
        s.dma("pool", kslc[64:68, :], I["kpos"][:, :], writes=[kslc])
        kwin = B("kwin", [128, 8, 128], BF16)
        s.memset("dve", kwin[64:128, :, :], 0.0, [kwin])
        vslc = B("vslc", [128, 65, 72], BF16)
        vwin = B("vwin", [128, 10, 72], BF16)
        s.memset("dve", vslc[:], 0.0, [vslc])
        s.memset("dve", vwin[:], 0.0, [vwin])
        s.memset("dve", vslc[:, 0:64, 64:65], 1.0, [vslc])
        s.memset("dve", vwin[:, 0:8, 64:65], 1.0, [vwin])
        kcaug = B("kcaug", [128, 512], BF16)
        s.memset("dve", kcaug[:], 0.0, [kcaug])
        s.dma("pool", kcaug[64:68, :], I["cpos"][:, :], writes=[kcaug])
        vcaug = B("vcaug", [128, 6, 72], BF16)
        s.memset("dve", vcaug[:], 0.0, [vcaug])
        s.memset("dve", vcaug[:, 0:4, 64:65], 1.0, [vcaug])
        kcraw = B("kcraw", [64, 528], BF16)
        vcraw = B("vcraw", [64, 528], BF16)
        s.memset("dve", kcraw[:], 0.0, [kcraw])
        s.memset("dve", vcraw[:], 0.0, [vcraw])
        cbias = B("cbias", [64, 2], F32)
        for (cw_, cp_, col) in ((cw_k, cp_k, 0), (cw_v, cp_v, 1)):
            p = psr.next()
            for l_ in range(32):
                s.mm(p[0:64, 0:1], cw_[:, l_, :], cp_[:, l_:l_ + 1], l_ == 0, l_ == 31, [cw_, cp_], [p])
            s.copy("dve", cbias[:, col:col + 1], p[0:64, 0:1], [p], [cbias])

        state = B("state", [128, 128], F32)
        stbf = Ring([B("stbf%d" % i, [128, 128], BF16) for i in range(2)])
        s.memset("dve", state[:], 0.0, [state])
        st_cur = stbf.next()
        s.memset("dve", st_cur[:], 0.0, [st_cur])
        qbpad = B("qbpad", [128, 4, 2, 128], BF16)
        s.memset("pool", qbpad[:], 0.0, [qbpad])
        atpad = Ring([B("atpad%d" % i, [128, 128], BF16) for i in range(2)])
        for a_ in atpad.bufs:
            s.memset("pool", a_[:], 0.0, [a_])
        pczero = B("pczero", [128, 128], F32)
        s.memset("pool", pczero[:], 0.0, [pczero])

        xtile = Ring([B("xtile%d" % i, [128, 16, 512], BF16) for i in range(1)])
        qaug = [Ring([B("qaug%d_%d" % (j, i), [128, 512], BF16) for i in range(2)]) for j in range(4)]
        for r_ in qaug:
            for b_ in r_.bufs:
                s.memset("dve", b_[64:128, :], 0.0, [b_])
        hqT = B("hqT", [128, 512], F32)
        hzT = B("hzT", [128, 512], F32)
        gate = Ring([B("gate%d" % i, [128, 8], F32) for i in range(8)])
        pcb = Ring([B("pcb%d" % i, [128, 128], F32) for i in range(6)])
        vvb = Ring([B("vvb%d" % i, [128, 128], BF16) for i in range(4)])
        sgb = Ring([B("sgb%d" % i, [128, 128], F32) for i in range(4)])
        outb = Ring([B("outb%d" % i, [128, 384], F32) for i in range(4)])
        ET = [[B("ET%d_%d" % (j, c), [128, 512], BF16) for c in range(4)] for j in range(2)]
        PT = Ring([B("PT%d" % i, [128, 512], BF16) for i in range(3)])
        impacc = B("impacc", [128, 4, 128], F32)
        rden = Ring([B("rden%d" % i, [128, 4], F32) for i in range(4)])
        selw = Ring([B("selw%d" % i, [128, 128], F32) for i in range(3)])
        selb = Ring([B("selb%d" % i, [128, 128], BF16) for i in range(4)])
        m8 = Ring([B("m8_%d" % i, [128, 16], F32) for i in range(2)])
        mbT = Ring([B("mbT%d" % i, [128, 512], BF16) for i in range(2)])
        oTs = Ring([B("oTs%d" % i, [65, 512], F32) for i in range(2)])
        coef = Ring([B("coef%d" % i, [128, 4], F32) for i in range(4)])
        vcT = Ring([B("vcT%d" % i, [64, 32], BF16) for i in range(2)])
        vct = Ring([B("vct%d" % i, [32, 64], BF16) for i in range(2)])
        hw_ = [B("hw%d" % i, [128, 512], F32) for i in range(7)]
        hb_ = [B("hb%d" % i, [128, 512], BF16) for i in range(4)]
        hs_ = Ring([B("hs%d" % i, [128, 16], F32) for i in range(2)])
        kdb = Ring([B("kdb%d" % i, [128, 128], BF16) for i in range(2)])
        hst = Ring([B("hst%d" % i, [128, 4], F32) for i in range(4)])
        hsq = B("hsq", [128, 128], F32)
        hsq2 = B("hsq2", [128, 128], F32)
        ptb = Ring([B("ptb%d" % i, [128, 128], BF16) for i in range(2)])
        pc_prev = pczero
        trs = Ring([B("trs%d" % i, [128, 4, 65], F32) for i in range(1)])
        mbh = [Ring([B("mbh%d_%d" % (h, i), [128, 512], BF16) for i in range(2)]) for h in range(2)]
        for h_ in range(2):
            for b_ in mbh[h_].bufs:
                s.memset("dve", b_[:], 0.0, [b_])
        imps = Ring([B("imps%d" % i, [128, 4, 128], F32) for i in range(1)])
        asb = Ring([B("asb%d" % i, [128, 128], F32) for i in range(2)])
        stgA = Ring([B("stgA%d" % i, [128, 136], F32) for i in range(1)])
        stgB = Ring([B("stgB%d" % i, [128, 384], F32) for i in range(1)])

        vslc_f = vslc[:].rearrange("p a b -> p (a b)")
        vwin_f = vwin[:].rearrange("p a b -> p (a b)")
        vcaug_f = vcaug[:].rearrange("p a b -> p (a b)")
        JA = 0
        trf = Buf(misc.t)
        trv = misc.t[:, 128:388].rearrange("p (a b) -> p a b", a=4)
        hstate = {"cur": st_cur}

        def epilogue(oa, hl, br, gts, obs):
            o_sb = oTs.next()
            s.copy("act", o_sb[:], oa[0:65, :], [oa], [o_sb])
            for tb in range(4):
                s.tr(trv[:, tb, :], o_sb[:, tb * 128:(tb + 1) * 128], identf[0:65, 0:65], [o_sb, identf], [trf])
            tv = trs.next()
            s.copy("dve", tv[:], trv[:, :, :], [trf], [tv])
            cf = coef.next()
            s.ts("dve", cf[:], tv[:, :, 64], 1e-30, None, ALU.max, None, [tv], [cf])
            s.recip(cf[:], cf[:], [cf], [cf])
            for tb in range(4):
                s.tt("dve", cf[:, tb:tb + 1], cf[:, tb:tb + 1], gts[tb][:, hl * 3 + br:hl * 3 + br + 1], ALU.mult, [cf, gts[tb]], [cf])
            for tb in range(4):
                dst = obs[tb][:, hl * 64:(hl + 1) * 64]
                if br == 0:
                    s.ts("dve", dst, tv[:, tb, 0:64], cf[:, tb:tb + 1], None, ALU.mult, None, [tv, cf], [obs[tb]])
                else:
                    s.stt("dve", dst, tv[:, tb, 0:64], cf[:, tb:tb + 1], dst, ALU.mult, ALU.add, [tv, cf, obs[tb]], [obs[tb]])

        def hgrn_tile(qt, vvs, sgs, obs):
            W = hw_
            s.act(W[0][:], hzT[:], AF.Sigmoid, [hzT], [W[0]])
            s.ts("dve", W[0][:], W[0][:], lbl[:, 6:7], lbl[:, 5:6], ALU.mult, ALU.add, [W[0], lbl], [W[0]])
            s.ts("dve", W[0][:], W[0][:], 1e-6, None, ALU.max, None, [W[0]], [W[0]])
            s.ts("dve", W[1][:], W[0][:], -1.0, 1.0, ALU.mult, ALU.add, [W[0]], [W[1]])
            s.act(W[2][:], W[0][:], AF.Ln, [W[0]], [W[2]])
            yield
            src, dst = W[2], W[3]
            for sh in (1, 2, 4, 8, 16, 32):
                sv = src[:].rearrange("p (c t) -> p c t", t=64)
                dv = dst[:].rearrange("p (c t) -> p c t", t=64)
                s.copy("pool", dv[:, :, 0:sh], sv[:, :, 0:sh], [src], [dst])
                s.tt("dve", dv[:, :, sh:64], sv[:, :, sh:64], sv[:, :, 0:64 - sh], ALU.add, [src], [dst])
                src, dst = dst, src
                yield
            b = src
            bv = b[:].rearrange("p (c t) -> p c t", t=64)
            hs = hs_.next()
            nh = hs_.next()
            s.copy("dve", hs[:, 0:8], bv[:, :, 31], [b], [hs])
            s.copy("dve", hs[:, 8:16], bv[:, :, 63], [b], [hs])
            s.ts("dve", nh[:, 0:8], hs[:, 0:8], -1.0, None, ALU.mult, None, [hs], [nh])
            s.act(nh[:, 8:16], hs[:, 8:16], AF.Exp, [hs], [nh])
            yield
            for c in range(8):
                cs = slice(c * 64, (c + 1) * 64)
                s.act(W[3][:, cs], b[:, cs], AF.Exp, [b, nh], [W[3]], bias=nh[:, c:c + 1])
                s.act(W[4][:, cs], b[:, cs], AF.Exp, [b, hs], [W[4]], bias=hs[:, c:c + 1], scale=-1.0)
                s.act(W[5][:, cs], b[:, cs], AF.Exp, [b, hs], [W[5]], bias=hs[:, 8 + c:9 + c], scale=-1.0)
                yield
            s.act(W[6][:], b[:], AF.Exp, [b], [W[6]])
            s.tt("pool", hb_[0][:], hqT[:], W[3][:], ALU.mult, [hqT, W[3]], [hb_[0]])
            s.tt("dve", hb_[1][:], W[1][:], W[4][:], ALU.mult, [W[1], W[4]], [hb_[1]])
            s.tt("pool", hb_[2][:], W[1][:], W[5][:], ALU.mult, [W[1], W[5]], [hb_[2]])
            yield
            hq4 = hqT[:].rearrange("p (a c t) -> p a c t", a=4, c=2)
            eb4 = W[6][:].rearrange("p (a c t) -> p a c t", a=4, c=2)
            s.tt("dve", qbpad[:, :, 0, 0:64], hq4[:, :, 0, :], eb4[:, :, 0, :], ALU.mult, [hqT, W[6]], [qbpad])
            s.tt("pool", qbpad[:, :, 1, 64:128], hq4[:, :, 1, :], eb4[:, :, 1, :], ALU.mult, [hqT, W[6]], [qbpad])
            yield
            for tb in range(4):
                blk = slice(tb * 128, (tb + 1) * 128)
                s.mm(hg[:, 0, :], hb_[1][:, blk], hb_[0][:, blk], True, True, [hb_[1], hb_[0]], [hg])
                at = atpad.next()
                a_sb = asb.next()
                s.copy("dve", a_sb[:], hg[:, 0, :], [hg], [a_sb])
                s.tt("pool", at[0:64, 0:64], a_sb[0:64, 0:64], tri2[0:64, :], ALU.mult, [a_sb, tri2], [at])
                s.tt("pool", at[64:128, 64:128], a_sb[64:128, 64:128], tri2[64:128, :], ALU.mult, [a_sb, tri2], [at])
                yield
                s.tr(pst[:, 5, :], hb_[2][:, blk], identb[:], [hb_[2], identb], [pst])
                kb = kdb.next()
                s.copy("act", kb[:], pst[:, 5, :], [pst], [kb])
                yield
                st0 = hstate["cur"]
                s.mm(hg[:, 1, :], qbpad[:, tb, 0, :], st0[:], True, False, [qbpad, st0], [hg])
                s.mm(hg[:, 1, :], at[:], vvs[tb][:], False, True, [at, vvs[tb]], [hg])
                s.mm(hg[:, 2, :], kb[0:64, :], vvs[tb][0:64, :], True, True, [kb, vvs[tb]], [hg])
                yield
                s.stt("dve", state[:], state[:], nh[:, 8 + 2 * tb:9 + 2 * tb], hg[:, 2, :], ALU.mult, ALU.add, [state, nh, hg], [state])
                st1 = stbf.next()
                s.copy("act", st1[:], state[:], [state], [st1])
                yield
                oa_sb = hsq
                s.copy("dve", oa_sb[:], hg[:, 1, :], [hg], [oa_sb])
                s.mm(hg[:, 3, :], qbpad[:, tb, 1, :], st1[:], True, True, [qbpad, st1], [hg])
                s.tt("dve", oa_sb[:], oa_sb[:], hg[:, 3, :], ALU.add, [oa_sb, hg], [oa_sb])
                yield
                s.mm(hg[:, 2, :], kb[64:128, :], vvs[tb][64:128, :], True, True, [kb, vvs[tb]], [hg])
                s.stt("dve", state[:], state[:], nh[:, 9 + 2 * tb:10 + 2 * tb], hg[:, 2, :], ALU.mult, ALU.add, [state, nh, hg], [state])
                st2 = stbf.next()
                s.copy("act", st2[:], state[:], [state], [st2])
                yield
                hstate["cur"] = st2
                t_ = hst.next()
                s.op("act", lambda e, a=(hsq2, oa_sb, t_): e.activation(out=a[0][:], in_=a[1][:], func=AF.Square, accum_out=a[2][:, 0:1]), [oa_sb], [hsq2, t_])
                s.ts("dve", t_[:, 1:2], t_[:, 0:1], 1.0 / 128, EPS, ALU.mult, ALU.add, [t_], [t_])
                s.act(t_[:, 2:3], t_[:, 1:2], AF.Sqrt, [t_], [t_])
                s.recip(t_[:, 3:4], t_[:, 2:3], [t_], [t_])
                yield
                s.stt("dve", obs[tb][:, 128:256], oa_sb[:], t_[:, 3:4], normg[:], ALU.mult, ALU.mult, [oa_sb, t_, normg], [obs[tb]])
                s.tt("pool", obs[tb][:, 128:256], obs[tb][:, 128:256], sgs[tb][:], ALU.mult, [obs[tb], sgs[tb]], [obs[tb]])

        g1w, g2w = 136, 384

        xtk = [Buf(xtile.bufs[0].t) for _ in range(16)]

        def load_x(qt_):
            for kc in range(16):
                s.dma("pool", xtile.bufs[0][:, kc, :], I["xT"][kc * 128:(kc + 1) * 128, qt_ * 512:(qt_ + 1) * 512], writes=[xtk[kc]])

        for qt in range(nqt):
            q0 = qt * 512
            s.mute = 1 not in ST
            xt = xtile.next()
            if qt == 0:
                load_x(0)
            qa = [qaug[j].next() for j in range(4)]
            for j in range(4):
                s.dma("pool", qa[j][64:68, :], I["qpos"][j, :, q0:q0 + 512], writes=[qa[j]])
            slot0 = (4 * qt) % 8
            s.dma("pool", kwin[64:68, slot0:slot0 + 4, :], I["kpos"][:, q0:q0 + 512].rearrange("r (a b) -> r a b", a=4), writes=[kwin])
            if qt > 0:
                s.copy("dve", kcraw[:, 0:16], kcraw[:, 512:528], [kcraw], [kcraw])
                s.copy("dve", vcraw[:, 0:16], vcraw[:, 512:528], [vcraw], [vcraw])

            s.mute = 2 not in ST
            def fm(col0, M):
                p = psr.next()
                for kc in range(16):
                    s.mm(p[0:M, :], wfm[:, kc, col0:col0 + M], xt[:, kc, :], kc == 0, kc == 15, [wfm, xtk[kc]], [p])
                return p
            for j in range(4):
                p = fm(j * 64, 64)
                s.act(qa[j][0:64, :], p[0:64, :], AF.Identity, [p], [qa[j]], scale=0.125)
            p = fm(256, 64)
            s.copy("act", kcraw[:, 16:528], p[0:64, :], [p], [kcraw])
            p = fm(320, 64)
            s.copy("dve", kslc[0:64, q0:q0 + 512], p[0:64, :], [p], [kslc])
            p = fm(384, 64)
            s.copy("act", kwin[0:64, slot0:slot0 + 4, :], p[0:64, :].rearrange("p (a b) -> p a b", a=4), [p], [kwin])
            p = fm(448, 64)
            s.copy("dve", vcraw[:, 16:528], p[0:64, :], [p], [vcraw])
            p = fm(512, 128)
            s.copy("act", hqT[:], p[:], [p], [hqT])
            p = fm(640, 128)
            s.copy("dve", hzT[:], p[:], [p], [hzT])

            s.mute = 3 not in ST
            gts, pcs, vvs, sgs, obs = [], [], [], [], []
            for tb in range(4):
                kt = 4 * qt + tb
                p = psr.next()
                for kc in range(16):
                    s.mm(p[:, 0:g1w], xt[:, kc, tb * 128:(tb + 1) * 128], wtm[:, kc, 0:g1w], kc == 0, kc == 15, [wtm, xtk[kc]], [p])
                sA = stgA.next()
                s.copy("dve", sA[:, 0:g1w], p[:, 0:g1w], [p], [sA])
                s.copy("pool", vslc[:, kt, 0:64], sA[:, 0:64], [sA], [vslc])
                s.copy("pool", vwin[:, kt % 8, 0:64], sA[:, 64:128], [sA], [vwin])
                g_ = gate.next()
                s.act(g_[:], sA[:, 128:136], AF.Sigmoid, [sA], [g_])
                gts.append(g_)
                p = psr.next()
                for kc in range(16):
                    s.mm(p[:, 0:g2w], xt[:, kc, tb * 128:(tb + 1) * 128], wtm[:, kc, g1w:g1w + g2w], kc == 0, kc == 15, [wtm, xtk[kc]], [p])
                pc = pcb.next()
                vv = vvb.next()
                sg = sgb.next()
                sB = stgB.next()
                s.copy("act", sB[:, 0:g2w], p[:, 0:g2w], [p], [sB])
                s.copy("pool", pc[:], sB[:, 0:128], [sB], [pc])
                s.copy("dve", vv[:], sB[:, 128:256], [sB], [vv])
                s.act(sg[:], sB[:, 256:384], AF.Silu, [sB], [sg])
                pcs.append(pc)
                vvs.append(vv)
                sgs.append(sg)
                obs.append(outb.next())

            if qt + 1 < nqt and 1 in ST:
                s.mute = False
                load_x(qt + 1)
            s.mute = 4 not in ST
            c_lo = 32 * qt - 1 if qt > 0 else 0
            c_hi = 32 * qt + 30
            ncb = c_hi - c_lo + 1
            i0 = 0 if qt > 0 else 1
            kview = kcraw[:].rearrange("p (c s) -> p c s", s=16)
            vview = vcraw[:].rearrange("p (c s) -> p c s", s=16)
            p = psr.next()
            for l_ in range(32):
                s.mm(p[0:64, 0:ncb], cw_k[:, l_, :], kview[:, i0 + l_ // 16:i0 + l_ // 16 + ncb, l_ % 16], l_ == 0, l_ == 31, [cw_k, kcraw], [p])
            s.act(kcaug[0:64, c_lo:c_hi + 1], p[0:64, 0:ncb], AF.Identity, [p, cbias], [kcaug], bias=cbias[:, 0:1])
            p = psr.next()
            for l_ in range(32):
                s.mm(p[0:64, 0:ncb], cw_v[:, l_, :], vview[:, i0 + l_ // 16:i0 + l_ // 16 + ncb, l_ % 16], l_ == 0, l_ == 31, [cw_v, vcraw], [p])
            vT_ = vcT.next()
            s.act(vT_[:, 0:ncb], p[0:64, 0:ncb], AF.Identity, [p, cbias], [vT_], bias=cbias[:, 1:2])
            s.tr(pst[0:ncb, 0, 0:64], vT_[:, 0:ncb], identb[0:64, 0:64], [vT_, identb], [pst])
            vt_ = vct.next()
            s.copy("dve", vt_[0:ncb, :], pst[0:ncb, 0, 0:64], [pst], [vt_])
            c = c_lo
            while c <= c_hi:
                ct_ = c // 128
                n_ = min(c_hi + 1, (ct_ + 1) * 128) - c
                s.dma("sp", vcaug[c % 128:c % 128 + n_, ct_, 0:64], vt_[c - c_lo:c - c_lo + n_, :], reads=[vt_], writes=[vcaug])
                c += n_

            hgen = hgrn_tile(qt, vvs, sgs, obs) if 9 in ST else iter(())
            s.mute = 5 not in ST
            nct = (32 * qt + 30) // 128 + 1
            oc_ps = {}
            for j in range(4):
                mine = j in (JA, JA + 1)
                ets = []
                for ct_ in range(nct):
                    dl = qt - 4 * ct_
                    p = psr.next()
                    last = dl > 4
                    s.mm(p[:], kcaug[:, ct_ * 128:(ct_ + 1) * 128], qa[j][:, :], True, last, [kcaug, qa[j]], [p])
                    if not last:
                        s.mm(p[:], identb[:], cmask[:, dl, :], False, True, [identb, cmask], [p])
                    e_ = ET[j % 2][ct_]
                    s.act(e_[:], p[:], AF.Exp, [p], [e_])
                    ets.append(e_)
                if mine:
                    oa = oacc_ps.next()
                    for ct_ in range(nct):
                        s.mm(oa[:, :], vcaug_f[:, ct_ * 72:ct_ * 72 + 128], ets[ct_][:], ct_ == 0, ct_ == nct - 1, [vcaug, ets[ct_]], [oa])
                    oc_ps[j] = oa
                for tb in range(4):
                    for ct_ in range(nct):
                        s.mm(impb[:, tb, :], ets[ct_][:, tb * 128:(tb + 1) * 128], selmap[:, ct_, 0:128], ct_ == 0, ct_ == nct - 1, [ets[ct_], selmap], [impb])
                for tb in range(4):
                    for ct_ in range(nct):
                        s.mm(denb[:, tb:tb + 1], ets[ct_][:, tb * 128:(tb + 1) * 128], selmap[:, ct_, 128:129], ct_ == 0, ct_ == nct - 1, [ets[ct_], selmap], [denb])
                rd = rden.next()
                s.ts("dve", rd[:], denb[:, 0:4], 1e-30, None, ALU.max, None, [denb], [rd])
                s.recip(rd[:], rd[:], [rd], [rd])
                im = imps.next()
                s.copy("act", im[:], impb[:], [impb], [im])
                for tb in range(4):
                    if j == 0:
                        s.ts("dve", impacc[:, tb, :], im[:, tb, :], rd[:, tb:tb + 1], None, ALU.mult, None, [im, rd], [impacc])
                    else:
                        s.stt("dve", impacc[:, tb, :], im[:, tb, :], rd[:, tb:tb + 1], impacc[:, tb, :], ALU.mult, ALU.add, [im, rd, impacc], [impacc])
                if mine:
                    epilogue(oc_ps[j], j - JA, 0, gts, obs)

            s.mute = 6 not in ST
            mb = mbT.next()
            sbl = []
            for tb in range(4):
                tbg = 4 * qt + tb
                c0 = 126 - 2 * tbg
                w = selw.next()
                s.tt("dve", w[:], impacc[:, tb, :], keep[:, c0:c0 + 128], ALU.mult, [impacc, keep], [w])
                s.tt("dve", w[:], w[:], addm[:, c0:c0 + 128], ALU.add, [w, addm], [w])
                s.memset("dve", w[:, 0:1], 1e6, [w])
                m = m8.next()
                w2 = selw.next()
                s.op("dve", lambda e, a=(m, w): e.max(out=a[0][:, 0:8], in_=a[1][:]), [w], [m])
                s.op("dve", lambda e, a=(w2, m, w): e.match_replace(out=a[0][:], in_to_replace=a[1][:, 0:8], in_values=a[2][:], imm_value=-3e38), [w, m], [w2])
                s.op("dve", lambda e, a=(m, w2): e.max(out=a[0][:, 8:16], in_=a[1][:]), [w2], [m])
                s.ts("dve", w2[:], w[:], m[:, 15:16], None, ALU.subtract, None, [w, m], [w2])
                s.ts("dve", w2[:], w2[:], 0.0, None, ALU.is_ge, None, [w2], [w2])
                sb_ = selb.next()
                s.ts("dve", sb_[:], w2[:], -NEG, NEG, ALU.mult, ALU.add, [w2], [sb_])
                sbl.append(sb_)

            s.mute = 8 not in ST
            for hl in range(2):
                j = JA + hl
                oa = oacc_ps.next()
                k_lo = max(0, 4 * qt - 4)
                nk = 4 * qt + 4
                pend = None
                for kt in range(k_lo, nk):
                    p = psr.next()
                    s.mm(p[:], kwin[:, kt % 8, :], qa[j][:, :], True, False, [kwin, qa[j]], [p])
                    s.mm(p[:], identb[:], dmask[:, kt - 4 * qt + 4, :], False, True, [identb, dmask], [p])
                    pt = PT.next()
                    s.act(pt[:], p[:], AF.Exp, [p], [pt])
                    if pend is not None:
                        s.mm(oa[:, :], vwin_f[:, (pend[0] % 8) * 72:(pend[0] % 8) * 72 + 128], pend[1][:], pend[0] == k_lo, False, [vwin, pend[1]], [oa])
                    pend = (kt, pt)
                s.mm(oa[:, :], vwin_f[:, (pend[0] % 8) * 72:(pend[0] % 8) * 72 + 128], pend[1][:], pend[0] == k_lo, True, [vwin, pend[1]], [oa])
                epilogue(oa, hl, 2, gts, obs)

            s.mute = 6 not in ST
            for tb in range(4):
                s.tr(pst[:, 1 + tb, :], sbl[tb][:], identb[:], [sbl[tb], identb], [pst])
            s.copy("act", mb[:], pst[:, 1:5, :].rearrange("p a b -> p (a b)"), [pst], [mb])
            mh = [mbh[0].next(), mbh[1].next()]
            s.copy("pool", mh[0][0:64, :], mb[0:64, :], [mb], [mh[0]])
            s.copy("pool", mh[1][64:128, :], mb[64:128, :], [mb], [mh[1]])

            s.mute = 7 not in ST
            for hl in range(2):
                j = JA + hl
                oa = oacc_ps.next()
                nk = 4 * qt + 4
                pend = None
                for kt in range(nk):
                    p = psr.next()
                    s.mm(p[:], kslc[:, kt * 128:(kt + 1) * 128], qa[j][:, :], True, False, [kslc, qa[j]], [p])
                    diag = kt >= 4 * qt
                    s.mm(p[:], E_all[:, kt % 32, :], mh[kt // 32][:], False, not diag, [E_all, mh[kt // 32]], [p])
                    if diag:
                        s.mm(p[:], identb[:], dmask[:, kt - 4 * qt + 4, :], False, True, [identb, dmask], [p])
                    pt = PT.next()
                    s.act(pt[:], p[:], AF.Exp, [p], [pt])
                    if pend is not None:
                        s.mm(oa[:, :], vslc_f[:, pend[0] * 72:pend[0] * 72 + 128], pend[1][:], pend[0] == 0, False, [vslc, pend[1]], [oa])
                    pend = (kt, pt)
                    if kt % 2 == 1:
                        _m = s.mute
                        s.mute = 9 not in ST
                        next(hgen, None)
                        s.mute = _m
                s.mm(oa[:, :], vslc_f[:, pend[0] * 72:pend[0] * 72 + 128], pend[1][:], pend[0] == 0, True, [vslc, pend[1]], [oa])
                epilogue(oa, hl, 1, gts, obs)

            s.mute = 9 not in ST
            for _ in hgen:
                pass

            s.mute = 10 not in ST
            for tb in range(4):
                first = (qt == 0 and tb == 0)
                p = psr.next()
                s.mm(p[:, 0:128], pcs[tb][:], poolP[:, 2 if first else 0, :], True, False, [pcs[tb], poolP], [p])
                s.mm(p[:, 0:128], pc_prev[:], poolP[:, 1, :], False, True, [pc_prev, poolP], [p])
                pt_ = ptb.next()
                s.copy("act", pt_[:], p[:, 0:128], [p], [pt_])
                p2 = psr.next()
                s.mm(p2[:, 0:128], pt_[:], poolw[:], True, True, [pt_, poolw], [p2])
                s.tt("dve", obs[tb][:, 256:384], p2[:, 0:128], poolscale[:], ALU.mult, [p2, poolscale], [obs[tb]])
                pc_prev = pcs[tb]

            s.mute = 11 not in ST
            for tb in range(4):
                s.dma("sp", om[q0 + tb * 128:q0 + (tb + 1) * 128, :], obs[tb][:], reads=[obs[tb]])
        s.emit()
        print("M program: ops", s.n, "sbuf peak", s.peak, "gate/pcb/vvb/sgb/outb offs", [x.bufs[0].t.manual_sbuf_range for x in (gate, pcb, vvb, sgb, outb)], flush=True)
    return nc


def _m_consts():
    cst = {}
    t = np.arange(S)
    cst["kpos"] = np.stack([np.ones(S), np.ones(S), t // 64, t % 64]).astype(np.float32)
    cp = 16 * np.arange(512) + 31
    cst["cpos"] = np.stack([np.ones(512), np.ones(512), cp // 64, cp % 64]).astype(np.float32)
    cl = np.arange(128)[:, None]
    tl = np.arange(512)[None, :]
    cst["cmask"] = np.stack([np.where(16 * cl + 31 - tl <= 512 * dl, 0.0, NEG) for dl in range(5)]).astype(np.float32)
    dm = []
    for rel in range(-4, 4):
        dist = tl - cl - 128 * rel
        dm.append(np.where((dist >= 0) & (dist < 512), 0.0, NEG))
    cst["dmask"] = np.stack(dm).astype(np.float32)
    rr = (np.arange(128) % 64)[:, None, None]
    cst["E_all"] = (rr == 2 * np.arange(32)[None, :, None] + (np.arange(128)[None, None, :] // 64)).astype(np.float32)
    c0 = np.arange(511)[:, None] * 16
    s0 = np.arange(128)[None, :] * 64
    ov = np.clip(np.minimum(c0 + 32, s0 + 64) - np.maximum(c0, s0), 0, None) / 32.0
    sm = np.zeros((512, 129), np.float32)
    sm[:511, :128] = ov
    sm[:, 128] = 1.0
    cst["selmap"] = _c(sm.reshape(4, 128, 129).transpose(1, 0, 2))
    r = np.arange(256)[None, :] - 126
    hi = (np.arange(128)[:, None] >= 64).astype(np.int64)
    forced = (r == hi) | (r == hi - 1)
    future = r >= hi + 1
    cst["keep"] = np.where(forced | future, 0.0, 1.0).astype(np.float32)
    cst["addm"] = np.where(forced, 1e6, np.where(future, -1e30, 0.0)).astype(np.float32)
    cst["ident"] = np.eye(128, dtype=np.float32)
    cst["tri2"] = ((np.arange(128)[:, None] % 64) <= np.arange(64)[None, :]).astype(np.float32)
    return cst


def prep_M(inp, l, x):
    cst = _m_consts()
    w_in = inp["w_in"][l]
    maps = []
    xTs = [_c(x[b].T) for b in range(2)]
    tt_ = np.arange(S)
    for c in range(NCORE):
        b, u = divmod(c, 4)
        g = u // 2
        ja = 2 * (u % 2)
        heads = [ja, ja + 1] + [j for j in range(4) if j not in (ja, ja + 1)]
        m = dict(cst)
        m["xT"] = xTs[b]

        def kvcol(br, kv):
            return OFF_KV + ((br * 2 + kv) * 2 + g) * 64
        cols = []
        for j in heads:
            cols += list(range(OFF_Q + (g * 4 + j) * 64, OFF_Q + (g * 4 + j + 1) * 64))
        for (br, kv) in ((0, 0), (1, 0), (2, 0), (0, 1)):
            cols += list(range(kvcol(br, kv), kvcol(br, kv) + 64))
        cols += list(range(OFF_H + u * 128, OFF_H + (u + 1) * 128))
        cols += list(range(OFF_H + 512 + u * 128, OFF_H + 512 + (u + 1) * 128))
        m["wfm"] = _c(w_in[:, cols].reshape(16, 128, 768).transpose(1, 0, 2))
        cols = list(range(kvcol(1, 1), kvcol(1, 1) + 64)) + list(range(kvcol(2, 1), kvcol(2, 1) + 64))
        gcols = [OFF_G + (g * 4 + j) * 3 + br for j in (ja, ja + 1) for br in range(3)]
        cols += gcols + [gcols[0], gcols[0]]
        cols += list(range(OFF_P + u * 128, OFF_P + (u + 1) * 128))
        cols += list(range(OFF_H + 1024 + u * 128, OFF_H + 1024 + (u + 1) * 128))
        cols += list(range(OFF_H + 1536 + u * 128, OFF_H + 1536 + (u + 1) * 128))
        m["wtm"] = _c(w_in[:, cols].reshape(16, 128, 520).transpose(1, 0, 2))
        qp = np.zeros((4, 4, S), np.float32)
        for i, j in enumerate(heads):
            sl = 2.0 ** (-(g * 4 + j + 1))
            qp[i, 0] = -64.0 * sl * (tt_ // 64)
            qp[i, 1] = -sl * (tt_ % 64)
            qp[i, 2] = 64.0 * sl
            qp[i, 3] = sl
        m["qpos"] = qp
        m["cw_k"] = _c(inp["nsa_cmp_w"][l][0].transpose(1, 0, 2))
        m["cw_v"] = _c(inp["nsa_cmp_w"][l][1].transpose(1, 0, 2))
        m["cp_k"] = _c(inp["nsa_cmp_pos"][l][0].T)
        m["cp_v"] = _c(inp["nsa_cmp_pos"][l][1].T)
        m["lblog"] = _c(inp["hgrn_lb_logits"][:, u * 128:(u + 1) * 128].T)
        m["lbsel"] = np.full((128, 1), float(l), np.float32)
        m["normg"] = _bc(inp["hgrn_norm_g"][l][u * 128:(u + 1) * 128])
        win = (2, 4, 8, 16)[u]
        sI = np.arange(128)[:, None]
        tI = np.arange(128)[None, :]
        P = np.zeros((128, 3, 128), np.float32)
        P[:, 0, :] = np.where((sI > tI - win) & (sI <= tI), 1.0 / win, 0.0) - (sI == tI)
        P[:, 1, :] = np.where(sI >= 128 + tI - win + 1, 1.0 / win, 0.0)
        cnt = np.minimum(tI + 1, win).astype(np.float32)
        P[:, 2, :] = np.where((sI > tI - win) & (sI <= tI), 1.0 / cnt, 0.0) - (sI == tI)
        m["poolP"] = P
        m["poolw"] = _c(inp["pool_w"][l][u])
        m["poolscale"] = _bc(inp["pool_scale"][l][u * 128:(u + 1) * 128])
        maps.append(m)
    return maps


def run_M(inp, l, x):
    if "M" not in _NC_CACHE:
        _NC_CACHE["M"] = build_M()
    maps = prep_M(inp, l, x)
    res = run_bass_kernel_spmd(_NC_CACHE["M"], maps, core_ids=list(range(NCORE)))
    oabc = np.zeros((2, S, 1536), np.float32)
    for c in range(NCORE):
        b, u = divmod(c, 4)
        o = res.results[c]["om"]
        for k in range(3):
            oabc[b, :, 512 * k + 128 * u:512 * k + 128 * (u + 1)] = o[:, 128 * k:128 * (k + 1)]
    return oabc


def kernel(**inputs):
    inp = {k: np.asarray(v) for k, v in inputs.items()}
    x = np.ascontiguousarray(inp["x"], dtype=np.float32)
    for l in range(2):
        oabc = run_M(inp, l, x)
        x = run_R(inp, l, x, oabc)
    return x
```

```python
import contextlib
import numpy as np
import concourse.bass as bass
import concourse.mybir as mybir
from concourse.bass_utils import run_bass_kernel_spmd

F32 = mybir.dt.float32
BF16 = mybir.dt.bfloat16
AF = mybir.ActivationFunctionType
ALU = mybir.AluOpType
AX = mybir.AxisListType

D = 2048
S = 8192
NCORE = 8
ALPHA = 4 ** 0.25
EPS = 1e-5
DFF = 5632
NEG = -30000.0
SBUF_BASE = 16512
SBUF_CAP = 229344


class Buf:
    __slots__ = ("t", "w", "r")

    def __init__(self, t):
        self.t = t
        self.w = None
        self.r = []

    def __getitem__(self, k):
        return self.t[k]


class Sched:
    CE = ("pe", "act", "dve", "pool")
    DQ = ("sp", "act", "pool")

    def __init__(self, nc, es, ring=8):
        self.nc = nc
        self.es = es
        self.ops = {e: [] for e in ("pe", "act", "dve", "pool", "sp")}
        self.csem = {e: es.enter_context(nc.semaphore("c_" + e)) for e in self.CE}
        self.ccnt = {e: 0 for e in self.CE}
        self.ring = ring
        self.dsem = {q: [es.enter_context(nc.semaphore("d_%s%d" % (q, i))) for i in range(ring)] for q in self.DQ}
        self.dcnt = {q: 0 for q in self.DQ}
        self.dtok = {q: [None] * ring for q in self.DQ}
        self.seen = {e: {} for e in self.ops}
        self.n = 0

    def buf(self, name, shape, dt, psum=False):
        if psum:
            t = self.es.enter_context(self.nc.psum_tensor(name, shape, dt))
            return Buf(t)
        n = 1
        for d_ in shape[1:]:
            n *= d_
        nbytes = n * (2 if dt == BF16 else 4)
        nbytes = (nbytes + 63) // 64 * 64
        self.uid = getattr(self, "uid", 0) + 1
        off = getattr(self, "off", SBUF_BASE)
        assert off + nbytes <= SBUF_CAP, ("SBUF overflow", name, off, nbytes)
        t = self.nc.alloc_sbuf_tensor_at("%s_%d" % (name, self.uid), list(shape), dt, offset=off)
        self.off = off + nbytes
        self.peak = max(getattr(self, "peak", 0), self.off)
        return Buf(t)

    def mark(self):
        return getattr(self, "off", SBUF_BASE)

    def release(self, m):
        self.barrier()
        self.off = m

    def barrier(self):
        toks = []
        for q in self.DQ:
            for t in self.dtok[q]:
                if t is not None:
                    toks.append(t)
        for e in self.CE:
            if self.ccnt[e]:
                toks.append((self.csem[e], self.ccnt[e], "c_" + e, e))
        for eng in self.ops:
            waits = []
            seen = self.seen[eng]
            for (sem, val, key, src) in toks:
                if seen.get(key, 0) >= val:
                    continue
                seen[key] = val
                waits.append((sem, val))
            if waits:
                self.ops[eng].append((waits, None, None, 0))

    def _waits(self, eng, reads, writes, extra=()):
        deps = []
        for b in reads:
            if b.w is not None:
                deps.append(b.w)
        for b in writes:
            if b.w is not None:
                deps.append(b.w)
            deps.extend(b.r)
        deps.extend(extra)
        waits = []
        seen = self.seen[eng]
        for (sem, val, key, src) in deps:
            if src == "pe" and eng == "pe":
                continue
            if seen.get(key, 0) >= val:
                continue
            seen[key] = val
            waits.append((sem, val))
        return waits

    def op(self, eng, fn, reads=(), writes=()):
        if getattr(self, 'mute', False):
            return None
        waits = self._waits(eng, reads, writes)
        self.ccnt[eng] += 1
        tok = (self.csem[eng], self.ccnt[eng], "c_" + eng, eng)
        for b in reads:
            b.r.append(tok)
        for b in writes:
            b.w = tok
            b.r = []
        self.ops[eng].append((waits, fn, self.csem[eng], 1))
        self.n += 1
        return tok

    def dma(self, q, out, in_, reads=(), writes=()):
        if getattr(self, 'mute', False):
            return None
        i = self.dcnt[q]
        slot = i % self.ring
        extra = []
        if self.dtok[q][slot] is not None:
            extra.append(self.dtok[q][slot])
        waits = self._waits(q, reads, writes, extra)
        self.dcnt[q] += 1
        sem = self.dsem[q][slot]
        tok = (sem, 16 * (i // self.ring + 1), "d_%s%d" % (q, slot), "dma_" + q)
        self.dtok[q][slot] = tok
        for b in reads:
            b.r.append(tok)
        for b in writes:
            b.w = tok
            b.r = []
        self.ops[q].append((waits, lambda e, o=out, i_=in_: e.dma_start(out=o, in_=i_), sem, 16))
        self.n += 1
        return tok

    def coll(self, kind, ins, outs, groups, reads=(), writes=()):
        q = "pool"
        i = self.dcnt[q]
        slot = i % self.ring
        extra = []
        if self.dtok[q][slot] is not None:
            extra.append(self.dtok[q][slot])
        waits = self._waits(q, reads, writes, extra)
        self.dcnt[q] += 1
        sem = self.dsem[q][slot]
        tok = (sem, 16 * (i // self.ring + 1), "d_%s%d" % (q, slot), "dma_" + q)
        self.dtok[q][slot] = tok
        for b in reads:
            b.r.append(tok)
        for b in writes:
            b.w = tok
            b.r = []
        self.ops[q].append((waits, lambda e, a=(kind, ins, outs, groups): e.collective_compute(a[0], ALU.bypass, replica_groups=a[3], ins=a[1], outs=a[2]), sem, 16))
        self.n += 1
        return tok

    def finish(self):
        extra = []
        for q in self.DQ:
            for t in self.dtok[q]:
                if t is not None:
                    extra.append(t)
        for e in self.CE:
            if self.ccnt[e]:
                extra.append((self.csem[e], self.ccnt[e], "c_" + e, e))
        waits = self._waits("sp", (), (), extra)
        self.ops["sp"].append((waits, None, None, 0))

    def emit(self):
        self.finish()
        nc = self.nc
        ops = self.ops

        def replay(name, e):
            for waits, fn, sem, inc in ops[name]:
                for (s_, v_) in waits:
                    e.wait_ge(s_, v_)
                if fn is not None:
                    fn(e).then_inc(sem, inc)

        with nc.Block() as block:
            @block.tensor
            def _(e):
                replay("pe", e)

            @block.scalar
            def _(e):
                replay("act", e)

            @block.vector
            def _(e):
                replay("dve", e)

            @block.gpsimd
            def _(e):
                replay("pool", e)

            @block.sync
            def _(e):
                replay("sp", e)

    def mm(self, out, lhsT, rhs, start, stop, reads, writes):
        return self.op("pe", lambda e, a=(out, lhsT, rhs, start, stop): e.matmul(a[0], a[1], a[2], start=a[3], stop=a[4]), reads, writes)

    def tr(self, out, in_, ident, reads, writes):
        return self.op("pe", lambda e, a=(out, in_, ident): e.transpose(a[0], a[1], a[2]), reads, writes)

    def act(self, out, in_, func, reads, writes, bias=None, scale=None, eng="act"):
        kw = {}
        if bias is not None:
            kw["bias"] = bias
        if scale is not None:
            kw["scale"] = scale
        return self.op("act", lambda e, a=(out, in_, func, kw): e.activation(out=a[0], in_=a[1], func=a[2], **a[3]), reads, writes)

    def tt(self, eng, out, in0, in1, op, reads, writes):
        return self.op(eng, lambda e, a=(out, in0, in1, op): e.tensor_tensor(a[0], a[1], a[2], a[3]), reads, writes)

    def ts(self, eng, out, in0, s1, s2, op0, op1, reads, writes):
        if s2 is None:
            return self.op(eng, lambda e, a=(out, in0, s1, op0): e.tensor_scalar(a[0], a[1], a[2], None, a[3]), reads, writes)
        return self.op(eng, lambda e, a=(out, in0, s1, s2, op0, op1): e.tensor_scalar(a[0], a[1], a[2], a[3], a[4], a[5]), reads, writes)

    def stt(self, eng, out, in0, scalar, in1, op0, op1, reads, writes):
        return self.op(eng, lambda e, a=(out, in0, scalar, in1, op0, op1): e.scalar_tensor_tensor(a[0], a[1], a[2], a[3], a[4], a[5]), reads, writes)

    def copy(self, eng, out, in_, reads, writes):
        if eng == "act":
            return self.op("act", lambda e, a=(out, in_): e.copy(a[0], a[1]), reads, writes)
        return self.op(eng, lambda e, a=(out, in_): e.tensor_copy(a[0], a[1]), reads, writes)

    def memset(self, eng, ap, val, writes):
        return self.op(eng, lambda e, a=(ap, val): e.memset(a[0], a[1]), (), writes)

    def rsum(self, eng, out, in_, reads, writes):
        return self.op(eng, lambda e, a=(out, in_): e.reduce_sum(a[0], a[1], AX.X), reads, writes)

    def recip(self, out, in_, reads, writes):
        return self.op("dve", lambda e, a=(out, in_): e.reciprocal(a[0], a[1]), reads, writes)


class Ring:
    def __init__(self, bufs):
        self.bufs = bufs
        self.i = 0

    def next(self):
        b = self.bufs[self.i % len(self.bufs)]
        self.i += 1
        return b


def _dram(nc, name, shape, dt=F32, kind="ExternalInput"):
    return nc.dram_tensor(name, list(shape), dt, kind=kind).ap()


TT = 2176
NB = TT // 128
TTILES = [(0, 512), (512, 512), (1024, 512), (1536, 512), (2048, 128)]
NFC = DFF // 128

R_INPUTS = {
    "xT": (D, TT), "xtok": (TT, D), "oT": (1536, TT),
    "w_uv": (128, 16, 1024), "w_mg": (16, 128, 4, 16, 128), "w_br": (16, 128, 4, 4, 128), "w_mix": (128, 16, D),
    "sgw": (128, 4, 128), "sgb": (128, 4), "sg_g": (128, 512), "sg_b": (128, 512), "trimask": (128, 128),
    "ln1_g": (128, D), "ln1_b": (128, D), "ln2_g": (128, D), "ln2_b": (128, D), "ln3_g": (128, D), "ln3_b": (128, D),
    "mem": (256, D), "memln_g": (128, D), "memln_b": (128, D),
    "wq": (128, 16, 512), "wk": (128, 16, 512), "wv": (128, 16, 512), "wo": (128, 4, D),
    "ffn_in": (NFC, 128, 2, 16, 128), "ffn_out": (128, NFC, D), "convw": (128, NFC, 4),
    "ident": (128, 128), "flag": (128, 1), "ones": (128, 128),
}


def build_R():
    nc = bass.Bass("TRN2", target_bir_lowering=False)
    I = {k: _dram(nc, k, v) for k, v in R_INPUTS.items()}
    xout = _dram(nc, "xout", (2048, D), kind="ExternalOutput")
    odT_d = _dram(nc, "odT_d", (4, 128, TT), BF16, kind="Internal")
    preT_d = _dram(nc, "preT_d", (16, 128, TT), BF16, kind="Internal")
    x1_d = _dram(nc, "x1_d", (TT, D), F32, kind="Internal")
    x1T_d = _dram(nc, "x1T_d", (16, 128, TT), BF16, kind="Internal")
    aT_d = _dram(nc, "aT_d", (4, 128, TT), BF16, kind="Internal")
    x2_d = _dram(nc, "x2_d", (TT, D), F32, kind="Internal")
    x2T_d = _dram(nc, "x2T_d", (16, 128, TT), BF16, kind="Internal")
    act_d = _dram(nc, "act_d", (NFC, 128, TT), BF16, kind="Internal")
    ffo_d = _dram(nc, "ffo_d", (TT, D), F32, kind="Internal")
    with contextlib.ExitStack() as es:
        s = Sched(nc, es)
        B = s.buf
        ident = B("ident", [128, 128], BF16)
        ones = B("ones", [128, 128], BF16)
        flag = B("flag", [128, 1], F32)
        kT = B("kT", [128, 4, 256], BF16)
        vtok = B("vtok", [128, 2, 512], BF16)
        ps = Ring([B("ps%d" % i, [128, 512], F32, psum=True) for i in range(6)])
        pst = Ring([B("pst%d" % i, [128, 8, 128], BF16, psum=True) for i in range(2)])
        s.dma("pool", ident[:], I["ident"][:, :], writes=[ident])
        s.dma("pool", ones[:], I["ones"][:, :], writes=[ones])
        s.dma("sp", flag[:], I["flag"][:, :], writes=[flag])

        class LN:
            def __init__(self, gname, bname):
                self.g = B("lng", [128, D], F32)
                self.b = B("lnb", [128, D], F32)
                self.sq = B("lnsq", [128, D], F32)
                self.zt = Ring([B("zt%d" % i, [128, D], F32) for i in range(3)])
                self.zb = Ring([B("zb%d" % i, [128, D], BF16) for i in range(3)])
                self.xr = Ring([B("xr%d" % i, [128, D], F32) for i in range(5)])
                self.st = Ring([B("st%d" % i, [128, 8], F32) for i in range(6)])
                self.ob = Ring([B("lnob%d" % i, [128, 16, 128], BF16) for i in range(3)])
                s.dma("sp", self.g[:], I[gname][:, :], writes=[self.g])
                s.dma("sp", self.b[:], I[bname][:, :], writes=[self.b])

            def norm_gen(self, z, out):
                t = self.st.next()
                sq = self.sq
                s.op("act", lambda e, a=(sq, z, t): e.activation(out=a[0][:], in_=a[1][:], func=AF.Identity, accum_out=a[2][:, 0:1]), [z], [sq, t])
                s.op("act", lambda e, a=(sq, z, t): e.activation(out=a[0][:], in_=a[1][:], func=AF.Square, accum_out=a[2][:, 1:2]), [z], [sq, t])
                s.ts("dve", t[:, 2:3], t[:, 0:1], 1.0 / D, None, ALU.mult, None, [t], [t])
                s.ts("dve", t[:, 3:4], t[:, 1:2], 1.0 / D, None, ALU.mult, None, [t], [t])
                s.stt("dve", t[:, 4:5], t[:, 2:3], -1.0, t[:, 2:3], ALU.mult, ALU.mult, [t], [t])
                s.tt("dve", t[:, 5:6], t[:, 3:4], t[:, 4:5], ALU.add, [t], [t])
                s.act(t[:, 6:7], t[:, 5:6], AF.Sqrt, [t], [t], bias=EPS)
                s.recip(t[:, 7:8], t[:, 6:7], [t], [t])
                yield
                s.stt("dve", out[:], z[:], t[:, 2:3], self.g[:], ALU.subtract, ALU.mult, [z, t, self.g], [out])
                s.stt("dve", out[:], out[:], t[:, 7:8], self.b[:], ALU.mult, ALU.add, [out, t, self.b], [out])

            def norm(self, z, out):
                for _ in self.norm_gen(z, out):
                    pass

            def to_featmajor(self, xf, dst_d, dst_db, tb, width=16):
                xb = self.zb.next()
                s.copy("act", xb[:, 0:width * 128], xf[:, 0:width * 128], [xf], [xb])
                ob = self.ob.next()
                for h in range((width + 7) // 8):
                    p = pst.next()
                    nj = min(8, width - h * 8)
                    for j in range(nj):
                        kc = h * 8 + j
                        s.tr(p[:, j, :], xb[:, kc * 128:(kc + 1) * 128], ident[:], [xb, ident], [p])
                    s.copy("dve", ob[:, h * 8:h * 8 + nj, :], p[:, 0:nj, :], [p], [ob])
                s.dma("sp", dst_d[0:width, :, tb * 128:(tb + 1) * 128].rearrange("c p t -> p c t"), ob[:, 0:width, :], reads=[ob], writes=[dst_db[tb]])

            def proj_ln_gen(self, tb, lhs_fn, nk, w, xres_ap, xres_db, out_d, out_db, outT_d, outT_db, final_out=None):
                z = self.zt.next()
                xres = self.xr.next()
                s.dma("sp", xres[:], xres_ap, reads=list(xres_db), writes=[xres])
                for dt_ in range(4):
                    p = ps.next()
                    for k in range(nk):
                        lt, lbuf = lhs_fn(k)
                        s.mm(p[:], lt, w[:, k, dt_ * 512:(dt_ + 1) * 512], k == 0, k == nk - 1, [lbuf, w], [p])
                    s.stt("dve", z[:, dt_ * 512:(dt_ + 1) * 512], xres[:, dt_ * 512:(dt_ + 1) * 512], ALPHA, p[:], ALU.mult, ALU.add, [xres, p], [z])
                yield
                o = self.xr.next()
                for _ in self.norm_gen(z, o):
                    yield
                yield
                if final_out is not None:
                    if tb >= 1:
                        s.dma("sp", final_out[(tb - 1) * 128:tb * 128, :], o[:], reads=[o])
                else:
                    s.dma("sp", out_d[tb * 128:(tb + 1) * 128, :], o[:], reads=[o], writes=[out_db[tb]])
                    self.to_featmajor(o, outT_d, outT_db, tb)

        def run_pipelined(make_gen, n, depth=2):
            active = []
            nxt = 0
            while nxt < n or active:
                if nxt < n and len(active) < depth:
                    active.append(make_gen(nxt))
                    nxt += 1
                for g_ in list(active):
                    try:
                        next(g_)
                    except StopIteration:
                        active.remove(g_)

        def DB(n):
            return [Buf(None) for _ in range(n)]

        odT_db, preT_db, x1_db, x1T_db, aT_db, x2_db, x2T_db, act_db, ffo_db = DB(NB), DB(16), DB(NB), DB(NB), DB(4), DB(NB), DB(NB), DB(NFC), DB(NB * 4)
        base = s.mark()

        ln = LN("memln_g", "memln_b")
        memT_d = _dram(nc, "memT_d", (16, 128, 256), BF16, kind="Internal")
        memT_db = DB(2)
        for mb in range(2):
            z = ln.zt.next()
            o = ln.xr.next()
            s.dma("sp", z[:], I["mem"][mb * 128:(mb + 1) * 128, :], writes=[z])
            ln.norm(z, o)
            ln.to_featmajor(o, memT_d, memT_db, mb)
        memT = B("memT", [128, 16, 256], BF16)
        wk = B("wk", [128, 16, 512], BF16)
        wv = B("wv", [128, 16, 512], BF16)
        s.dma("sp", memT[:], memT_d[:, :, :].rearrange("c p t -> p c t"), reads=memT_db, writes=[memT])
        s.dma("pool", wk[:], I["wk"][:, :, :], writes=[wk])
        s.dma("pool", wv[:], I["wv"][:, :, :], writes=[wv])
        for h in range(4):
            p = ps.next()
            for kc in range(16):
                s.mm(p[:, 0:256], wk[:, kc, h * 128:(h + 1) * 128], memT[:, kc, :], kc == 0, kc == 15, [wk, memT], [p])
            s.copy("dve", kT[:, h, :], p[:, 0:256], [p], [kT])
        for mb in range(2):
            p = ps.next()
            for kc in range(16):
                s.mm(p[:], memT[:, kc, mb * 128:(mb + 1) * 128], wv[:, kc, :], kc == 0, kc == 15, [wv, memT], [p])
            s.copy("act", vtok[:, mb, :], p[:], [p], [vtok])
        s.release(base)

        xT = B("xT", [128, 16, TT], BF16)
        s.dma("pool", xT[:, 0:8, :], I["xT"][0:1024, :].rearrange("(kc p) t -> p kc t", p=128), writes=[xT])
        s.dma("pool", xT[:, 8:16, :], I["xT"][1024:2048, :].rearrange("(kc p) t -> p kc t", p=128), writes=[xT])
        w_uv = B("w_uv", [128, 16, 1024], BF16)
        sgw = B("sgw", [128, 4, 128], F32)
        sgwb = B("sgwb", [128, 4, 128], BF16)
        tri = B("tri", [128, 128], F32)
        sgb = B("sgb", [128, 4], F32)
        sg_g = B("sg_g", [128, 512], F32)
        sg_b = B("sg_b", [128, 512], F32)
        s.dma("pool", w_uv[:, 0:8, :], I["w_uv"][:, 0:8, :], writes=[w_uv])
        s.dma("pool", w_uv[:, 8:16, :], I["w_uv"][:, 8:16, :], writes=[w_uv])
        s.dma("sp", sgw[:], I["sgw"][:, :, :], writes=[sgw])
        s.dma("sp", tri[:], I["trimask"][:, :], writes=[tri])
        s.dma("sp", sgb[:], I["sgb"][:, :], writes=[sgb])
        s.dma("sp", sg_g[:], I["sg_g"][:, :], writes=[sg_g])
        s.dma("sp", sg_b[:], I["sg_b"][:, :], writes=[sg_b])
        for g in range(4):
            s.tt("dve", sgwb[:, g, :], sgw[:, g, :], tri[:], ALU.mult, [sgw, tri], [sgwb])
        uvt = Ring([B("uvt%d" % i, [128, 512], F32) for i in range(12)])
        gt = Ring([B("gt%d" % i, [128, 512], F32) for i in range(6)])
        vlb = Ring([B("vlb%d" % i, [128, 512], BF16) for i in range(3)])
        odt = Ring([B("odt%d" % i, [128, 512], BF16) for i in range(3)])
        odo = Ring([B("odo%d" % i, [128, 4, 128], BF16) for i in range(3)])
        stA = Ring([B("stA%d" % i, [128, 8], F32) for i in range(6)])

        def gelu(src_ps, dst):
            x = uvt.next()
            t1 = uvt.next()
            s.copy("act", x[:], src_ps[:], [src_ps], [x])
            s.tt("dve", t1[:], x[:], x[:], ALU.mult, [x], [t1])
            s.ts("dve", t1[:], t1[:], 0.044715, 1.0, ALU.mult, ALU.add, [t1], [t1])
            s.tt("dve", t1[:], t1[:], x[:], ALU.mult, [t1, x], [t1])
            s.act(t1[:], t1[:], AF.Sigmoid, [t1], [t1], scale=1.5957691216057308)
            s.tt("pool", dst[:], t1[:], x[:], ALU.mult, [t1, x], [dst])

        def genA(tb):
            pu = ps.next()
            pv = ps.next()
            for (p, c0) in ((pu, 0), (pv, 512)):
                for kc in range(16):
                    s.mm(p[:], xT[:, kc, tb * 128:(tb + 1) * 128], w_uv[:, kc, c0:c0 + 512], kc == 0, kc == 15, [xT, w_uv], [p])
            gu = gt.next()
            gv = gt.next()
            gelu(pu, gu)
            gelu(pv, gv)
            yield
            t = stA.next()
            scr = uvt.next()
            s.op("act", lambda e, a=(scr, gv, t): e.activation(out=a[0][:], in_=a[1][:], func=AF.Identity, accum_out=a[2][:, 0:1]), [gv], [scr, t])
            s.op("act", lambda e, a=(scr, gv, t): e.activation(out=a[0][:], in_=a[1][:], func=AF.Square, accum_out=a[2][:, 1:2]), [gv], [scr, t])
            s.ts("dve", t[:, 2:3], t[:, 0:1], 1.0 / 512, None, ALU.mult, None, [t], [t])
            s.ts("dve", t[:, 3:4], t[:, 1:2], 1.0 / 512, None, ALU.mult, None, [t], [t])
            s.stt("dve", t[:, 4:5], t[:, 2:3], -1.0, t[:, 2:3], ALU.mult, ALU.mult, [t], [t])
            s.tt("dve", t[:, 5:6], t[:, 3:4], t[:, 4:5], ALU.add, [t], [t])
            s.act(t[:, 6:7], t[:, 5:6], AF.Sqrt, [t], [t], bias=EPS)
            s.recip(t[:, 7:8], t[:, 6:7], [t], [t])
            yield
            s.stt("dve", gv[:], gv[:], t[:, 2:3], sg_g[:], ALU.subtract, ALU.mult, [gv, t, sg_g], [gv])
            vb = vlb.next()
            s.stt("dve", vb[:], gv[:], t[:, 7:8], sg_b[:], ALU.mult, ALU.add, [gv, t, sg_b], [vb])
            yield
            pg = ps.next()
            for g in range(4):
                s.mm(pg[:, g * 128:(g + 1) * 128], sgwb[:, g, :], vb[:, g * 128:(g + 1) * 128], True, True, [sgwb, vb], [pg])
            od = odt.next()
            for g in range(4):
                s.stt("dve", od[:, g * 128:(g + 1) * 128], pg[:, g * 128:(g + 1) * 128], sgb[:, g:g + 1], gu[:, g * 128:(g + 1) * 128],
                      ALU.add, ALU.mult, [pg, sgb, gu], [od])
            yield
            p = pst.next()
            for j_ in range(4):
                s.tr(p[:, j_, :], od[:, j_ * 128:(j_ + 1) * 128], ident[:], [od, ident], [p])
            oo = odo.next()
            s.copy("act", oo[:], p[:, 0:4, :], [p], [oo])
            s.dma("sp", odT_d[:, :, tb * 128:(tb + 1) * 128].rearrange("c p t -> p c t"), oo[:], reads=[oo], writes=[odT_db[tb]])
        run_pipelined(genA, NB)
        mB = s.mark()
        s.barrier()
        s.off = xT_end = base + 16 * TT * 2
        assert xT_end % 64 == 0

        oT = B("oT", [128, 16, TT], BF16)
        s.dma("pool", oT[:, 0:6, :], I["oT"][0:768, :].rearrange("(kc p) t -> p kc t", p=128), writes=[oT])
        s.dma("pool", oT[:, 6:12, :], I["oT"][768:1536, :].rearrange("(kc p) t -> p kc t", p=128), writes=[oT])
        s.dma("sp", oT[:, 12:16, :], odT_d[:, :, :].rearrange("c p t -> p c t"), reads=odT_db, writes=[oT])
        wmg = Ring([B("wmg%d" % i, [128, 4, 16, 128], BF16) for i in range(2)])
        wbr = Ring([B("wbr%d" % i, [128, 4, 4, 128], BF16) for i in range(2)])
        gsb = Ring([B("gsb%d" % i, [128, 512], F32) for i in range(3)])
        acc = Ring([B("acc%d" % i, [128, 512], F32) for i in range(2)])
        preo = Ring([B("preo%d" % i, [128, TT], BF16) for i in range(2)])
        wq_ = []
        for dc in range(min(1, 16)):
            wm = wmg.next()
            wb = wbr.next()
            s.dma("pool", wm[:], I["w_mg"][dc, :, :, :, :], writes=[wm])
            s.dma("pool", wb[:], I["w_br"][dc, :, :, :, :], writes=[wb])
            wq_.append((wm, wb))
        for dc in range(16):
            if dc + 1 < 16:
                wm = wmg.next()
                wb = wbr.next()
                s.dma("pool", wm[:], I["w_mg"][dc + 1, :, :, :, :], writes=[wm])
                s.dma("pool", wb[:], I["w_br"][dc + 1, :, :, :, :], writes=[wb])
                wq_.append((wm, wb))
            wm, wb = wq_[dc]
            po = preo.next()
            for (t0, tw) in TTILES:
                a = acc.next()
                for n in range(4):
                    pm = ps.next()
                    py = ps.next()
                    for kc in range(16):
                        s.mm(pm[:, 0:tw], wm[:, n, kc, :], xT[:, kc, t0:t0 + tw], kc == 0, kc == 15, [wm, xT], [pm])
                    for k4 in range(4):
                        s.mm(py[:, 0:tw], wb[:, n, k4, :], oT[:, n * 4 + k4, t0:t0 + tw], k4 == 0, k4 == 3, [wb, oT], [py])
                    gs = gsb.next()
                    s.act(gs[:, 0:tw], pm[:, 0:tw], AF.Sigmoid, [pm], [gs])
                    if n == 0:
                        s.tt("dve", a[:, 0:tw], gs[:, 0:tw], py[:, 0:tw], ALU.mult, [gs, py], [a])
                    else:
                        s.tt("dve", gs[:, 0:tw], gs[:, 0:tw], py[:, 0:tw], ALU.mult, [gs, py], [gs])
                        if n < 3:
                            s.tt("pool", a[:, 0:tw], a[:, 0:tw], gs[:, 0:tw], ALU.add, [a, gs], [a])
                        else:
                            s.tt("pool", po[:, t0:t0 + tw], a[:, 0:tw], gs[:, 0:tw], ALU.add, [a, gs], [po])
            s.dma("sp", preT_d[dc, :, :], po[:], reads=[po], writes=[preT_db[dc]])
        s.release(base)

        ln = LN("ln1_g", "ln1_b")
        wmix = B("wmix", [128, 16, D], BF16)
        s.dma("pool", wmix[:, 0:8, :], I["w_mix"][:, 0:8, :], writes=[wmix])
        s.dma("pool", wmix[:, 8:16, :], I["w_mix"][:, 8:16, :], writes=[wmix])
        prb = Ring([B("prb%d" % i, [128, 16, 128], BF16) for i in range(3)])

        def genC(tb):
            pb = prb.next()
            s.dma("sp", pb[:], preT_d[:, :, tb * 128:(tb + 1) * 128].rearrange("c p t -> p c t"), reads=preT_db, writes=[pb])
            yield from ln.proj_ln_gen(tb, lambda k, pb=pb: (pb[:, k, :], pb), 16, wmix, I["xtok"][tb * 128:(tb + 1) * 128, :], (), x1_d, x1_db, x1T_d, x1T_db)
        run_pipelined(genC, NB)
        s.release(base)

        x1T = B("x1T", [128, 16, TT], BF16)
        s.dma("sp", x1T[:], x1T_d[:, :, :].rearrange("c p t -> p c t"), reads=x1T_db, writes=[x1T])
        wq = B("wq", [128, 16, 512], BF16)
        s.dma("pool", wq[:], I["wq"][:, :, :], writes=[wq])
        qTb = Ring([B("qTb%d" % i, [128, 512], BF16) for i in range(2)])
        pTb = Ring([B("pTb%d" % i, [128, 512], BF16) for i in range(4)])
        rdb = Ring([B("rdb%d" % i, [128, 512], F32) for i in range(2)])
        aTo = Ring([B("aTo%d" % i, [128, TT], BF16) for i in range(2)])
        for h in range(4):
            ao = aTo.next()
            for (t0, tw) in TTILES:
                p = ps.next()
                for kc in range(16):
                    s.mm(p[:, 0:tw], wq[:, kc, h * 128:(h + 1) * 128], x1T[:, kc, t0:t0 + tw], kc == 0, kc == 15, [wq, x1T], [p])
                q = qTb.next()
                s.act(q[:, 0:tw], p[:, 0:tw], AF.Identity, [p], [q], scale=128 ** -0.5)
                pts = []
                for mb in range(2):
                    p2 = ps.next()
                    s.mm(p2[:, 0:tw], kT[:, h, mb * 128:(mb + 1) * 128], q[:, 0:tw], True, True, [kT, q], [p2])
                    pt = pTb.next()
                    s.act(pt[:, 0:tw], p2[:, 0:tw], AF.Exp, [p2], [pt])
                    pts.append(pt)
                po_ = ps.next()
                pd = ps.next()
                for mb in range(2):
                    s.mm(po_[:, 0:tw], vtok[:, mb, h * 128:(h + 1) * 128], pts[mb][:, 0:tw], mb == 0, mb == 1, [vtok, pts[mb]], [po_])
                for mb in range(2):
                    s.mm(pd[:, 0:tw], ones[:], pts[mb][:, 0:tw], mb == 0, mb == 1, [ones, pts[mb]], [pd])
                rd = rdb.next()
                s.recip(rd[:, 0:tw], pd[:, 0:tw], [pd], [rd])
                s.tt("dve", ao[:, t0:t0 + tw], po_[:, 0:tw], rd[:, 0:tw], ALU.mult, [po_, rd], [ao])
            s.dma("sp", aT_d[h, :, :], ao[:], reads=[ao], writes=[aT_db[h]])
        s.release(base)

        ln = LN("ln2_g", "ln2_b")
        wo = B("wo", [128, 4, D], BF16)
        s.dma("pool", wo[:], I["wo"][:, :, :], writes=[wo])
        aTr = Ring([B("aTr%d" % i, [128, 4, 128], BF16) for i in range(3)])

        def genD2(tb):
            ab = aTr.next()
            s.dma("sp", ab[:], aT_d[:, :, tb * 128:(tb + 1) * 128].rearrange("c p t -> p c t"), reads=aT_db, writes=[ab])
            yield from ln.proj_ln_gen(tb, lambda k, ab=ab: (ab[:, k, :], ab), 4, wo, x1_d[tb * 128:(tb + 1) * 128, :], [x1_db[tb]], x2_d, x2_db, x2T_d, x2T_db)
        run_pipelined(genD2, NB)
        s.release(base)

        x2T = B("x2T", [128, 16, TT], BF16)
        s.dma("sp", x2T[:], x2T_d[:, :, :].rearrange("c p t -> p c t"), reads=x2T_db, writes=[x2T])
        cw = B("cw", [128, NFC, 4], F32)
        s.dma("sp", cw[:], I["convw"][:, :, :], writes=[cw])
        wf = Ring([B("wf%d" % i, [128, 2, 16, 128], BF16) for i in range(3)])
        gbuf = Ring([B("gbuf%d" % i, [128, TT + 2], F32) for i in range(2)])
        ubuf = Ring([B("ubuf%d" % i, [128, TT], F32) for i in range(2)])
        cbuf = Ring([B("cbuf%d" % i, [128, TT], F32) for i in range(2)])
        sbuf_ = Ring([B("sbuf%d" % i, [128, TT], F32) for i in range(2)])
        abuf = Ring([B("abuf%d" % i, [128, TT], BF16) for i in range(2)])
        for g_ in gbuf.bufs:
            s.memset("dve", g_[:, 0:2], 0.0, [g_])
        wfq = []
        for fc in range(2):
            w = wf.next()
            s.dma("pool", w[:], I["ffn_in"][fc, :, :, :, :], writes=[w])
            wfq.append(w)
        for fc in range(NFC):
            if fc + 2 < NFC:
                w = wf.next()
                s.dma("pool", w[:], I["ffn_in"][fc + 2, :, :, :, :], writes=[w])
                wfq.append(w)
            w = wfq[fc]
            gb = gbuf.next()
            ub = ubuf.next()
            for (t0, tw) in TTILES:
                pg = ps.next()
                pu = ps.next()
                for kc in range(16):
                    s.mm(pg[:, 0:tw], w[:, 0, kc, :], x2T[:, kc, t0:t0 + tw], kc == 0, kc == 15, [w, x2T], [pg])
                for kc in range(16):
                    s.mm(pu[:, 0:tw], w[:, 1, kc, :], x2T[:, kc, t0:t0 + tw], kc == 0, kc == 15, [w, x2T], [pu])
                if t0 == 0:
                    s.op("act", lambda e, a=(gb, pg, flag): e.activation(out=a[0][:, 2:130], in_=a[1][:, 0:128], func=AF.Copy, scale=a[2][:, 0:1]), [pg, flag], [gb])
                    s.copy("act", gb[:, 130:2 + tw], pg[:, 128:tw], [pg], [gb])
                else:
                    s.copy("act", gb[:, 2 + t0:2 + t0 + tw], pg[:, 0:tw], [pg], [gb])
                s.copy("dve", ub[:, t0:t0 + tw], pu[:, 0:tw], [pu], [ub])
            cb = cbuf.next()
            s.ts("dve", cb[:], gb[:, 2:TT + 2], cw[:, fc, 2:3], cw[:, fc, 3:4], ALU.mult, ALU.add, [gb, cw], [cb])
            s.stt("dve", cb[:], gb[:, 1:TT + 1], cw[:, fc, 1:2], cb[:], ALU.mult, ALU.add, [gb, cw, cb], [cb])
            s.stt("dve", cb[:], gb[:, 0:TT], cw[:, fc, 0:1], cb[:], ALU.mult, ALU.add, [gb, cw, cb], [cb])
            sb = sbuf_.next()
            s.act(sb[:], cb[:], AF.Silu, [cb], [sb])
            ab = abuf.next()
            s.tt("pool", ab[:], sb[:], ub[:], ALU.mult, [sb, ub], [ab])
            s.dma("sp", act_d[fc, :, :], ab[:], reads=[ab], writes=[act_db[fc]])
        s.release(base)

        wfo = Ring([B("wfo%d" % i, [128, NFC, 512], BF16) for i in range(2)])
        acb = Ring([B("acb%d" % i, [128, NFC, 128], BF16) for i in range(4)])
        fob = Ring([B("fob%d" % i, [128, 512], F32) for i in range(3)])
        for dt_ in range(4):
            w = wfo.next()
            for f0 in range(0, NFC, 11):
                s.dma("pool", w[:, f0:f0 + 11, :], I["ffn_out"][:, f0:f0 + 11, dt_ * 512:(dt_ + 1) * 512], writes=[w])
            for tb in range(NB):
                step = dt_ * NB + tb
                if step == 0:
                    abq = []
                    for st_ in range(3):
                        ab = acb.next()
                        tb_ = st_ % NB
                        s.dma("sp", ab[:], act_d[:, :, tb_ * 128:(tb_ + 1) * 128].rearrange("c p t -> p c t"), reads=act_db, writes=[ab])
                        abq.append(ab)
                if step + 3 < 4 * NB:
                    ab = acb.next()
                    tb_ = (step + 3) % NB
                    s.dma("sp", ab[:], act_d[:, :, tb_ * 128:(tb_ + 1) * 128].rearrange("c p t -> p c t"), reads=act_db, writes=[ab])
                    abq.append(ab)
                ab = abq[step]
                p = ps.next()
                for fc in range(NFC):
                    s.mm(p[:], ab[:, fc, :], w[:, fc, :], fc == 0, fc == NFC - 1, [ab, w], [p])
                fo = fob.next()
                if tb % 2 == 0:
                    s.copy("act", fo[:], p[:], [p], [fo])
                else:
                    s.copy("dve", fo[:], p[:], [p], [fo])
                s.dma("sp", ffo_d[tb * 128:(tb + 1) * 128, dt_ * 512:(dt_ + 1) * 512], fo[:], reads=[fo], writes=[ffo_db[tb * 4 + dt_]])
        s.release(base)

        ln = LN("ln3_g", "ln3_b")
        def genF2(i):
            tb = i + 1
            z = ln.zt.next()
            xres = ln.xr.next()
            ff = ln.xr.next()
            s.dma("sp", xres[:], x2_d[tb * 128:(tb + 1) * 128, :], reads=[x2_db[tb]], writes=[xres])
            s.dma("sp", ff[:], ffo_d[tb * 128:(tb + 1) * 128, :], reads=ffo_db[tb * 4:tb * 4 + 4], writes=[ff])
            s.stt("dve", z[:], xres[:], ALPHA, ff[:], ALU.mult, ALU.add, [xres, ff], [z])
            yield
            for _ in ln.norm_gen(z, z):
                yield
            yield
            s.dma("sp", xout[(tb - 1) * 128:tb * 128, :], z[:], reads=[z])
        run_pipelined(genF2, NB - 1)
        s.emit()
        print("R program: ops", s.n, "sbuf peak", s.peak, flush=True)
    return nc


OFF_Q, OFF_KV, OFF_G, OFF_H, OFF_P, OFF_SG, OFF_MG = 0, 512, 1280, 1304, 3352, 3864, 4888


def _bc(v, n=128):
    return np.ascontiguousarray(np.broadcast_to(np.asarray(v, np.float32)[None, :], (n, v.shape[0])))


def _c(a):
    return np.ascontiguousarray(a, dtype=np.float32)


def prep_R_shared(inp, l):
    w_in = inp["w_in"][l]
    sh = {}
    sh["w_uv"] = _c(w_in[:, OFF_SG:OFF_SG + 1024].reshape(16, 128, 1024).transpose(1, 0, 2))
    mg = w_in[:, OFF_MG:OFF_MG + 8192].reshape(16, 128, 4, 16, 128)
    sh["w_mg"] = _c(mg.transpose(3, 1, 2, 0, 4))
    br = inp["w_branch"][l].reshape(4, 4, 128, 16, 128)
    sh["w_br"] = _c(br.transpose(3, 2, 0, 1, 4))
    sh["w_mix"] = _c(inp["w_mix_out"][l].reshape(16, 128, D).transpose(1, 0, 2))
    sh["sgw"] = _c(inp["sg_w"][l].transpose(2, 0, 1))
    sh["sgb"] = _c(inp["sg_b"][l].T)
    sh["sg_g"] = _bc(inp["sg_ln_g"][l])
    sh["sg_b"] = _bc(inp["sg_ln_b"][l])
    sh["trimask"] = _c(np.triu(np.ones((128, 128), np.float32)))
    for i, nm in ((1, "ln_mix"), (2, "ln_x"), (3, "ln_ffn")):
        sh["ln%d_g" % i] = _bc(inp[nm + "_g"][l])
        sh["ln%d_b" % i] = _bc(inp[nm + "_b"][l])
    sh["memln_g"] = _bc(inp["mem_ln_g"])
    sh["memln_b"] = _bc(inp["mem_ln_b"])
    for nm, k in (("wq", "xattn_q"), ("wk", "xattn_k"), ("wv", "xattn_v")):
        sh[nm] = _c(inp[k][l].reshape(16, 128, 512).transpose(1, 0, 2))
    sh["wo"] = _c(inp["xattn_o"][l].reshape(4, 128, D).transpose(1, 0, 2))
    fi = inp["ffn_in"][l].reshape(16, 128, 2, NFC, 128)
    sh["ffn_in"] = _c(fi.transpose(3, 1, 2, 0, 4))
    sh["ffn_out"] = _c(inp["ffn_out"][l].reshape(NFC, 128, D).transpose(1, 0, 2))
    cw = np.concatenate([inp["ffn_conv_w"][l], inp["ffn_conv_b"][l][None, :]], axis=0)
    sh["convw"] = _c(cw.reshape(4, NFC, 128).transpose(2, 1, 0))
    sh["ident"] = np.eye(128, dtype=np.float32)
    sh["ones"] = np.ones((128, 128), np.float32)
    return sh


def prep_R(inp, l, x, oabc):
    sh = prep_R_shared(inp, l)
    maps = []
    for c in range(NCORE):
        b, u = divmod(c, 4)
        t0 = 2048 * u
        m = dict(sh)
        xs = np.zeros((TT, D), np.float32)
        os_ = np.zeros((TT, 1536), np.float32)
        lo = t0 - 128
        if u == 0:
            xs[128:] = x[b, 0:2048]
            os_[128:] = oabc[b, 0:2048]
        else:
            xs[:] = x[b, lo:lo + TT]
            os_[:] = oabc[b, lo:lo + TT]
        m["xtok"] = xs
        m["xT"] = _c(xs.T)
        m["oT"] = _c(os_.T)
        m["mem"] = _c(inp["mem"][b])
        m["flag"] = np.full((128, 1), 0.0 if u == 0 else 1.0, np.float32)
        maps.append(m)
    return maps


_NC_CACHE = {}


def run_R(inp, l, x, oabc):
    if "R" not in _NC_CACHE:
        _NC_CACHE["R"] = build_R()
    maps = prep_R(inp, l, x, oabc)
    res = run_bass_kernel_spmd(_NC_CACHE["R"], maps, core_ids=list(range(NCORE)))
    out = np.zeros((2, S, D), np.float32)
    for c in range(NCORE):
        b, u = divmod(c, 4)
        out[b, 2048 * u:2048 * (u + 1)] = res.results[c]["xout"]
    return out


NQT = S // 512
M_INPUTS = {
    "xT": (D, S), "wfm": (128, 16, 768), "wtm": (128, 16, 520),
    "qpos": (4, 4, S), "kpos": (4, S), "cpos": (4, 512),
    "cmask": (5, 128, 512), "dmask": (8, 128, 512), "E_all": (128, 32, 128), "selmap": (128, 4, 129),
    "keep": (128, 256), "addm": (128, 256), "ident": (128, 128), "tri2": (128, 64),
    "cw_k": (64, 32, 64), "cw_v": (64, 32, 64), "cp_k": (64, 32), "cp_v": (64, 32),
    "lblog": (128, 2), "lbsel": (128, 1), "normg": (128, 128),
    "poolP": (128, 3, 128), "poolw": (128, 128), "poolscale": (128, 128),
}


def build_M(nqt=NQT, ST=(1, 2, 3, 4, 5, 6, 7, 8, 9, 10, 11)):
    nc = bass.Bass("TRN2", target_bir_lowering=False)
    I = {k: _dram(nc, k, v) for k, v in M_INPUTS.items()}
    om = _dram(nc, "om", (S, 384), kind="ExternalOutput")
    with contextlib.ExitStack() as es:
        s = Sched(nc, es)
        B = s.buf

        def PB(name, shape, dt):
            return B(name, shape, dt, psum=True)

        psr = Ring([PB("psr%d" % i, [128, 512], F32) for i in range(2)])
        pst = PB("pst", [128, 8, 128], BF16)
        oacc_ps = Ring([PB("oacc%d" % i, [128, 512], F32) for i in range(2)])
        impb = PB("impb", [128, 512], F32)
        misc = PB("misc", [128, 512], F32)
        denb = misc
        hg = PB("hg", [128, 4, 128], F32)

        wfm = B("wfm", [128, 16, 768], BF16)
        wtm = B("wtm", [128, 16, 520], BF16)
        for kc in range(16):
            s.dma("pool", wfm[:, kc, :], I["wfm"][:, kc, :], writes=[wfm])
            s.dma("pool", wtm[:, kc, :], I["wtm"][:, kc, :], writes=[wtm])
        identb = B("identb", [128, 128], BF16)
        identf = B("identf", [128, 128], F32)
        s.dma("pool", identb[:], I["ident"][:, :], writes=[identb])
        s.dma("sp", identf[:], I["ident"][:, :], writes=[identf])
        cmask = B("cmask", [128, 5, 512], BF16)
        dmask = B("dmask", [128, 8, 512], BF16)
        for i in range(5):
            s.dma("pool", cmask[:, i, :], I["cmask"][i, :, :], writes=[cmask])
        for i in range(8):
            s.dma("pool", dmask[:, i, :], I["dmask"][i, :, :], writes=[dmask])
        E_all = B("E_all", [128, 32, 128], BF16)
        s.dma("pool", E_all[:], I["E_all"][:, :, :], writes=[E_all])
        selmap = B("selmap", [128, 4, 144], BF16)
        s.dma("pool", selmap[:, :, 0:129], I["selmap"][:, :, :], writes=[selmap])
        keep = B("keep", [128, 256], F32)
        addm = B("addm", [128, 256], F32)
        tri2 = B("tri2", [128, 64], F32)
        s.dma("sp", keep[:], I["keep"][:, :], writes=[keep])
        s.dma("sp", addm[:], I["addm"][:, :], writes=[addm])
        s.dma("sp", tri2[:], I["tri2"][:, :], writes=[tri2])
        cw_k = B("cw_k", [64, 32, 64], BF16)
        cw_v = B("cw_v", [64, 32, 64], BF16)
        cp_k = B("cp_k", [64, 32], BF16)
        cp_v = B("cp_v", [64, 32], BF16)
        s.dma("pool", cw_k[:], I["cw_k"][:, :, :], writes=[cw_k])
        s.dma("pool", cw_v[:], I["cw_v"][:, :, :], writes=[cw_v])
        s.dma("pool", cp_k[:], I["cp_k"][:, :], writes=[cp_k])
        s.dma("pool", cp_v[:], I["cp_v"][:, :], writes=[cp_v])
        normg = B("normg", [128, 128], F32)
        poolP = B("poolP", [128, 3, 128], F32)
        poolw = B("poolw", [128, 128], BF16)
        poolscale = B("poolscale", [128, 128], F32)
        lbl = B("lbl", [128, 8], F32)
        s.dma("sp", normg[:], I["normg"][:, :], writes=[normg])
        s.dma("sp", poolP[:], I["poolP"][:, :, :], writes=[poolP])
        s.dma("pool", poolw[:], I["poolw"][:, :], writes=[poolw])
        s.dma("sp", poolscale[:], I["poolscale"][:, :], writes=[poolscale])
        s.dma("sp", lbl[:, 0:2], I["lblog"][:, :], writes=[lbl])
        s.dma("sp", lbl[:, 2:3], I["lbsel"][:, :], writes=[lbl])
        s.tt("dve", lbl[:, 3:4], lbl[:, 1:2], lbl[:, 0:1], ALU.subtract, [lbl], [lbl])
        s.act(lbl[:, 4:5], lbl[:, 3:4], AF.Sigmoid, [lbl], [lbl])
        s.tt("dve", lbl[:, 5:6], lbl[:, 4:5], lbl[:, 2:3], ALU.mult, [lbl], [lbl])
        s.ts("dve", lbl[:, 6:7], lbl[:, 5:6], -1.0, 1.0, ALU.mult, ALU.add, [lbl], [lbl])

        kslc = B("kslc", [128, S], BF16)
        s.memset("dve", kslc[64:128, :], 0.0, [kslc])
        s.dma("pool", kslc[64:68, :], I["kpos"][:, :], writes=[kslc])
        kwin = B("kwin", [128, 8, 128], BF16)
        s.memset("dve", kwin[64:128, :, :], 0.0, [kwin])
        vslc = B("vslc", [128, 65, 72], BF16)
        vwin = B("vwin", [128, 10, 72], BF16)
        s.memset("dve", vslc[:], 0.0, [vslc])
        s.memset("dve", vwin[:], 0.0, [vwin])
        s.memset("dve", vslc[:, 0:64, 64:65], 1.0, [vslc])
        s.memset("dve", vwin[:, 0:8, 64:65], 1.0, [vwin])
        kcaug = B("kcaug", [128, 512], BF16)
        s.memset("dve", kcaug[:], 0.0, [kcaug])
        s.dma("pool", kcaug[64:68, :], I["cpos"][:, :], writes=[kcaug])
        vcaug = B("vcaug", [128, 6, 72], BF16)
        s.memset("dve", vcaug[:], 0.0, [vcaug])
        s.memset("dve", vcaug[:, 0:4, 64:65], 1.0, [vcaug])
        kcraw = B("kcraw", [64, 528], BF16)
        vcraw = B("vcraw", [64, 528], BF16)
        s.memset("dve", kcraw[:], 0.0, [kcraw])
        s.memset("dve", vcraw[:], 0.0, [vcraw])
        cbias = B("cbias", [64, 2], F32)
        for (cw_, cp_, col) in ((cw_k, cp_k, 0), (cw_v, cp_v, 1)):
            p = psr.next()
            for l_ in range(32):
                s.mm(p[0:64, 0:1], cw_[:, l_, :], cp_[:, l_:l_ + 1], l_ == 0, l_ == 31, [cw_, cp_], [p])
            s.copy("dve", cbias[:, col:col + 1], p[0:64, 0:1], [p], [cbias])

        state = B("state", [128, 128], F32)
        stbf = Ring([B("stbf%d" % i, [128, 128], BF16) for i in range(2)])
        s.memset("dve", state[:], 0.0, [state])
        st_cur = stbf.next()
        s.memset("dve", st_cur[:], 0.0, [st_cur])
        qbpad = B("qbpad", [128, 4, 2, 128], BF16)
        s.memset("pool", qbpad[:], 0.0, [qbpad])
        atpad = Ring([B("atpad%d" % i, [128, 128], BF16) for i in range(2)])
        for a_ in atpad.bufs:
            s.memset("pool", a_[:], 0.0, [a_])
        pczero = B("pczero", [128, 128], F32)
        s.memset("pool", pczero[:], 0.0, [pczero])

        xtile = Ring([B("xtile%d" % i, [128, 16, 512], BF16) for i in range(1)])
        qaug = [Ring([B("qaug%d_%d" % (j, i), [128, 512], BF16) for i in range(2)]) for j in range(4)]
        for r_ in qaug:
            for b_ in r_.bufs:
                s.memset("dve", b_[64:128, :], 0.0, [b_])
        hqT = B("hqT", [128, 512], F32)
        hzT = B("hzT", [128, 512], F32)
        gate = Ring([B("gate%d" % i, [128, 8], F32) for i in range(8)])
        pcb = Ring([B("pcb%d" % i, [128, 128], F32) for i in range(6)])
        vvb = Ring([B("vvb%d" % i, [128, 128], BF16) for i in range(4)])
        sgb = Ring([B("sgb%d" % i, [128, 128], F32) for i in range(4)])
        outb = Ring([B("outb%d" % i, [128, 384], F32) for i in range(4)])
        ET = [[B("ET%d_%d" % (j, c), [128, 512], BF16) for c in range(4)] for j in range(2)]
        PT = Ring([B("PT%d" % i, [128, 512], BF16) for i in range(4)])
        impacc = B("impacc", [128, 4, 128], F32)
        rden = Ring([B("rden%d" % i, [128, 4], F32) for i in range(4)])
        selw = Ring([B("selw%d" % i, [128, 128], F32) for i in range(3)])
        selb = Ring([B("selb%d" % i, [128, 128], BF16) for i in range(4)])
        m8 = Ring([B("m8_%d" % i, [128, 16], F32) for i in range(2)])
        mbT = Ring([B("mbT%d" % i, [128, 512], BF16) for i in range(2)])
        oTs = Ring([B("oTs%d" % i, [65, 512], F32) for i in range(2)])
        coef = Ring([B("coef%d" % i, [128, 4], F32) for i in range(4)])
        vcT = Ring([B("vcT%d" % i, [64, 32], BF16) for i in range(2)])
        vct = Ring([B("vct%d" % i, [32, 64], BF16) for i in range(2)])
        hw_ = [B("hw%d" % i, [128, 512], F32) for i in range(7)]
        hb_ = [B("hb%d" % i, [128, 512], BF16) for i in range(4)]
        hs_ = Ring([B("hs%d" % i, [128, 16], F32) for i in range(2)])
        kdb = Ring([B("kdb%d" % i, [128, 128], BF16) for i in range(2)])
        hst = Ring([B("hst%d" % i, [128, 4], F32) for i in range(4)])
        hsq = B("hsq", [128, 128], F32)
        hsq2 = B("hsq2", [128, 128], F32)
        ptb = Ring([B("ptb%d" % i, [128, 128], BF16) for i in range(2)])
        pc_prev = pczero
        trs = Ring([B("trs%d" % i, [128, 4, 65], F32) for i in range(1)])
        mbh = [Ring([B("mbh%d_%d" % (h, i), [128, 512], BF16) for i in range(2)]) for h in range(2)]
        for h_ in range(2):
            for b_ in mbh[h_].bufs:
                s.memset("dve", b_[:], 0.0, [b_])
        imps = Ring([B("imps%d" % i, [128, 4, 128], F32) for i in range(1)])
        asb = Ring([B("asb%d" % i, [128, 128], F32) for i in range(2)])
        stgA = Ring([B("stgA%d" % i, [128, 136], F32) for i in range(1)])
        stgB = Ring([B("stgB%d" % i, [128, 384], F32) for i in range(1)])

        vslc_f = vslc[:].rearrange("p a b -> p (a b)")
        vwin_f = vwin[:].rearrange("p a b -> p (a b)")
        vcaug_f = vcaug[:].rearrange("p a b -> p (a b)")
        psa = Ring([psr.bufs[0], psr.bufs[1], impb])
        JA = 0
        trf = Buf(misc.t)
        trv = misc.t[:, 128:388].rearrange("p (a b) -> p a b", a=4)
        hstate = {"cur": st_cur}

        def epilogue(oa, hl, br, gts, obs):
            o_sb = oTs.next()
            s.copy("act", o_sb[:], oa[0:65, :], [oa], [o_sb])
            for tb in range(4):
                s.tr(trv[:, tb, :], o_sb[:, tb * 128:(tb + 1) * 128], identf[0:65, 0:65], [o_sb, identf], [trf])
            tv = trs.next()
            s.copy("dve", tv[:], trv[:, :, :], [trf], [tv])
            cf = coef.next()
            s.ts("dve", cf[:], tv[:, :, 64], 1e-30, None, ALU.max, None, [tv], [cf])
            s.recip(cf[:], cf[:], [cf], [cf])
            for tb in range(4):
                s.tt("dve", cf[:, tb:tb + 1], cf[:, tb:tb + 1], gts[tb][:, hl * 3 + br:hl * 3 + br + 1], ALU.mult, [cf, gts[tb]], [cf])
            for tb in range(4):
                dst = obs[tb][:, hl * 64:(hl + 1) * 64]
                if br == 0:
                    s.ts("dve", dst, tv[:, tb, 0:64], cf[:, tb:tb + 1], None, ALU.mult, None, [tv, cf], [obs[tb]])
                else:
                    s.stt("dve", dst, tv[:, tb, 0:64], cf[:, tb:tb + 1], dst, ALU.mult, ALU.add, [tv, cf, obs[tb]], [obs[tb]])

        def hgrn_tile(qt, vvs, sgs, obs):
            W = hw_
            s.act(W[0][:], hzT[:], AF.Sigmoid, [hzT], [W[0]])
            s.ts("dve", W[0][:], W[0][:], lbl[:, 6:7], lbl[:, 5:6], ALU.mult, ALU.add, [W[0], lbl], [W[0]])
            s.ts("dve", W[0][:], W[0][:], 1e-6, None, ALU.max, None, [W[0]], [W[0]])
            s.ts("dve", W[1][:], W[0][:], -1.0, 1.0, ALU.mult, ALU.add, [W[0]], [W[1]])
            s.act(W[2][:], W[0][:], AF.Ln, [W[0]], [W[2]])
            yield
            src, dst = W[2], W[3]
            for sh in (1, 2, 4, 8, 16, 32):
                sv = src[:].rearrange("p (c t) -> p c t", t=64)
                dv = dst[:].rearrange("p (c t) -> p c t", t=64)
                s.copy("pool", dv[:, :, 0:sh], sv[:, :, 0:sh], [src], [dst])
                s.tt("dve", dv[:, :, sh:64], sv[:, :, sh:64], sv[:, :, 0:64 - sh], ALU.add, [src], [dst])
                src, dst = dst, src
                yield
            b = src
            bv = b[:].rearrange("p (c t) -> p c t", t=64)
            hs = hs_.next()
            nh = hs_.next()
            s.copy("dve", hs[:, 0:8], bv[:, :, 31], [b], [hs])
            s.copy("dve", hs[:, 8:16], bv[:, :, 63], [b], [hs])
            s.ts("dve", nh[:, 0:8], hs[:, 0:8], -1.0, None, ALU.mult, None, [hs], [nh])
            s.act(nh[:, 8:16], hs[:, 8:16], AF.Exp, [hs], [nh])
            yield
            for c in range(8):
                cs = slice(c * 64, (c + 1) * 64)
                s.act(W[3][:, cs], b[:, cs], AF.Exp, [b, nh], [W[3]], bias=nh[:, c:c + 1])
                s.act(W[4][:, cs], b[:, cs], AF.Exp, [b, hs], [W[4]], bias=hs[:, c:c + 1], scale=-1.0)
                s.act(W[5][:, cs], b[:, cs], AF.Exp, [b, hs], [W[5]], bias=hs[:, 8 + c:9 + c], scale=-1.0)
                yield
            s.act(W[6][:], b[:], AF.Exp, [b], [W[6]])
            s.tt("pool", hb_[0][:], hqT[:], W[3][:], ALU.mult, [hqT, W[3]], [hb_[0]])
            s.tt("dve", hb_[1][:], W[1][:], W[4][:], ALU.mult, [W[1], W[4]], [hb_[1]])
            s.tt("pool", hb_[2][:], W[1][:], W[5][:], ALU.mult, [W[1], W[5]], [hb_[2]])
            yield
            hq4 = hqT[:].rearrange("p (a c t) -> p a c t", a=4, c=2)
            eb4 = W[6][:].rearrange("p (a c t) -> p a c t", a=4, c=2)
            s.tt("dve", qbpad[:, :, 0, 0:64], hq4[:, :, 0, :], eb4[:, :, 0, :], ALU.mult, [hqT, W[6]], [qbpad])
            s.tt("pool", qbpad[:, :, 1, 64:128], hq4[:, :, 1, :], eb4[:, :, 1, :], ALU.mult, [hqT, W[6]], [qbpad])
            yield
            for tb in range(4):
                blk = slice(tb * 128, (tb + 1) * 128)
                s.mm(hg[:, 0, :], hb_[1][:, blk], hb_[0][:, blk], True, True, [hb_[1], hb_[0]], [hg])
                at = atpad.next()
                a_sb = asb.next()
                s.copy("dve", a_sb[:], hg[:, 0, :], [hg], [a_sb])
                s.tt("pool", at[0:64, 0:64], a_sb[0:64, 0:64], tri2[0:64, :], ALU.mult, [a_sb, tri2], [at])
                s.tt("pool", at[64:128, 64:128], a_sb[64:128, 64:128], tri2[64:128, :], ALU.mult, [a_sb, tri2], [at])
                yield
                s.tr(pst[:, 5, :], hb_[2][:, blk], identb[:], [hb_[2], identb], [pst])
                kb = kdb.next()
                s.copy("act", kb[:], pst[:, 5, :], [pst], [kb])
                yield
                st0 = hstate["cur"]
                s.mm(hg[:, 1, :], qbpad[:, tb, 0, :], st0[:], True, False, [qbpad, st0], [hg])
                s.mm(hg[:, 1, :], at[:], vvs[tb][:], False, True, [at, vvs[tb]], [hg])
                s.mm(hg[:, 2, :], kb[0:64, :], vvs[tb][0:64, :], True, True, [kb, vvs[tb]], [hg])
                yield
                s.stt("dve", state[:], state[:], nh[:, 8 + 2 * tb:9 + 2 * tb], hg[:, 2, :], ALU.mult, ALU.add, [state, nh, hg], [state])
                st1 = stbf.next()
                s.copy("act", st1[:], state[:], [state], [st1])
                yield
                oa_sb = hsq
                s.copy("dve", oa_sb[:], hg[:, 1, :], [hg], [oa_sb])
                s.mm(hg[:, 3, :], qbpad[:, tb, 1, :], st1[:], True, True, [qbpad, st1], [hg])
                s.tt("dve", oa_sb[:], oa_sb[:], hg[:, 3, :], ALU.add, [oa_sb, hg], [oa_sb])
                yield
                s.mm(hg[:, 2, :], kb[64:128, :], vvs[tb][64:128, :], True, True, [kb, vvs[tb]], [hg])
                s.stt("dve", state[:], state[:], nh[:, 9 + 2 * tb:10 + 2 * tb], hg[:, 2, :], ALU.mult, ALU.add, [state, nh, hg], [state])
                st2 = stbf.next()
                s.copy("act", st2[:], state[:], [state], [st2])
                yield
                hstate["cur"] = st2
                t_ = hst.next()
                s.op("act", lambda e, a=(hsq2, oa_sb, t_): e.activation(out=a[0][:], in_=a[1][:], func=AF.Square, accum_out=a[2][:, 0:1]), [oa_sb], [hsq2, t_])
                s.ts("dve", t_[:, 1:2], t_[:, 0:1], 1.0 / 128, EPS, ALU.mult, ALU.add, [t_], [t_])
                s.act(t_[:, 2:3], t_[:, 1:2], AF.Sqrt, [t_], [t_])
                s.recip(t_[:, 3:4], t_[:, 2:3], [t_], [t_])
                yield
                s.stt("dve", obs[tb][:, 128:256], oa_sb[:], t_[:, 3:4], normg[:], ALU.mult, ALU.mult, [oa_sb, t_, normg], [obs[tb]])
                s.tt("pool", obs[tb][:, 128:256], obs[tb][:, 128:256], sgs[tb][:], ALU.mult, [obs[tb], sgs[tb]], [obs[tb]])

        g1w, g2w = 136, 384

        xtk = [Buf(xtile.bufs[0].t) for _ in range(16)]

        def load_x(qt_):
            for kc in range(16):
                s.dma("pool", xtile.bufs[0][:, kc, :], I["xT"][kc * 128:(kc + 1) * 128, qt_ * 512:(qt_ + 1) * 512], writes=[xtk[kc]])

        for qt in range(nqt):
            q0 = qt * 512
            s.mute = 1 not in ST
            xt = xtile.next()
            if qt == 0:
                load_x(0)
            qa = [qaug[j].next() for j in range(4)]
            for j in range(4):
                s.dma("pool", qa[j][64:68, :], I["qpos"][j, :, q0:q0 + 512], writes=[qa[j]])
            slot0 = (4 * qt) % 8
            s.dma("pool", kwin[64:68, slot0:slot0 + 4, :], I["kpos"][:, q0:q0 + 512].rearrange("r (a b) -> r a b", a=4), writes=[kwin])
            if qt > 0:
                s.copy("dve", kcraw[:, 0:16], kcraw[:, 512:528], [kcraw], [kcraw])
                s.copy("dve", vcraw[:, 0:16], vcraw[:, 512:528], [vcraw], [vcraw])

            s.mute = 2 not in ST
            def fm(col0, M):
                p = psr.next()
                for kc in range(16):
                    s.mm(p[0:M, :], wfm[:, kc, col0:col0 + M], xt[:, kc, :], kc == 0, kc == 15, [wfm, xtk[kc]], [p])
                return p
            for j in range(4):
                p = fm(j * 64, 64)
                s.act(qa[j][0:64, :], p[0:64, :], AF.Identity, [p], [qa[j]], scale=0.125)
            p = fm(256, 64)
            s.copy("act", kcraw[:, 16:528], p[0:64, :], [p], [kcraw])
            p = fm(320, 64)
            s.copy("dve", kslc[0:64, q0:q0 + 512], p[0:64, :], [p], [kslc])
            p = fm(384, 64)
            s.copy("act", kwin[0:64, slot0:slot0 + 4, :], p[0:64, :].rearrange("p (a b) -> p a b", a=4), [p], [kwin])
            p = fm(448, 64)
            s.copy("dve", vcraw[:, 16:528], p[0:64, :], [p], [vcraw])
            p = fm(512, 128)
            s.copy("act", hqT[:], p[:], [p], [hqT])
            p = fm(640, 128)
            s.copy("dve", hzT[:], p[:], [p], [hzT])

            s.mute = 3 not in ST
            gts, pcs, vvs, sgs, obs = [], [], [], [], []
            for tb in range(4):
                kt = 4 * qt + tb
                p = psr.next()
                for kc in range(16):
                    s.mm(p[:, 0:g1w], xt[:, kc, tb * 128:(tb + 1) * 128], wtm[:, kc, 0:g1w], kc == 0, kc == 15, [wtm, xtk[kc]], [p])
                sA = stgA.next()
                s.copy("dve", sA[:, 0:g1w], p[:, 0:g1w], [p], [sA])
                s.copy("pool", vslc[:, kt, 0:64], sA[:, 0:64], [sA], [vslc])
                s.copy("pool", vwin[:, kt % 8, 0:64], sA[:, 64:128], [sA], [vwin])
                g_ = gate.next()
                s.act(g_[:], sA[:, 128:136], AF.Sigmoid, [sA], [g_])
                gts.append(g_)
                p = psr.next()
                for kc in range(16):
                    s.mm(p[:, 0:g2w], xt[:, kc, tb * 128:(tb + 1) * 128], wtm[:, kc, g1w:g1w + g2w], kc == 0, kc == 15, [wtm, xtk[kc]], [p])
                pc = pcb.next()
                vv = vvb.next()
                sg = sgb.next()
                sB = stgB.next()
                s.copy("act", sB[:, 0:g2w], p[:, 0:g2w], [p], [sB])
                s.copy("pool", pc[:], sB[:, 0:128], [sB], [pc])
                s.copy("dve", vv[:], sB[:, 128:256], [sB], [vv])
                s.act(sg[:], sB[:, 256:384], AF.Silu, [sB], [sg])
                pcs.append(pc)
                vvs.append(vv)
                sgs.append(sg)
                obs.append(outb.next())

            if qt + 1 < nqt and 1 in ST:
                s.mute = False
                load_x(qt + 1)
            s.mute = 4 not in ST
            c_lo = 32 * qt - 1 if qt > 0 else 0
            c_hi = 32 * qt + 30
            ncb = c_hi - c_lo + 1
            i0 = 0 if qt > 0 else 1
            kview = kcraw[:].rearrange("p (c s) -> p c s", s=16)
            vview = vcraw[:].rearrange("p (c s) -> p c s", s=16)
            p = psr.next()
            for l_ in range(32):
                s.mm(p[0:64, 0:ncb], cw_k[:, l_, :], kview[:, i0 + l_ // 16:i0 + l_ // 16 + ncb, l_ % 16], l_ == 0, l_ == 31, [cw_k, kcraw], [p])
            s.act(kcaug[0:64, c_lo:c_hi + 1], p[0:64, 0:ncb], AF.Identity, [p, cbias], [kcaug], bias=cbias[:, 0:1])
            p = psr.next()
            for l_ in range(32):
                s.mm(p[0:64, 0:ncb], cw_v[:, l_, :], vview[:, i0 + l_ // 16:i0 + l_ // 16 + ncb, l_ % 16], l_ == 0, l_ == 31, [cw_v, vcraw], [p])
            vT_ = vcT.next()
            s.act(vT_[:, 0:ncb], p[0:64, 0:ncb], AF.Identity, [p, cbias], [vT_], bias=cbias[:, 1:2])
            s.tr(pst[0:ncb, 0, 0:64], vT_[:, 0:ncb], identb[0:64, 0:64], [vT_, identb], [pst])
            vt_ = vct.next()
            s.copy("dve", vt_[0:ncb, :], pst[0:ncb, 0, 0:64], [pst], [vt_])
            c = c_lo
            while c <= c_hi:
                ct_ = c // 128
                n_ = min(c_hi + 1, (ct_ + 1) * 128) - c
                s.dma("sp", vcaug[c % 128:c % 128 + n_, ct_, 0:64], vt_[c - c_lo:c - c_lo + n_, :], reads=[vt_], writes=[vcaug])
                c += n_

            hgen = hgrn_tile(qt, vvs, sgs, obs) if 9 in ST else iter(())
            s.mute = 5 not in ST
            nct = (32 * qt + 30) // 128 + 1
            oc_ps = {}
            for j in range(4):
                mine = j in (JA, JA + 1)
                ets = []
                for ct_ in range(nct):
                    dl = qt - 4 * ct_
                    p = psr.next()
                    last = dl > 4
                    s.mm(p[:], kcaug[:, ct_ * 128:(ct_ + 1) * 128], qa[j][:, :], True, last, [kcaug, qa[j]], [p])
                    if not last:
                        s.mm(p[:], identb[:], cmask[:, dl, :], False, True, [identb, cmask], [p])
                    e_ = ET[j % 2][ct_]
                    s.act(e_[:], p[:], AF.Exp, [p], [e_])
                    ets.append(e_)
                if mine:
                    oa = oacc_ps.next()
                    for ct_ in range(nct):
                        s.mm(oa[:, :], vcaug_f[:, ct_ * 72:ct_ * 72 + 128], ets[ct_][:], ct_ == 0, ct_ == nct - 1, [vcaug, ets[ct_]], [oa])
                    oc_ps[j] = oa
                for tb in range(4):
                    for ct_ in range(nct):
                        s.mm(impb[:, tb * 128:(tb + 1) * 128], ets[ct_][:, tb * 128:(tb + 1) * 128], selmap[:, ct_, 0:128], ct_ == 0, ct_ == nct - 1, [ets[ct_], selmap], [impb])
                for tb in range(4):
                    for ct_ in range(nct):
                        s.mm(denb[:, tb:tb + 1], ets[ct_][:, tb * 128:(tb + 1) * 128], selmap[:, ct_, 128:129], ct_ == 0, ct_ == nct - 1, [ets[ct_], selmap], [denb])
                rd = rden.next()
                s.ts("dve", rd[:], denb[:, 0:4], 1e-30, None, ALU.max, None, [denb], [rd])
                s.recip(rd[:], rd[:], [rd], [rd])
                im = imps.next()
                s.copy("act", im[:].rearrange("p a b -> p (a b)"), impb[:], [impb], [im])
                for tb in range(4):
                    if j == 0:
                        s.ts("dve", impacc[:, tb, :], im[:, tb, :], rd[:, tb:tb + 1], None, ALU.mult, None, [im, rd], [impacc])
                    else:
                        s.stt("dve", impacc[:, tb, :], im[:, tb, :], rd[:, tb:tb + 1], impacc[:, tb, :], ALU.mult, ALU.add, [im, rd, impacc], [impacc])
                if mine:
                    epilogue(oc_ps[j], j - JA, 0, gts, obs)

            s.mute = 6 not in ST
            mb = mbT.next()
            sbl = []
            for tb in range(4):
                tbg = 4 * qt + tb
                c0 = 126 - 2 * tbg
                w = selw.next()
                s.tt("dve", w[:], impacc[:, tb, :], keep[:, c0:c0 + 128], ALU.mult, [impacc, keep], [w])
                s.tt("dve", w[:], w[:], addm[:, c0:c0 + 128], ALU.add, [w, addm], [w])
                s.memset("dve", w[:, 0:1], 1e6, [w])
                m = m8.next()
                w2 = selw.next()
                s.op("dve", lambda e, a=(m, w): e.max(out=a[0][:, 0:8], in_=a[1][:]), [w], [m])
                s.op("dve", lambda e, a=(w2, m, w): e.match_replace(out=a[0][:], in_to_replace=a[1][:, 0:8], in_values=a[2][:], imm_value=-3e38), [w, m], [w2])
                s.op("dve", lambda e, a=(m, w2): e.max(out=a[0][:, 8:16], in_=a[1][:]), [w2], [m])
                s.ts("dve", w2[:], w[:], m[:, 15:16], None, ALU.subtract, None, [w, m], [w2])
                s.ts("dve", w2[:], w2[:], 0.0, None, ALU.is_ge, None, [w2], [w2])
                sb_ = selb.next()
                s.ts("dve", sb_[:], w2[:], -NEG, NEG, ALU.mult, ALU.add, [w2], [sb_])
                sbl.append(sb_)

            s.mute = 8 not in ST
            for hl in range(2):
                j = JA + hl
                oa = oacc_ps.next()
                k_lo = max(0, 4 * qt - 4)
                nk = 4 * qt + 4
                pend = None
                for kt in range(k_lo, nk):
                    p = psa.next()
                    s.mm(p[:], kwin[:, kt % 8, :], qa[j][:, :], True, False, [kwin, qa[j]], [p])
                    s.mm(p[:], identb[:], dmask[:, kt - 4 * qt + 4, :], False, True, [identb, dmask], [p])
                    pt = PT.next()
                    s.act(pt[:], p[:], AF.Exp, [p], [pt])
                    if pend is not None:
                        s.mm(oa[:, :], vwin_f[:, (pend[0] % 8) * 72:(pend[0] % 8) * 72 + 128], pend[1][:], pend[0] == k_lo, False, [vwin, pend[1]], [oa])
                    pend = (kt, pt)
                s.mm(oa[:, :], vwin_f[:, (pend[0] % 8) * 72:(pend[0] % 8) * 72 + 128], pend[1][:], pend[0] == k_lo, True, [vwin, pend[1]], [oa])
                epilogue(oa, hl, 2, gts, obs)

            s.mute = 6 not in ST
            for tb in range(4):
                s.tr(pst[:, 1 + tb, :], sbl[tb][:], identb[:], [sbl[tb], identb], [pst])
            s.copy("act", mb[:], pst[:, 1:5, :].rearrange("p a b -> p (a b)"), [pst], [mb])
            mh = [mbh[0].next(), mbh[1].next()]
            s.copy("pool", mh[0][0:64, :], mb[0:64, :], [mb], [mh[0]])
            s.copy("pool", mh[1][64:128, :], mb[64:128, :], [mb], [mh[1]])

            s.mute = 7 not in ST
            for hl in range(2):
                j = JA + hl
                oa = oacc_ps.next()
                nk = 4 * qt + 4
                pend = None
                for kt in range(nk):
                    p = psa.next()
                    s.mm(p[:], kslc[:, kt * 128:(kt + 1) * 128], qa[j][:, :], True, False, [kslc, qa[j]], [p])
                    diag = kt >= 4 * qt
                    s.mm(p[:], E_all[:, kt % 32, :], mh[kt // 32][:], False, not diag, [E_all, mh[kt // 32]], [p])
                    if diag:
                        s.mm(p[:], identb[:], dmask[:, kt - 4 * qt + 4, :], False, True, [identb, dmask], [p])
                    pt = PT.next()
                    s.act(pt[:], p[:], AF.Exp, [p], [pt])
                    if pend is not None:
                        s.mm(oa[:, :], vslc_f[:, pend[0] * 72:pend[0] * 72 + 128], pend[1][:], pend[0] == 0, False, [vslc, pend[1]], [oa])
                    pend = (kt, pt)
                    if kt % 2 == 1:
                        _m = s.mute
                        s.mute = 9 not in ST
                        next(hgen, None)
                        s.mute = _m
                s.mm(oa[:, :], vslc_f[:, pend[0] * 72:pend[0] * 72 + 128], pend[1][:], pend[0] == 0, True, [vslc, pend[1]], [oa])
                epilogue(oa, hl, 1, gts, obs)

            s.mute = 9 not in ST
            for _ in hgen:
                pass

            s.mute = 10 not in ST
            for tb in range(4):
                first = (qt == 0 and tb == 0)
                p = psr.next()
                s.mm(p[:, 0:128], pcs[tb][:], poolP[:, 2 if first else 0, :], True, False, [pcs[tb], poolP], [p])
                s.mm(p[:, 0:128], pc_prev[:], poolP[:, 1, :], False, True, [pc_prev, poolP], [p])
                pt_ = ptb.next()
                s.copy("act", pt_[:], p[:, 0:128], [p], [pt_])
                p2 = psr.next()
                s.mm(p2[:, 0:128], pt_[:], poolw[:], True, True, [pt_, poolw], [p2])
                s.tt("dve", obs[tb][:, 256:384], p2[:, 0:128], poolscale[:], ALU.mult, [p2, poolscale], [obs[tb]])
                pc_prev = pcs[tb]

            s.mute = 11 not in ST
            for tb in range(4):
                s.dma("sp", om[q0 + tb * 128:q0 + (tb + 1) * 128, :], obs[tb][:], reads=[obs[tb]])
        s.emit()
        print("M program: ops", s.n, "sbuf peak", s.peak, "gate/pcb/vvb/sgb/outb offs", [x.bufs[0].t.manual_sbuf_range for x in (gate, pcb, vvb, sgb, outb)], flush=True)
    return nc


def _m_consts():
    cst = {}
    t = np.arange(S)
    cst["kpos"] = np.stack([np.ones(S), np.ones(S), t // 64, t % 64]).astype(np.float32)
    cp = 16 * np.arange(512) + 31
    cst["cpos"] = np.stack([np.ones(512), np.ones(512), cp // 64, cp % 64]).astype(np.float32)
    cl = np.arange(128)[:, None]
    tl = np.arange(512)[None, :]
    cst["cmask"] = np.stack([np.where(16 * cl + 31 - tl <= 512 * dl, 0.0, NEG) for dl in range(5)]).astype(np.float32)
    dm = []
    for rel in range(-4, 4):
        dist = tl - cl - 128 * rel
        dm.append(np.where((dist >= 0) & (dist < 512), 0.0, NEG))
    cst["dmask"] = np.stack(dm).astype(np.float32)
    rr = (np.arange(128) % 64)[:, None, None]
    cst["E_all"] = (rr == 2 * np.arange(32)[None, :, None] + (np.arange(128)[None, None, :] // 64)).astype(np.float32)
    c0 = np.arange(511)[:, None] * 16
    s0 = np.arange(128)[None, :] * 64
    ov = np.clip(np.minimum(c0 + 32, s0 + 64) - np.maximum(c0, s0), 0, None) / 32.0
    sm = np.zeros((512, 129), np.float32)
    sm[:511, :128] = ov
    sm[:, 128] = 1.0
    cst["selmap"] = _c(sm.reshape(4, 128, 129).transpose(1, 0, 2))
    r = np.arange(256)[None, :] - 126
    hi = (np.arange(128)[:, None] >= 64).astype(np.int64)
    forced = (r == hi) | (r == hi - 1)
    future = r >= hi + 1
    cst["keep"] = np.where(forced | future, 0.0, 1.0).astype(np.float32)
    cst["addm"] = np.where(forced, 1e6, np.where(future, -1e30, 0.0)).astype(np.float32)
    cst["ident"] = np.eye(128, dtype=np.float32)
    cst["tri2"] = ((np.arange(128)[:, None] % 64) <= np.arange(64)[None, :]).astype(np.float32)
    return cst


def prep_M(inp, l, x):
    cst = _m_consts()
    w_in = inp["w_in"][l]
    maps = []
    xTs = [_c(x[b].T) for b in range(2)]
    tt_ = np.arange(S)
    for c in range(NCORE):
        b, u = divmod(c, 4)
        g = u // 2
        ja = 2 * (u % 2)
        heads = [ja, ja + 1] + [j for j in range(4) if j not in (ja, ja + 1)]
        m = dict(cst)
        m["xT"] = xTs[b]

        def kvcol(br, kv):
            return OFF_KV + ((br * 2 + kv) * 2 + g) * 64
        cols = []
        for j in heads:
            cols += list(range(OFF_Q + (g * 4 + j) * 64, OFF_Q + (g * 4 + j + 1) * 64))
        for (br, kv) in ((0, 0), (1, 0), (2, 0), (0, 1)):
            cols += list(range(kvcol(br, kv), kvcol(br, kv) + 64))
        cols += list(range(OFF_H + u * 128, OFF_H + (u + 1) * 128))
        cols += list(range(OFF_H + 512 + u * 128, OFF_H + 512 + (u + 1) * 128))
        m["wfm"] = _c(w_in[:, cols].reshape(16, 128, 768).transpose(1, 0, 2))
        cols = list(range(kvcol(1, 1), kvcol(1, 1) + 64)) + list(range(kvcol(2, 1), kvcol(2, 1) + 64))
        gcols = [OFF_G + (g * 4 + j) * 3 + br for j in (ja, ja + 1) for br in range(3)]
        cols += gcols + [gcols[0], gcols[0]]
        cols += list(range(OFF_P + u * 128, OFF_P + (u + 1) * 128))
        cols += list(range(OFF_H + 1024 + u * 128, OFF_H + 1024 + (u + 1) * 128))
        cols += list(range(OFF_H + 1536 + u * 128, OFF_H + 1536 + (u + 1) * 128))
        m["wtm"] = _c(w_in[:, cols].reshape(16, 128, 520).transpose(1, 0, 2))
        qp = np.zeros((4, 4, S), np.float32)
        for i, j in enumerate(heads):
            sl = 2.0 ** (-(g * 4 + j + 1))
            qp[i, 0] = -64.0 * sl * (tt_ // 64)
            qp[i, 1] = -sl * (tt_ % 64)
            qp[i, 2] = 64.0 * sl
            qp[i, 3] = sl
        m["qpos"] = qp
        m["cw_k"] = _c(inp["nsa_cmp_w"][l][0].transpose(1, 0, 2))
        m["cw_v"] = _c(inp["nsa_cmp_w"][l][1].transpose(1, 0, 2))
        m["cp_k"] = _c(inp["nsa_cmp_pos"][l][0].T)
        m["cp_v"] = _c(inp["nsa_cmp_pos"][l][1].T)
        m["lblog"] = _c(inp["hgrn_lb_logits"][:, u * 128:(u + 1) * 128].T)
        m["lbsel"] = np.full((128, 1), float(l), np.float32)
        m["normg"] = _bc(inp["hgrn_norm_g"][l][u * 128:(u + 1) * 128])
        win = (2, 4, 8, 16)[u]
        sI = np.arange(128)[:, None]
        tI = np.arange(128)[None, :]
        P = np.zeros((128, 3, 128), np.float32)
        P[:, 0, :] = np.where((sI > tI - win) & (sI <= tI), 1.0 / win, 0.0) - (sI == tI)
        P[:, 1, :] = np.where(sI >= 128 + tI - win + 1, 1.0 / win, 0.0)
        cnt = np.minimum(tI + 1, win).astype(np.float32)
        P[:, 2, :] = np.where((sI > tI - win) & (sI <= tI), 1.0 / cnt, 0.0) - (sI == tI)
        m["poolP"] = P
        m["poolw"] = _c(inp["pool_w"][l][u])
        m["poolscale"] = _bc(inp["pool_scale"][l][u * 128:(u + 1) * 128])
        maps.append(m)
    return maps


def run_M(inp, l, x):
    if "M" not in _NC_CACHE:
        _NC_CACHE["M"] = build_M()
    maps = prep_M(inp, l, x)
    res = run_bass_kernel_spmd(_NC_CACHE["M"], maps, core_ids=list(range(NCORE)))
    oabc = np.zeros((2, S, 1536), np.float32)
    for c in range(NCORE):
        b, u = divmod(c, 4)
        o = res.results[c]["om"]
        for k in range(3):
            oabc[b, :, 512 * k + 128 * u:512 * k + 128 * (u + 1)] = o[:, 128 * k:128 * (k + 1)]
    return oabc


def kernel(**inputs):
    inp = {k: np.asarray(v) for k, v in inputs.items()}
    x = np.ascontiguousarray(inp["x"], dtype=np.float32)
    for l in range(2):
        oabc = run_M(inp, l, x)
        x = run_R(inp, l, x, oabc)
    return x
```

```python
import contextlib
import numpy as np
import concourse.bass as bass
import concourse.mybir as mybir
from concourse.bass_utils import run_bass_kernel_spmd

F32 = mybir.dt.float32
BF16 = mybir.dt.bfloat16
AF = mybir.ActivationFunctionType
ALU = mybir.AluOpType
AX = mybir.AxisListType

D = 2048
S = 8192
NCORE = 8
ALPHA = 4 ** 0.25
EPS = 1e-5
DFF = 5632
NEG = -30000.0
SBUF_BASE = 16512
SBUF_CAP = 229344


class Buf:
    __slots__ = ("t", "w", "r")

    def __init__(self, t):
        self.t = t
        self.w = None
        self.r = []

    def __getitem__(self, k):
        return self.t[k]


class Sched:
    CE = ("pe", "act", "dve", "pool")
    DQ = ("sp", "act", "pool")

    def __init__(self, nc, es, ring=8):
        self.nc = nc
        self.es = es
        self.ops = {e: [] for e in ("pe", "act", "dve", "pool", "sp")}
        self.csem = {e: es.enter_context(nc.semaphore("c_" + e)) for e in self.CE}
        self.ccnt = {e: 0 for e in self.CE}
        self.ring = ring
        self.dsem = {q: [es.enter_context(nc.semaphore("d_%s%d" % (q, i))) for i in range(ring)] for q in self.DQ}
        self.dcnt = {q: 0 for q in self.DQ}
        self.dtok = {q: [None] * ring for q in self.DQ}
        self.seen = {e: {} for e in self.ops}
        self.n = 0

    def buf(self, name, shape, dt, psum=False):
        if psum:
            t = self.es.enter_context(self.nc.psum_tensor(name, shape, dt))
            return Buf(t)
        n = 1
        for d_ in shape[1:]:
            n *= d_
        nbytes = n * (2 if dt == BF16 else 4)
        nbytes = (nbytes + 63) // 64 * 64
        self.uid = getattr(self, "uid", 0) + 1
        off = getattr(self, "off", SBUF_BASE)
        assert off + nbytes <= SBUF_CAP, ("SBUF overflow", name, off, nbytes)
        t = self.nc.alloc_sbuf_tensor_at("%s_%d" % (name, self.uid), list(shape), dt, offset=off)
        self.off = off + nbytes
        self.peak = max(getattr(self, "peak", 0), self.off)
        return Buf(t)

    def mark(self):
        return getattr(self, "off", SBUF_BASE)

    def release(self, m):
        self.barrier()
        self.off = m

    def barrier(self):
        toks = []
        for q in self.DQ:
            for t in self.dtok[q]:
                if t is not None:
                    toks.append(t)
        for e in self.CE:
            if self.ccnt[e]:
                toks.append((self.csem[e], self.ccnt[e], "c_" + e, e))
        for eng in self.ops:
            waits = []
            seen = self.seen[eng]
            for (sem, val, key, src) in toks:
                if seen.get(key, 0) >= val:
                    continue
                seen[key] = val
                waits.append((sem, val))
            if waits:
                self.ops[eng].append((waits, None, None, 0))

    def _waits(self, eng, reads, writes, extra=()):
        deps = []
        for b in reads:
            if b.w is not None:
                deps.append(b.w)
        for b in writes:
            if b.w is not None:
                deps.append(b.w)
            deps.extend(b.r)
        deps.extend(extra)
        waits = []
        seen = self.seen[eng]
        for (sem, val, key, src) in deps:
            if src == "pe" and eng == "pe":
                continue
            if seen.get(key, 0) >= val:
                continue
            seen[key] = val
            waits.append((sem, val))
        return waits

    def op(self, eng, fn, reads=(), writes=()):
        if getattr(self, 'mute', False):
            return None
        waits = self._waits(eng, reads, writes)
        self.ccnt[eng] += 1
        tok = (self.csem[eng], self.ccnt[eng], "c_" + eng, eng)
        for b in reads:
            b.r.append(tok)
        for b in writes:
            b.w = tok
            b.r = []
        self.ops[eng].append((waits, fn, self.csem[eng], 1))
        self.n += 1
        return tok

    def dma(self, q, out, in_, reads=(), writes=()):
        if getattr(self, 'mute', False):
            return None
        i = self.dcnt[q]
        slot = i % self.ring
        extra = []
        if self.dtok[q][slot] is not None:
            extra.append(self.dtok[q][slot])
        waits = self._waits(q, reads, writes, extra)
        self.dcnt[q] += 1
        sem = self.dsem[q][slot]
        tok = (sem, 16 * (i // self.ring + 1), "d_%s%d" % (q, slot), "dma_" + q)
        self.dtok[q][slot] = tok
        for b in reads:
            b.r.append(tok)
        for b in writes:
            b.w = tok
            b.r = []
        self.ops[q].append((waits, lambda e, o=out, i_=in_: e.dma_start(out=o, in_=i_), sem, 16))
        self.n += 1
        return tok

    def coll(self, kind, ins, outs, groups, reads=(), writes=()):
        q = "pool"
        i = self.dcnt[q]
        slot = i % self.ring
        extra = []
        if self.dtok[q][slot] is not None:
            extra.append(self.dtok[q][slot])
        waits = self._waits(q, reads, writes, extra)
        self.dcnt[q] += 1
        sem = self.dsem[q][slot]
        tok = (sem, 16 * (i // self.ring + 1), "d_%s%d" % (q, slot), "dma_" + q)
        self.dtok[q][slot] = tok
        for b in reads:
            b.r.append(tok)
        for b in writes:
            b.w = tok
            b.r = []
        self.ops[q].append((waits, lambda e, a=(kind, ins, outs, groups): e.collective_compute(a[0], ALU.bypass, replica_groups=a[3], ins=a[1], outs=a[2]), sem, 16))
        self.n += 1
        return tok

    def finish(self):
        extra = []
        for q in self.DQ:
            for t in self.dtok[q]:
                if t is not None:
                    extra.append(t)
        for e in self.CE:
            if self.ccnt[e]:
                extra.append((self.csem[e], self.ccnt[e], "c_" + e, e))
        waits = self._waits("sp", (), (), extra)
        self.ops["sp"].append((waits, None, None, 0))

    def emit(self):
        self.finish()
        nc = self.nc
        ops = self.ops

        def replay(name, e):
            for waits, fn, sem, inc in ops[name]:
                for (s_, v_) in waits:
                    e.wait_ge(s_, v_)
                if fn is not None:
                    fn(e).then_inc(sem, inc)

        with nc.Block() as block:
            @block.tensor
            def _(e):
                replay("pe", e)

            @block.scalar
            def _(e):
                replay("act", e)

            @block.vector
            def _(e):
                replay("dve", e)

            @block.gpsimd
            def _(e):
                replay("pool", e)

            @block.sync
            def _(e):
                replay("sp", e)

    def mm(self, out, lhsT, rhs, start, stop, reads, writes):
        return self.op("pe", lambda e, a=(out, lhsT, rhs, start, stop): e.matmul(a[0], a[1], a[2], start=a[3], stop=a[4]), reads, writes)

    def tr(self, out, in_, ident, reads, writes):
        return self.op("pe", lambda e, a=(out, in_, ident): e.transpose(a[0], a[1], a[2]), reads, writes)

    def act(self, out, in_, func, reads, writes, bias=None, scale=None, eng="act"):
        kw = {}
        if bias is not None:
            kw["bias"] = bias
        if scale is not None:
            kw["scale"] = scale
        return self.op("act", lambda e, a=(out, in_, func, kw): e.activation(out=a[0], in_=a[1], func=a[2], **a[3]), reads, writes)

    def tt(self, eng, out, in0, in1, op, reads, writes):
        return self.op(eng, lambda e, a=(out, in0, in1, op): e.tensor_tensor(a[0], a[1], a[2], a[3]), reads, writes)

    def ts(self, eng, out, in0, s1, s2, op0, op1, reads, writes):
        if s2 is None:
            return self.op(eng, lambda e, a=(out, in0, s1, op0): e.tensor_scalar(a[0], a[1], a[2], None, a[3]), reads, writes)
        return self.op(eng, lambda e, a=(out, in0, s1, s2, op0, op1): e.tensor_scalar(a[0], a[1], a[2], a[3], a[4], a[5]), reads, writes)

    def stt(self, eng, out, in0, scalar, in1, op0, op1, reads, writes):
        return self.op(eng, lambda e, a=(out, in0, scalar, in1, op0, op1): e.scalar_tensor_tensor(a[0], a[1], a[2], a[3], a[4], a[5]), reads, writes)

    def copy(self, eng, out, in_, reads, writes):
        if eng == "act":
            return self.op("act", lambda e, a=(out, in_): e.copy(a[0], a[1]), reads, writes)
        return self.op(eng, lambda e, a=(out, in_): e.tensor_copy(a[0], a[1]), reads, writes)

    def memset(self, eng, ap, val, writes):
        return self.op(eng, lambda e, a=(ap, val): e.memset(a[0], a[1]), (), writes)

    def rsum(self, eng, out, in_, reads, writes):
        return self.op(eng, lambda e, a=(out, in_): e.reduce_sum(a[0], a[1], AX.X), reads, writes)

    def recip(self, out, in_, reads, writes):
        return self.op("dve", lambda e, a=(out, in_): e.reciprocal(a[0], a[1]), reads, writes)


class Ring:
    def __init__(self, bufs):
        self.bufs = bufs
        self.i = 0

    def next(self):
        b = self.bufs[self.i % len(self.bufs)]
        self.i += 1
        return b


def _dram(nc, name, shape, dt=F32, kind="ExternalInput"):
    return nc.dram_tensor(name, list(shape), dt, kind=kind).ap()


TT = 2176
NB = TT // 128
TTILES = [(0, 512), (512, 512), (1024, 512), (1536, 512), (2048, 128)]
NFC = DFF // 128

R_INPUTS = {
    "xT": (D, TT), "xtok": (TT, D), "oT": (1536, TT),
    "w_uv": (128, 16, 1024), "w_mg": (16, 128, 4, 16, 128), "w_br": (16, 128, 4, 4, 128), "w_mix": (128, 16, D),
    "sgw": (128, 4, 128), "sgb": (128, 4), "sg_g": (128, 512), "sg_b": (128, 512), "trimask": (128, 128),
    "ln1_g": (128, D), "ln1_b": (128, D), "ln2_g": (128, D), "ln2_b": (128, D), "ln3_g": (128, D), "ln3_b": (128, D),
    "mem": (256, D), "memln_g": (128, D), "memln_b": (128, D),
    "wq": (128, 16, 512), "wk": (128, 16, 512), "wv": (128, 16, 512), "wo": (128, 4, D),
    "ffn_in": (NFC, 128, 2, 16, 128), "ffn_out": (128, NFC, D), "convw": (128, NFC, 4),
    "ident": (128, 128), "flag": (128, 1), "ones": (128, 128),
}


def build_R():
    nc = bass.Bass("TRN2", target_bir_lowering=False)
    I = {k: _dram(nc, k, v) for k, v in R_INPUTS.items()}
    xout = _dram(nc, "xout", (2048, D), kind="ExternalOutput")
    odT_d = _dram(nc, "odT_d", (4, 128, TT), BF16, kind="Internal")
    preT_d = _dram(nc, "preT_d", (16, 128, TT), BF16, kind="Internal")
    x1_d = _dram(nc, "x1_d", (TT, D), F32, kind="Internal")
    x1T_d = _dram(nc, "x1T_d", (16, 128, TT), BF16, kind="Internal")
    aT_d = _dram(nc, "aT_d", (4, 128, TT), BF16, kind="Internal")
    x2_d = _dram(nc, "x2_d", (TT, D), F32, kind="Internal")
    x2T_d = _dram(nc, "x2T_d", (16, 128, TT), BF16, kind="Internal")
    act_d = _dram(nc, "act_d", (NFC, 128, TT), BF16, kind="Internal")
    ffo_d = _dram(nc, "ffo_d", (TT, D), F32, kind="Internal")
    with contextlib.ExitStack() as es:
        s = Sched(nc, es)
        B = s.buf
        ident = B("ident", [128, 128], BF16)
        ones = B("ones", [128, 128], BF16)
        flag = B("flag", [128, 1], F32)
        kT = B("kT", [128, 4, 256], BF16)
        vtok = B("vtok", [128, 2, 512], BF16)
        ps = Ring([B("ps%d" % i, [128, 512], F32, psum=True) for i in range(6)])
        pst = Ring([B("pst%d" % i, [128, 8, 128], BF16, psum=True) for i in range(2)])
        s.dma("pool", ident[:], I["ident"][:, :], writes=[ident])
        s.dma("pool", ones[:], I["ones"][:, :], writes=[ones])
        s.dma("sp", flag[:], I["flag"][:, :], writes=[flag])

        class LN:
            def __init__(self, gname, bname):
                self.g = B("lng", [128, D], F32)
                self.b = B("lnb", [128, D], F32)
                self.sq = B("lnsq", [128, D], F32)
                self.zt = Ring([B("zt%d" % i, [128, D], F32) for i in range(3)])
                self.zb = Ring([B("zb%d" % i, [128, D], BF16) for i in range(3)])
                self.xr = Ring([B("xr%d" % i, [128, D], F32) for i in range(5)])
                self.st = Ring([B("st%d" % i, [128, 8], F32) for i in range(6)])
                self.ob = Ring([B("lnob%d" % i, [128, 16, 128], BF16) for i in range(3)])
                s.dma("sp", self.g[:], I[gname][:, :], writes=[self.g])
                s.dma("sp", self.b[:], I[bname][:, :], writes=[self.b])

            def norm_gen(self, z, out):
                t = self.st.next()
                sq = self.sq
                s.op("act", lambda e, a=(sq, z, t): e.activation(out=a[0][:], in_=a[1][:], func=AF.Identity, accum_out=a[2][:, 0:1]), [z], [sq, t])
                s.op("act", lambda e, a=(sq, z, t): e.activation(out=a[0][:], in_=a[1][:], func=AF.Square, accum_out=a[2][:, 1:2]), [z], [sq, t])
                s.ts("dve", t[:, 2:3], t[:, 0:1], 1.0 / D, None, ALU.mult, None, [t], [t])
                s.ts("dve", t[:, 3:4], t[:, 1:2], 1.0 / D, None, ALU.mult, None, [t], [t])
                s.stt("dve", t[:, 4:5], t[:, 2:3], -1.0, t[:, 2:3], ALU.mult, ALU.mult, [t], [t])
                s.tt("dve", t[:, 5:6], t[:, 3:4], t[:, 4:5], ALU.add, [t], [t])
                s.act(t[:, 6:7], t[:, 5:6], AF.Sqrt, [t], [t], bias=EPS)
                s.recip(t[:, 7:8], t[:, 6:7], [t], [t])
                yield
                s.stt("dve", out[:], z[:], t[:, 2:3], self.g[:], ALU.subtract, ALU.mult, [z, t, self.g], [out])
                s.stt("dve", out[:], out[:], t[:, 7:8], self.b[:], ALU.mult, ALU.add, [out, t, self.b], [out])

            def norm(self, z, out):
                for _ in self.norm_gen(z, out):
                    pass

            def to_featmajor(self, xf, dst_d, dst_db, tb, width=16):
                xb = self.zb.next()
                s.copy("act", xb[:, 0:width * 128], xf[:, 0:width * 128], [xf], [xb])
                ob = self.ob.next()
                for h in range((width + 7) // 8):
                    p = pst.next()
                    nj = min(8, width - h * 8)
                    for j in range(nj):
                        kc = h * 8 + j
                        s.tr(p[:, j, :], xb[:, kc * 128:(kc + 1) * 128], ident[:], [xb, ident], [p])
                    s.copy("dve", ob[:, h * 8:h * 8 + nj, :], p[:, 0:nj, :], [p], [ob])
                s.dma("sp", dst_d[0:width, :, tb * 128:(tb + 1) * 128].rearrange("c p t -> p c t"), ob[:, 0:width, :], reads=[ob], writes=[dst_db[tb]])

            def proj_ln_gen(self, tb, lhs_fn, nk, w, xres_ap, xres_db, out_d, out_db, outT_d, outT_db, final_out=None):
                z = self.zt.next()
                xres = self.xr.next()
                s.dma("sp", xres[:], xres_ap, reads=list(xres_db), writes=[xres])
                for dt_ in range(4):
                    p = ps.next()
                    for k in range(nk):
                        lt, lbuf = lhs_fn(k)
                        s.mm(p[:], lt, w[:, k, dt_ * 512:(dt_ + 1) * 512], k == 0, k == nk - 1, [lbuf, w], [p])
                    s.stt("dve", z[:, dt_ * 512:(dt_ + 1) * 512], xres[:, dt_ * 512:(dt_ + 1) * 512], ALPHA, p[:], ALU.mult, ALU.add, [xres, p], [z])
                yield
                o = self.xr.next()
                for _ in self.norm_gen(z, o):
                    yield
                yield
                if final_out is not None:
                    if tb >= 1:
                        s.dma("sp", final_out[(tb - 1) * 128:tb * 128, :], o[:], reads=[o])
                else:
                    s.dma("sp", out_d[tb * 128:(tb + 1) * 128, :], o[:], reads=[o], writes=[out_db[tb]])
                    self.to_featmajor(o, outT_d, outT_db, tb)

        def run_pipelined(make_gen, n, depth=2):
            active = []
            nxt = 0
            while nxt < n or active:
                if nxt < n and len(active) < depth:
                    active.append(make_gen(nxt))
                    nxt += 1
                for g_ in list(active):
                    try:
                        next(g_)
                    except StopIteration:
                        active.remove(g_)

        def DB(n):
            return [Buf(None) for _ in range(n)]

        odT_db, preT_db, x1_db, x1T_db, aT_db, x2_db, x2T_db, act_db, ffo_db = DB(NB), DB(16), DB(NB), DB(NB), DB(4), DB(NB), DB(NB), DB(NFC), DB(NB * 4)
        base = s.mark()

        ln = LN("memln_g", "memln_b")
        memT_d = _dram(nc, "memT_d", (16, 128, 256), BF16, kind="Internal")
        memT_db = DB(2)
        for mb in range(2):
            z = ln.zt.next()
            o = ln.xr.next()
            s.dma("sp", z[:], I["mem"][mb * 128:(mb + 1) * 128, :], writes=[z])
            ln.norm(z, o)
            ln.to_featmajor(o, memT_d, memT_db, mb)
        memT = B("memT", [128, 16, 256], BF16)
        wk = B("wk", [128, 16, 512], BF16)
        wv = B("wv", [128, 16, 512], BF16)
        s.dma("sp", memT[:], memT_d[:, :, :].rearrange("c p t -> p c t"), reads=memT_db, writes=[memT])
        s.dma("pool", wk[:], I["wk"][:, :, :], writes=[wk])
        s.dma("pool", wv[:], I["wv"][:, :, :], writes=[wv])
        for h in range(4):
            p = ps.next()
            for kc in range(16):
                s.mm(p[:, 0:256], wk[:, kc, h * 128:(h + 1) * 128], memT[:, kc, :], kc == 0, kc == 15, [wk, memT], [p])
            s.copy("dve", kT[:, h, :], p[:, 0:256], [p], [kT])
        for mb in range(2):
            p = ps.next()
            for kc in range(16):
                s.mm(p[:], memT[:, kc, mb * 128:(mb + 1) * 128], wv[:, kc, :], kc == 0, kc == 15, [wv, memT], [p])
            s.copy("act", vtok[:, mb, :], p[:], [p], [vtok])
        s.release(base)

        xT = B("xT", [128, 16, TT], BF16)
        s.dma("pool", xT[:, 0:8, :], I["xT"][0:1024, :].rearrange("(kc p) t -> p kc t", p=128), writes=[xT])
        s.dma("pool", xT[:, 8:16, :], I["xT"][1024:2048, :].rearrange("(kc p) t -> p kc t", p=128), writes=[xT])
        oT = B("oT", [128, 12, TT], BF16)
        s.dma("pool", oT[:, 0:6, :], I["oT"][0:768, :].rearrange("(kc p) t -> p kc t", p=128), writes=[oT])
        s.dma("pool", oT[:, 6:12, :], I["oT"][768:1536, :].rearrange("(kc p) t -> p kc t", p=128), writes=[oT])
        w_uv = B("w_uv", [128, 16, 1024], BF16)
        sgw = B("sgw", [128, 4, 128], F32)
        sgwb = B("sgwb", [128, 4, 128], BF16)
        tri = B("tri", [128, 128], F32)
        sgb = B("sgb", [128, 4], F32)
        sg_g = B("sg_g", [128, 512], F32)
        sg_b = B("sg_b", [128, 512], F32)
        s.dma("pool", w_uv[:, 0:8, :], I["w_uv"][:, 0:8, :], writes=[w_uv])
        s.dma("pool", w_uv[:, 8:16, :], I["w_uv"][:, 8:16, :], writes=[w_uv])
        s.dma("sp", sgw[:], I["sgw"][:, :, :], writes=[sgw])
        s.dma("sp", tri[:], I["trimask"][:, :], writes=[tri])
        s.dma("sp", sgb[:], I["sgb"][:, :], writes=[sgb])
        s.dma("sp", sg_g[:], I["sg_g"][:, :], writes=[sg_g])
        s.dma("sp", sg_b[:], I["sg_b"][:, :], writes=[sg_b])
        for g in range(4):
            s.tt("dve", sgwb[:, g, :], sgw[:, g, :], tri[:], ALU.mult, [sgw, tri], [sgwb])
        uvt = Ring([B("uvt%d" % i, [128, 512], F32) for i in range(10)])
        gt = Ring([B("gt%d" % i, [128, 512], F32) for i in range(6)])
        vlb = Ring([B("vlb%d" % i, [128, 512], BF16) for i in range(3)])
        odt = Ring([B("odt%d" % i, [128, 512], BF16) for i in range(3)])
        odo = Ring([B("odo%d" % i, [128, 4, 128], BF16) for i in range(3)])
        stA = Ring([B("stA%d" % i, [128, 8], F32) for i in range(6)])

        def gelu(src_ps, dst):
            x = uvt.next()
            t1 = uvt.next()
            s.copy("act", x[:], src_ps[:], [src_ps], [x])
            s.tt("dve", t1[:], x[:], x[:], ALU.mult, [x], [t1])
            s.ts("dve", t1[:], t1[:], 0.044715, 1.0, ALU.mult, ALU.add, [t1], [t1])
            s.tt("dve", t1[:], t1[:], x[:], ALU.mult, [t1, x], [t1])
            s.act(t1[:], t1[:], AF.Sigmoid, [t1], [t1], scale=1.5957691216057308)
            s.tt("pool", dst[:], t1[:], x[:], ALU.mult, [t1, x], [dst])

        def genA(tb):
            pu = ps.next()
            pv = ps.next()
            for (p, c0) in ((pu, 0), (pv, 512)):
                for kc in range(16):
                    s.mm(p[:], xT[:, kc, tb * 128:(tb + 1) * 128], w_uv[:, kc, c0:c0 + 512], kc == 0, kc == 15, [xT, w_uv], [p])
            gu = gt.next()
            gv = gt.next()
            gelu(pu, gu)
            gelu(pv, gv)
            yield
            t = stA.next()
            scr = uvt.next()
            s.op("act", lambda e, a=(scr, gv, t): e.activation(out=a[0][:], in_=a[1][:], func=AF.Identity, accum_out=a[2][:, 0:1]), [gv], [scr, t])
            s.op("act", lambda e, a=(scr, gv, t): e.activation(out=a[0][:], in_=a[1][:], func=AF.Square, accum_out=a[2][:, 1:2]), [gv], [scr, t])
            s.ts("dve", t[:, 2:3], t[:, 0:1], 1.0 / 512, None, ALU.mult, None, [t], [t])
            s.ts("dve", t[:, 3:4], t[:, 1:2], 1.0 / 512, None, ALU.mult, None, [t], [t])
            s.stt("dve", t[:, 4:5], t[:, 2:3], -1.0, t[:, 2:3], ALU.mult, ALU.mult, [t], [t])
            s.tt("dve", t[:, 5:6], t[:, 3:4], t[:, 4:5], ALU.add, [t], [t])
            s.act(t[:, 6:7], t[:, 5:6], AF.Sqrt, [t], [t], bias=EPS)
            s.recip(t[:, 7:8], t[:, 6:7], [t], [t])
            yield
            s.stt("dve", gv[:], gv[:], t[:, 2:3], sg_g[:], ALU.subtract, ALU.mult, [gv, t, sg_g], [gv])
            vb = vlb.next()
            s.stt("dve", vb[:], gv[:], t[:, 7:8], sg_b[:], ALU.mult, ALU.add, [gv, t, sg_b], [vb])
            yield
            pg = ps.next()
            for g in range(4):
                s.mm(pg[:, g * 128:(g + 1) * 128], sgwb[:, g, :], vb[:, g * 128:(g + 1) * 128], True, True, [sgwb, vb], [pg])
            od = odt.next()
            for g in range(4):
                s.stt("dve", od[:, g * 128:(g + 1) * 128], pg[:, g * 128:(g + 1) * 128], sgb[:, g:g + 1], gu[:, g * 128:(g + 1) * 128],
                      ALU.add, ALU.mult, [pg, sgb, gu], [od])
            yield
            p = pst.next()
            for j_ in range(4):
                s.tr(p[:, j_, :], od[:, j_ * 128:(j_ + 1) * 128], ident[:], [od, ident], [p])
            oo = odo.next()
            s.copy("act", oo[:], p[:, 0:4, :], [p], [oo])
            s.dma("sp", odT_d[:, :, tb * 128:(tb + 1) * 128].rearrange("c p t -> p c t"), oo[:], reads=[oo], writes=[odT_db[tb]])
        run_pipelined(genA, NB)
        mB = s.mark()
        s.barrier()
        s.off = xT_end = base + 28 * TT * 2
        assert xT_end % 64 == 0

        odTb = B("odTb", [128, 4, TT], BF16)
        s.dma("sp", odTb[:], odT_d[:, :, :].rearrange("c p t -> p c t"), reads=odT_db, writes=[odTb])
        wmg = Ring([B("wmg%d" % i, [128, 4, 16, 128], BF16) for i in range(2)])
        wbr = Ring([B("wbr%d" % i, [128, 4, 4, 128], BF16) for i in range(2)])
        gsb = Ring([B("gsb%d" % i, [128, 512], F32) for i in range(3)])
        acc = Ring([B("acc%d" % i, [128, 512], F32) for i in range(2)])
        preo = Ring([B("preo%d" % i, [128, TT], BF16) for i in range(2)])
        wq_ = []
        for dc in range(min(1, 16)):
            wm = wmg.next()
            wb = wbr.next()
            s.dma("pool", wm[:], I["w_mg"][dc, :, :, :, :], writes=[wm])
            s.dma("pool", wb[:], I["w_br"][dc, :, :, :, :], writes=[wb])
            wq_.append((wm, wb))
        for dc in range(16):
            if dc + 1 < 16:
                wm = wmg.next()
                wb = wbr.next()
                s.dma("pool", wm[:], I["w_mg"][dc + 1, :, :, :, :], writes=[wm])
                s.dma("pool", wb[:], I["w_br"][dc + 1, :, :, :, :], writes=[wb])
                wq_.append((wm, wb))
            wm, wb = wq_[dc]
            po = preo.next()
            for (t0, tw) in TTILES:
                a = acc.next()
                for n in range(4):
                    pm = ps.next()
                    py = ps.next()
                    for kc in range(16):
                        s.mm(pm[:, 0:tw], wm[:, n, kc, :], xT[:, kc, t0:t0 + tw], kc == 0, kc == 15, [wm, xT], [pm])
                    for k4 in range(4):
                        s.mm(py[:, 0:tw], wb[:, n, k4, :], (oT[:, n * 4 + k4, t0:t0 + tw] if n < 3 else odTb[:, k4, t0:t0 + tw]), k4 == 0, k4 == 3, [wb, oT if n < 3 else odTb], [py])
                    gs = gsb.next()
                    s.act(gs[:, 0:tw], pm[:, 0:tw], AF.Sigmoid, [pm], [gs])
                    if n == 0:
                        s.tt("dve", a[:, 0:tw], gs[:, 0:tw], py[:, 0:tw], ALU.mult, [gs, py], [a])
                    else:
                        s.tt("dve", gs[:, 0:tw], gs[:, 0:tw], py[:, 0:tw], ALU.mult, [gs, py], [gs])
                        if n < 3:
                            s.tt("pool", a[:, 0:tw], a[:, 0:tw], gs[:, 0:tw], ALU.add, [a, gs], [a])
                        else:
                            s.tt("pool", po[:, t0:t0 + tw], a[:, 0:tw], gs[:, 0:tw], ALU.add, [a, gs], [po])
            s.dma("sp", preT_d[dc, :, :], po[:], reads=[po], writes=[preT_db[dc]])
        s.release(base)

        ln = LN("ln1_g", "ln1_b")
        wmix = B("wmix", [128, 16, D], BF16)
        s.dma("pool", wmix[:, 0:8, :], I["w_mix"][:, 0:8, :], writes=[wmix])
        s.dma("pool", wmix[:, 8:16, :], I["w_mix"][:, 8:16, :], writes=[wmix])
        prb = Ring([B("prb%d" % i, [128, 16, 128], BF16) for i in range(3)])

        def genC(tb):
            pb = prb.next()
            s.dma("sp", pb[:], preT_d[:, :, tb * 128:(tb + 1) * 128].rearrange("c p t -> p c t"), reads=preT_db, writes=[pb])
            yield from ln.proj_ln_gen(tb, lambda k, pb=pb: (pb[:, k, :], pb), 16, wmix, I["xtok"][tb * 128:(tb + 1) * 128, :], (), x1_d, x1_db, x1T_d, x1T_db)
        run_pipelined(genC, NB)
        s.release(base)

        x1T = B("x1T", [128, 16, TT], BF16)
        s.dma("sp", x1T[:], x1T_d[:, :, :].rearrange("c p t -> p c t"), reads=x1T_db, writes=[x1T])
        wq = B("wq", [128, 16, 512], BF16)
        s.dma("pool", wq[:], I["wq"][:, :, :], writes=[wq])
        qTb = Ring([B("qTb%d" % i, [128, 512], BF16) for i in range(3)])
        pTb = Ring([B("pTb%d" % i, [128, 512], BF16) for i in range(6)])
        rdb = Ring([B("rdb%d" % i, [128, 512], F32) for i in range(3)])
        aTo = Ring([B("aTo%d" % i, [128, TT], BF16) for i in range(2)])
        aos = [aTo.next() for _ in range(2)]

        def genD1(it):
            h, ti = divmod(it, len(TTILES))
            (t0, tw) = TTILES[ti]
            ao = aos[h % 2]
            p = ps.next()
            for kc in range(16):
                s.mm(p[:, 0:tw], wq[:, kc, h * 128:(h + 1) * 128], x1T[:, kc, t0:t0 + tw], kc == 0, kc == 15, [wq, x1T], [p])
            q = qTb.next()
            s.act(q[:, 0:tw], p[:, 0:tw], AF.Identity, [p], [q], scale=128 ** -0.5)
            yield
            pts = []
            for mb in range(2):
                p2 = ps.next()
                s.mm(p2[:, 0:tw], kT[:, h, mb * 128:(mb + 1) * 128], q[:, 0:tw], True, True, [kT, q], [p2])
                pt = pTb.next()
                s.act(pt[:, 0:tw], p2[:, 0:tw], AF.Exp, [p2], [pt])
                pts.append(pt)
            yield
            po_ = ps.next()
            pd = ps.next()
            for mb in range(2):
                s.mm(po_[:, 0:tw], vtok[:, mb, h * 128:(h + 1) * 128], pts[mb][:, 0:tw], mb == 0, mb == 1, [vtok, pts[mb]], [po_])
            for mb in range(2):
                s.mm(pd[:, 0:tw], ones[:], pts[mb][:, 0:tw], mb == 0, mb == 1, [ones, pts[mb]], [pd])
            rd = rdb.next()
            s.recip(rd[:, 0:tw], pd[:, 0:tw], [pd], [rd])
            s.tt("dve", ao[:, t0:t0 + tw], po_[:, 0:tw], rd[:, 0:tw], ALU.mult, [po_, rd], [ao])
            if ti == len(TTILES) - 1:
                s.dma("sp", aT_d[h, :, :], ao[:], reads=[ao], writes=[aT_db[h]])
        run_pipelined(genD1, 4 * len(TTILES))
        s.release(base)

        ln = LN("ln2_g", "ln2_b")
        wo = B("wo", [128, 4, D], BF16)
        s.dma("pool", wo[:], I["wo"][:, :, :], writes=[wo])
        aTr = Ring([B("aTr%d" % i, [128, 4, 128], BF16) for i in range(3)])

        def genD2(tb):
            ab = aTr.next()
            s.dma("sp", ab[:], aT_d[:, :, tb * 128:(tb + 1) * 128].rearrange("c p t -> p c t"), reads=aT_db, writes=[ab])
            yield from ln.proj_ln_gen(tb, lambda k, ab=ab: (ab[:, k, :], ab), 4, wo, x1_d[tb * 128:(tb + 1) * 128, :], [x1_db[tb]], x2_d, x2_db, x2T_d, x2T_db)
        run_pipelined(genD2, NB)
        s.release(base)

        x2T = B("x2T", [128, 16, TT], BF16)
        s.dma("sp", x2T[:], x2T_d[:, :, :].rearrange("c p t -> p c t"), reads=x2T_db, writes=[x2T])
        cw = B("cw", [128, NFC, 4], F32)
        s.dma("sp", cw[:], I["convw"][:, :, :], writes=[cw])
        wf = Ring([B("wf%d" % i, [128, 2, 16, 128], BF16) for i in range(3)])
        gbuf = Ring([B("gbuf%d" % i, [128, TT + 2], F32) for i in range(2)])
        ubuf = Ring([B("ubuf%d" % i, [128, TT], F32) for i in range(2)])
        cbuf = Ring([B("cbuf%d" % i, [128, TT], F32) for i in range(2)])
        sbuf_ = Ring([B("sbuf%d" % i, [128, TT], F32) for i in range(2)])
        abuf = Ring([B("abuf%d" % i, [128, TT], BF16) for i in range(2)])
        for g_ in gbuf.bufs:
            s.memset("dve", g_[:, 0:2], 0.0, [g_])
        wfq = []
        for fc in range(2):
            w = wf.next()
            s.dma("pool", w[:], I["ffn_in"][fc, :, :, :, :], writes=[w])
            wfq.append(w)
        for fc in range(NFC):
            if fc + 2 < NFC:
                w = wf.next()
                s.dma("pool", w[:], I["ffn_in"][fc + 2, :, :, :, :], writes=[w])
                wfq.append(w)
            w = wfq[fc]
            gb = gbuf.next()
            ub = ubuf.next()
            for (t0, tw) in TTILES:
                pg = ps.next()
                pu = ps.next()
                for kc in range(16):
                    s.mm(pg[:, 0:tw], w[:, 0, kc, :], x2T[:, kc, t0:t0 + tw], kc == 0, kc == 15, [w, x2T], [pg])
                for kc in range(16):
                    s.mm(pu[:, 0:tw], w[:, 1, kc, :], x2T[:, kc, t0:t0 + tw], kc == 0, kc == 15, [w, x2T], [pu])
                if t0 == 0:
                    s.op("act", lambda e, a=(gb, pg, flag): e.activation(out=a[0][:, 2:130], in_=a[1][:, 0:128], func=AF.Copy, scale=a[2][:, 0:1]), [pg, flag], [gb])
                    s.copy("act", gb[:, 130:2 + tw], pg[:, 128:tw], [pg], [gb])
                else:
                    s.copy("act", gb[:, 2 + t0:2 + t0 + tw], pg[:, 0:tw], [pg], [gb])
                s.copy("dve", ub[:, t0:t0 + tw], pu[:, 0:tw], [pu], [ub])
            cb = cbuf.next()
            s.ts("dve", cb[:], gb[:, 2:TT + 2], cw[:, fc, 2:3], cw[:, fc, 3:4], ALU.mult, ALU.add, [gb, cw], [cb])
            s.stt("dve", cb[:], gb[:, 1:TT + 1], cw[:, fc, 1:2], cb[:], ALU.mult, ALU.add, [gb, cw, cb], [cb])
            s.stt("dve", cb[:], gb[:, 0:TT], cw[:, fc, 0:1], cb[:], ALU.mult, ALU.add, [gb, cw, cb], [cb])
            sb = sbuf_.next()
            s.act(sb[:], cb[:], AF.Silu, [cb], [sb])
            ab = abuf.next()
            s.tt("pool", ab[:], sb[:], ub[:], ALU.mult, [sb, ub], [ab])
            s.dma("sp", act_d[fc, :, :], ab[:], reads=[ab], writes=[act_db[fc]])
        s.release(base)

        wfo = Ring([B("wfo%d" % i, [128, NFC, 512], BF16) for i in range(2)])
        acb = Ring([B("acb%d" % i, [128, NFC, 128], BF16) for i in range(4)])
        fob = Ring([B("fob%d" % i, [128, 512], F32) for i in range(3)])
        for dt_ in range(4):
            w = wfo.next()
            for f0 in range(0, NFC, 11):
                s.dma("pool", w[:, f0:f0 + 11, :], I["ffn_out"][:, f0:f0 + 11, dt_ * 512:(dt_ + 1) * 512], writes=[w])
            for tb in range(NB):
                step = dt_ * NB + tb
                if step == 0:
                    abq = []
                    for st_ in range(3):
                        ab = acb.next()
                        tb_ = st_ % NB
                        s.dma("sp", ab[:], act_d[:, :, tb_ * 128:(tb_ + 1) * 128].rearrange("c p t -> p c t"), reads=act_db, writes=[ab])
                        abq.append(ab)
                if step + 3 < 4 * NB:
                    ab = acb.next()
                    tb_ = (step + 3) % NB
                    s.dma("sp", ab[:], act_d[:, :, tb_ * 128:(tb_ + 1) * 128].rearrange("c p t -> p c t"), reads=act_db, writes=[ab])
                    abq.append(ab)
                ab = abq[step]
                p = ps.next()
                for fc in range(NFC):
                    s.mm(p[:], ab[:, fc, :], w[:, fc, :], fc == 0, fc == NFC - 1, [ab, w], [p])
                fo = fob.next()
                if tb % 2 == 0:
                    s.copy("act", fo[:], p[:], [p], [fo])
                else:
                    s.copy("dve", fo[:], p[:], [p], [fo])
                s.dma("sp", ffo_d[tb * 128:(tb + 1) * 128, dt_ * 512:(dt_ + 1) * 512], fo[:], reads=[fo], writes=[ffo_db[tb * 4 + dt_]])
        s.release(base)

        ln = LN("ln3_g", "ln3_b")
        def genF2(i):
            tb = i + 1
            z = ln.zt.next()
            xres = ln.xr.next()
            ff = ln.xr.next()
            s.dma("sp", xres[:], x2_d[tb * 128:(tb + 1) * 128, :], reads=[x2_db[tb]], writes=[xres])
            s.dma("sp", ff[:], ffo_d[tb * 128:(tb + 1) * 128, :], reads=ffo_db[tb * 4:tb * 4 + 4], writes=[ff])
            s.stt("dve", z[:], xres[:], ALPHA, ff[:], ALU.mult, ALU.add, [xres, ff], [z])
            yield
            for _ in ln.norm_gen(z, z):
                yield
            yield
            s.dma("sp", xout[(tb - 1) * 128:tb * 128, :], z[:], reads=[z])
        run_pipelined(genF2, NB - 1)
        s.emit()
        print("R program: ops", s.n, "sbuf peak", s.peak, flush=True)
    return nc


OFF_Q, OFF_KV, OFF_G, OFF_H, OFF_P, OFF_SG, OFF_MG = 0, 512, 1280, 1304, 3352, 3864, 4888


def _bc(v, n=128):
    return np.ascontiguousarray(np.broadcast_to(np.asarray(v, np.float32)[None, :], (n, v.shape[0])))


def _c(a):
    return np.ascontiguousarray(a, dtype=np.float32)


def prep_R_shared(inp, l):
    w_in = inp["w_in"][l]
    sh = {}
    sh["w_uv"] = _c(w_in[:, OFF_SG:OFF_SG + 1024].reshape(16, 128, 1024).transpose(1, 0, 2))
    mg = w_in[:, OFF_MG:OFF_MG + 8192].reshape(16, 128, 4, 16, 128)
    sh["w_mg"] = _c(mg.transpose(3, 1, 2, 0, 4))
    br = inp["w_branch"][l].reshape(4, 4, 128, 16, 128)
    sh["w_br"] = _c(br.transpose(3, 2, 0, 1, 4))
    sh["w_mix"] = _c(inp["w_mix_out"][l].reshape(16, 128, D).transpose(1, 0, 2))
    sh["sgw"] = _c(inp["sg_w"][l].transpose(2, 0, 1))
    sh["sgb"] = _c(inp["sg_b"][l].T)
    sh["sg_g"] = _bc(inp["sg_ln_g"][l])
    sh["sg_b"] = _bc(inp["sg_ln_b"][l])
    sh["trimask"] = _c(np.triu(np.ones((128, 128), np.float32)))
    for i, nm in ((1, "ln_mix"), (2, "ln_x"), (3, "ln_ffn")):
        sh["ln%d_g" % i] = _bc(inp[nm + "_g"][l])
        sh["ln%d_b" % i] = _bc(inp[nm + "_b"][l])
    sh["memln_g"] = _bc(inp["mem_ln_g"])
    sh["memln_b"] = _bc(inp["mem_ln_b"])
    for nm, k in (("wq", "xattn_q"), ("wk", "xattn_k"), ("wv", "xattn_v")):
        sh[nm] = _c(inp[k][l].reshape(16, 128, 512).transpose(1, 0, 2))
    sh["wo"] = _c(inp["xattn_o"][l].reshape(4, 128, D).transpose(1, 0, 2))
    fi = inp["ffn_in"][l].reshape(16, 128, 2, NFC, 128)
    sh["ffn_in"] = _c(fi.transpose(3, 1, 2, 0, 4))
    sh["ffn_out"] = _c(inp["ffn_out"][l].reshape(NFC, 128, D).transpose(1, 0, 2))
    cw = np.concatenate([inp["ffn_conv_w"][l], inp["ffn_conv_b"][l][None, :]], axis=0)
    sh["convw"] = _c(cw.reshape(4, NFC, 128).transpose(2, 1, 0))
    sh["ident"] = np.eye(128, dtype=np.float32)
    sh["ones"] = np.ones((128, 128), np.float32)
    return sh


def prep_R(inp, l, x, oabc):
    sh = prep_R_shared(inp, l)
    maps = []
    for c in range(NCORE):
        b, u = divmod(c, 4)
        t0 = 2048 * u
        m = dict(sh)
        xs = np.zeros((TT, D), np.float32)
        os_ = np.zeros((TT, 1536), np.float32)
        lo = t0 - 128
        if u == 0:
            xs[128:] = x[b, 0:2048]
            os_[128:] = oabc[b, 0:2048]
        else:
            xs[:] = x[b, lo:lo + TT]
            os_[:] = oabc[b, lo:lo + TT]
        m["xtok"] = xs
        m["xT"] = _c(xs.T)
        m["oT"] = _c(os_.T)
        m["mem"] = _c(inp["mem"][b])
        m["flag"] = np.full((128, 1), 0.0 if u == 0 else 1.0, np.float32)
        maps.append(m)
    return maps


_NC_CACHE = {}


def run_R(inp, l, x, oabc):
    if "R" not in _NC_CACHE:
        _NC_CACHE["R"] = build_R()
    maps = prep_R(inp, l, x, oabc)
    res = run_bass_kernel_spmd(_NC_CACHE["R"], maps, core_ids=list(range(NCORE)))
    out = np.zeros((2, S, D), np.float32)
    for c in range(NCORE):
        b, u = divmod(c, 4)
        out[b, 2048 * u:2048 * (u + 1)] = res.results[c]["xout"]
    return out


NQT = S // 512
M_INPUTS = {
    "xT": (D, S), "wfm": (128, 16, 768), "wtm": (128, 16, 520),
    "qpos": (4, 4, S), "kpos": (4, S), "cpos": (4, 512),
    "cmask": (5, 128, 512), "dmask": (8, 128, 512), "E_all": (128, 32, 128), "selmap": (128, 4, 129),
    "keep": (128, 256), "addm": (128, 256), "ident": (128, 128), "tri2": (128, 64),
    "cw_k": (64, 32, 64), "cw_v": (64, 32, 64), "cp_k": (64, 32), "cp_v": (64, 32),
    "lblog": (128, 2), "lbsel": (128, 1), "normg": (128, 128),
    "poolP": (128, 3, 128), "poolw": (128, 128), "poolscale": (128, 128),
}


def build_M(nqt=NQT, ST=(1, 2, 3, 4, 5, 6, 7, 8, 9, 10, 11)):
    nc = bass.Bass("TRN2", target_bir_lowering=False)
    I = {k: _dram(nc, k, v) for k, v in M_INPUTS.items()}
    om = _dram(nc, "om", (S, 384), kind="ExternalOutput")
    with contextlib.ExitStack() as es:
        s = Sched(nc, es)
        B = s.buf

        def PB(name, shape, dt):
            return B(name, shape, dt, psum=True)

        psr = Ring([PB("psr%d" % i, [128, 512], F32) for i in range(2)])
        pst = PB("pst", [128, 8, 128], BF16)
        oacc_ps = Ring([PB("oacc%d" % i, [128, 512], F32) for i in range(2)])
        impb = PB("impb", [128, 512], F32)
        misc = PB("misc", [128, 512], F32)
        denb = misc
        hg = PB("hg", [128, 4, 128], F32)

        wfm = B("wfm", [128, 16, 768], BF16)
        wtm = B("wtm", [128, 16, 520], BF16)
        for kc in range(16):
            s.dma("pool", wfm[:, kc, :], I["wfm"][:, kc, :], writes=[wfm])
            s.dma("pool", wtm[:, kc, :], I["wtm"][:, kc, :], writes=[wtm])
        identb = B("identb", [128, 128], BF16)
        identf = B("identf", [128, 128], F32)
        s.dma("pool", identb[:], I["ident"][:, :], writes=[identb])
        s.dma("sp", identf[:], I["ident"][:, :], writes=[identf])
        cmask = B("cmask", [128, 5, 512], BF16)
        dmask = B("dmask", [128, 8, 512], BF16)
        for i in range(5):
            s.dma("pool", cmask[:, i, :], I["cmask"][i, :, :], writes=[cmask])
        for i in range(8):
            s.dma("pool", dmask[:, i, :], I["dmask"][i, :, :], writes=[dmask])
        E_all = B("E_all", [128, 32, 128], BF16)
        s.dma("pool", E_all[:], I["E_all"][:, :, :], writes=[E_all])
        selmap = B("selmap", [128, 4, 144], BF16)
        s.dma("pool", selmap[:, :, 0:129], I["selmap"][:, :, :], writes=[selmap])
        keep = B("keep", [128, 256], F32)
        addm = B("addm", [128, 256], F32)
        tri2 = B("tri2", [128, 64], F32)
        s.dma("sp", keep[:], I["keep"][:, :], writes=[keep])
        s.dma("sp", addm[:], I["addm"][:, :], writes=[addm])
        s.dma("sp", tri2[:], I["tri2"][:, :], writes=[tri2])
        cw_k = B("cw_k", [64, 32, 64], BF16)
        cw_v = B("cw_v", [64, 32, 64], BF16)
        cp_k = B("cp_k", [64, 32], BF16)
        cp_v = B("cp_v", [64, 32], BF16)
        s.dma("pool", cw_k[:], I["cw_k"][:, :, :], writes=[cw_k])
        s.dma("pool", cw_v[:], I["cw_v"][:, :, :], writes=[cw_v])
        s.dma("pool", cp_k[:], I["cp_k"][:, :], writes=[cp_k])
        s.dma("pool", cp_v[:], I["cp_v"][:, :], writes=[cp_v])
        normg = B("normg", [128, 128], F32)
        poolP = B("poolP", [128, 3, 128], F32)
        poolw = B("poolw", [128, 128], BF16)
        poolscale = B("poolscale", [128, 128], F32)
        lbl = B("lbl", [128, 8], F32)
        s.dma("sp", normg[:], I["normg"][:, :], writes=[normg])
        s.dma("sp", poolP[:], I["poolP"][:, :, :], writes=[poolP])
        s.dma("pool", poolw[:], I["poolw"][:, :], writes=[poolw])
        s.dma("sp", poolscale[:], I["poolscale"][:, :], writes=[poolscale])
        s.dma("sp", lbl[:, 0:2], I["lblog"][:, :], writes=[lbl])
        s.dma("sp", lbl[:, 2:3], I["lbsel"][:, :], writes=[lbl])
        s.tt("dve", lbl[:, 3:4], lbl[:, 1:2], lbl[:, 0:1], ALU.subtract, [lbl], [lbl])
        s.act(lbl[:, 4:5], lbl[:, 3:4], AF.Sigmoid, [lbl], [lbl])
        s.tt("dve", lbl[:, 5:6], lbl[:, 4:5], lbl[:, 2:3], ALU.mult, [lbl], [lbl])
        s.ts("dve", lbl[:, 6:7], lbl[:, 5:6], -1.0, 1.0, ALU.mult, ALU.add, [lbl], [lbl])

        kslc = B("kslc", [128, S], BF16)
        s.memset("dve", kslc[64:128, :], 0.0, [kslc])
        s.dma("pool", kslc[64:68, :], I["kpos"][:, :], writes=[kslc])
        kwin = B("kwin", [128, 8, 128], BF16)
        s.memset("dve", kwin[64:128, :, :], 0.0, [kwin])
        vslc = B("vslc", [128, 65, 72], BF16)
        vwin = B("vwin", [128, 10, 72], BF16)
        s.memset("dve", vslc[:], 0.0, [vslc])
        s.memset("dve", vwin[:], 0.0, [vwin])
        s.memset("dve", vslc[:, 0:64, 64:65], 1.0, [vslc])
        s.memset("dve", vwin[:, 0:8, 64:65], 1.0, [vwin])
        kcaug = B("kcaug", [128, 512], BF16)
        s.memset("dve", kcaug[:], 0.0, [kcaug])
        s.dma("pool", kcaug[64:68, :], I["cpos"][:, :], writes=[kcaug])
        vcaug = B("vcaug", [128, 6, 72], BF16)
        s.memset("dve", vcaug[:], 0.0, [vcaug])
        s.memset("dve", vcaug[:, 0:4, 64:65], 1.0, [vcaug])
        kcraw = B("kcraw", [64, 528], BF16)
        vcraw = B("vcraw", [64, 528], BF16)
        s.memset("dve", kcraw[:], 0.0, [kcraw])
        s.memset("dve", vcraw[:], 0.0, [vcraw])
        cbias = B("cbias", [64, 2], F32)
        for (cw_, cp_, col) in ((cw_k, cp_k, 0), (cw_v, cp_v, 1)):
            p = psr.next()
            for l_ in range(32):
                s.mm(p[0:64, 0:1], cw_[:, l_, :], cp_[:, l_:l_ + 1], l_ == 0, l_ == 31, [cw_, cp_], [p])
            s.copy("dve", cbias[:, col:col + 1], p[0:64, 0:1], [p], [cbias])

        state = B("state", [128, 128], F32)
        stbf = Ring([B("stbf%d" % i, [128, 128], BF16) for i in range(2)])
        s.memset("dve", state[:], 0.0, [state])
        st_cur = stbf.next()
        s.memset("dve", st_cur[:], 0.0, [st_cur])
        qbpad = B("qbpad", [128, 4, 2, 128], BF16)
        s.memset("pool", qbpad[:], 0.0, [qbpad])
        atpad = Ring([B("atpad%d" % i, [128, 128], BF16) for i in range(2)])
        for a_ in atpad.bufs:
            s.memset("pool", a_[:], 0.0, [a_])
        pczero = B("pczero", [128, 128], F32)
        s.memset("pool", pczero[:], 0.0, [pczero])

        xtile = Ring([B("xtile%d" % i, [128, 16, 512], BF16) for i in range(1)])
        qaug = [Ring([B("qaug%d_%d" % (j, i), [128, 512], BF16) for i in range(2)]) for j in range(4)]
        for r_ in qaug:
            for b_ in r_.bufs:
                s.memset("dve", b_[64:128, :], 0.0, [b_])
        hqT = B("hqT", [128, 512], F32)
        hzT = B("hzT", [128, 512], F32)
        gate = Ring([B("gate%d" % i, [128, 8], F32) for i in range(8)])
        pcb = Ring([B("pcb%d" % i, [128, 128], F32) for i in range(6)])
        vvb = Ring([B("vvb%d" % i, [128, 128], BF16) for i in range(4)])
        sgb = Ring([B("sgb%d" % i, [128, 128], F32) for i in range(4)])
        outb = Ring([B("outb%d" % i, [128, 384], F32) for i in range(4)])
        ET = [[B("ET%d_%d" % (j, c), [128, 512], BF16) for c in range(4)] for j in range(2)]
        PT = Ring([B("PT%d" % i, [128, 512], BF16) for i in range(4)])
        impacc = B("impacc", [128, 4, 128], F32)
        rden = Ring([B("rden%d" % i, [128, 4], F32) for i in range(4)])
        selw = Ring([B("selw%d" % i, [128, 128], F32) for i in range(3)])
        selb = Ring([B("selb%d" % i, [128, 128], BF16) for i in range(4)])
        m8 = Ring([B("m8_%d" % i, [128, 16], F32) for i in range(2)])
        mbT = Ring([B("mbT%d" % i, [128, 512], BF16) for i in range(2)])
        oTs = Ring([B("oTs%d" % i, [65, 512], F32) for i in range(2)])
        coef = Ring([B("coef%d" % i, [128, 4], F32) for i in range(4)])
        vcT = Ring([B("vcT%d" % i, [64, 32], BF16) for i in range(2)])
        vct = Ring([B("vct%d" % i, [32, 64], BF16) for i in range(2)])
        hw_ = [B("hw%d" % i, [128, 512], F32) for i in range(7)]
        hb_ = [B("hb%d" % i, [128, 512], BF16) for i in range(4)]
        hs_ = Ring([B("hs%d" % i, [128, 16], F32) for i in range(2)])
        kdb = Ring([B("kdb%d" % i, [128, 128], BF16) for i in range(2)])
        hst = Ring([B("hst%d" % i, [128, 4], F32) for i in range(4)])
        hsq = B("hsq", [128, 128], F32)
        hsq2 = B("hsq2", [128, 128], F32)
        ptb = Ring([B("ptb%d" % i, [128, 128], BF16) for i in range(2)])
        pc_prev = pczero
        trs = Ring([B("trs%d" % i, [128, 4, 65], F32) for i in range(1)])
        mbh = [Ring([B("mbh%d_%d" % (h, i), [128, 512], BF16) for i in range(2)]) for h in range(2)]
        for h_ in range(2):
            for b_ in mbh[h_].bufs:
                s.memset("dve", b_[:], 0.0, [b_])
        imps = Ring([B("imps%d" % i, [128, 4, 128], F32) for i in range(1)])
        asb = Ring([B("asb%d" % i, [128, 128], F32) for i in range(2)])
        stgA = Ring([B("stgA%d" % i, [128, 136], F32) for i in range(1)])
        stgB = Ring([B("stgB%d" % i, [128, 384], F32) for i in range(1)])

        vslc_f = vslc[:].rearrange("p a b -> p (a b)")
        vwin_f = vwin[:].rearrange("p a b -> p (a b)")
        vcaug_f = vcaug[:].rearrange("p a b -> p (a b)")
        psa = Ring([psr.bufs[0], psr.bufs[1], impb])
        JA = 0
        trf = Buf(misc.t)
        trv = misc.t[:, 128:388].rearrange("p (a b) -> p a b", a=4)
        hstate = {"cur": st_cur}

        def epilogue(oa, hl, br, gts, obs):
            o_sb = oTs.next()
            s.copy("act", o_sb[:], oa[0:65, :], [oa], [o_sb])
            for tb in range(4):
                s.tr(trv[:, tb, :], o_sb[:, tb * 128:(tb + 1) * 128], identf[0:65, 0:65], [o_sb, identf], [trf])
            tv = trs.next()
            s.copy("dve", tv[:], trv[:, :, :], [trf], [tv])
            cf = coef.next()
            s.ts("dve", cf[:], tv[:, :, 64], 1e-30, None, ALU.max, None, [tv], [cf])
            s.recip(cf[:], cf[:], [cf], [cf])
            for tb in range(4):
                s.tt("dve", cf[:, tb:tb + 1], cf[:, tb:tb + 1], gts[tb][:, hl * 3 + br:hl * 3 + br + 1], ALU.mult, [cf, gts[tb]], [cf])
            for tb in range(4):
                dst = obs[tb][:, hl * 64:(hl + 1) * 64]
                if br == 0:
                    s.ts("dve", dst, tv[:, tb, 0:64], cf[:, tb:tb + 1], None, ALU.mult, None, [tv, cf], [obs[tb]])
                else:
                    s.stt("dve", dst, tv[:, tb, 0:64], cf[:, tb:tb + 1], dst, ALU.mult, ALU.add, [tv, cf, obs[tb]], [obs[tb]])

        def hgrn_tile(qt, vvs, sgs, obs):
            W = hw_
            s.act(W[0][:], hzT[:], AF.Sigmoid, [hzT], [W[0]])
            s.ts("dve", W[0][:], W[0][:], lbl[:, 6:7], lbl[:, 5:6], ALU.mult, ALU.add, [W[0], lbl], [W[0]])
            s.ts("dve", W[0][:], W[0][:], 1e-6, None, ALU.max, None, [W[0]], [W[0]])
            s.ts("dve", W[1][:], W[0][:], -1.0, 1.0, ALU.mult, ALU.add, [W[0]], [W[1]])
            s.act(W[2][:], W[0][:], AF.Ln, [W[0]], [W[2]])
            yield
            src, dst = W[2], W[3]
            for sh in (1, 2, 4, 8, 16, 32):
                sv = src[:].rearrange("p (c t) -> p c t", t=64)
                dv = dst[:].rearrange("p (c t) -> p c t", t=64)
                s.copy("pool", dv[:, :, 0:sh], sv[:, :, 0:sh], [src], [dst])
                s.tt("dve", dv[:, :, sh:64], sv[:, :, sh:64], sv[:, :, 0:64 - sh], ALU.add, [src], [dst])
                src, dst = dst, src
                yield
            b = src
            bv = b[:].rearrange("p (c t) -> p c t", t=64)
            hs = hs_.next()
            nh = hs_.next()
            s.copy("dve", hs[:, 0:8], bv[:, :, 31], [b], [hs])
            s.copy("dve", hs[:, 8:16], bv[:, :, 63], [b], [hs])
            s.ts("dve", nh[:, 0:8], hs[:, 0:8], -1.0, None, ALU.mult, None, [hs], [nh])
            s.act(nh[:, 8:16], hs[:, 8:16], AF.Exp, [hs], [nh])
            yield
            for c in range(8):
                cs = slice(c * 64, (c + 1) * 64)
                s.act(W[3][:, cs], b[:, cs], AF.Exp, [b, nh], [W[3]], bias=nh[:, c:c + 1])
                s.act(W[4][:, cs], b[:, cs], AF.Exp, [b, hs], [W[4]], bias=hs[:, c:c + 1], scale=-1.0)
                s.act(W[5][:, cs], b[:, cs], AF.Exp, [b, hs], [W[5]], bias=hs[:, 8 + c:9 + c], scale=-1.0)
                yield
            s.act(W[6][:], b[:], AF.Exp, [b], [W[6]])
            s.tt("pool", hb_[0][:], hqT[:], W[3][:], ALU.mult, [hqT, W[3]], [hb_[0]])
            s.tt("dve", hb_[1][:], W[1][:], W[4][:], ALU.mult, [W[1], W[4]], [hb_[1]])
            s.tt("pool", hb_[2][:], W[1][:], W[5][:], ALU.mult, [W[1], W[5]], [hb_[2]])
            yield
            hq4 = hqT[:].rearrange("p (a c t) -> p a c t", a=4, c=2)
            eb4 = W[6][:].rearrange("p (a c t) -> p a c t", a=4, c=2)
            s.tt("dve", qbpad[:, :, 0, 0:64], hq4[:, :, 0, :], eb4[:, :, 0, :], ALU.mult, [hqT, W[6]], [qbpad])
            s.tt("pool", qbpad[:, :, 1, 64:128], hq4[:, :, 1, :], eb4[:, :, 1, :], ALU.mult, [hqT, W[6]], [qbpad])
            yield
            for tb in range(4):
                blk = slice(tb * 128, (tb + 1) * 128)
                s.mm(hg[:, 0, :], hb_[1][:, blk], hb_[0][:, blk], True, True, [hb_[1], hb_[0]], [hg])
                at = atpad.next()
                a_sb = asb.next()
                s.copy("dve", a_sb[:], hg[:, 0, :], [hg], [a_sb])
                s.tt("pool", at[0:64, 0:64], a_sb[0:64, 0:64], tri2[0:64, :], ALU.mult, [a_sb, tri2], [at])
                s.tt("pool", at[64:128, 64:128], a_sb[64:128, 64:128], tri2[64:128, :], ALU.mult, [a_sb, tri2], [at])
                yield
                s.tr(pst[:, 5, :], hb_[2][:, blk], identb[:], [hb_[2], identb], [pst])
                kb = kdb.next()
                s.copy("act", kb[:], pst[:, 5, :], [pst], [kb])
                yield
                st0 = hstate["cur"]
                s.mm(hg[:, 1, :], qbpad[:, tb, 0, :], st0[:], True, False, [qbpad, st0], [hg])
                s.mm(hg[:, 1, :], at[:], vvs[tb][:], False, True, [at, vvs[tb]], [hg])
                s.mm(hg[:, 2, :], kb[0:64, :], vvs[tb][0:64, :], True, True, [kb, vvs[tb]], [hg])
                yield
                s.stt("dve", state[:], state[:], nh[:, 8 + 2 * tb:9 + 2 * tb], hg[:, 2, :], ALU.mult, ALU.add, [state, nh, hg], [state])
                st1 = stbf.next()
                s.copy("act", st1[:], state[:], [state], [st1])
                yield
                oa_sb = hsq
                s.copy("dve", oa_sb[:], hg[:, 1, :], [hg], [oa_sb])
                s.mm(hg[:, 3, :], qbpad[:, tb, 1, :], st1[:], True, True, [qbpad, st1], [hg])
                s.tt("dve", oa_sb[:], oa_sb[:], hg[:, 3, :], ALU.add, [oa_sb, hg], [oa_sb])
                yield
                s.mm(hg[:, 2, :], kb[64:128, :], vvs[tb][64:128, :], True, True, [kb, vvs[tb]], [hg])
                s.stt("dve", state[:], state[:], nh[:, 9 + 2 * tb:10 + 2 * tb], hg[:, 2, :], ALU.mult, ALU.add, [state, nh, hg], [state])
                st2 = stbf.next()
                s.copy("act", st2[:], state[:], [state], [st2])
                yield
                hstate["cur"] = st2
                t_ = hst.next()
                s.op("act", lambda e, a=(hsq2, oa_sb, t_): e.activation(out=a[0][:], in_=a[1][:], func=AF.Square, accum_out=a[2][:, 0:1]), [oa_sb], [hsq2, t_])
                s.ts("dve", t_[:, 1:2], t_[:, 0:1], 1.0 / 128, EPS, ALU.mult, ALU.add, [t_], [t_])
                s.act(t_[:, 2:3], t_[:, 1:2], AF.Sqrt, [t_], [t_])
                s.recip(t_[:, 3:4], t_[:, 2:3], [t_], [t_])
                yield
                s.stt("dve", obs[tb][:, 128:256], oa_sb[:], t_[:, 3:4], normg[:], ALU.mult, ALU.mult, [oa_sb, t_, normg], [obs[tb]])
                s.tt("pool", obs[tb][:, 128:256], obs[tb][:, 128:256], sgs[tb][:], ALU.mult, [obs[tb], sgs[tb]], [obs[tb]])

        g1w, g2w = 136, 384

        xtk = [Buf(xtile.bufs[0].t) for _ in range(16)]

        def load_x(qt_):
            for kc in range(16):
                s.dma("pool", xtile.bufs[0][:, kc, :], I["xT"][kc * 128:(kc + 1) * 128, qt_ * 512:(qt_ + 1) * 512], writes=[xtk[kc]])

        for qt in range(nqt):
            q0 = qt * 512
            s.mute = 1 not in ST
            xt = xtile.next()
            if qt == 0:
                load_x(0)
            qa = [qaug[j].next() for j in range(4)]
            for j in range(4):
                s.dma("pool", qa[j][64:68, :], I["qpos"][j, :, q0:q0 + 512], writes=[qa[j]])
            slot0 = (4 * qt) % 8
            s.dma("pool", kwin[64:68, slot0:slot0 + 4, :], I["kpos"][:, q0:q0 + 512].rearrange("r (a b) -> r a b", a=4), writes=[kwin])
            if qt > 0:
                s.copy("dve", kcraw[:, 0:16], kcraw[:, 512:528], [kcraw], [kcraw])
                s.copy("dve", vcraw[:, 0:16], vcraw[:, 512:528], [vcraw], [vcraw])

            s.mute = 2 not in ST
            def fm(col0, M):
                p = psr.next()
                for kc in range(16):
                    s.mm(p[0:M, :], wfm[:, kc, col0:col0 + M], xt[:, kc, :], kc == 0, kc == 15, [wfm, xtk[kc]], [p])
                return p
            for j in range(4):
                p = fm(j * 64, 64)
                s.act(qa[j][0:64, :], p[0:64, :], AF.Identity, [p], [qa[j]], scale=0.125)
            p = fm(256, 64)
            s.copy("act", kcraw[:, 16:528], p[0:64, :], [p], [kcraw])
            p = fm(320, 64)
            s.copy("dve", kslc[0:64, q0:q0 + 512], p[0:64, :], [p], [kslc])
            p = fm(384, 64)
            s.copy("act", kwin[0:64, slot0:slot0 + 4, :], p[0:64, :].rearrange("p (a b) -> p a b", a=4), [p], [kwin])
            p = fm(448, 64)
            s.copy("dve", vcraw[:, 16:528], p[0:64, :], [p], [vcraw])
            p = fm(512, 128)
            s.copy("act", hqT[:], p[:], [p], [hqT])
            p = fm(640, 128)
            s.copy("dve", hzT[:], p[:], [p], [hzT])

            s.mute = 3 not in ST
            gts, pcs, vvs, sgs, obs = [], [], [], [], []
            for tb in range(4):
                kt = 4 * qt + tb
                p = psr.next()
                for kc in range(16):
                    s.mm(p[:, 0:g1w], xt[:, kc, tb * 128:(tb + 1) * 128], wtm[:, kc, 0:g1w], kc == 0, kc == 15, [wtm, xtk[kc]], [p])
                sA = stgA.next()
                s.copy("dve", sA[:, 0:g1w], p[:, 0:g1w], [p], [sA])
                s.copy("pool", vslc[:, kt, 0:64], sA[:, 0:64], [sA], [vslc])
                s.copy("pool", vwin[:, kt % 8, 0:64], sA[:, 64:128], [sA], [vwin])
                g_ = gate.next()
                s.act(g_[:], sA[:, 128:136], AF.Sigmoid, [sA], [g_])
                gts.append(g_)
                p = psr.next()
                for kc in range(16):
                    s.mm(p[:, 0:g2w], xt[:, kc, tb * 128:(tb + 1) * 128], wtm[:, kc, g1w:g1w + g2w], kc == 0, kc == 15, [wtm, xtk[kc]], [p])
                pc = pcb.next()
                vv = vvb.next()
                sg = sgb.next()
                sB = stgB.next()
                s.copy("act", sB[:, 0:g2w], p[:, 0:g2w], [p], [sB])
                s.copy("pool", pc[:], sB[:, 0:128], [sB], [pc])
                s.copy("dve", vv[:], sB[:, 128:256], [sB], [vv])
                s.act(sg[:], sB[:, 256:384], AF.Silu, [sB], [sg])
                pcs.append(pc)
                vvs.append(vv)
                sgs.append(sg)
                obs.append(outb.next())

            if qt + 1 < nqt and 1 in ST:
                s.mute = False
                load_x(qt + 1)
            s.mute = 4 not in ST
            c_lo = 32 * qt - 1 if qt > 0 else 0
            c_hi = 32 * qt + 30
            ncb = c_hi - c_lo + 1
            i0 = 0 if qt > 0 else 1
            kview = kcraw[:].rearrange("p (c s) -> p c s", s=16)
            vview = vcraw[:].rearrange("p (c s) -> p c s", s=16)
            p = psr.next()
            for l_ in range(32):
                s.mm(p[0:64, 0:ncb], cw_k[:, l_, :], kview[:, i0 + l_ // 16:i0 + l_ // 16 + ncb, l_ % 16], l_ == 0, l_ == 31, [cw_k, kcraw], [p])
            s.act(kcaug[0:64, c_lo:c_hi + 1], p[0:64, 0:ncb], AF.Identity, [p, cbias], [kcaug], bias=cbias[:, 0:1])
            p = psr.next()
            for l_ in range(32):
                s.mm(p[0:64, 0:ncb], cw_v[:, l_, :], vview[:, i0 + l_ // 16:i0 + l_ // 16 + ncb, l_ % 16], l_ == 0, l_ == 31, [cw_v, vcraw], [p])
            vT_ = vcT.next()
            s.act(vT_[:, 0:ncb], p[0:64, 0:ncb], AF.Identity, [p, cbias], [vT_], bias=cbias[:, 1:2])
            s.tr(pst[0:ncb, 0, 0:64], vT_[:, 0:ncb], identb[0:64, 0:64], [vT_, identb], [pst])
            vt_ = vct.next()
            s.copy("dve", vt_[0:ncb, :], pst[0:ncb, 0, 0:64], [pst], [vt_])
            c = c_lo
            while c <= c_hi:
                ct_ = c // 128
                n_ = min(c_hi + 1, (ct_ + 1) * 128) - c
                s.dma("sp", vcaug[c % 128:c % 128 + n_, ct_, 0:64], vt_[c - c_lo:c - c_lo + n_, :], reads=[vt_], writes=[vcaug])
                c += n_

            hgen = hgrn_tile(qt, vvs, sgs, obs) if 9 in ST else iter(())
            s.mute = 5 not in ST
            nct = (32 * qt + 30) // 128 + 1
            oc_ps = {}
            for j in range(4):
                mine = j in (JA, JA + 1)
                ets = []
                for ct_ in range(nct):
                    dl = qt - 4 * ct_
                    p = psr.next()
                    last = dl > 4
                    s.mm(p[:], kcaug[:, ct_ * 128:(ct_ + 1) * 128], qa[j][:, :], True, last, [kcaug, qa[j]], [p])
                    if not last:
                        s.mm(p[:], identb[:], cmask[:, dl, :], False, True, [identb, cmask], [p])
                    e_ = ET[j % 2][ct_]
                    s.act(e_[:], p[:], AF.Exp, [p], [e_])
                    ets.append(e_)
                if mine:
                    oa = oacc_ps.next()
                    for ct_ in range(nct):
                        s.mm(oa[:, :], vcaug_f[:, ct_ * 72:ct_ * 72 + 128], ets[ct_][:], ct_ == 0, ct_ == nct - 1, [vcaug, ets[ct_]], [oa])
                    oc_ps[j] = oa
                for tb in range(4):
                    for ct_ in range(nct):
                        s.mm(impb[:, tb * 128:(tb + 1) * 128], ets[ct_][:, tb * 128:(tb + 1) * 128], selmap[:, ct_, 0:128], ct_ == 0, ct_ == nct - 1, [ets[ct_], selmap], [impb])
                for tb in range(4):
                    for ct_ in range(nct):
                        s.mm(denb[:, tb:tb + 1], ets[ct_][:, tb * 128:(tb + 1) * 128], selmap[:, ct_, 128:129], ct_ == 0, ct_ == nct - 1, [ets[ct_], selmap], [denb])
                rd = rden.next()
                s.ts("dve", rd[:], denb[:, 0:4], 1e-30, None, ALU.max, None, [denb], [rd])
                s.recip(rd[:], rd[:], [rd], [rd])
                im = imps.next()
                s.copy("act", im[:].rearrange("p a b -> p (a b)"), impb[:], [impb], [im])
                for tb in range(4):
                    if j == 0:
                        s.ts("dve", impacc[:, tb, :], im[:, tb, :], rd[:, tb:tb + 1], None, ALU.mult, None, [im, rd], [impacc])
                    else:
                        s.stt("dve", impacc[:, tb, :], im[:, tb, :], rd[:, tb:tb + 1], impacc[:, tb, :], ALU.mult, ALU.add, [im, rd, impacc], [impacc])
                if mine:
                    epilogue(oc_ps[j], j - JA, 0, gts, obs)

            s.mute = 6 not in ST
            mb = mbT.next()
            sbl = []
            for tb in range(4):
                tbg = 4 * qt + tb
                c0 = 126 - 2 * tbg
                w = selw.next()
                s.tt("dve", w[:], impacc[:, tb, :], keep[:, c0:c0 + 128], ALU.mult, [impacc, keep], [w])
                s.tt("dve", w[:], w[:], addm[:, c0:c0 + 128], ALU.add, [w, addm], [w])
                s.memset("dve", w[:, 0:1], 1e6, [w])
                m = m8.next()
                w2 = selw.next()
                s.op("dve", lambda e, a=(m, w): e.max(out=a[0][:, 0:8], in_=a[1][:]), [w], [m])
                s.op("dve", lambda e, a=(w2, m, w): e.match_replace(out=a[0][:], in_to_replace=a[1][:, 0:8], in_values=a[2][:], imm_value=-3e38), [w, m], [w2])
                s.op("dve", lambda e, a=(m, w2): e.max(out=a[0][:, 8:16], in_=a[1][:]), [w2], [m])
                s.ts("dve", w2[:], w[:], m[:, 15:16], None, ALU.subtract, None, [w, m], [w2])
                s.ts("dve", w2[:], w2[:], 0.0, None, ALU.is_ge, None, [w2], [w2])
                sb_ = selb.next()
                s.ts("dve", sb_[:], w2[:], -NEG, NEG, ALU.mult, ALU.add, [w2], [sb_])
                sbl.append(sb_)

            s.mute = 8 not in ST
            for hl in range(2):
                j = JA + hl
                oa = oacc_ps.next()
                k_lo = max(0, 4 * qt - 4)
                nk = 4 * qt + 4
                pend = None
                for kt in range(k_lo, nk):
                    p = psa.next()
                    s.mm(p[:], kwin[:, kt % 8, :], qa[j][:, :], True, False, [kwin, qa[j]], [p])
                    s.mm(p[:], identb[:], dmask[:, kt - 4 * qt + 4, :], False, True, [identb, dmask], [p])
                    pt = PT.next()
                    s.act(pt[:], p[:], AF.Exp, [p], [pt])
                    if pend is not None:
                        s.mm(oa[:, :], vwin_f[:, (pend[0] % 8) * 72:(pend[0] % 8) * 72 + 128], pend[1][:], pend[0] == k_lo, False, [vwin, pend[1]], [oa])
                    pend = (kt, pt)
                s.mm(oa[:, :], vwin_f[:, (pend[0] % 8) * 72:(pend[0] % 8) * 72 + 128], pend[1][:], pend[0] == k_lo, True, [vwin, pend[1]], [oa])
                epilogue(oa, hl, 2, gts, obs)

            s.mute = 6 not in ST
            for tb in range(4):
                s.tr(pst[:, 1 + tb, :], sbl[tb][:], identb[:], [sbl[tb], identb], [pst])
            s.copy("act", mb[:], pst[:, 1:5, :].rearrange("p a b -> p (a b)"), [pst], [mb])
            mh = [mbh[0].next(), mbh[1].next()]
            s.copy("pool", mh[0][0:64, :], mb[0:64, :], [mb], [mh[0]])
            s.copy("pool", mh[1][64:128, :], mb[64:128, :], [mb], [mh[1]])

            s.mute = 7 not in ST
            for hl in range(2):
                j = JA + hl
                oa = oacc_ps.next()
                nk = 4 * qt + 4
                pend = None
                for kt in range(nk):
                    p = psa.next()
                    s.mm(p[:], kslc[:, kt * 128:(kt + 1) * 128], qa[j][:, :], True, False, [kslc, qa[j]], [p])
                    diag = kt >= 4 * qt
                    s.mm(p[:], E_all[:, kt % 32, :], mh[kt // 32][:], False, not diag, [E_all, mh[kt // 32]], [p])
                    if diag:
                        s.mm(p[:], identb[:], dmask[:, kt - 4 * qt + 4, :], False, True, [identb, dmask], [p])
                    pt = PT.next()
                    s.act(pt[:], p[:], AF.Exp, [p], [pt])
                    if pend is not None:
                        s.mm(oa[:, :], vslc_f[:, pend[0] * 72:pend[0] * 72 + 128], pend[1][:], pend[0] == 0, False, [vslc, pend[1]], [oa])
                    pend = (kt, pt)
                    if kt % 2 == 1:
                        _m = s.mute
                        s.mute = 9 not in ST
                        next(hgen, None)
                        s.mute = _m
                s.mm(oa[:, :], vslc_f[:, pend[0] * 72:pend[0] * 72 + 128], pend[1][:], pend[0] == 0, True, [vslc, pend[1]], [oa])
                epilogue(oa, hl, 1, gts, obs)

            s.mute = 9 not in ST
            for _ in hgen:
                pass

            s.mute = 10 not in ST
            for tb in range(4):
                first = (qt == 0 and tb == 0)
                p = psr.next()
                s.mm(p[:, 0:128], pcs[tb][:], poolP[:, 2 if first else 0, :], True, False, [pcs[tb], poolP], [p])
                s.mm(p[:, 0:128], pc_prev[:], poolP[:, 1, :], False, True, [pc_prev, poolP], [p])
                pt_ = ptb.next()
                s.copy("act", pt_[:], p[:, 0:128], [p], [pt_])
                p2 = psr.next()
                s.mm(p2[:, 0:128], pt_[:], poolw[:], True, True, [pt_, poolw], [p2])
                s.tt("dve", obs[tb][:, 256:384], p2[:, 0:128], poolscale[:], ALU.mult, [p2, poolscale], [obs[tb]])
                pc_prev = pcs[tb]

            s.mute = 11 not in ST
            for tb in range(4):
                s.dma("sp", om[q0 + tb * 128:q0 + (tb + 1) * 128, :], obs[tb][:], reads=[obs[tb]])
        s.emit()
        print("M program: ops", s.n, "sbuf peak", s.peak, "gate/pcb/vvb/sgb/outb offs", [x.bufs[0].t.manual_sbuf_range for x in (gate, pcb, vvb, sgb, outb)], flush=True)
    return nc


def _m_consts():
    cst = {}
    t = np.arange(S)
    cst["kpos"] = np.stack([np.ones(S), np.ones(S), t // 64, t % 64]).astype(np.float32)
    cp = 16 * np.arange(512) + 31
    cst["cpos"] = np.stack([np.ones(512), np.ones(512), cp // 64, cp % 64]).astype(np.float32)
    cl = np.arange(128)[:, None]
    tl = np.arange(512)[None, :]
    cst["cmask"] = np.stack([np.where(16 * cl + 31 - tl <= 512 * dl, 0.0, NEG) for dl in range(5)]).astype(np.float32)
    dm = []
    for rel in range(-4, 4):
        dist = tl - cl - 128 * rel
        dm.append(np.where((dist >= 0) & (dist < 512), 0.0, NEG))
    cst["dmask"] = np.stack(dm).astype(np.float32)
    rr = (np.arange(128) % 64)[:, None, None]
    cst["E_all"] = (rr == 2 * np.arange(32)[None, :, None] + (np.arange(128)[None, None, :] // 64)).astype(np.float32)
    c0 = np.arange(511)[:, None] * 16
    s0 = np.arange(128)[None, :] * 64
    ov = np.clip(np.minimum(c0 + 32, s0 + 64) - np.maximum(c0, s0), 0, None) / 32.0
    sm = np.zeros((512, 129), np.float32)
    sm[:511, :128] = ov
    sm[:, 128] = 1.0
    cst["selmap"] = _c(sm.reshape(4, 128, 129).transpose(1, 0, 2))
    r = np.arange(256)[None, :] - 126
    hi = (np.arange(128)[:, None] >= 64).astype(np.int64)
    forced = (r == hi) | (r == hi - 1)
    future = r >= hi + 1
    cst["keep"] = np.where(forced | future, 0.0, 1.0).astype(np.float32)
    cst["addm"] = np.where(forced, 1e6, np.where(future, -1e30, 0.0)).astype(np.float32)
    cst["ident"] = np.eye(128, dtype=np.float32)
    cst["tri2"] = ((np.arange(128)[:, None] % 64) <= np.arange(64)[None, :]).astype(np.float32)
    return cst


def prep_M(inp, l, x):
    cst = _m_consts()
    w_in = inp["w_in"][l]
    maps = []
    xTs = [_c(x[b].T) for b in range(2)]
    tt_ = np.arange(S)
    for c in range(NCORE):
        b, u = divmod(c, 4)
        g = u // 2
        ja = 2 * (u % 2)
        heads = [ja, ja + 1] + [j for j in range(4) if j not in (ja, ja + 1)]
        m = dict(cst)
        m["xT"] = xTs[b]

        def kvcol(br, kv):
            return OFF_KV + ((br * 2 + kv) * 2 + g) * 64
        cols = []
        for j in heads:
            cols += list(range(OFF_Q + (g * 4 + j) * 64, OFF_Q + (g * 4 + j + 1) * 64))
        for (br, kv) in ((0, 0), (1, 0), (2, 0), (0, 1)):
            cols += list(range(kvcol(br, kv), kvcol(br, kv) + 64))
        cols += list(range(OFF_H + u * 128, OFF_H + (u + 1) * 128))
        cols += list(range(OFF_H + 512 + u * 128, OFF_H + 512 + (u + 1) * 128))
        m["wfm"] = _c(w_in[:, cols].reshape(16, 128, 768).transpose(1, 0, 2))
        cols = list(range(kvcol(1, 1), kvcol(1, 1) + 64)) + list(range(kvcol(2, 1), kvcol(2, 1) + 64))
        gcols = [OFF_G + (g * 4 + j) * 3 + br for j in (ja, ja + 1) for br in range(3)]
        cols += gcols + [gcols[0], gcols[0]]
        cols += list(range(OFF_P + u * 128, OFF_P + (u + 1) * 128))
        cols += list(range(OFF_H + 1024 + u * 128, OFF_H + 1024 + (u + 1) * 128))
        cols += list(range(OFF_H + 1536 + u * 128, OFF_H + 1536 + (u + 1) * 128))
        m["wtm"] = _c(w_in[:, cols].reshape(16, 128, 520).transpose(1, 0, 2))
        qp = np.zeros((4, 4, S), np.float32)
        for i, j in enumerate(heads):
            sl = 2.0 ** (-(g * 4 + j + 1))
            qp[i, 0] = -64.0 * sl * (tt_ // 64)
            qp[i, 1] = -sl * (tt_ % 64)
            qp[i, 2] = 64.0 * sl
            qp[i, 3] = sl
        m["qpos"] = qp
        m["cw_k"] = _c(inp["nsa_cmp_w"][l][0].transpose(1, 0, 2))
        m["cw_v"] = _c(inp["nsa_cmp_w"][l][1].transpose(1, 0, 2))
        m["cp_k"] = _c(inp["nsa_cmp_pos"][l][0].T)
        m["cp_v"] = _c(inp["nsa_cmp_pos"][l][1].T)
        m["lblog"] = _c(inp["hgrn_lb_logits"][:, u * 128:(u + 1) * 128].T)
        m["lbsel"] = np.full((128, 1), float(l), np.float32)
        m["normg"] = _bc(inp["hgrn_norm_g"][l][u * 128:(u + 1) * 128])
        win = (2, 4, 8, 16)[u]
        sI = np.arange(128)[:, None]
        tI = np.arange(128)[None, :]
        P = np.zeros((128, 3, 128), np.float32)
        P[:, 0, :] = np.where((sI > tI - win) & (sI <= tI), 1.0 / win, 0.0) - (sI == tI)
        P[:, 1, :] = np.where(sI >= 128 + tI - win + 1, 1.0 / win, 0.0)
        cnt = np.minimum(tI + 1, win).astype(np.float32)
        P[:, 2, :] = np.where((sI > tI - win) & (sI <= tI), 1.0 / cnt, 0.0) - (sI == tI)
        m["poolP"] = P
        m["poolw"] = _c(inp["pool_w"][l][u])
        m["poolscale"] = _bc(inp["pool_scale"][l][u * 128:(u + 1) * 128])
        maps.append(m)
    return maps


def run_M(inp, l, x):
    if "M" not in _NC_CACHE:
        _NC_CACHE["M"] = build_M()
    maps = prep_M(inp, l, x)
    res = run_bass_kernel_spmd(_NC_CACHE["M"], maps, core_ids=list(range(NCORE)))
    oabc = np.zeros((2, S, 1536), np.float32)
    for c in range(NCORE):
        b, u = divmod(c, 4)
        o = res.results[c]["om"]
        for k in range(3):
            oabc[b, :, 512 * k + 128 * u:512 * k + 128 * (u + 1)] = o[:, 128 * k:128 * (k + 1)]
    return oabc


def kernel(**inputs):
    inp = {k: np.asarray(v) for k, v in inputs.items()}
    x = np.ascontiguousarray(inp["x"], dtype=np.float32)
    for l in range(2):
        oabc = run_M(inp, l, x)
        x = run_R(inp, l, x, oabc)
    return x
```
